# Optimizing a Trainium2 kernel written in Bass

```python
import math
import jax
import jax.numpy as jnp
from jax import lax
import numpy as np

D_MODEL = 1024
BATCH = 4
SEQ = 4096
DEPTH = 2

EPS = 1e-6
CHUNK = 64
NEG_BIG = -1e30
TINY = 1e-30

HG_HEADS = 4
HG_DK = 128
HG_DV = 128
HG_KW = HG_HEADS * HG_DK
HG_VW = HG_HEADS * HG_DV

ML_HEADS = 4
ML_DK = 64
ML_DV = 128
ML_KW = ML_HEADS * ML_DK
ML_VW = ML_HEADS * ML_DV

AT_HEADS = 8
AT_KV_HEADS = 2
AT_GROUP = AT_HEADS // AT_KV_HEADS
AT_HD = 64
WINDOW = 128
N_BUCKETS = 32
MAX_DISTANCE = 128

DN_HEADS = 4
DN_DK = 128
DN_DV = 128
DN_KW = DN_HEADS * DN_DK
DN_VW = DN_HEADS * DN_DV
CONV_K = 4

N_BRANCH = 4
BRANCH_W = 512
D_FF = 4 * D_MODEL

IN_SPLITS = (
    HG_KW, HG_KW, HG_VW, HG_VW,
    ML_KW, ML_KW, ML_VW, ML_HEADS, ML_HEADS, ML_VW,
    AT_HEADS * AT_HD, AT_KV_HEADS * AT_HD, AT_KV_HEADS * AT_HD,
    2 * DN_KW + DN_VW, DN_HEADS, DN_HEADS, DN_VW,
    N_BRANCH * D_MODEL,
)
N_IN = sum(IN_SPLITS)

kernel_name = "hybrid_parallel_gated_mixers"


def _rmsnorm(x, g):
    xf = x.astype(jnp.float32)
    y = xf * lax.rsqrt(jnp.mean(xf * xf, axis=-1, keepdims=True) + EPS)
    return (y * g.astype(jnp.float32)).astype(x.dtype)


def _head_rms(x):
    return x * lax.rsqrt(jnp.mean(x * x, axis=-1, keepdims=True) + EPS)


def _l2norm(x):
    return x * lax.rsqrt(jnp.sum(x * x, axis=-1, keepdims=True) + EPS)


def _split_cols(z, sizes):
    parts, start = [], 0
    for s in sizes:
        parts.append(z[..., start:start + s])
        start += s
    return parts


def _chunk_heads(t, n_heads):
    b, t_len, w = t.shape
    return t.reshape(b, t_len // CHUNK, CHUNK, n_heads, w // n_heads).transpose(1, 0, 3, 2, 4)


def _chunk_gate(t):
    b, t_len, h = t.shape
    return t.reshape(b, t_len // CHUNK, CHUNK, h).transpose(1, 0, 3, 2)


def _unchunk(o):
    n, b, h, c, d = o.shape
    return o.transpose(1, 0, 3, 2, 4).reshape(b, n * c, h, d)


def _causal_dwconv(x, w):
    return lax.conv_general_dilated(
        x, w[:, None, :], window_strides=(1,), padding=((CONV_K - 1, 0),),
        dimension_numbers=('NWC', 'WIO', 'NWC'), feature_group_count=x.shape[-1])


def _t5_bucket(n):
    max_exact = N_BUCKETS // 2
    nf = jnp.maximum(n, max_exact).astype(jnp.float32)
    large = max_exact + (jnp.log(nf / max_exact) / math.log(MAX_DISTANCE / max_exact)
                         * (N_BUCKETS - max_exact)).astype(jnp.int32)
    large = jnp.minimum(large, N_BUCKETS - 1)
    return jnp.where(n < max_exact, n, large)


def hgrn2_mixer(q_pre, f_pre, i_pre, g_pre, lower_bound, out_g):
    f32 = jnp.float32
    b, t_len, _ = q_pre.shape
    z = f_pre.astype(f32)
    lb = lower_bound.astype(f32)
    f = lb + (1.0 - lb) * jax.nn.sigmoid(z)
    log_f = jnp.log(jnp.maximum(f, TINY))
    k = (1.0 - lb) * jax.nn.sigmoid(-z)
    q = jax.nn.silu(q_pre.astype(f32))
    qc = _chunk_heads(q, HG_HEADS)
    kc = _chunk_heads(k, HG_HEADS)
    lfc = _chunk_heads(log_f, HG_HEADS)
    vc = _chunk_heads(i_pre.astype(f32), HG_HEADS)
    causal = jnp.tril(jnp.ones((CHUNK, CHUNK), bool))

    def step(S, inp):
        q_, k_, v_, lf = inp
        cum = jnp.cumsum(lf, axis=2)
        diff = cum[:, :, :, None, :] - cum[:, :, None, :, :]
        dec = jnp.exp(jnp.where(causal[:, :, None], diff, NEG_BIG))
        a = jnp.einsum('bhtd,bhtsd,bhsd->bhts', q_, dec, k_)
        o = (jnp.einsum('bhts,bhsv->bhtv', a, v_)
             + jnp.einsum('bhtd,bhdv->bhtv', q_ * jnp.exp(cum), S))
        last = cum[:, :, -1:, :]
        S = (jnp.exp(last[:, :, 0, :])[..., None] * S
             + jnp.einsum('bhsd,bhsv->bhdv', k_ * jnp.exp(last - cum), v_))
        return S, o

    S0 = jnp.zeros((b, HG_HEADS, HG_DK, HG_DV), f32)
    _, o = lax.scan(step, S0, (qc, kc, vc, lfc))
    o = _head_rms(_unchunk(o)) * out_g.astype(f32).reshape(HG_HEADS, HG_DV)
    return o.reshape(b, t_len, HG_VW) * jax.nn.silu(g_pre.astype(f32))


def mlstm_mixer(q_pre, k_pre, v_pre, i_pre, f_pre, o_pre, if_bias, out_g):
    f32 = jnp.float32
    b, t_len, _ = q_pre.shape
    qc = _chunk_heads(q_pre.astype(f32), ML_HEADS)
    kc = _chunk_heads(k_pre.astype(f32), ML_HEADS) * (ML_DK ** -0.5)
    vc = _chunk_heads(v_pre.astype(f32), ML_HEADS)
    bias = if_bias.astype(f32)
    lic = _chunk_gate(i_pre.astype(f32) + bias[0])
    lfc = _chunk_gate(jax.nn.log_sigmoid(f_pre.astype(f32) + bias[1]))
    causal = jnp.tril(jnp.ones((CHUNK, CHUNK), bool))

    def step(carry, inp):
        Cm, n, m = carry
        q, k, v, li, lf = inp
        cum = jnp.cumsum(lf, axis=-1)
        logd = jnp.where(causal, cum[..., :, None] - cum[..., None, :] + li[..., None, :], NEG_BIG)
        m_inter = cum + m[..., None]
        m_t = jnp.maximum(m_inter, jnp.max(logd, axis=-1))
        s = jnp.einsum('bhtd,bhsd->bhts', q, k) * jnp.exp(logd - m_t[..., None])
        w_inter = jnp.exp(m_inter - m_t)
        num = (jnp.einsum('bhts,bhsv->bhtv', s, v)
               + w_inter[..., None] * jnp.einsum('bhtd,bhdv->bhtv', q, Cm))
        den = jnp.sum(s, axis=-1) + w_inter * jnp.einsum('bhtd,bhd->bht', q, n)
        h = num / jnp.maximum(jnp.abs(den), jnp.exp(-m_t))[..., None]
        m_new = m_t[..., -1]
        w_s = jnp.exp(cum[..., -1:] - cum + li - m_new[..., None])
        decay = jnp.exp(cum[..., -1] + m - m_new)
        Cm = decay[..., None, None] * Cm + jnp.einsum('bhs,bhsd,bhsv->bhdv', w_s, k, v)
        n = decay[..., None] * n + jnp.einsum('bhs,bhsd->bhd', w_s, k)
        return (Cm, n, m_new), h

    init = (jnp.zeros((b, ML_HEADS, ML_DK, ML_DV), f32),
            jnp.zeros((b, ML_HEADS, ML_DK), f32),
            jnp.zeros((b, ML_HEADS), f32))
    _, h = lax.scan(step, init, (qc, kc, vc, lic, lfc))
    h = _head_rms(_unchunk(h)) * out_g.astype(f32).reshape(ML_HEADS, ML_DV)
    return h.reshape(b, t_len, ML_VW) * jax.nn.sigmoid(o_pre.astype(f32))


def swa_mixer(q_pre, k_pre, v_pre, q_g, k_g, sinks, rel_table):
    f32 = jnp.float32
    b, t_len, _ = q_pre.shape
    nb = t_len // WINDOW
    q = _head_rms(q_pre.astype(f32).reshape(b, t_len, AT_HEADS, AT_HD)) * q_g.astype(f32)
    k = _head_rms(k_pre.astype(f32).reshape(b, t_len, AT_KV_HEADS, AT_HD)) * k_g.astype(f32)
    v = v_pre.astype(f32).reshape(b, t_len, AT_KV_HEADS, AT_HD)
    qb = q.reshape(b, nb, WINDOW, AT_KV_HEADS, AT_GROUP, AT_HD)
    kb = k.reshape(b, nb, WINDOW, AT_KV_HEADS, AT_HD)
    vb = v.reshape(b, nb, WINDOW, AT_KV_HEADS, AT_HD)
    pad = ((0, 0), (1, 0), (0, 0), (0, 0), (0, 0))
    kw = jnp.concatenate([jnp.pad(kb, pad)[:, :-1], kb], axis=2)
    vw = jnp.concatenate([jnp.pad(vb, pad)[:, :-1], vb], axis=2)
    logits = jnp.einsum('bnqkgd,bnskd->bnkgqs', qb, kw) * (AT_HD ** -0.5)
    qpos = jnp.arange(WINDOW)[:, None] + WINDOW
    kpos = jnp.arange(2 * WINDOW)[None, :]
    dist = qpos - kpos
    in_window = (dist >= 0) & (dist < WINDOW)
    bias = rel_table.astype(f32)[_t5_bucket(jnp.maximum(dist, 0))]
    bias = bias.transpose(2, 0, 1).reshape(AT_KV_HEADS, AT_GROUP, WINDOW, 2 * WINDOW)
    first = (jnp.arange(nb) == 0)[:, None, None]
    valid = in_window[None] & ~(first & (kpos < WINDOW)[None])
    logits = jnp.where(valid[None, :, None, None], logits + bias, NEG_BIG)
    sink = sinks.astype(f32).reshape(AT_KV_HEADS, AT_GROUP)[None, None, :, :, None, None]
    mx = jnp.maximum(jnp.max(logits, axis=-1, keepdims=True), sink)
    p = jnp.exp(logits - mx)
    denom = jnp.sum(p, axis=-1, keepdims=True) + jnp.exp(sink - mx)
    out = jnp.einsum('bnkgqs,bnskd->bnqkgd', p / denom, vw)
    return out.reshape(b, t_len, AT_HEADS * AT_HD)


def gated_deltanet_mixer(qkv_pre, beta_pre, a_pre, z_pre, conv_w, a_log, dt_bias, out_g):
    f32 = jnp.float32
    b, t_len, _ = qkv_pre.shape
    qkv = jax.nn.silu(_causal_dwconv(qkv_pre.astype(f32), conv_w.astype(f32)))
    q, k, v = _split_cols(qkv, (DN_KW, DN_KW, DN_VW))
    qc = _l2norm(_chunk_heads(q, DN_HEADS)) * (DN_DK ** -0.5)
    kc = _l2norm(_chunk_heads(k, DN_HEADS))
    vc = _chunk_heads(v, DN_HEADS)
    beta = _chunk_gate(jax.nn.sigmoid(beta_pre.astype(f32)))
    g = -jnp.exp(a_log.astype(f32)) * jax.nn.softplus(a_pre.astype(f32) + dt_bias.astype(f32))
    gam = jnp.cumsum(_chunk_gate(g), axis=-1)
    incl = jnp.tril(jnp.ones((CHUNK, CHUNK), bool))
    strict = jnp.tril(jnp.ones((CHUNK, CHUNK), bool), k=-1)
    decay = jnp.exp(jnp.where(incl, gam[..., :, None] - gam[..., None, :], NEG_BIG))
    kk = jnp.einsum('nbhtd,nbhsd->nbhts', kc, kc)
    lower = jnp.where(strict, beta[..., :, None] * kk * decay, 0.0) + jnp.eye(CHUNK, dtype=f32)
    rhs = jnp.concatenate([vc * beta[..., None], kc * (beta * jnp.exp(gam))[..., None]], axis=-1)
    sol = lax.linalg.triangular_solve(lower, rhs, left_side=True, lower=True, unit_diagonal=True)
    u, w = sol[..., :DN_DV], sol[..., DN_DV:]
    attn = jnp.einsum('nbhtd,nbhsd->nbhts', qc, kc) * decay
    q_dec = qc * jnp.exp(gam)[..., None]
    k_dec = kc * jnp.exp(gam[..., -1:] - gam)[..., None]
    g_last = jnp.exp(gam[..., -1])

    def step(S, inp):
        u_, w_, a_, qd, kd, gl = inp
        v_new = u_ - jnp.einsum('bhtk,bhkv->bhtv', w_, S)
        o = jnp.einsum('bhtk,bhkv->bhtv', qd, S) + jnp.einsum('bhts,bhsv->bhtv', a_, v_new)
        S = gl[..., None, None] * S + jnp.einsum('bhsk,bhsv->bhkv', kd, v_new)
        return S, o

    S0 = jnp.zeros((b, DN_HEADS, DN_DK, DN_DV), f32)
    _, o = lax.scan(step, S0, (u, w, attn, q_dec, k_dec, g_last))
    o = _head_rms(_unchunk(o)) * out_g.astype(f32)
    return o.reshape(b, t_len, DN_VW) * jax.nn.silu(z_pre.astype(f32))


def setup_inputs(seed: int = 0) -> dict:
    key = jax.random.key(seed)
    ks = jax.random.split(key, 24)

    def nrm(k, shape, scale):
        return jax.random.normal(k, shape, jnp.float32) * scale

    x = nrm(ks[0], (BATCH, SEQ, D_MODEL), 1.0)
    norm_mix_g = 1.0 + nrm(ks[1], (DEPTH, D_MODEL), 0.02)
    w_in = nrm(ks[2], (DEPTH, D_MODEL, N_IN), D_MODEL ** -0.5)
    hgrn_lb_table = nrm(ks[3], (DEPTH, HG_KW), 0.5)
    hgrn_out_g = 1.0 + nrm(ks[4], (DEPTH, HG_VW), 0.02)
    mlstm_if_bias = (jnp.array([-1.0, 3.0], jnp.float32)[None, :, None]
                     + nrm(ks[5], (DEPTH, 2, ML_HEADS), 0.3))
    mlstm_out_g = 1.0 + nrm(ks[6], (DEPTH, ML_VW), 0.02)
    attn_q_norm_g = 1.0 + nrm(ks[7], (DEPTH, AT_HD), 0.02)
    attn_k_norm_g = 1.0 + nrm(ks[8], (DEPTH, AT_HD), 0.02)
    attn_sinks = nrm(ks[9], (DEPTH, AT_HEADS), 0.5)
    rel_bias_table = nrm(ks[10], (N_BUCKETS, AT_HEADS), 0.5)
    dn_conv_w = nrm(ks[11], (DEPTH, CONV_K, 2 * DN_KW + DN_VW), CONV_K ** -0.5)
    dn_a_log = jnp.log(jax.random.uniform(ks[12], (DEPTH, DN_HEADS), jnp.float32, 1.0, 16.0))
    dt = jnp.exp(jax.random.uniform(ks[13], (DEPTH, DN_HEADS), jnp.float32,
                                    math.log(1e-3), math.log(1e-1)))
    dn_dt_bias = dt + jnp.log(-jnp.expm1(-dt))
    dn_out_g = 1.0 + nrm(ks[14], (DEPTH, DN_DV), 0.02)
    w_branch = nrm(ks[15], (DEPTH, N_BRANCH, BRANCH_W, D_MODEL), BRANCH_W ** -0.5)
    w_out = nrm(ks[16], (DEPTH, D_MODEL, D_MODEL), D_MODEL ** -0.5)
    norm_mlp_g = 1.0 + nrm(ks[17], (DEPTH, D_MODEL), 0.02)
    w_up = nrm(ks[18], (DEPTH, D_MODEL, D_FF), D_MODEL ** -0.5)
    w_down = nrm(ks[19], (DEPTH, D_FF, D_MODEL), D_FF ** -0.5)
    return {
        'x': x, 'norm_mix_g': norm_mix_g, 'w_in': w_in,
        'hgrn_lb_table': hgrn_lb_table, 'hgrn_out_g': hgrn_out_g,
        'mlstm_if_bias': mlstm_if_bias, 'mlstm_out_g': mlstm_out_g,
        'attn_q_norm_g': attn_q_norm_g, 'attn_k_norm_g': attn_k_norm_g,
        'attn_sinks': attn_sinks, 'rel_bias_table': rel_bias_table,
        'dn_conv_w': dn_conv_w, 'dn_a_log': dn_a_log, 'dn_dt_bias': dn_dt_bias,
        'dn_out_g': dn_out_g, 'w_branch': w_branch, 'w_out': w_out,
        'norm_mlp_g': norm_mlp_g, 'w_up': w_up, 'w_down': w_down,
    }


def reference(x, norm_mix_g, w_in, hgrn_lb_table, hgrn_out_g, mlstm_if_bias, mlstm_out_g,
              attn_q_norm_g, attn_k_norm_g, attn_sinks, rel_bias_table,
              dn_conv_w, dn_a_log, dn_dt_bias, dn_out_g,
              w_branch, w_out, norm_mlp_g, w_up, w_down):
    b, t_len, _ = x.shape
    lb_p = jax.nn.softmax(hgrn_lb_table.astype(jnp.float32), axis=0)
    lower_bounds = jnp.cumsum(lb_p, axis=0) - lb_p[0]
    for l in range(DEPTH):
        h = _rmsnorm(x, norm_mix_g[l])
        z = jnp.einsum('btd,dn->btn', h, w_in[l])
        (hq, hf, hi, hg, mq, mk, mv, mi, mf, mo, aq, ak, av,
         dqkv, db, da, dz, gate_pre) = _split_cols(z, IN_SPLITS)
        o_a = hgrn2_mixer(hq, hf, hi, hg, lower_bounds[l], hgrn_out_g[l])
        o_b = mlstm_mixer(mq, mk, mv, mi, mf, mo, mlstm_if_bias[l], mlstm_out_g[l])
        o_c = swa_mixer(aq, ak, av, attn_q_norm_g[l], attn_k_norm_g[l], attn_sinks[l], rel_bias_table)
        o_d = gated_deltanet_mixer(dqkv, db, da, dz, dn_conv_w[l], dn_a_log[l], dn_dt_bias[l], dn_out_g[l])
        branches = jnp.stack([o_a, o_b, o_c, o_d], axis=2).astype(x.dtype)
        proj = jnp.einsum('btnw,nwd->btnd', branches, w_branch[l])
        gates = jax.nn.sigmoid(gate_pre).reshape(b, t_len, N_BRANCH, D_MODEL)
        merged = jnp.sum(gates * proj, axis=2)
        x = x + merged @ w_out[l]
        h2 = _rmsnorm(x, norm_mlp_g[l])
        x = x + jnp.square(jax.nn.relu(h2 @ w_up[l])) @ w_down[l]
    return x
```

```python
import types
import numpy as np
from contextlib import ExitStack
import concourse.bass as bass
import concourse.mybir as mybir
from concourse.bass_utils import run_bass_kernel_spmd

F32 = mybir.dt.float32
BF16 = mybir.dt.bfloat16
AF = mybir.ActivationFunctionType
ALU = mybir.AluOpType
AX = mybir.AxisListType
ENGS = ['pe', 'act', 'dve', 'pool', 'sp']

D = 1024
NIN = 10512
EPS = 1e-6


def _freeze(fn):
    if fn is None or fn.__closure__ is None:
        return fn
    cells = []
    for c in fn.__closure__:
        try:
            cells.append(types.CellType(c.cell_contents))
        except ValueError:
            cells.append(c)
    return types.FunctionType(fn.__code__, fn.__globals__, fn.__name__, fn.__defaults__, tuple(cells))


class Prog:
    def __init__(self, nc, nd=8):
        self.nc = nc
        self.ND = nd
        self.lists = {e: [] for e in ENGS}
        self.cnt = {e: 0 for e in ENGS}
        self.waited = {e: {} for e in ENGS}
        self.W = {}
        self.Rd = {}
        self.dma_n = {e: 0 for e in ENGS}
        self.semkeys = set()
        self.st = ExitStack()
        self.ntens = 0
        self.alias = {}

    def sb(self, shape, dt, name=None):
        self.ntens += 1
        return self.st.enter_context(self.nc.sbuf_tensor(name or f"t{self.ntens}", list(shape), dt))

    def ps(self, shape, dt=F32, name=None):
        self.ntens += 1
        return self.st.enter_context(self.nc.psum_tensor(name or f"p{self.ntens}", list(shape), dt))

    def _deps(self, eng, reads, writes):
        deps = {}

        def add(d):
            for sk, v in d.items():
                if deps.get(sk, 0) < v:
                    deps[sk] = v
        for r in reads:
            add(self.W.get(r, {}))
        for w in writes:
            add(self.W.get(w, {}))
            add(self.Rd.get(w, {}))
        out = []
        for sk, v in deps.items():
            if sk == ('c', 'pe') and eng == 'pe':
                continue
            if self.waited[eng].get(sk, 0) >= v:
                continue
            self.waited[eng][sk] = v
            out.append((sk, v))
        return out

    def _rec(self, tok, reads, writes):
        sk, v = tok
        for r in reads:
            d = self.Rd.setdefault(r, {})
            d[sk] = max(d.get(sk, 0), v)
        for w in writes:
            d = self.W.setdefault(w, {})
            d[sk] = max(d.get(sk, 0), v)

    def _x(self, keys):
        out = []
        for k in keys:
            if isinstance(k, (list, tuple)):
                out.extend(self._x(k))
            elif k in self.alias:
                out.extend(self.alias[k])
            else:
                out.append(k)
        return out

    def op(self, eng, fn, reads=(), writes=(), inc=True):
        reads = self._x(reads)
        writes = self._x(writes)
        waits = self._deps(eng, reads, writes)
        sk = ('c', eng)
        tok = (sk, self.cnt[eng] + 1)
        if inc:
            self.cnt[eng] += 1
        self.semkeys.add(sk)
        self.lists[eng].append((waits, _freeze(fn), sk if inc else None, 1))
        self._rec(tok, reads, writes)

    def dma(self, q, out, in_, reads=(), writes=(), **kw):
        reads = self._x(reads)
        writes = self._x(writes)
        j = self.dma_n[q]
        self.dma_n[q] += 1
        slot = j % self.ND
        val = 16 * (j // self.ND + 1)
        sk = ('d', q, slot)
        self.semkeys.add(sk)
        waits = self._deps(q, reads, writes)
        if j >= self.ND and self.waited[q].get(sk, 0) < val - 16:
            self.waited[q][sk] = val - 16
            waits.append((sk, val - 16))
        self.lists[q].append((waits, (lambda e, o=out, i=in_, k=kw: e.dma_start(out=o, in_=i, **k)), sk, 16))
        self._rec((sk, val), reads, writes)

    def final_wait(self, eng, keys):
        keys = self._x(keys)
        waits = self._deps(eng, keys, ())
        self.lists[eng].append((waits, None, None, 0))

    def emit(self):
        nc = self.nc
        st = self.st
        sems = {}
        for sk in sorted(self.semkeys, key=str):
            sems[sk] = st.enter_context(nc.semaphore("s_" + "_".join(map(str, sk))))
        block = st.enter_context(nc.Block())
        lists = self.lists

        def run(name, e):
            for waits, fn, sk, incv in lists[name]:
                for wsk, v in waits:
                    e.wait_ge(sems[wsk], v)
                if fn is None:
                    continue
                ins = fn(e)
                if sk is not None:
                    ins.then_inc(sems[sk], incv)

        @block.tensor
        def _(e):
            run('pe', e)

        @block.scalar
        def _(e):
            run('act', e)

        @block.vector
        def _(e):
            run('dve', e)

        @block.gpsimd
        def _(e):
            run('pool', e)

        @block.sync
        def _(e):
            run('sp', e)
        st.close()


def _t5_bucket_np(n):
    max_exact = 16
    nf = np.maximum(n, max_exact).astype(np.float32)
    large = max_exact + (np.log(nf / max_exact) / np.log(np.float32(128 / max_exact)) * 16).astype(np.int32)
    large = np.minimum(large, 31)
    return np.where(n < max_exact, n, large)


def host_consts():
    c = {}
    s = np.arange(128)[:, None]
    t = np.arange(128)[None, :]
    c['tri_incl'] = (s <= t).astype(np.float32)
    c['tri_strict'] = (s < t).astype(np.float32)
    c['hgmask'] = ((s <= t) & (s // 32 == t // 32)).astype(np.float32)
    c['negmask'] = np.where(s <= t, 0.0, -1e30).astype(np.float32)
    c['ident'] = np.eye(128, dtype=np.float32)
    sel = np.zeros((128, 128), np.float32)
    sel[127, :] = 1.0
    c['sel_last'] = sel
    rm = np.zeros((128, 4), np.float32)
    for j in range(4):
        rm[32 * j:32 * j + 32, j] = 1.0
    c['rowm'] = rm
    rs = np.ones((128, 128), np.float32)
    rs[:, 0::32] = 0.0
    c['resetm'] = rs
    selh = np.zeros((4, 4, 128), np.float32)
    for h in range(4):
        selh[h, h, :] = 1.0
    c['selh'] = selh.reshape(4, 512)
    c['ones'] = np.ones((128, 128), np.float32)
    bk = _t5_bucket_np(np.arange(128))
    oh = np.zeros((32, 128), np.float32)
    oh[bk, np.arange(128)] = 1.0
    c['bias_oh'] = oh
    ab = np.zeros((128, 384), np.float32)
    for dd in range(128):
        ab[dd, 255 - dd] = 1.0
    c['antiband'] = ab
    return c


CONST_SHAPES = {
    'tri_incl': (128, 128), 'tri_strict': (128, 128), 'hgmask': (128, 128), 'negmask': (128, 128),
    'ident': (128, 128), 'sel_last': (128, 128), 'rowm': (128, 4), 'resetm': (128, 128),
    'selh': (4, 512), 'ones': (128, 128), 'bias_oh': (32, 128), 'antiband': (128, 384),
}

PARAM_SHAPES = {
    'norm_mix_g': (2, 1024), 'w_in': (2, 1024, NIN), 'hgrn_lb_table': (2, 128, 4), 'hgrn_out_g': (2, 512),
    'mlstm_if_bias': (2, 8), 'mlstm_out_g': (2, 512), 'attn_q_norm_g': (2, 64, 1), 'attn_k_norm_g': (2, 64, 1),
    'attn_sinks': (2, 8), 'rel_bias_table': (32, 8), 'dn_conv_w': (2, 4, 128, 12), 'dn_a_log': (2, 4),
    'dn_dt_bias': (2, 4), 'dn_alog_row': (4, 2), 'dn_dtb_row': (4, 2), 'dn_out_g': (2, 128), 'w_branch': (2, 4, 512, 1024), 'w_out': (2, 1024, 1024),
    'norm_mlp_g': (2, 1024), 'w_up': (2, 1024, 4096), 'w_down': (2, 4096, 1024),
}

C_HQ, C_HF, C_HI, C_HG = 0, 512, 1024, 1536
C_MQ, C_MK, C_MV, C_MI, C_MF, C_MO = 2048, 2304, 2560, 3072, 3076, 3080
C_AQ, C_AK, C_AV = 3592, 4104, 4232
C_DQKV, C_DB, C_DA, C_DZ = 4360, 5896, 5900, 5904
C_GATE = 6416


def build(T, L=2, enable=(1, 1, 1, 1), dbg=None):
    nc = bass.Bass("TRN2", target_bir_lowering=False)
    NTILES = T // 128
    x_in = nc.dram_tensor("x", [T, D], F32, kind="ExternalInput").ap()
    y_out = nc.dram_tensor("y", [T, D], F32, kind="ExternalOutput").ap()
    prm = {k: nc.dram_tensor(k, list(s), F32, kind="ExternalInput").ap() for k, s in PARAM_SHAPES.items()}
    cst = {k: nc.dram_tensor("c_" + k, list(s), F32, kind="ExternalInput").ap() for k, s in CONST_SHAPES.items()}
    dbg_out = {}
    p = Prog(nc)
    sb, ps = p.sb, p.ps

    def load_const(name, dt=F32, q='sp'):
        shp = CONST_SHAPES[name]
        t_ = sb(shp, dt, "k_" + name + ("_b" if dt == BF16 else ""))
        p.dma(q, t_[:], cst[name], writes=['k_' + name])
        return t_
    tri_incl = load_const('tri_incl')
    tri_strict = load_const('tri_strict')
    hgmask = load_const('hgmask')
    negmask = load_const('negmask')
    ident_f = load_const('ident')
    ident_b = load_const('ident', BF16, 'pool')
    sel_last = load_const('sel_last')
    rowm = load_const('rowm')
    resetm = load_const('resetm')
    selh = load_const('selh')
    ones_f = load_const('ones')
    ones_b = load_const('ones', BF16, 'pool')
    KC = ['k_tri_incl', 'k_tri_strict', 'k_hgmask', 'k_negmask', 'k_ident', 'k_sel_last', 'k_rowm', 'k_resetm',
          'k_selh', 'k_ones']

    gmix = sb([128, L, 1024], BF16, "gmix")
    gmlp = sb([128, L, 1024], BF16, "gmlp")
    hog = sb([128, L, 512], BF16, "hog")
    mog = sb([128, L, 512], BF16, "mog")
    dog = sb([128, L, 128], F32, "dog")
    mifb = sb([128, L, 8], F32, "mifb")
    sinks = sb([128, L, 8], F32, "sinks")
    esink = sb([128, L, 8], F32, "esink")
    alog = sb([128, L, 4], F32, "alog")
    nexpa = sb([128, L, 4], F32, "nexpa")
    dtb = sb([128, L, 4], F32, "dtb")
    lbt = sb([128, 2, 4], F32, "lbt")
    lb = sb([128, 2, 4], F32, "lb")
    oml = sb([128, 2, 4], F32, "oml")
    convw = sb([128, L, 4, 12], F32, "convw")
    qkg = sb([64, L, 2], F32, "qkg")
    gk8 = sb([64, L], F32, "gk8")
    dtbrow = sb([4, 2], F32, "dtbrow")
    nexparow = sb([4, 2], F32, "nexparow")
    for l in range(L):
        p.dma('pool', gmix[:, l, :], prm['norm_mix_g'][l:l + 1, :].partition_broadcast(128), writes=['prm'])
        p.dma('pool', gmlp[:, l, :], prm['norm_mlp_g'][l:l + 1, :].partition_broadcast(128), writes=['prm'])
        p.dma('pool', hog[:, l, :], prm['hgrn_out_g'][l:l + 1, :].partition_broadcast(128), writes=['prm'])
        p.dma('pool', mog[:, l, :], prm['mlstm_out_g'][l:l + 1, :].partition_broadcast(128), writes=['prm'])
        p.dma('sp', dog[:, l, :], prm['dn_out_g'][l:l + 1, :].partition_broadcast(128), writes=['prm'])
        p.dma('sp', mifb[:, l, :], prm['mlstm_if_bias'][l:l + 1, :].partition_broadcast(128), writes=['prm'])
        p.dma('sp', sinks[:, l, :], prm['attn_sinks'][l:l + 1, :].partition_broadcast(128), writes=['prm'])
        p.dma('sp', alog[:, l, :], prm['dn_a_log'][l:l + 1, :].partition_broadcast(128), writes=['prm'])
        p.dma('sp', dtb[:, l, :], prm['dn_dt_bias'][l:l + 1, :].partition_broadcast(128), writes=['prm'])
        for j in range(4):
            p.dma('sp', convw[:, l, j, :], prm['dn_conv_w'][l, j], writes=['prm'])
        p.dma('sp', qkg[:, l, 0:1], prm['attn_q_norm_g'][l], writes=['prm'])
        p.dma('sp', qkg[:, l, 1:2], prm['attn_k_norm_g'][l], writes=['prm'])
    for l2 in range(2):
        p.dma('sp', lbt[:, l2, :], prm['hgrn_lb_table'][l2], writes=['prm'])
    p.dma('sp', dtbrow[:], prm['dn_dtb_row'], writes=['dtbrow'])
    p.dma('sp', nexparow[:], prm['dn_alog_row'], writes=['nexparow'])
    p.op('act', lambda e: e.activation(out=nexparow[:], in_=nexparow[:], func=AF.Exp), reads=['nexparow'], writes=['nexparow'])
    p.op('dve', lambda e: e.tensor_scalar(out=nexparow[:], in0=nexparow[:], scalar1=-1.0, scalar2=None, op0=ALU.mult),
         reads=['nexparow'], writes=['nexparow'])
    p.op('dve', lambda e: e.memset(lb[:], 0.0), writes=['lb'])
    p.op('dve', lambda e: e.tensor_sub(out=lb[:, 1, :], in0=lbt[:, 1, :], in1=lbt[:, 0, :]), reads=['prm', 'lb'],
         writes=['lb'])
    p.op('act', lambda e: e.activation(out=lb[:, 1, :], in_=lb[:, 1, :], func=AF.Sigmoid), reads=['lb'], writes=['lb'])
    p.op('dve', lambda e: e.tensor_scalar(out=oml[:], in0=lb[:], scalar1=-1.0, scalar2=1.0, op0=ALU.mult, op1=ALU.add),
         reads=['lb'], writes=['oml'])
    p.op('act', lambda e: e.activation(out=esink[:], in_=sinks[:], func=AF.Exp), reads=['prm'], writes=['esink'])
    p.op('act', lambda e: e.activation(out=nexpa[:], in_=alog[:], func=AF.Exp), reads=['prm'], writes=['nexpa'])
    p.op('dve', lambda e: e.tensor_scalar(out=nexpa[:], in0=nexpa[:], scalar1=-1.0, scalar2=None, op0=ALU.mult),
         reads=['nexpa'], writes=['nexpa'])
    p.op('dve', lambda e: e.tensor_tensor(out=gk8[:], in0=qkg[:, :, 0], in1=qkg[:, :, 1], op=ALU.mult), reads=['prm'],
         writes=['gk8'])
    p.op('dve', lambda e: e.tensor_scalar(out=gk8[:], in0=gk8[:], scalar1=0.125, scalar2=None, op0=ALU.mult),
         reads=['gk8'], writes=['gk8'])

    PS = [ps([128, 512], F32, f"PS{i}") for i in range(7)]
    PT = ps([128, 1024], BF16, "PSTR")

    def K(i, lo=0, hi=512):
        return [f'PS{i}.bank']
    for i in range(7):
        p.alias[f'PS{i}'] = K(i)
    p.alias['PS5b'] = K(5, 128, 256)
    p.alias['PS5c'] = K(5, 256, 384)
    p.alias['PS6b'] = K(6, 128, 256)
    p.alias['PS6c'] = K(6, 256, 384)
    p.alias['PS6d'] = K(6, 384, 512)
    for i in range(5):
        p.alias[f'scr{i}'] = [f'scr.{i}']
    p.alias['h_q'] = ['scr.0']
    p.alias['h_f'] = ['scr.1']
    p.alias['h_k'] = ['scr.2']
    p.alias['h_cum'] = ['scr.3']
    p.alias['h_e'] = ['scr.4']
    p.alias['a_z'] = ['scr.0', 'scr.1', 'scr.2']
    p.alias['a_sq'] = ['scr.2', 'scr.3', 'scr.4']
    p.alias['d_y'] = ['scr.0', 'scr.1', 'scr.2']
    EB = sb([128, 2, 8, 128], F32, "EB")
    relt = sb([32, 8], F32, "relt")
    boh = sb([32, 128], F32, "boh")
    aband = sb([128, 384], F32, "aband")
    vecE = sb([128, 8], F32, "vecE")
    p.dma('sp', relt[:], prm['rel_bias_table'], writes=['relt'])
    p.dma('sp', boh[:], cst['bias_oh'], writes=['boh'])
    p.dma('sp', aband[:], cst['antiband'], writes=['aband'])
    p.op('pe', lambda e: e.matmul(PS[0][:, 0:8], lhsT=boh[:], rhs=relt[:], start=True, stop=True), reads=['boh', 'relt'],
         writes=K(0))
    p.op('act', lambda e: e.activation(out=vecE[:], in_=PS[0][:, 0:8], func=AF.Exp), reads=K(0), writes=['vecE'])
    for blk in range(2):
        for t0 in range(0, 128, 64):
            pst = PS[1]
            for tt in range(64):
                off = 255 - (t0 + tt + (128 if blk == 0 else 0))
                p.op('pe', (lambda e, off=off, tt=tt: e.matmul(pst[:, tt * 8:(tt + 1) * 8], lhsT=aband[:, off:off + 128],
                                                               rhs=vecE[:], start=True, stop=True)),
                     reads=['aband', 'vecE'], writes=K(1), inc=(tt == 63))
            p.op('dve', (lambda e, blk=blk, t0=t0: e.tensor_copy(
                out=EB[:, blk, :, t0:t0 + 64], in_=pst[:].rearrange("p (t g) -> p g t", g=8))),
                reads=K(1), writes=['EB'])

    xt = sb([128, 1024], F32, "xt")
    hbf = sb([128, 1024], BF16, "hbf")
    hT = sb([128, 8, 128], BF16, "hT")
    NWB = 2
    wbuf = [sb([128, 8, 520], BF16, f"wbuf{i}") for i in range(NWB)]
    wb_n = [0]
    sm = sb([128, 64], F32, "small")
    obf = sb([128, 4, 512], BF16, "obf")
    oT = sb([128, 16, 128], BF16, "oT")
    gates = sb([128, 4096], BF16, "gates")
    merged = sb([128, 1024], F32, "merged")
    mbf = sb([128, 1024], BF16, "mbf")
    mT = sb([128, 8, 128], BF16, "mT")
    uT = sb([128, 32, 128], BF16, "uT")
    tmpA = sb([128, 512], F32, "tmpA")
    tmpB = sb([128, 512], F32, "tmpB")
    tmpC = sb([128, 512], F32, "tmpC")
    junk = sb([128, 1024], BF16, "junk")

    scr = sb([128, 2560], F32, "scr")

    class V:
        def __init__(self, ap):
            self.ap = ap

        def __getitem__(self, k):
            return self.ap[k]
    h_q = V(scr[:, 0:512].rearrange("p (a b) -> p a b", a=4))
    h_f = V(scr[:, 512:1024].rearrange("p (a b) -> p a b", a=4))
    h_k = V(scr[:, 1024:1536].rearrange("p (a b) -> p a b", a=4))
    h_cum = V(scr[:, 1536:2048].rearrange("p (a b) -> p a b", a=4))
    h_e = V(scr[:, 2048:2560].rearrange("p (a b) -> p a b", a=4))
    h_qp = sb([128, 4, 128], BF16, "h_qp")
    h_kp = sb([128, 4, 128], BF16, "h_kp")
    h_kpp = sb([128, 4, 128], BF16, "h_kpp")
    h_edec = sb([128, 4, 4], F32, "h_edec")
    h_v = sb([128, 512], BF16, "h_v")
    h_g = sb([128, 512], F32, "h_g")
    h_AT = sb([128, 128], BF16, "h_AT")
    h_kj = sb([128, 4, 128], BF16, "h_kj")
    h_qj = sb([128, 4, 4, 128], BF16, "h_qj")
    h_S = [sb([128, L, 4, 128], F32, "h_S")]
    h_Sb = sb([128, L, 4, 128], BF16, "h_Sb")

    m_q = sb([64, 4, 128], BF16, "m_q")
    m_kT = sb([64, 4, 128], BF16, "m_kT")
    m_k = sb([128, 256], BF16, "m_k")
    m_v = sb([128, 512], F32, "m_v")
    m_if = sb([128, 8], F32, "m_if")
    m_o = sb([128, 512], F32, "m_o")
    m_va = sb([128, 4, 130], BF16, "m_va")
    m_AT = sb([128, 128], BF16, "m_AT")
    m_C = sb([64, L, 4, 130], F32, "m_C")
    m_Cb = sb([64, L, 4, 130], BF16, "m_Cb")

    a_q = sb([64, 8, 128], BF16, "a_q")
    a_z = V(scr[0:64, 0:1280].rearrange("p (a b) -> p a b", a=10))
    a_sq = V(scr[0:64, 1280:2560].rearrange("p (a b) -> p a b", a=10))
    a_kT = sb([64, L, 2, 2, 128], BF16, "a_kT")
    a_v = sb([128, L, 2, 2, 66], BF16, "a_v")
    a_P = sb([128, 128], F32, "a_P")
    a_PT = sb([128, 2, 128], BF16, "a_PT")

    d_x = sb([128, L, 12, 132], BF16, "d_x")
    d_y = V(scr[:, 0:1536].rearrange("p (a b) -> p a b", a=12))
    d_sq = sb([128, 128], BF16, "d_sq")
    d_qT = sb([128, 4, 128], BF16, "d_qT")
    d_kT = sb([128, 4, 128], BF16, "d_kT")
    d_vT = sb([128, 4, 128], BF16, "d_vT")
    d_v = sb([128, 4, 128], F32, "d_v")
    d_k = sb([128, 4, 128], BF16, "d_k")
    d_ba = sb([128, 8], F32, "d_ba")
    d_arow = sb([4, 128], F32, "d_arow")
    d_grow = sb([4, 128], F32, "d_grow")
    d_z = sb([128, 512], F32, "d_z")
    d_E1 = sb([128, 128], F32, "d_E1")
    d_E1s = sb([128, 128], F32, "d_E1s")
    d_P = [sb([128, 128], F32, f"d_P{i}") for i in range(2)]
    d_PT = [sb([128, 128], F32, f"d_PT{i}") for i in range(2)]
    d_TT = sb([128, 128], F32, "d_TT")
    d_r = sb([128, 128], F32, "d_r")
    d_vn = sb([128, 128], BF16, "d_vn")
    d_vc = sb([128, 128], BF16, "d_vc")
    d_aT = sb([128, 128], BF16, "d_aT")
    d_o1 = sb([128, 128], F32, "d_o1")
    d_S = sb([128, L, 4, 128], F32, "d_S")
    d_Sb = sb([128, L, 4, 128], BF16, "d_Sb")

    for (t_, k) in [(h_S[0], 'h_S'), (h_Sb, 'h_Sb'), (m_C, 'm_C'), (m_Cb, 'm_Cb'), (d_S, 'd_S'), (d_Sb, 'd_Sb'),
                    (d_x, 'd_x'), (h_qj, 'h_qj'), (a_kT, 'a_kT'), (a_v, 'a_v')]:
        p.op('pool', (lambda e, t_=t_: e.memset(t_[:], 0.0)), writes=[k])

    def wload(src_ap, ncol, key_src=None):
        i = wb_n[0] % NWB
        wb_n[0] += 1
        key = f'wbuf{i}'
        p.dma('pool', wbuf[i][:, :, 0:ncol], src_ap.rearrange("(kc p) n -> p kc n", p=128), writes=[key])
        return wbuf[i], key

    def wload_rows(src_ap, key_src=None):
        i = wb_n[0] % NWB
        wb_n[0] += 1
        key = f'wbuf{i}'
        view = wbuf[i][:].rearrange("p a b -> p (a b)")[:, 0:4096].rearrange("p (r n) -> p r n", r=4)
        p.dma('pool', view, src_ap.rearrange("(r p) n -> p r n", p=128), writes=[key])
        return view, key

    def proj_fm(wb, wkey, c0, ncol, ps_ap, pskey):
        for kc in range(8):
            p.op('pe', (lambda e, kc=kc: e.matmul(ps_ap, lhsT=wb[:, kc, c0:c0 + ncol], rhs=hT[:, kc, :],
                                                  start=(kc == 0), stop=(kc == 7))),
                 reads=[wkey, 'hT'], writes=[pskey], inc=(kc == 7))

    def proj_tm(wb, wkey, c0, ncol, ps_ap, pskey):
        for kc in range(8):
            p.op('pe', (lambda e, kc=kc: e.matmul(ps_ap, lhsT=hT[:, kc, :], rhs=wb[:, kc, c0:c0 + ncol],
                                                  start=(kc == 0), stop=(kc == 7))),
                 reads=[wkey, 'hT'], writes=[pskey], inc=(kc == 7))

    def rmsnorm_to_T(src, gt, dstT, dstkey, l):
        p.op('act', lambda e: e.activation(out=junk[:], in_=src[:], func=AF.Square, accum_out=sm[:, 0:1]),
             reads=['xt'], writes=['junk', 'sm0'])
        p.op('act', lambda e: e.activation(out=sm[:, 1:2], in_=sm[:, 0:1], func=AF.Ln, scale=1.0 / 1024, bias=epsb[:, 0:1]),
             reads=['sm0', 'epsb'], writes=['sm1'])
        p.op('act', lambda e: e.activation(out=sm[:, 2:3], in_=sm[:, 1:2], func=AF.Exp, scale=-0.5),
             reads=['sm1'], writes=['sm2'])
        p.op('dve', lambda e: e.scalar_tensor_tensor(out=hbf[:], in0=src[:], scalar=sm[:, 2:3], in1=gt[:, l, :],
                                                     op0=ALU.mult, op1=ALU.mult),
             reads=['xt', 'sm2', 'prm'], writes=['hbf'])
        for kc in range(8):
            p.op('pe', (lambda e, kc=kc: e.transpose(out=PT[:, kc * 128:(kc + 1) * 128], in_=hbf[:, kc * 128:(kc + 1) * 128],
                                                     identity=ident_b[:])),
                 reads=['hbf', 'k_ident'], writes=['PT'], inc=(kc == 7))
        p.op('act', lambda e: e.copy(out=dstT[:].rearrange("p a b -> p (a b)"), in_=PT[:]), reads=['PT'], writes=[dstkey])

    epsb = sb([128, 2], F32, "epsb")
    p.op('dve', lambda e: e.memset(epsb[:, 0:1], EPS), writes=['epsb'])
    p.op('dve', lambda e: e.memset(epsb[:, 1:2], 1.0), writes=['epsb'])

    def head_norm_gate(src_ap, srckey, srcreads, gtile_ap, gate_ap, gatekey, out_ap, smc):
        p.op('act', lambda e: e.activation(out=junk[:, 0:128], in_=src_ap, func=AF.Square, accum_out=sm[:, smc:smc + 1]),
             reads=[srckey] + srcreads, writes=['junk', f'sm{smc}'])
        p.op('act', lambda e: e.activation(out=sm[:, smc + 1:smc + 2], in_=sm[:, smc:smc + 1], func=AF.Ln, scale=1.0 / 128,
                                           bias=epsb[:, 0:1]), reads=[f'sm{smc}', 'epsb'], writes=[f'sm{smc+1}'])
        p.op('act', lambda e: e.activation(out=sm[:, smc + 2:smc + 3], in_=sm[:, smc + 1:smc + 2], func=AF.Exp, scale=-0.5),
             reads=[f'sm{smc+1}'], writes=[f'sm{smc+2}'])
        p.op('dve', lambda e: e.scalar_tensor_tensor(out=tmpC[:, 0:128], in0=src_ap, scalar=sm[:, smc + 2:smc + 3],
                                                     in1=gtile_ap, op0=ALU.mult, op1=ALU.mult),
             reads=[srckey, f'sm{smc+2}', 'prm'] + srcreads, writes=['tmpC'])
        p.op('dve', lambda e: e.tensor_tensor(out=out_ap, in0=tmpC[:, 0:128], in1=gate_ap, op=ALU.mult),
             reads=['tmpC', gatekey], writes=['obf'])

    for ti in range(NTILES):
        p.dma('sp', xt[:], x_in[ti * 128:(ti + 1) * 128, :], writes=['xt'])
        for l in range(L):
            par = ti % 2
            W_in = prm['w_in'][l]
            rmsnorm_to_T(xt, gmix, hT, 'hT', l)
            if not all(enable):
                p.op('pool', lambda e: e.memset(obf[:], 0.0), writes=['obf'])

            if enable[0]:
                wb, wk = wload(W_in[:, C_HQ:C_HQ + 512], 512)
                for h in range(4):
                    proj_fm(wb, wk, h * 128, 128, PS[0][:, h * 128:(h + 1) * 128], 'PS0')
                p.op('act', lambda e: e.activation(out=h_q[:].rearrange("p a b -> p (a b)"), in_=PS[0][:], func=AF.Silu),
                     reads=['PS0'], writes=['h_q'])
                wb, wk = wload(W_in[:, C_HF:C_HF + 512], 512)
                for h in range(4):
                    proj_fm(wb, wk, h * 128, 128, PS[1][:, h * 128:(h + 1) * 128], 'PS1')
                p.op('act', lambda e: e.activation(out=h_f[:].rearrange("p a b -> p (a b)"), in_=PS[1][:], func=AF.Sigmoid),
                     reads=['PS1'], writes=['h_f'])
                for h in range(4):
                    p.op('dve', (lambda e, h=h: e.tensor_scalar(out=h_f[:, h, :], in0=h_f[:, h, :], scalar1=oml[:, l, h:h + 1],
                                                                scalar2=lb[:, l, h:h + 1], op0=ALU.mult, op1=ALU.add)),
                         reads=['h_f', 'oml', 'lb'], writes=['h_f'])
                hf2 = h_f[:].rearrange("p a b -> p (a b)")
                hk2 = h_k[:].rearrange("p a b -> p (a b)")
                hc2 = h_cum[:].rearrange("p a b -> p (a b)")
                he2 = h_e[:].rearrange("p a b -> p (a b)")
                hq2 = h_q[:].rearrange("p a b -> p (a b)")
                p.op('dve', lambda e: e.tensor_scalar(out=hk2, in0=hf2, scalar1=-1.0, scalar2=1.0, op0=ALU.mult, op1=ALU.add),
                     reads=['h_f'], writes=['h_k'])
                p.op('act', lambda e: e.activation(out=hf2, in_=hf2, func=AF.Ln), reads=['h_f', 'h_k'], writes=['h_f'])
                for h in range(4):
                    p.op('dve', (lambda e, h=h: e.tensor_tensor_scan(out=h_cum[:, h, :], data0=resetm[:], data1=h_f[:, h, :],
                                                                     initial=0.0, op0=ALU.mult, op1=ALU.add)),
                         reads=['h_f', 'k_resetm'], writes=['h_cum'])
                p.op('act', lambda e: e.activation(out=he2, in_=hc2, func=AF.Exp), reads=['h_cum'], writes=['h_e'])
                p.op('dve', lambda e: e.tensor_tensor(out=h_qp[:].rearrange("p a b -> p (a b)"), in0=hq2, in1=he2, op=ALU.mult),
                     reads=['h_q', 'h_e'], writes=['h_qp'])
                p.op('act', lambda e: e.activation(out=he2, in_=hc2, func=AF.Exp, scale=-1.0), reads=['h_cum', 'h_qp'],
                     writes=['h_e'])
                p.op('dve', lambda e: e.tensor_tensor(out=h_kp[:].rearrange("p a b -> p (a b)"), in0=hk2, in1=he2, op=ALU.mult),
                     reads=['h_k', 'h_e'], writes=['h_kp'])
                cl = h_cum[:].rearrange("p a (j i) -> p a j i", i=32)[:, :, :, 31]
                p.op('act', lambda e: e.activation(out=h_edec[:], in_=cl, func=AF.Exp), reads=['h_cum'], writes=['h_edec'])
                p.op('dve', lambda e: e.tensor_tensor(
                    out=h_e[:].rearrange("p a (j i) -> p a j i", i=32),
                    in0=h_cum[:].rearrange("p a (j i) -> p a j i", i=32)[:, :, :, 31:32].broadcast_to([128, 4, 4, 32]),
                    in1=h_cum[:].rearrange("p a (j i) -> p a j i", i=32), op=ALU.subtract),
                    reads=['h_cum', 'h_kp'], writes=['h_e'])
                p.op('act', lambda e: e.activation(out=he2, in_=he2, func=AF.Exp), reads=['h_e'], writes=['h_e'])
                p.op('dve', lambda e: e.tensor_tensor(out=h_kpp[:].rearrange("p a b -> p (a b)"), in0=hk2, in1=he2, op=ALU.mult),
                     reads=['h_k', 'h_e'], writes=['h_kpp'])
                wb, wk = wload(W_in[:, C_HI:C_HI + 512], 512)
                proj_tm(wb, wk, 0, 512, PS[0][:], 'PS0')
                p.op('act', lambda e: e.copy(out=h_v[:], in_=PS[0][:]), reads=['PS0'], writes=['h_v'])
                wb, wk = wload(W_in[:, C_HG:C_HG + 512], 512)
                proj_tm(wb, wk, 0, 512, PS[1][:], 'PS1')
                p.op('act', lambda e: e.activation(out=h_g[:], in_=PS[1][:], func=AF.Silu), reads=['PS1'], writes=['h_g'])
                for h in range(4):
                    p.op('pe', (lambda e, h=h: e.matmul(PS[2][:, 0:128], lhsT=h_kp[:, h, :], rhs=h_qp[:, h, :], start=True,
                                                        stop=True)), reads=['h_kp', 'h_qp'], writes=['PS2'])
                    p.op('dve', lambda e: e.tensor_tensor(out=h_AT[:], in0=PS[2][:, 0:128], in1=hgmask[:], op=ALU.mult),
                         reads=['PS2', 'k_hgmask'], writes=['h_AT'])
                    p.op('pe', (lambda e, h=h: e.transpose(out=PT[:, 0:128], in_=h_kpp[:, h, :], identity=ident_b[:])),
                         reads=['h_kpp', 'k_ident'], writes=['PT'])
                    for j in range(4):
                        p.op('dve', (lambda e, j=j: e.tensor_scalar(out=h_kj[:, j, :], in0=PT[:, 0:128], scalar1=rowm[:, j:j + 1],
                                                                    scalar2=None, op0=ALU.mult)),
                             reads=['PT', 'k_rowm'], writes=['h_kj'])
                        p.op('act', (lambda e, h=h, j=j: e.copy(out=h_qj[:, h, j, 32 * j:32 * j + 32],
                                                                in_=h_qp[:, h, 32 * j:32 * j + 32])),
                             reads=['h_qp'], writes=['h_qj'])
                    p.op('pe', (lambda e, h=h: e.matmul(PS[3][:, 0:128], lhsT=h_AT[:], rhs=h_v[:, h * 128:(h + 1) * 128],
                                                        start=True, stop=False)), reads=['h_AT', 'h_v'], writes=['PS3'])
                    for j in range(4):
                        p.op('pe', (lambda e, h=h, j=j: e.matmul(PS[3][:, 0:128], lhsT=h_qj[:, h, j, :], rhs=h_Sb[:, l, h, :],
                                                                 start=False, stop=(j == 3))),
                             reads=['h_qj', 'h_Sb'], writes=['PS3'])
                        p.op('pe', (lambda e, h=h, j=j: e.matmul(PS[4][:, 0:128], lhsT=h_kj[:, j, :],
                                                                 rhs=h_v[:, h * 128:(h + 1) * 128], start=True, stop=True)),
                             reads=['h_kj', 'h_v'], writes=['PS4'])
                        p.op('dve', (lambda e, h=h, j=j: e.scalar_tensor_tensor(
                            out=h_S[0][:, l, h, :], in0=h_S[0][:, l, h, :], scalar=h_edec[:, h, j:j + 1], in1=PS[4][:, 0:128],
                            op0=ALU.mult, op1=ALU.add)), reads=['h_S', 'h_edec', 'PS4'], writes=['h_S'])
                        p.op('act', (lambda e, h=h: e.copy(out=h_Sb[:, l, h, :], in_=h_S[0][:, l, h, :])),
                             reads=['h_S'], writes=['h_Sb'])
                    head_norm_gate(PS[3][:, 0:128], 'PS3', [], hog[:, l, h * 128:(h + 1) * 128], h_g[:, h * 128:(h + 1) * 128],
                                   'h_g', obf[:, 0, h * 128:(h + 1) * 128], 4)

            if enable[1]:
                wb, wk = wload(W_in[:, C_MQ:C_MQ + 512], 512)
                for h in range(4):
                    proj_fm(wb, wk, h * 64, 64, PS[0][0:64, h * 128:(h + 1) * 128], 'PS0')
                    proj_fm(wb, wk, 256 + h * 64, 64, PS[1][0:64, h * 128:(h + 1) * 128], 'PS1')
                p.op('act', lambda e: e.copy(out=m_q[:].rearrange("p a b -> p (a b)"), in_=PS[0][0:64, :]), reads=['PS0'],
                     writes=['m_q'])
                p.op('act', lambda e: e.mul(out=m_kT[:].rearrange("p a b -> p (a b)"), in_=PS[1][0:64, :], mul=0.125),
                     reads=['PS1'], writes=['m_kT'])
                proj_tm(wb, wk, 256, 256, PS[2][:, 0:256], 'PS2')
                p.op('act', lambda e: e.mul(out=m_k[:], in_=PS[2][:, 0:256], mul=0.125), reads=['PS2'], writes=['m_k'])
                wb, wk = wload(W_in[:, C_MV:C_MV + 520], 520)
                proj_tm(wb, wk, 0, 512, PS[0][:], 'PS0')
                p.op('act', lambda e: e.copy(out=m_v[:], in_=PS[0][:]), reads=['PS0'], writes=['m_v'])
                proj_tm(wb, wk, 512, 8, PS[1][:, 0:8], 'PS1')
                p.op('dve', lambda e: e.tensor_tensor(out=m_if[:], in0=PS[1][:, 0:8], in1=mifb[:, l, :], op=ALU.add),
                     reads=['PS1', 'prm'], writes=['m_if'])
                wb, wk = wload(W_in[:, C_MO:C_MO + 512], 512)
                proj_tm(wb, wk, 0, 512, PS[2][:], 'PS2')
                p.op('act', lambda e: e.activation(out=m_o[:], in_=PS[2][:], func=AF.Sigmoid), reads=['PS2'], writes=['m_o'])
                p.op('act', lambda e: e.activation(out=sm[:, 8:12], in_=m_if[:, 4:8], func=AF.Exp, scale=-1.0),
                     reads=['m_if'], writes=['sm8'])
                p.op('act', lambda e: e.activation(out=sm[:, 8:12], in_=sm[:, 8:12], func=AF.Ln, bias=epsb[:, 1:2]),
                     reads=['sm8', 'epsb'], writes=['sm8'])
                p.op('pe', lambda e: e.matmul(PS[3][:, 0:4], lhsT=tri_incl[:], rhs=sm[:, 8:12], start=True, stop=True),
                     reads=['sm8', 'k_tri_incl'], writes=['PS3'])
                p.op('act', lambda e: e.copy(out=sm[:, 12:16], in_=PS[3][:, 0:4]), reads=['PS3'], writes=['sm12'])
                p.op('dve', lambda e: e.tensor_tensor(out=sm[:, 16:20], in0=PS[3][:, 0:4], in1=m_if[:, 0:4], op=ALU.add),
                     reads=['PS3', 'm_if'], writes=['sm16'])
                p.op('act', lambda e: e.activation(out=sm[:, 16:20], in_=sm[:, 16:20], func=AF.Exp), reads=['sm16'],
                     writes=['sm16'])
                p.op('act', lambda e: e.activation(out=sm[:, 20:24], in_=sm[:, 12:16], func=AF.Exp, scale=-1.0),
                     reads=['sm12'], writes=['sm20'])
                p.op('pe', lambda e: e.matmul(PS[3][0:64, 8:12], lhsT=sel_last[:, 0:64], rhs=sm[:, 12:16], start=True, stop=True),
                     reads=['sm12', 'k_sel_last'], writes=['PS3'])
                p.op('act', lambda e: e.activation(out=sm[0:64, 24:28], in_=PS[3][0:64, 8:12], func=AF.Exp, scale=-1.0),
                     reads=['PS3'], writes=['sm24'])
                for h in range(4):
                    p.op('dve', (lambda e, h=h: e.tensor_scalar(out=m_va[:, h, 0:128], in0=m_v[:, h * 128:(h + 1) * 128],
                                                                scalar1=sm[:, 16 + h:17 + h], scalar2=None, op0=ALU.mult)),
                         reads=['m_v', 'sm16'], writes=['m_va'])
                    p.op('dve', (lambda e, h=h: e.tensor_copy(out=m_va[:, h, 128:129], in_=sm[:, 16 + h:17 + h])),
                         reads=['sm16'], writes=['m_va'])
                for h in range(4):
                    p.op('pe', (lambda e, h=h: e.matmul(PS[4][:, 0:128], lhsT=m_kT[:, h, :], rhs=m_q[:, h, :], start=True,
                                                        stop=True)), reads=['m_kT', 'm_q'], writes=['PS4'])
                    p.op('dve', lambda e: e.tensor_tensor(out=m_AT[:], in0=PS[4][:, 0:128], in1=tri_incl[:], op=ALU.mult),
                         reads=['PS4', 'k_tri_incl'], writes=['m_AT'])
                    p.op('pe', (lambda e, h=h: e.matmul(PS[5][:, 0:129], lhsT=m_AT[:], rhs=m_va[:, h, 0:129], start=True,
                                                        stop=False)), reads=['m_AT', 'm_va'], writes=['PS5'], inc=False)
                    p.op('pe', (lambda e, h=h: e.matmul(PS[5][:, 0:129], lhsT=m_q[:, h, :], rhs=m_Cb[:, l, h, 0:129], start=False,
                                                        stop=True)), reads=['m_q', 'm_Cb'], writes=['PS5'])
                    p.op('dve', (lambda e, h=h: e.tensor_tensor(out=sm[:, 28:29], in0=PS[5][:, 128:129], in1=sm[:, 20 + h:21 + h],
                                                                op=ALU.mult)), reads=['PS5', 'sm20'], writes=['sm28'])
                    p.op('dve', lambda e: e.scalar_tensor_tensor(out=sm[:, 31:32], in0=sm[:, 28:29], scalar=-1.0, in1=sm[:, 28:29],
                                                                 op0=ALU.mult, op1=ALU.max), reads=['sm28'], writes=['sm31'])
                    p.op('dve', lambda e: e.tensor_scalar(out=sm[:, 28:29], in0=sm[:, 31:32], scalar1=1.0, scalar2=None,
                                                          op0=ALU.max), reads=['sm31'], writes=['sm28'])
                    p.op('dve', lambda e: e.reciprocal(out=sm[:, 29:30], in_=sm[:, 28:29]), reads=['sm28'], writes=['sm29'])
                    p.op('dve', (lambda e, h=h: e.tensor_tensor(out=sm[:, 30:31], in0=sm[:, 29:30], in1=sm[:, 20 + h:21 + h],
                                                                op=ALU.mult)), reads=['sm29', 'sm20'], writes=['sm30'])
                    p.op('act', lambda e: e.activation(out=tmpA[:, 0:128], in_=PS[5][:, 0:128], func=AF.Copy, scale=sm[:, 30:31]),
                         reads=['PS5', 'sm30'], writes=['tmpA'])
                    p.op('pe', (lambda e, h=h: e.matmul(PS[6][0:64, 0:129], lhsT=m_k[:, h * 64:(h + 1) * 64], rhs=m_va[:, h, 0:129],
                                                        start=True, stop=True)), reads=['m_k', 'm_va'], writes=['PS6'])
                    p.op('dve', (lambda e, h=h: e.tensor_tensor(out=m_C[:, l, h, 0:129], in0=m_C[:, l, h, 0:129],
                                                                in1=PS[6][0:64, 0:129], op=ALU.add)),
                         reads=['m_C', 'PS6'], writes=['m_C'])
                    p.op('dve', (lambda e, h=h: e.tensor_scalar(out=m_C[:, l, h, 0:129], in0=m_C[:, l, h, 0:129],
                                                                scalar1=sm[0:64, 24 + h:25 + h], scalar2=None, op0=ALU.mult)),
                         reads=['m_C', 'sm24'], writes=['m_C'])
                    p.op('act', (lambda e, h=h: e.copy(out=m_Cb[:, l, h, 0:129], in_=m_C[:, l, h, 0:129])), reads=['m_C'],
                         writes=['m_Cb'])
                    head_norm_gate(tmpA[:, 0:128], 'tmpA', [], mog[:, l, h * 128:(h + 1) * 128], m_o[:, h * 128:(h + 1) * 128],
                                   'm_o', obf[:, 1, h * 128:(h + 1) * 128], 32)

            if enable[2]:
                wb, wk = wload(W_in[:, C_AQ:C_AQ + 512], 512)
                wb2, wk2 = wload(W_in[:, C_AK:C_AK + 256], 256)
                for g in range(8):
                    proj_fm(wb, wk, g * 64, 64, PS[g // 4][0:64, (g % 4) * 128:(g % 4 + 1) * 128], f'PS{g // 4}')
                for kv in range(2):
                    proj_fm(wb2, wk2, kv * 64, 64, PS[2][0:64, kv * 128:(kv + 1) * 128], 'PS2')
                az2 = a_z[:].rearrange("p a b -> p (a b)")
                asq2 = a_sq[:].rearrange("p a b -> p (a b)")
                p.op('act', lambda e: e.copy(out=az2[:, 0:512], in_=PS[0][0:64, :]), reads=['PS0'], writes=['a_z'])
                p.op('act', lambda e: e.copy(out=az2[:, 512:1024], in_=PS[1][0:64, :]), reads=['PS1'], writes=['a_z'])
                p.op('act', lambda e: e.copy(out=az2[:, 1024:1280], in_=PS[2][0:64, 0:256]), reads=['PS2'], writes=['a_z'])
                p.op('dve', lambda e: e.tensor_tensor(out=asq2, in0=az2, in1=az2, op=ALU.mult), reads=['a_z'], writes=['a_sq'])
                for i3 in range(3):
                    w3 = 512 if i3 < 2 else 256
                    p.op('pe', (lambda e, i3=i3, w3=w3: e.matmul(PS[3][0:64, 0:w3], lhsT=ones_f[0:64, 0:64],
                                                                 rhs=asq2[:, i3 * 512:i3 * 512 + w3], start=True, stop=True)),
                         reads=['a_sq', 'k_ones'], writes=['PS3'])
                    p.op('act', (lambda e, i3=i3, w3=w3: e.activation(out=asq2[:, i3 * 512:i3 * 512 + w3], in_=PS[3][0:64, 0:w3],
                                                                      func=AF.Ln, scale=1.0 / 64, bias=epsb[0:64, 0:1])),
                         reads=['PS3', 'epsb'], writes=['a_sq'])
                p.op('act', lambda e: e.activation(out=asq2, in_=asq2, func=AF.Exp, scale=-0.5), reads=['a_sq'], writes=['a_sq'])
                p.op('dve', lambda e: e.tensor_tensor(out=a_q[:].rearrange("p a b -> p (a b)"), in0=az2[:, 0:1024],
                                                      in1=asq2[:, 0:1024], op=ALU.mult), reads=['a_z', 'a_sq'], writes=['a_q'])
                for kv in range(2):
                    p.op('dve', (lambda e, kv=kv: e.scalar_tensor_tensor(
                        out=a_kT[:, l, kv, par, :], in0=a_z[:, 8 + kv, :], scalar=gk8[:, l:l + 1], in1=a_sq[:, 8 + kv, :],
                        op0=ALU.mult, op1=ALU.mult)), reads=['a_z', 'a_sq', 'gk8'], writes=['a_kT'])
                proj_tm(wb2, wk2, 128, 128, PS[4][:, 0:128], 'PS4')
                for kv in range(2):
                    p.op('act', (lambda e, kv=kv: e.copy(out=a_v[:, l, par, kv, 0:64], in_=PS[4][:, kv * 64:(kv + 1) * 64])),
                         reads=['PS4'], writes=['a_v'])
                    p.op('dve', (lambda e, kv=kv: e.memset(a_v[:, l, par, kv, 64:65], 1.0)), writes=['a_v'])
                for g in range(8):
                    kv = g // 4
                    blks = [1] if ti == 0 else [0, 1]
                    for bi, blk in enumerate(blks):
                        slot = par if blk == 1 else 1 - par
                        p.op('pe', (lambda e, g=g, kv=kv, slot=slot: e.matmul(PS[5][:, 0:128], lhsT=a_kT[:, l, kv, slot, :],
                                                                              rhs=a_q[:, g, :], start=True, stop=True)),
                             reads=['a_kT', 'a_q'], writes=['PS5'])
                        p.op('act', lambda e: e.activation(out=a_P[:], in_=PS[5][:, 0:128], func=AF.Exp), reads=['PS5'],
                             writes=['a_P'])
                        p.op('dve', (lambda e, g=g, blk=blk: e.tensor_tensor(out=a_PT[:, blk, :], in0=a_P[:], in1=EB[:, blk, g, :],
                                                                             op=ALU.mult)),
                             reads=['a_P', 'EB'], writes=['a_PT'])
                    for bi, blk in enumerate(blks):
                        slot = par if blk == 1 else 1 - par
                        p.op('pe', (lambda e, kv=kv, slot=slot, blk=blk, bi=bi: e.matmul(
                            PS[6][:, 0:65], lhsT=a_PT[:, blk, :], rhs=a_v[:, l, slot, kv, 0:65], start=(bi == 0),
                            stop=(bi == len(blks) - 1))), reads=['a_PT', 'a_v'], writes=['PS6'], inc=(bi == len(blks) - 1))
                    p.op('dve', (lambda e, g=g: e.tensor_tensor(out=sm[:, 40:41], in0=PS[6][:, 64:65], in1=esink[:, l, g:g + 1],
                                                                op=ALU.add)), reads=['PS6', 'esink'], writes=['sm40'])
                    p.op('dve', lambda e: e.reciprocal(out=sm[:, 41:42], in_=sm[:, 40:41]), reads=['sm40'], writes=['sm41'])
                    p.op('dve', (lambda e, g=g: e.tensor_scalar(out=obf[:, 2, g * 64:(g + 1) * 64], in0=PS[6][:, 0:64],
                                                                scalar1=sm[:, 41:42], scalar2=None, op0=ALU.mult)),
                         reads=['PS6', 'sm41'], writes=['obf'])

            if enable[3]:
                for c3 in range(3):
                    wb, wk = wload(W_in[:, C_DQKV + c3 * 512:C_DQKV + (c3 + 1) * 512], 512)
                    for c4 in range(4):
                        proj_fm(wb, wk, c4 * 128, 128, PS[c3][:, c4 * 128:(c4 + 1) * 128], f'PS{c3}')
                    p.op('act', (lambda e, c3=c3: e.copy(out=d_x[:, l, c3 * 4:(c3 + 1) * 4, 3:131],
                                                         in_=PS[c3][:].rearrange("p (a b) -> p a b", a=4))),
                         reads=[f'PS{c3}'], writes=['d_x'])
                for cc in range(12):
                    p.op('dve', (lambda e, cc=cc: e.tensor_scalar(out=d_y[:, cc, :], in0=d_x[:, l, cc, 0:128],
                                                                  scalar1=convw[:, l, 0, cc:cc + 1], scalar2=None, op0=ALU.mult)),
                         reads=['d_x', 'prm'], writes=['d_y'])
                    for j in range(1, 4):
                        p.op('dve', (lambda e, cc=cc, j=j: e.scalar_tensor_tensor(
                            out=d_y[:, cc, :], in0=d_x[:, l, cc, j:j + 128], scalar=convw[:, l, j, cc:cc + 1], in1=d_y[:, cc, :],
                            op0=ALU.mult, op1=ALU.add)), reads=['d_x', 'prm', 'd_y'], writes=['d_y'])
                p.op('pool', lambda e: e.tensor_copy(out=d_x[:, l, :, 0:3], in_=d_x[:, l, :, 128:131]), reads=['d_x', 'd_y'],
                     writes=['d_x'])
                dy2 = d_y[:].rearrange("p a b -> p (a b)")
                p.op('act', lambda e: e.activation(out=dy2, in_=dy2, func=AF.Silu), reads=['d_y'], writes=['d_y'])
                for cc in range(8):
                    p.op('dve', (lambda e, cc=cc: e.tensor_tensor(out=tmpA[:, 0:128], in0=d_y[:, cc, :], in1=d_y[:, cc, :],
                                                                  op=ALU.mult)), reads=['d_y'], writes=['tmpA'])
                    p.op('pe', lambda e: e.matmul(PS[3][:, 0:128], lhsT=ones_f[:], rhs=tmpA[:, 0:128], start=True, stop=True),
                         reads=['tmpA', 'k_ones'], writes=['PS3'])
                    p.op('act', lambda e: e.activation(out=tmpB[:, 0:128], in_=PS[3][:, 0:128], func=AF.Ln, bias=epsb[:, 0:1]),
                         reads=['PS3', 'epsb'], writes=['tmpB'])
                    p.op('act', lambda e: e.activation(out=tmpB[:, 0:128], in_=tmpB[:, 0:128], func=AF.Exp, scale=-0.5),
                         reads=['tmpB'], writes=['tmpB'])
                    if cc < 4:
                        p.op('dve', (lambda e, cc=cc: e.scalar_tensor_tensor(
                            out=d_qT[:, cc, :], in0=d_y[:, cc, :], scalar=float(128 ** -0.5), in1=tmpB[:, 0:128],
                            op0=ALU.mult, op1=ALU.mult)), reads=['d_y', 'tmpB'], writes=['d_qT'])
                    else:
                        p.op('dve', (lambda e, cc=cc: e.tensor_tensor(out=d_kT[:, cc - 4, :], in0=d_y[:, cc, :],
                                                                      in1=tmpB[:, 0:128], op=ALU.mult)),
                             reads=['d_y', 'tmpB'], writes=['d_kT'])
                p.op('dve', lambda e: e.tensor_copy(out=d_vT[:], in_=d_y[:, 8:12, :]), reads=['d_y'], writes=['d_vT'])
                for h in range(4):
                    p.op('pe', (lambda e, h=h: e.transpose(out=PT[:, h * 128:(h + 1) * 128], in_=d_vT[:, h, :],
                                                           identity=ident_b[:])), reads=['d_vT', 'k_ident'], writes=['PT'],
                         inc=False)
                    p.op('pe', (lambda e, h=h: e.transpose(out=PT[:, 512 + h * 128:512 + (h + 1) * 128], in_=d_kT[:, h, :],
                                                           identity=ident_b[:])), reads=['d_kT', 'k_ident'], writes=['PT'],
                         inc=(h == 3))
                p.op('act', lambda e: e.copy(out=d_v[:].rearrange("p a b -> p (a b)"), in_=PT[:, 0:512]), reads=['PT'],
                     writes=['d_v'])
                p.op('act', lambda e: e.copy(out=d_k[:].rearrange("p a b -> p (a b)"), in_=PT[:, 512:1024]), reads=['PT'],
                     writes=['d_k'])
                wb, wk = wload(W_in[:, C_DB:C_DB + 520], 520)
                proj_tm(wb, wk, 0, 8, PS[0][:, 0:8], 'PS0')
                p.op('act', lambda e: e.copy(out=d_ba[:], in_=PS[0][:, 0:8]), reads=['PS0'], writes=['d_ba'])
                proj_fm(wb, wk, 4, 4, PS[1][0:4, 0:128], 'PS1')
                p.op('act', lambda e: e.copy(out=d_arow[:], in_=PS[1][0:4, 0:128]), reads=['PS1'], writes=['d_arow'])
                proj_tm(wb, wk, 8, 512, PS[2][:], 'PS2')
                p.op('act', lambda e: e.activation(out=d_z[:], in_=PS[2][:], func=AF.Silu), reads=['PS2'], writes=['d_z'])
                p.op('act', lambda e: e.activation(out=sm[:, 44:48], in_=d_ba[:, 0:4], func=AF.Sigmoid), reads=['d_ba'],
                     writes=['sm44'])
                p.op('dve', lambda e: e.tensor_tensor(out=sm[:, 48:52], in0=d_ba[:, 4:8], in1=dtb[:, l, :], op=ALU.add),
                     reads=['d_ba', 'prm'], writes=['sm48'])
                p.op('act', lambda e: e.activation(out=sm[:, 48:52], in_=sm[:, 48:52], func=AF.Exp), reads=['sm48'],
                     writes=['sm48'])
                p.op('act', lambda e: e.activation(out=sm[:, 48:52], in_=sm[:, 48:52], func=AF.Ln, bias=epsb[:, 1:2]),
                     reads=['sm48', 'epsb'], writes=['sm48'])
                p.op('dve', lambda e: e.tensor_tensor(out=sm[:, 48:52], in0=sm[:, 48:52], in1=nexpa[:, l, :], op=ALU.mult),
                     reads=['sm48', 'nexpa'], writes=['sm48'])
                p.op('pe', lambda e: e.matmul(PS[3][:, 0:4], lhsT=tri_incl[:], rhs=sm[:, 48:52], start=True, stop=True),
                     reads=['sm48', 'k_tri_incl'], writes=['PS3'])
                p.op('act', lambda e: e.copy(out=sm[:, 52:56], in_=PS[3][:, 0:4]), reads=['PS3'], writes=['sm52'])
                p.op('dve', lambda e: e.tensor_scalar(out=sm[:, 56:60], in0=sm[:, 52:56], scalar1=-1.0, scalar2=None,
                                                      op0=ALU.mult), reads=['sm52'], writes=['sm56'])
                p.op('pe', lambda e: e.matmul(PS[3][:, 8:12], lhsT=sel_last[:], rhs=sm[:, 52:56], start=True, stop=True),
                     reads=['sm52', 'k_sel_last'], writes=['PS3'])
                p.op('act', lambda e: e.activation(out=sm[:, 60:64], in_=PS[3][:, 8:12], func=AF.Exp), reads=['PS3'],
                     writes=['sm60'])
                p.op('dve', lambda e: e.tensor_tensor(out=tmpC[:, 500:504], in0=PS[3][:, 8:12], in1=sm[:, 52:56],
                                                      op=ALU.subtract), reads=['PS3', 'sm52'], writes=['tmpC5'])
                p.op('act', lambda e: e.activation(out=tmpC[:, 500:504], in_=tmpC[:, 500:504], func=AF.Exp), reads=['tmpC5'],
                     writes=['tmpC5'])
                p.op('dve', lambda e: e.tensor_tensor(out=tmpC[:, 504:508], in0=tmpC[:, 500:504], in1=sm[:, 44:48], op=ALU.mult),
                     reads=['tmpC5', 'sm44'], writes=['tmpC6'])
                p.op('act', lambda e: e.activation(out=tmpC[:, 508:512], in_=sm[:, 52:56], func=AF.Exp), reads=['sm52'],
                     writes=['tmpC7'])
                p.op('dve', lambda e: e.tensor_scalar(out=tmpC[:, 496:500], in0=tmpC[:, 508:512], scalar1=-1.0, scalar2=None,
                                                      op0=ALU.mult), reads=['tmpC7'], writes=['tmpC4'])
                p.op('dve', lambda e: e.tensor_scalar(out=tmpC[:, 492:496], in0=sm[:, 44:48], scalar1=-1.0, scalar2=None,
                                                      op0=ALU.mult), reads=['sm44'], writes=['tmpC3'])
                p.op('act', lambda e: e.activation(out=d_grow[:], in_=d_arow[:], func=AF.Exp, bias=dtbrow[:, l:l + 1]),
                     reads=['d_arow', 'dtbrow'], writes=['d_grow'])
                p.op('act', lambda e: e.activation(out=d_grow[:], in_=d_grow[:], func=AF.Ln, bias=epsb[0:4, 1:2]),
                     reads=['d_grow', 'epsb'], writes=['d_grow'])
                p.op('dve', lambda e: e.tensor_scalar(out=d_grow[:], in0=d_grow[:], scalar1=nexparow[:, l:l + 1], scalar2=None,
                                                      op0=ALU.mult), reads=['d_grow', 'nexparow'], writes=['d_grow'])
                p.op('dve', lambda e: e.tensor_tensor_scan(out=d_grow[:], data0=ones_f[0:4, :], data1=d_grow[:], initial=0.0,
                                                           op0=ALU.mult, op1=ALU.add), reads=['d_grow', 'k_ones'],
                     writes=['d_grow'])
                for h in range(4):
                    p.op('pe', (lambda e, h=h: e.matmul(PS[4][:, 0:128], lhsT=selh[:, h * 128:(h + 1) * 128], rhs=d_grow[:],
                                                        start=True, stop=True)), reads=['k_selh', 'd_grow'], writes=['PS4'])
                    p.op('dve', lambda e: e.tensor_tensor(out=tmpA[:, 0:128], in0=PS[4][:, 0:128], in1=negmask[:], op=ALU.add),
                         reads=['PS4', 'k_negmask'], writes=['tmpA'])
                    p.op('act', (lambda e, h=h: e.activation(out=d_E1[:], in_=tmpA[:, 0:128], func=AF.Exp,
                                                             bias=sm[:, 56 + h:57 + h])), reads=['tmpA', 'sm56'], writes=['d_E1'])
                    p.op('dve', lambda e: e.tensor_tensor(out=d_E1s[:], in0=d_E1[:], in1=tri_strict[:], op=ALU.mult),
                         reads=['d_E1', 'k_tri_strict'], writes=['d_E1s'])
                    p.op('pe', (lambda e, h=h: e.matmul(PS[4][:, 128:256], lhsT=d_kT[:, h, :], rhs=d_kT[:, h, :], start=True,
                                                        stop=True)), reads=['d_kT'], writes=['PS4'])
                    p.op('dve', (lambda e, h=h: e.scalar_tensor_tensor(out=d_PT[0][:], in0=PS[4][:, 128:256],
                                                                       scalar=tmpC[:, 492 + h:493 + h], in1=d_E1s[:],
                                                                       op0=ALU.mult, op1=ALU.mult)),
                         reads=['PS4', 'tmpC3', 'd_E1s'], writes=['d_PT0'])
                    p.op('pe', lambda e: e.transpose(out=PS[5][:, 0:128], in_=d_PT[0][:], identity=ident_f[:]),
                         reads=['d_PT0', 'k_ident'], writes=['PS5'])
                    p.op('act', lambda e: e.copy(out=d_P[0][:], in_=PS[5][:, 0:128]), reads=['PS5'], writes=['d_P0'])
                    p.op('dve', lambda e: e.tensor_tensor(out=d_TT[:], in0=d_PT[0][:], in1=ident_f[:], op=ALU.add),
                         reads=['d_PT0', 'k_ident'], writes=['d_TT'])
                    cur = 0
                    for lvl in range(6):
                        nxt = 1 - cur
                        p.op('pe', (lambda e, cur=cur: e.matmul(PS[5][:, 0:128], lhsT=d_PT[cur][:], rhs=d_P[cur][:], start=True,
                                                                stop=True)), reads=[f'd_PT{cur}', f'd_P{cur}'], writes=['PS5'])
                        if lvl < 5:
                            p.op('pe', (lambda e, cur=cur: e.matmul(PS[5][:, 128:256], lhsT=d_P[cur][:], rhs=d_PT[cur][:],
                                                                    start=True, stop=True)),
                                 reads=[f'd_PT{cur}', f'd_P{cur}'], writes=['PS5b'])
                        p.op('act', (lambda e, nxt=nxt: e.copy(out=d_P[nxt][:], in_=PS[5][:, 0:128])), reads=['PS5'],
                             writes=[f'd_P{nxt}'])
                        if lvl < 5:
                            p.op('act', (lambda e, nxt=nxt: e.copy(out=d_PT[nxt][:], in_=PS[5][:, 128:256])), reads=['PS5b'],
                                 writes=[f'd_PT{nxt}'])
                        p.op('pe', (lambda e, nxt=nxt: e.matmul(PS[6][:, 0:128], lhsT=d_P[nxt][:], rhs=d_TT[:], start=True,
                                                                stop=True)), reads=[f'd_P{nxt}', 'd_TT'], writes=['PS6'])
                        p.op('dve', lambda e: e.tensor_tensor(out=d_TT[:], in0=d_TT[:], in1=PS[6][:, 0:128], op=ALU.add),
                             reads=['d_TT', 'PS6'], writes=['d_TT'])
                        cur = nxt
                    p.op('pe', (lambda e, h=h: e.matmul(PS[4][:, 256:384], lhsT=d_kT[:, h, :], rhs=d_Sb[:, l, h, :], start=True,
                                                        stop=True)), reads=['d_kT', 'd_Sb'], writes=['PS4'])
                    p.op('dve', (lambda e, h=h: e.scalar_tensor_tensor(out=d_r[:], in0=PS[4][:, 256:384],
                                                                       scalar=tmpC[:, 496 + h:497 + h], in1=d_v[:, h, :],
                                                                       op0=ALU.mult, op1=ALU.add)),
                         reads=['PS4', 'tmpC4', 'd_v'], writes=['d_r'])
                    p.op('pe', lambda e: e.matmul(PS[5][:, 256:384], lhsT=d_TT[:], rhs=d_r[:], start=True, stop=True),
                         reads=['d_TT', 'd_r'], writes=['PS5c'])
                    p.op('dve', (lambda e, h=h: e.tensor_scalar(out=d_vn[:], in0=PS[5][:, 256:384], scalar1=sm[:, 44 + h:45 + h],
                                                                scalar2=None, op0=ALU.mult)), reads=['PS5c', 'sm44'],
                         writes=['d_vn'])
                    p.op('dve', (lambda e, h=h: e.tensor_scalar(out=d_vc[:], in0=PS[5][:, 256:384],
                                                                scalar1=tmpC[:, 504 + h:505 + h], scalar2=None, op0=ALU.mult)),
                         reads=['PS5c', 'tmpC6'], writes=['d_vc'])
                    p.op('pe', (lambda e, h=h: e.matmul(PS[4][:, 384:512], lhsT=d_kT[:, h, :], rhs=d_qT[:, h, :], start=True,
                                                        stop=True)), reads=['d_kT', 'd_qT'], writes=['PS4'])
                    p.op('dve', lambda e: e.tensor_tensor(out=d_aT[:], in0=PS[4][:, 384:512], in1=d_E1[:], op=ALU.mult),
                         reads=['PS4', 'd_E1'], writes=['d_aT'])
                    p.op('pe', (lambda e, h=h: e.matmul(PS[6][:, 128:256], lhsT=d_qT[:, h, :], rhs=d_Sb[:, l, h, :], start=True,
                                                        stop=True)), reads=['d_qT', 'd_Sb'], writes=['PS6b'])
                    p.op('act', (lambda e, h=h: e.activation(out=d_o1[:], in_=PS[6][:, 128:256], func=AF.Copy,
                                                             scale=tmpC[:, 508 + h:509 + h])), reads=['PS6b', 'tmpC7'],
                         writes=['d_o1'])
                    p.op('pe', lambda e: e.matmul(PS[6][:, 256:384], lhsT=d_aT[:], rhs=d_vn[:], start=True, stop=True),
                         reads=['d_aT', 'd_vn'], writes=['PS6c'])
                    p.op('dve', lambda e: e.tensor_tensor(out=tmpB[:, 0:128], in0=PS[6][:, 256:384], in1=d_o1[:], op=ALU.add),
                         reads=['PS6c', 'd_o1'], writes=['tmpB'])
                    p.op('pe', (lambda e, h=h: e.matmul(PS[6][:, 384:512], lhsT=d_k[:, h, :], rhs=d_vc[:], start=True, stop=True)),
                         reads=['d_k', 'd_vc'], writes=['PS6d'])
                    p.op('dve', (lambda e, h=h: e.scalar_tensor_tensor(out=d_S[:, l, h, :], in0=d_S[:, l, h, :],
                                                                       scalar=sm[:, 60 + h:61 + h], in1=PS[6][:, 384:512],
                                                                       op0=ALU.mult, op1=ALU.add)),
                         reads=['d_S', 'sm60', 'PS6d'], writes=['d_S'])
                    p.op('act', (lambda e, h=h: e.copy(out=d_Sb[:, l, h, :], in_=d_S[:, l, h, :])), reads=['d_S'],
                         writes=['d_Sb'])
                    head_norm_gate(tmpB[:, 0:128], 'tmpB', [], dog[:, l, :], d_z[:, h * 128:(h + 1) * 128], 'd_z',
                                   obf[:, 3, h * 128:(h + 1) * 128], 36)

            for gi in range(8):
                wb, wk = wload(W_in[:, C_GATE + gi * 512:C_GATE + (gi + 1) * 512], 512)
                pst, pk = (PS[0], 'PS0') if gi % 2 == 0 else (PS[1], 'PS1')
                proj_tm(wb, wk, 0, 512, pst[:], pk)
                p.op('act', (lambda e, gi=gi, pst=pst: e.activation(out=gates[:, gi * 512:(gi + 1) * 512], in_=pst[:],
                                                                    func=AF.Sigmoid)), reads=[pk], writes=['gates'])
            for half in range(2):
                for c8 in range(8):
                    cc = half * 8 + c8
                    p.op('pe', (lambda e, cc=cc, c8=c8: e.transpose(
                        out=PT[:, c8 * 128:(c8 + 1) * 128],
                        in_=obf[:].rearrange("p a b -> p (a b)")[:, cc * 128:(cc + 1) * 128], identity=ident_b[:])),
                        reads=['obf', 'k_ident'], writes=['PT'], inc=(c8 == 7))
                p.op('act', (lambda e, half=half: e.copy(out=oT[:, half * 8:(half + 1) * 8, :].rearrange("p a b -> p (a b)"),
                                                         in_=PT[:])), reads=['PT'], writes=['oT'])
            for n in range(4):
                wv, wk = wload_rows(prm['w_branch'][l, n])
                for half in range(2):
                    pst, pk = (PS[2], 'PS2') if half == 0 else (PS[3], 'PS3')
                    for wc in range(4):
                        p.op('pe', (lambda e, n=n, wc=wc, half=half, pst=pst: e.matmul(
                            pst[:], lhsT=oT[:, n * 4 + wc, :], rhs=wv[:, wc, half * 512:(half + 1) * 512], start=(wc == 0),
                            stop=(wc == 3))), reads=['oT', wk], writes=[pk], inc=(wc == 3))
                    if n == 0:
                        p.op('dve', (lambda e, n=n, half=half, pst=pst: e.tensor_tensor(
                            out=merged[:, half * 512:(half + 1) * 512], in0=pst[:],
                            in1=gates[:, n * 1024 + half * 512:n * 1024 + (half + 1) * 512], op=ALU.mult)),
                            reads=[pk, 'gates'], writes=['merged'])
                    else:
                        p.op('dve', (lambda e, n=n, half=half, pst=pst: e.tensor_tensor(
                            out=tmpA[:], in0=pst[:], in1=gates[:, n * 1024 + half * 512:n * 1024 + (half + 1) * 512],
                            op=ALU.mult)), reads=[pk, 'gates'], writes=['tmpA'])
                        p.op('dve', (lambda e, half=half: e.tensor_tensor(
                            out=merged[:, half * 512:(half + 1) * 512], in0=merged[:, half * 512:(half + 1) * 512], in1=tmpA[:],
                            op=ALU.add)), reads=['tmpA', 'merged'], writes=['merged'])
            p.op('act', lambda e: e.copy(out=mbf[:], in_=merged[:]), reads=['merged'], writes=['mbf'])
            for kc in range(8):
                p.op('pe', (lambda e, kc=kc: e.transpose(out=PT[:, kc * 128:(kc + 1) * 128], in_=mbf[:, kc * 128:(kc + 1) * 128],
                                                         identity=ident_b[:])), reads=['mbf', 'k_ident'], writes=['PT'],
                     inc=(kc == 7))
            p.op('act', lambda e: e.copy(out=mT[:].rearrange("p a b -> p (a b)"), in_=PT[:]), reads=['PT'], writes=['mT'])
            for half in range(2):
                wb, wk = wload(prm['w_out'][l][:, half * 512:(half + 1) * 512], 512)
                pst, pk = (PS[0], 'PS0') if half == 0 else (PS[1], 'PS1')
                for kc in range(8):
                    p.op('pe', (lambda e, kc=kc, pst=pst, wb=wb: e.matmul(pst[:], lhsT=mT[:, kc, :], rhs=wb[:, kc, 0:512],
                                                                          start=(kc == 0), stop=(kc == 7))),
                         reads=['mT', wk], writes=[pk], inc=(kc == 7))
                p.op('dve', (lambda e, half=half, pst=pst: e.tensor_tensor(out=xt[:, half * 512:(half + 1) * 512],
                                                                           in0=xt[:, half * 512:(half + 1) * 512], in1=pst[:],
                                                                           op=ALU.add)), reads=[pk, 'xt'], writes=['xt'])
            rmsnorm_to_T(xt, gmlp, hT, 'hT', l)
            for fi in range(8):
                wb, wk = wload(prm['w_up'][l][:, fi * 512:(fi + 1) * 512], 512)
                pst, pk = (PS[2], 'PS2') if fi % 2 == 0 else (PS[3], 'PS3')
                for f4 in range(4):
                    proj_fm(wb, wk, f4 * 128, 128, pst[:, f4 * 128:(f4 + 1) * 128], pk)
                p.op('act', (lambda e, pst=pst: e.activation(out=tmpB[:], in_=pst[:], func=AF.Relu)), reads=[pk], writes=['tmpB'])
                p.op('dve', (lambda e, fi=fi, pst=pst: e.tensor_tensor(
                    out=uT[:, fi * 4:(fi + 1) * 4, :].rearrange("p a b -> p (a b)"), in0=tmpB[:], in1=pst[:], op=ALU.mult)),
                    reads=['tmpB', pk], writes=['uT'])
            for fi in range(8):
                wv, wk = wload_rows(prm['w_down'][l][fi * 512:(fi + 1) * 512, :])
                for half in range(2):
                    pk = 'PS0' if half == 0 else 'PS1'
                    pst = PS[0] if half == 0 else PS[1]
                    for f4 in range(4):
                        p.op('pe', (lambda e, fi=fi, f4=f4, half=half, pst=pst, wv=wv: e.matmul(
                            pst[:], lhsT=uT[:, fi * 4 + f4, :], rhs=wv[:, f4, half * 512:(half + 1) * 512],
                            start=(fi == 0 and f4 == 0), stop=(fi == 7 and f4 == 3))),
                            reads=['uT', wk], writes=[pk], inc=(f4 == 3))
            for half in range(2):
                pk = 'PS0' if half == 0 else 'PS1'
                pst = PS[0] if half == 0 else PS[1]
                p.op('dve', (lambda e, half=half, pst=pst: e.tensor_tensor(out=xt[:, half * 512:(half + 1) * 512],
                                                                           in0=xt[:, half * 512:(half + 1) * 512], in1=pst[:],
                                                                           op=ALU.add)), reads=[pk, 'xt'], writes=['xt'])
        p.dma('sp', y_out[ti * 128:(ti + 1) * 128, :], xt[:], reads=['xt'], writes=['yout'])
    p.final_wait('sp', ['yout'])
    p.emit()
    return nc


def host_params(inputs):
    f = lambda k: np.ascontiguousarray(np.asarray(inputs[k], dtype=np.float32))
    m = {}
    for k in ['norm_mix_g', 'w_in', 'hgrn_out_g', 'mlstm_out_g', 'attn_sinks', 'rel_bias_table', 'dn_a_log', 'dn_dt_bias',
              'dn_out_g', 'w_branch', 'w_out', 'norm_mlp_g', 'w_up', 'w_down']:
        m[k] = f(k)
    m['mlstm_if_bias'] = f('mlstm_if_bias').reshape(2, 8)
    m['hgrn_lb_table'] = np.ascontiguousarray(f('hgrn_lb_table').reshape(2, 4, 128).transpose(0, 2, 1))
    m['attn_q_norm_g'] = f('attn_q_norm_g').reshape(2, 64, 1)
    m['attn_k_norm_g'] = f('attn_k_norm_g').reshape(2, 64, 1)
    m['dn_conv_w'] = np.ascontiguousarray(f('dn_conv_w').reshape(2, 4, 12, 128).transpose(0, 1, 3, 2))
    m['dn_alog_row'] = np.ascontiguousarray(f('dn_a_log').T)
    m['dn_dtb_row'] = np.ascontiguousarray(f('dn_dt_bias').T)
    return m


def kernel(**inputs):
    x = np.ascontiguousarray(np.asarray(inputs['x'], dtype=np.float32))
    B, T, _ = x.shape
    nc = build(T, 2)
    consts = host_consts()
    hp = host_params(inputs)
    in_maps = []
    for b in range(B):
        m = {'x': x[b]}
        m.update(hp)
        for k, v in consts.items():
            m['c_' + k] = v
        in_maps.append(m)
    res = run_bass_kernel_spmd(nc, in_maps, core_ids=list(range(B)))
    return np.stack([np.asarray(r['y'], dtype=np.float32) for r in res.results], axis=0)
```

```python
import types
import numpy as np
from contextlib import ExitStack
import concourse.bass as bass
import concourse.mybir as mybir
from concourse.bass_utils import run_bass_kernel_spmd

F32 = mybir.dt.float32
BF16 = mybir.dt.bfloat16
AF = mybir.ActivationFunctionType
ALU = mybir.AluOpType
AX = mybir.AxisListType
ENGS = ['pe', 'act', 'dve', 'pool', 'sp']

D = 1024
NIN = 10512
EPS = 1e-6


def _freeze(fn):
    if fn is None or fn.__closure__ is None:
        return fn
    cells = []
    for c in fn.__closure__:
        try:
            cells.append(types.CellType(c.cell_contents))
        except ValueError:
            cells.append(c)
    return types.FunctionType(fn.__code__, fn.__globals__, fn.__name__, fn.__defaults__, tuple(cells))


class V3:
    def __init__(self, ap):
        self.ap = ap

    def __getitem__(self, k):
        return self.ap[k]


class Prog:
    def __init__(self, nc, nd=8):
        self.nc = nc
        self.ND = nd
        self.lists = {e: [] for e in ENGS}
        self.cnt = {e: 0 for e in ENGS}
        self.waited = {e: {} for e in ENGS}
        self.W = {}
        self.Rd = {}
        self.dma_n = {e: 0 for e in ENGS}
        self.semkeys = set()
        self.st = ExitStack()
        self.ntens = 0
        self.alias = {}

    def sb(self, shape, dt, name=None):
        self.ntens += 1
        return self.st.enter_context(self.nc.sbuf_tensor(name or f"t{self.ntens}", list(shape), dt))

    def ps(self, shape, dt=F32, name=None):
        self.ntens += 1
        return self.st.enter_context(self.nc.psum_tensor(name or f"p{self.ntens}", list(shape), dt))

    def _deps(self, eng, reads, writes):
        deps = {}

        def add(d):
            for sk, v in d.items():
                if deps.get(sk, 0) < v:
                    deps[sk] = v
        for r in reads:
            add(self.W.get(r, {}))
        for w in writes:
            add(self.W.get(w, {}))
            add(self.Rd.get(w, {}))
        out = []
        for sk, v in deps.items():
            if sk == ('c', 'pe') and eng == 'pe':
                continue
            if self.waited[eng].get(sk, 0) >= v:
                continue
            self.waited[eng][sk] = v
            out.append((sk, v))
        return out

    def _rec(self, tok, reads, writes):
        sk, v = tok
        for r in reads:
            d = self.Rd.setdefault(r, {})
            d[sk] = max(d.get(sk, 0), v)
        for w in writes:
            d = self.W.setdefault(w, {})
            d[sk] = max(d.get(sk, 0), v)

    def _x(self, keys):
        out = []
        for k in keys:
            if isinstance(k, (list, tuple)):
                out.extend(self._x(k))
            elif k in self.alias:
                out.extend(self.alias[k])
            else:
                out.append(k)
        return out

    def op(self, eng, fn, reads=(), writes=(), inc=True):
        reads = self._x(reads)
        writes = self._x(writes)
        waits = self._deps(eng, reads, writes)
        sk = ('c', eng)
        tok = (sk, self.cnt[eng] + 1)
        if inc:
            self.cnt[eng] += 1
        self.semkeys.add(sk)
        self.lists[eng].append((waits, _freeze(fn), sk if inc else None, 1))
        self._rec(tok, reads, writes)

    def dma(self, q, out, in_, reads=(), writes=(), **kw):
        reads = self._x(reads)
        writes = self._x(writes)
        j = self.dma_n[q]
        self.dma_n[q] += 1
        slot = j % self.ND
        val = 16 * (j // self.ND + 1)
        sk = ('d', q, slot)
        self.semkeys.add(sk)
        waits = self._deps(q, reads, writes)
        if j >= self.ND and self.waited[q].get(sk, 0) < val - 16:
            self.waited[q][sk] = val - 16
            waits.append((sk, val - 16))
        self.lists[q].append((waits, (lambda e, o=out, i=in_, k=kw: e.dma_start(out=o, in_=i, **k)), sk, 16))
        self._rec((sk, val), reads, writes)

    def final_wait(self, eng, keys):
        keys = self._x(keys)
        waits = self._deps(eng, keys, ())
        self.lists[eng].append((waits, None, None, 0))

    def emit(self):
        nc = self.nc
        st = self.st
        sems = {}
        for sk in sorted(self.semkeys, key=str):
            sems[sk] = st.enter_context(nc.semaphore("s_" + "_".join(map(str, sk))))
        block = st.enter_context(nc.Block())
        lists = self.lists

        def run(name, e):
            for waits, fn, sk, incv in lists[name]:
                for wsk, v in waits:
                    e.wait_ge(sems[wsk], v)
                if fn is None:
                    continue
                ins = fn(e)
                if sk is not None:
                    ins.then_inc(sems[sk], incv)

        @block.tensor
        def _(e):
            run('pe', e)

        @block.scalar
        def _(e):
            run('act', e)

        @block.vector
        def _(e):
            run('dve', e)

        @block.gpsimd
        def _(e):
            run('pool', e)

        @block.sync
        def _(e):
            run('sp', e)
        st.close()


def _t5_bucket_np(n):
    max_exact = 16
    nf = np.maximum(n, max_exact).astype(np.float32)
    large = max_exact + (np.log(nf / max_exact) / np.log(np.float32(128 / max_exact)) * 16).astype(np.int32)
    large = np.minimum(large, 31)
    return np.where(n < max_exact, n, large)


def host_consts():
    c = {}
    s = np.arange(128)[:, None]
    t = np.arange(128)[None, :]
    c['tri_incl'] = (s <= t).astype(np.float32)
    c['tri_strict'] = (s < t).astype(np.float32)
    c['hgmask'] = ((s <= t) & (s // 32 == t // 32)).astype(np.float32)
    c['negmask'] = np.where(s <= t, 0.0, -1e30).astype(np.float32)
    c['ident'] = np.eye(128, dtype=np.float32)
    sel = np.zeros((128, 128), np.float32)
    sel[127, :] = 1.0
    c['sel_last'] = sel
    rm = np.zeros((128, 4), np.float32)
    for j in range(4):
        rm[32 * j:32 * j + 32, j] = 1.0
    c['rowm'] = rm
    rs = np.ones((128, 128), np.float32)
    rs[:, 0::32] = 0.0
    c['resetm'] = rs
    selh = np.zeros((4, 4, 128), np.float32)
    for h in range(4):
        selh[h, h, :] = 1.0
    c['selh'] = selh.reshape(4, 512)
    c['ones'] = np.ones((128, 128), np.float32)
    bk = _t5_bucket_np(np.arange(128))
    oh = np.zeros((32, 128), np.float32)
    oh[bk, np.arange(128)] = 1.0
    c['bias_oh'] = oh
    ab = np.zeros((128, 384), np.float32)
    for dd in range(128):
        ab[dd, 255 - dd] = 1.0
    c['antiband'] = ab
    return c


CONST_SHAPES = {
    'tri_incl': (128, 128), 'tri_strict': (128, 128), 'hgmask': (128, 128), 'negmask': (128, 128),
    'ident': (128, 128), 'sel_last': (128, 128), 'rowm': (128, 4), 'resetm': (128, 128),
    'selh': (4, 512), 'ones': (128, 128), 'bias_oh': (32, 128), 'antiband': (128, 384),
}

PARAM_SHAPES = {
    'norm_mix_g': (2, 1024), 'w_in': (2, 1024, NIN), 'hgrn_lb_table': (2, 128, 4), 'hgrn_out_g': (2, 512),
    'mlstm_if_bias': (2, 8), 'mlstm_out_g': (2, 512), 'attn_q_norm_g': (2, 64, 1), 'attn_k_norm_g': (2, 64, 1),
    'attn_sinks': (2, 8), 'rel_bias_table': (32, 8), 'dn_conv_w': (2, 4, 128, 12), 'dn_a_log': (2, 4),
    'dn_dt_bias': (2, 4), 'dn_alog_row': (4, 2), 'dn_dtb_row': (4, 2), 'dn_out_g': (2, 128), 'w_branch': (2, 4, 512, 1024), 'w_out': (2, 1024, 1024),
    'norm_mlp_g': (2, 1024), 'w_up': (2, 1024, 4096), 'w_down': (2, 4096, 1024),
}

C_HQ, C_HF, C_HI, C_HG = 0, 512, 1024, 1536
C_MQ, C_MK, C_MV, C_MI, C_MF, C_MO = 2048, 2304, 2560, 3072, 3076, 3080
C_AQ, C_AK, C_AV = 3592, 4104, 4232
C_DQKV, C_DB, C_DA, C_DZ = 4360, 5896, 5900, 5904
C_GATE = 6416


def build(T, L=2, enable=(1, 1, 1, 1), dbg=None):
    nc = bass.Bass("TRN2", target_bir_lowering=False)
    NTILES = T // 128
    x_in = nc.dram_tensor("x", [T, D], F32, kind="ExternalInput").ap()
    y_out = nc.dram_tensor("y", [T, D], F32, kind="ExternalOutput").ap()
    prm = {k: nc.dram_tensor(k, list(s), F32, kind="ExternalInput").ap() for k, s in PARAM_SHAPES.items()}
    cst = {k: nc.dram_tensor("c_" + k, list(s), F32, kind="ExternalInput").ap() for k, s in CONST_SHAPES.items()}
    dbg_out = {}
    p = Prog(nc)
    sb, ps = p.sb, p.ps

    def load_const(name, dt=F32, q='sp'):
        shp = CONST_SHAPES[name]
        t_ = sb(shp, dt, "k_" + name + ("_b" if dt == BF16 else ""))
        p.dma(q, t_[:], cst[name], writes=['k_' + name])
        return t_
    tri_incl = load_const('tri_incl')
    tri_strict = load_const('tri_strict')
    hgmask = load_const('hgmask')
    negmask = load_const('negmask')
    ident_f = load_const('ident')
    ident_b = load_const('ident', BF16, 'pool')
    sel_last = load_const('sel_last')
    rowm = load_const('rowm')
    resetm = load_const('resetm')
    selh = load_const('selh')
    ones_f = load_const('ones')
    ones_b = load_const('ones', BF16, 'pool')
    KC = ['k_tri_incl', 'k_tri_strict', 'k_hgmask', 'k_negmask', 'k_ident', 'k_sel_last', 'k_rowm', 'k_resetm',
          'k_selh', 'k_ones']

    gmix = sb([128, L, 1024], BF16, "gmix")
    gmlp = sb([128, L, 1024], BF16, "gmlp")
    hog = sb([128, L, 512], BF16, "hog")
    mog = sb([128, L, 512], BF16, "mog")
    dog = sb([128, L, 128], F32, "dog")
    mifb = sb([128, L, 8], F32, "mifb")
    sinks = sb([128, L, 8], F32, "sinks")
    esink = sb([128, L, 8], F32, "esink")
    alog = sb([128, L, 4], F32, "alog")
    nexpa = sb([128, L, 4], F32, "nexpa")
    dtb = sb([128, L, 4], F32, "dtb")
    lbt = sb([128, 2, 4], F32, "lbt")
    lb = sb([128, 2, 4], F32, "lb")
    oml = sb([128, 2, 4], F32, "oml")
    convw = sb([128, L, 4, 12], F32, "convw")
    qkg = sb([64, L, 2], F32, "qkg")
    gk8 = sb([64, L], F32, "gk8")
    dtbrow = sb([4, 2], F32, "dtbrow")
    nexparow = sb([4, 2], F32, "nexparow")
    for l in range(L):
        p.dma('pool', gmix[:, l, :], prm['norm_mix_g'][l:l + 1, :].partition_broadcast(128), writes=['prm'])
        p.dma('pool', gmlp[:, l, :], prm['norm_mlp_g'][l:l + 1, :].partition_broadcast(128), writes=['prm'])
        p.dma('pool', hog[:, l, :], prm['hgrn_out_g'][l:l + 1, :].partition_broadcast(128), writes=['prm'])
        p.dma('pool', mog[:, l, :], prm['mlstm_out_g'][l:l + 1, :].partition_broadcast(128), writes=['prm'])
        p.dma('sp', dog[:, l, :], prm['dn_out_g'][l:l + 1, :].partition_broadcast(128), writes=['prm'])
        p.dma('sp', mifb[:, l, :], prm['mlstm_if_bias'][l:l + 1, :].partition_broadcast(128), writes=['prm'])
        p.dma('sp', sinks[:, l, :], prm['attn_sinks'][l:l + 1, :].partition_broadcast(128), writes=['prm'])
        p.dma('sp', alog[:, l, :], prm['dn_a_log'][l:l + 1, :].partition_broadcast(128), writes=['prm'])
        p.dma('sp', dtb[:, l, :], prm['dn_dt_bias'][l:l + 1, :].partition_broadcast(128), writes=['prm'])
        for j in range(4):
            p.dma('sp', convw[:, l, j, :], prm['dn_conv_w'][l, j], writes=['prm'])
        p.dma('sp', qkg[:, l, 0:1], prm['attn_q_norm_g'][l], writes=['prm'])
        p.dma('sp', qkg[:, l, 1:2], prm['attn_k_norm_g'][l], writes=['prm'])
    for l2 in range(2):
        p.dma('sp', lbt[:, l2, :], prm['hgrn_lb_table'][l2], writes=['prm'])
    p.dma('sp', dtbrow[:], prm['dn_dtb_row'], writes=['dtbrow'])
    p.dma('sp', nexparow[:], prm['dn_alog_row'], writes=['nexparow'])
    p.op('act', lambda e: e.activation(out=nexparow[:], in_=nexparow[:], func=AF.Exp), reads=['nexparow'], writes=['nexparow'])
    p.op('dve', lambda e: e.tensor_scalar(out=nexparow[:], in0=nexparow[:], scalar1=-1.0, scalar2=None, op0=ALU.mult),
         reads=['nexparow'], writes=['nexparow'])
    p.op('dve', lambda e: e.memset(lb[:], 0.0), writes=['lb'])
    p.op('dve', lambda e: e.tensor_sub(out=lb[:, 1, :], in0=lbt[:, 1, :], in1=lbt[:, 0, :]), reads=['prm', 'lb'],
         writes=['lb'])
    p.op('act', lambda e: e.activation(out=lb[:, 1, :], in_=lb[:, 1, :], func=AF.Sigmoid), reads=['lb'], writes=['lb'])
    p.op('dve', lambda e: e.tensor_scalar(out=oml[:], in0=lb[:], scalar1=-1.0, scalar2=1.0, op0=ALU.mult, op1=ALU.add),
         reads=['lb'], writes=['oml'])
    p.op('act', lambda e: e.activation(out=esink[:], in_=sinks[:], func=AF.Exp), reads=['prm'], writes=['esink'])
    p.op('act', lambda e: e.activation(out=nexpa[:], in_=alog[:], func=AF.Exp), reads=['prm'], writes=['nexpa'])
    p.op('dve', lambda e: e.tensor_scalar(out=nexpa[:], in0=nexpa[:], scalar1=-1.0, scalar2=None, op0=ALU.mult),
         reads=['nexpa'], writes=['nexpa'])
    p.op('dve', lambda e: e.tensor_tensor(out=gk8[:], in0=qkg[:, :, 0], in1=qkg[:, :, 1], op=ALU.mult), reads=['prm'],
         writes=['gk8'])
    p.op('dve', lambda e: e.tensor_scalar(out=gk8[:], in0=gk8[:], scalar1=0.125, scalar2=None, op0=ALU.mult),
         reads=['gk8'], writes=['gk8'])

    PS = [ps([128, 512], F32, f"PS{i}") for i in range(7)]
    PT = ps([128, 1024], BF16, "PSTR")

    def K(i, lo=0, hi=512):
        return [f'PS{i}.bank']
    for i in range(7):
        p.alias[f'PS{i}'] = K(i)
    p.alias['PS5b'] = K(5, 128, 256)
    p.alias['PS5c'] = K(5, 256, 384)
    p.alias['PS6b'] = K(6, 128, 256)
    p.alias['PS6c'] = K(6, 256, 384)
    p.alias['PS6d'] = K(6, 384, 512)
    for i in range(5):
        p.alias[f'scr{i}'] = [f'scr.{i}']
    p.alias['h_q'] = ['scr.0']
    p.alias['h_f'] = ['scr.1']
    p.alias['h_k'] = ['scr.2']
    p.alias['h_cum'] = ['scr.3']
    p.alias['h_e'] = ['scr.4']
    p.alias['a_z'] = ['scr.0', 'scr.1', 'scr.2']
    p.alias['a_sq'] = ['scr.2', 'scr.3', 'scr.4']
    p.alias['d_y'] = ['scr.0', 'scr.1', 'scr.2']
    EB = sb([128, 2, 8, 128], F32, "EB")
    relt = sb([32, 8], F32, "relt")
    boh = sb([32, 128], F32, "boh")
    aband = sb([128, 384], F32, "aband")
    vecE = sb([128, 8], F32, "vecE")
    p.dma('sp', relt[:], prm['rel_bias_table'], writes=['relt'])
    p.dma('sp', boh[:], cst['bias_oh'], writes=['boh'])
    p.dma('sp', aband[:], cst['antiband'], writes=['aband'])
    p.op('pe', lambda e: e.matmul(PS[0][:, 0:8], lhsT=boh[:], rhs=relt[:], start=True, stop=True), reads=['boh', 'relt'],
         writes=K(0))
    p.op('act', lambda e: e.activation(out=vecE[:], in_=PS[0][:, 0:8], func=AF.Exp), reads=K(0), writes=['vecE'])
    for blk in range(2):
        for t0 in range(0, 128, 64):
            pst = PS[1]
            for tt in range(64):
                off = 255 - (t0 + tt + (128 if blk == 0 else 0))
                p.op('pe', (lambda e, off=off, tt=tt: e.matmul(pst[:, tt * 8:(tt + 1) * 8], lhsT=aband[:, off:off + 128],
                                                               rhs=vecE[:], start=True, stop=True)),
                     reads=['aband', 'vecE'], writes=K(1), inc=(tt == 63))
            p.op('dve', (lambda e, blk=blk, t0=t0: e.tensor_copy(
                out=EB[:, blk, :, t0:t0 + 64], in_=pst[:].rearrange("p (t g) -> p g t", g=8))),
                reads=K(1), writes=['EB'])

    xt = sb([128, 1024], F32, "xt")
    hbf = sb([128, 1024], BF16, "hbf")
    hT = sb([128, 8, 128], BF16, "hT")
    NWB = 3
    wbuf = [sb([128, 8, 520], BF16, f"wbuf{i}") for i in range(NWB)]
    wb_n = [0]
    sm = sb([128, 64], F32, "small")
    obf = sb([128, 4, 512], BF16, "obf")
    oT = sb([128, 16, 128], BF16, "oT")
    gates = sb([128, 4096], BF16, "gates")
    merged = sb([128, 1024], F32, "merged")
    mbf = sb([128, 1024], BF16, "mbf")
    mT = sb([128, 8, 128], BF16, "mT")
    uT = sb([128, 32, 128], BF16, "uT")
    tmpA = sb([128, 512], F32, "tmpA")
    tmpB = sb([128, 512], F32, "tmpB")
    tmpC = sb([128, 512], F32, "tmpC")
    junk = sb([128, 1024], BF16, "junk")

    scr = sb([128, 2560], F32, "scr")

    class V:
        def __init__(self, ap):
            self.ap = ap

        def __getitem__(self, k):
            return self.ap[k]
    h_q = V(scr[:, 0:512].rearrange("p (a b) -> p a b", a=4))
    h_f = V(scr[:, 512:1024].rearrange("p (a b) -> p a b", a=4))
    h_k = V(scr[:, 1024:1536].rearrange("p (a b) -> p a b", a=4))
    h_cum = V(scr[:, 1536:2048].rearrange("p (a b) -> p a b", a=4))
    h_e = V(scr[:, 2048:2560].rearrange("p (a b) -> p a b", a=4))
    h_qp = sb([128, 4, 128], BF16, "h_qp")
    h_kp = sb([128, 4, 128], BF16, "h_kp")
    h_kpp = sb([128, 4, 128], BF16, "h_kpp")
    h_edec = sb([128, 4, 4], F32, "h_edec")
    h_v = sb([128, 512], BF16, "h_v")
    h_g = sb([128, 512], F32, "h_g")
    h_AT = sb([128, 128], BF16, "h_AT")
    h_kj = sb([128, 4, 128], BF16, "h_kj")
    h_qj = sb([128, 4, 4, 128], BF16, "h_qj")
    h_S = [sb([128, L, 4, 128], F32, "h_S")]
    h_Sb = sb([128, L, 4, 128], BF16, "h_Sb")

    m_q = sb([64, 4, 128], BF16, "m_q")
    m_kT = sb([64, 4, 128], BF16, "m_kT")
    m_k = sb([128, 256], BF16, "m_k")
    m_v = sb([128, 512], F32, "m_v")
    m_if = sb([128, 8], F32, "m_if")
    m_o = sb([128, 512], F32, "m_o")
    m_va = sb([128, 4, 130], BF16, "m_va")
    m_AT = sb([128, 128], BF16, "m_AT")
    m_C = sb([64, L, 4, 130], F32, "m_C")
    m_Cb = sb([64, L, 4, 130], BF16, "m_Cb")

    a_q = sb([64, 8, 128], BF16, "a_q")
    a_z = V(scr[0:64, 0:1280].rearrange("p (a b) -> p a b", a=10))
    a_sq = V(scr[0:64, 1280:2560].rearrange("p (a b) -> p a b", a=10))
    a_kT = sb([64, L, 2, 2, 128], BF16, "a_kT")
    a_v = sb([128, L, 2, 2, 66], BF16, "a_v")
    a_P = sb([128, 128], F32, "a_P")
    a_PT = sb([128, 2, 128], BF16, "a_PT")

    d_x = sb([128, L, 12, 132], BF16, "d_x")
    d_y = V(scr[:, 0:1536].rearrange("p (a b) -> p a b", a=12))
    d_sq = sb([128, 128], BF16, "d_sq")
    d_qT = sb([128, 4, 128], BF16, "d_qT")
    d_kT = sb([128, 4, 128], BF16, "d_kT")
    d_vT = sb([128, 4, 128], BF16, "d_vT")
    d_v = sb([128, 4, 128], F32, "d_v")
    d_k = sb([128, 4, 128], BF16, "d_k")
    d_ba = sb([128, 8], F32, "d_ba")
    d_arow = sb([4, 128], F32, "d_arow")
    d_grow = sb([4, 128], F32, "d_grow")
    d_z = sb([128, 512], F32, "d_z")
    d_E1 = sb([128, 128], F32, "d_E1")
    d_E1s = sb([128, 128], F32, "d_E1s")
    d_P = [sb([128, 128], F32, f"d_P{i}") for i in range(2)]
    d_PT = [sb([128, 128], F32, f"d_PT{i}") for i in range(2)]
    d_TT = sb([128, 128], F32, "d_TT")
    d_r = sb([128, 128], F32, "d_r")
    d_vn = sb([128, 128], BF16, "d_vn")
    d_vc = sb([128, 128], BF16, "d_vc")
    d_aT = sb([128, 128], BF16, "d_aT")
    d_o1 = sb([128, 128], F32, "d_o1")
    d_S = sb([128, L, 4, 128], F32, "d_S")
    d_Sb = sb([128, L, 4, 128], BF16, "d_Sb")

    for (t_, k) in [(h_S[0], 'h_S'), (h_Sb, 'h_Sb'), (m_C, 'm_C'), (m_Cb, 'm_Cb'), (d_S, 'd_S'), (d_Sb, 'd_Sb'),
                    (d_x, 'd_x'), (h_qj, 'h_qj'), (a_kT, 'a_kT'), (a_v, 'a_v')]:
        p.op('pool', (lambda e, t_=t_: e.memset(t_[:], 0.0)), writes=[k])

    NPIECE = 43
    wscr = nc.dram_tensor("wscr", [L, NPIECE, 128, 4160], BF16, kind="Internal").ap()
    piece_ctr = {}

    def _piece(l_, ti_):
        k = (l_, ti_)
        i = piece_ctr.get(k, 0)
        piece_ctr[k] = i + 1
        assert i < NPIECE
        return i

    def wload(src_ap, ncol, l_, ti_):
        pi = _piece(l_, ti_)
        scr_ap = wscr[l_, pi, :, 0:8 * ncol]
        skey = ('wscr', l_, pi)
        if ti_ == 0:
            p.dma('pool', scr_ap.rearrange("p (kc n) -> p kc n", kc=8), src_ap.rearrange("(kc p) n -> p kc n", p=128),
                  writes=[skey])
        i = wb_n[0] % NWB
        wb_n[0] += 1
        key = f'wbuf{i}'
        dst = wbuf[i][:].rearrange("p a b -> p (a b)")[:, 0:8 * ncol]
        p.dma('sp', dst, scr_ap, reads=[skey], writes=[key])
        return V3(dst.rearrange("p (kc n) -> p kc n", kc=8)), key

    def wload_rows(src_ap, l_, ti_):
        pi = _piece(l_, ti_)
        scr_ap = wscr[l_, pi, :, 0:4096]
        skey = ('wscr', l_, pi)
        if ti_ == 0:
            p.dma('pool', scr_ap.rearrange("p (r n) -> p r n", r=4), src_ap.rearrange("(r p) n -> p r n", p=128),
                  writes=[skey])
        i = wb_n[0] % NWB
        wb_n[0] += 1
        key = f'wbuf{i}'
        dst = wbuf[i][:].rearrange("p a b -> p (a b)")[:, 0:4096]
        p.dma('sp', dst, scr_ap, reads=[skey], writes=[key])
        return V3(dst.rearrange("p (r n) -> p r n", r=4)), key

    def proj_fm(wb, wkey, c0, ncol, ps_ap, pskey):
        for kc in range(8):
            p.op('pe', (lambda e, kc=kc: e.matmul(ps_ap, lhsT=wb[:, kc, c0:c0 + ncol], rhs=hT[:, kc, :],
                                                  start=(kc == 0), stop=(kc == 7))),
                 reads=[wkey, 'hT'], writes=[pskey], inc=(kc == 7))

    def proj_tm(wb, wkey, c0, ncol, ps_ap, pskey):
        for kc in range(8):
            p.op('pe', (lambda e, kc=kc: e.matmul(ps_ap, lhsT=hT[:, kc, :], rhs=wb[:, kc, c0:c0 + ncol],
                                                  start=(kc == 0), stop=(kc == 7))),
                 reads=[wkey, 'hT'], writes=[pskey], inc=(kc == 7))

    def rmsnorm_to_T(src, gt, dstT, dstkey, l):
        p.op('act', lambda e: e.activation(out=junk[:], in_=src[:], func=AF.Square, accum_out=sm[:, 0:1]),
             reads=['xt'], writes=['junk', 'sm0'])
        p.op('act', lambda e: e.activation(out=sm[:, 1:2], in_=sm[:, 0:1], func=AF.Ln, scale=1.0 / 1024, bias=epsb[:, 0:1]),
             reads=['sm0', 'epsb'], writes=['sm1'])
        p.op('act', lambda e: e.activation(out=sm[:, 2:3], in_=sm[:, 1:2], func=AF.Exp, scale=-0.5),
             reads=['sm1'], writes=['sm2'])
        p.op('dve', lambda e: e.scalar_tensor_tensor(out=hbf[:], in0=src[:], scalar=sm[:, 2:3], in1=gt[:, l, :],
                                                     op0=ALU.mult, op1=ALU.mult),
             reads=['xt', 'sm2', 'prm'], writes=['hbf'])
        for kc in range(8):
            p.op('pe', (lambda e, kc=kc: e.transpose(out=PT[:, kc * 128:(kc + 1) * 128], in_=hbf[:, kc * 128:(kc + 1) * 128],
                                                     identity=ident_b[:])),
                 reads=['hbf', 'k_ident'], writes=['PT'], inc=(kc == 7))
        p.op('act', lambda e: e.copy(out=dstT[:].rearrange("p a b -> p (a b)"), in_=PT[:]), reads=['PT'], writes=[dstkey])

    epsb = sb([128, 2], F32, "epsb")
    p.op('dve', lambda e: e.memset(epsb[:, 0:1], EPS), writes=['epsb'])
    p.op('dve', lambda e: e.memset(epsb[:, 1:2], 1.0), writes=['epsb'])

    def head_norm_gate(src_ap, srckey, srcreads, gtile_ap, gate_ap, gatekey, out_ap, smc):
        p.op('act', lambda e: e.activation(out=junk[:, 0:128], in_=src_ap, func=AF.Square, accum_out=sm[:, smc:smc + 1]),
             reads=[srckey] + srcreads, writes=['junk', f'sm{smc}'])
        p.op('act', lambda e: e.activation(out=sm[:, smc + 1:smc + 2], in_=sm[:, smc:smc + 1], func=AF.Ln, scale=1.0 / 128,
                                           bias=epsb[:, 0:1]), reads=[f'sm{smc}', 'epsb'], writes=[f'sm{smc+1}'])
        p.op('act', lambda e: e.activation(out=sm[:, smc + 2:smc + 3], in_=sm[:, smc + 1:smc + 2], func=AF.Exp, scale=-0.5),
             reads=[f'sm{smc+1}'], writes=[f'sm{smc+2}'])
        p.op('dve', lambda e: e.scalar_tensor_tensor(out=tmpC[:, 0:128], in0=src_ap, scalar=sm[:, smc + 2:smc + 3],
                                                     in1=gtile_ap, op0=ALU.mult, op1=ALU.mult),
             reads=[srckey, f'sm{smc+2}', 'prm'] + srcreads, writes=['tmpC'])
        p.op('dve', lambda e: e.tensor_tensor(out=out_ap, in0=tmpC[:, 0:128], in1=gate_ap, op=ALU.mult),
             reads=['tmpC', gatekey], writes=['obf'])

    for ti in range(NTILES):
        p.dma('sp', xt[:], x_in[ti * 128:(ti + 1) * 128, :], writes=['xt'])
        for l in range(L):
            par = ti % 2
            W_in = prm['w_in'][l]
            rmsnorm_to_T(xt, gmix, hT, 'hT', l)
            if not all(enable):
                p.op('pool', lambda e: e.memset(obf[:], 0.0), writes=['obf'])

            if enable[0]:
                wb, wk = wload(W_in[:, C_HQ:C_HQ + 512], 512, l, ti)
                for h in range(4):
                    proj_fm(wb, wk, h * 128, 128, PS[0][:, h * 128:(h + 1) * 128], 'PS0')
                p.op('act', lambda e: e.activation(out=h_q[:].rearrange("p a b -> p (a b)"), in_=PS[0][:], func=AF.Silu),
                     reads=['PS0'], writes=['h_q'])
                wb, wk = wload(W_in[:, C_HF:C_HF + 512], 512, l, ti)
                for h in range(4):
                    proj_fm(wb, wk, h * 128, 128, PS[1][:, h * 128:(h + 1) * 128], 'PS1')
                p.op('act', lambda e: e.activation(out=h_f[:].rearrange("p a b -> p (a b)"), in_=PS[1][:], func=AF.Sigmoid),
                     reads=['PS1'], writes=['h_f'])
                for h in range(4):
                    p.op('dve', (lambda e, h=h: e.tensor_scalar(out=h_f[:, h, :], in0=h_f[:, h, :], scalar1=oml[:, l, h:h + 1],
                                                                scalar2=lb[:, l, h:h + 1], op0=ALU.mult, op1=ALU.add)),
                         reads=['h_f', 'oml', 'lb'], writes=['h_f'])
                hf2 = h_f[:].rearrange("p a b -> p (a b)")
                hk2 = h_k[:].rearrange("p a b -> p (a b)")
                hc2 = h_cum[:].rearrange("p a b -> p (a b)")
                he2 = h_e[:].rearrange("p a b -> p (a b)")
                hq2 = h_q[:].rearrange("p a b -> p (a b)")
                p.op('dve', lambda e: e.tensor_scalar(out=hk2, in0=hf2, scalar1=-1.0, scalar2=1.0, op0=ALU.mult, op1=ALU.add),
                     reads=['h_f'], writes=['h_k'])
                p.op('act', lambda e: e.activation(out=hf2, in_=hf2, func=AF.Ln), reads=['h_f', 'h_k'], writes=['h_f'])
                for h in range(4):
                    p.op('dve', (lambda e, h=h: e.tensor_tensor_scan(out=h_cum[:, h, :], data0=resetm[:], data1=h_f[:, h, :],
                                                                     initial=0.0, op0=ALU.mult, op1=ALU.add)),
                         reads=['h_f', 'k_resetm'], writes=['h_cum'])
                p.op('act', lambda e: e.activation(out=he2, in_=hc2, func=AF.Exp), reads=['h_cum'], writes=['h_e'])
                p.op('dve', lambda e: e.tensor_tensor(out=h_qp[:].rearrange("p a b -> p (a b)"), in0=hq2, in1=he2, op=ALU.mult),
                     reads=['h_q', 'h_e'], writes=['h_qp'])
                p.op('act', lambda e: e.activation(out=he2, in_=hc2, func=AF.Exp, scale=-1.0), reads=['h_cum', 'h_qp'],
                     writes=['h_e'])
                p.op('dve', lambda e: e.tensor_tensor(out=h_kp[:].rearrange("p a b -> p (a b)"), in0=hk2, in1=he2, op=ALU.mult),
                     reads=['h_k', 'h_e'], writes=['h_kp'])
                cl = h_cum[:].rearrange("p a (j i) -> p a j i", i=32)[:, :, :, 31]
                p.op('act', lambda e: e.activation(out=h_edec[:], in_=cl, func=AF.Exp), reads=['h_cum'], writes=['h_edec'])
                p.op('dve', lambda e: e.tensor_tensor(
                    out=h_e[:].rearrange("p a (j i) -> p a j i", i=32),
                    in0=h_cum[:].rearrange("p a (j i) -> p a j i", i=32)[:, :, :, 31:32].broadcast_to([128, 4, 4, 32]),
                    in1=h_cum[:].rearrange("p a (j i) -> p a j i", i=32), op=ALU.subtract),
                    reads=['h_cum', 'h_kp'], writes=['h_e'])
                p.op('act', lambda e: e.activation(out=he2, in_=he2, func=AF.Exp), reads=['h_e'], writes=['h_e'])
                p.op('dve', lambda e: e.tensor_tensor(out=h_kpp[:].rearrange("p a b -> p (a b)"), in0=hk2, in1=he2, op=ALU.mult),
                     reads=['h_k', 'h_e'], writes=['h_kpp'])
                wb, wk = wload(W_in[:, C_HI:C_HI + 512], 512, l, ti)
                proj_tm(wb, wk, 0, 512, PS[0][:], 'PS0')
                p.op('act', lambda e: e.copy(out=h_v[:], in_=PS[0][:]), reads=['PS0'], writes=['h_v'])
                wb, wk = wload(W_in[:, C_HG:C_HG + 512], 512, l, ti)
                proj_tm(wb, wk, 0, 512, PS[1][:], 'PS1')
                p.op('act', lambda e: e.activation(out=h_g[:], in_=PS[1][:], func=AF.Silu), reads=['PS1'], writes=['h_g'])
                for h in range(4):
                    p.op('pe', (lambda e, h=h: e.matmul(PS[2][:, 0:128], lhsT=h_kp[:, h, :], rhs=h_qp[:, h, :], start=True,
                                                        stop=True)), reads=['h_kp', 'h_qp'], writes=['PS2'])
                    p.op('dve', lambda e: e.tensor_tensor(out=h_AT[:], in0=PS[2][:, 0:128], in1=hgmask[:], op=ALU.mult),
                         reads=['PS2', 'k_hgmask'], writes=['h_AT'])
                    p.op('pe', (lambda e, h=h: e.transpose(out=PT[:, 0:128], in_=h_kpp[:, h, :], identity=ident_b[:])),
                         reads=['h_kpp', 'k_ident'], writes=['PT'])
                    for j in range(4):
                        p.op('dve', (lambda e, j=j: e.tensor_scalar(out=h_kj[:, j, :], in0=PT[:, 0:128], scalar1=rowm[:, j:j + 1],
                                                                    scalar2=None, op0=ALU.mult)),
                             reads=['PT', 'k_rowm'], writes=['h_kj'])
                        p.op('act', (lambda e, h=h, j=j: e.copy(out=h_qj[:, h, j, 32 * j:32 * j + 32],
                                                                in_=h_qp[:, h, 32 * j:32 * j + 32])),
                             reads=['h_qp'], writes=['h_qj'])
                    p.op('pe', (lambda e, h=h: e.matmul(PS[3][:, 0:128], lhsT=h_AT[:], rhs=h_v[:, h * 128:(h + 1) * 128],
                                                        start=True, stop=False)), reads=['h_AT', 'h_v'], writes=['PS3'])
                    for j in range(4):
                        p.op('pe', (lambda e, h=h, j=j: e.matmul(PS[3][:, 0:128], lhsT=h_qj[:, h, j, :], rhs=h_Sb[:, l, h, :],
                                                                 start=False, stop=(j == 3))),
                             reads=['h_qj', 'h_Sb'], writes=['PS3'])
                        p.op('pe', (lambda e, h=h, j=j: e.matmul(PS[4][:, 0:128], lhsT=h_kj[:, j, :],
                                                                 rhs=h_v[:, h * 128:(h + 1) * 128], start=True, stop=True)),
                             reads=['h_kj', 'h_v'], writes=['PS4'])
                        p.op('dve', (lambda e, h=h, j=j: e.scalar_tensor_tensor(
                            out=h_S[0][:, l, h, :], in0=h_S[0][:, l, h, :], scalar=h_edec[:, h, j:j + 1], in1=PS[4][:, 0:128],
                            op0=ALU.mult, op1=ALU.add)), reads=['h_S', 'h_edec', 'PS4'], writes=['h_S'])
                        p.op('act', (lambda e, h=h: e.copy(out=h_Sb[:, l, h, :], in_=h_S[0][:, l, h, :])),
                             reads=['h_S'], writes=['h_Sb'])
                    head_norm_gate(PS[3][:, 0:128], 'PS3', [], hog[:, l, h * 128:(h + 1) * 128], h_g[:, h * 128:(h + 1) * 128],
                                   'h_g', obf[:, 0, h * 128:(h + 1) * 128], 4)

            if enable[1]:
                wb, wk = wload(W_in[:, C_MQ:C_MQ + 512], 512, l, ti)
                for h in range(4):
                    proj_fm(wb, wk, h * 64, 64, PS[0][0:64, h * 128:(h + 1) * 128], 'PS0')
                    proj_fm(wb, wk, 256 + h * 64, 64, PS[1][0:64, h * 128:(h + 1) * 128], 'PS1')
                p.op('act', lambda e: e.copy(out=m_q[:].rearrange("p a b -> p (a b)"), in_=PS[0][0:64, :]), reads=['PS0'],
                     writes=['m_q'])
                p.op('act', lambda e: e.mul(out=m_kT[:].rearrange("p a b -> p (a b)"), in_=PS[1][0:64, :], mul=0.125),
                     reads=['PS1'], writes=['m_kT'])
                proj_tm(wb, wk, 256, 256, PS[2][:, 0:256], 'PS2')
                p.op('act', lambda e: e.mul(out=m_k[:], in_=PS[2][:, 0:256], mul=0.125), reads=['PS2'], writes=['m_k'])
                wb, wk = wload(W_in[:, C_MV:C_MV + 520], 520, l, ti)
                proj_tm(wb, wk, 0, 512, PS[0][:], 'PS0')
                p.op('act', lambda e: e.copy(out=m_v[:], in_=PS[0][:]), reads=['PS0'], writes=['m_v'])
                proj_tm(wb, wk, 512, 8, PS[1][:, 0:8], 'PS1')
                p.op('dve', lambda e: e.tensor_tensor(out=m_if[:], in0=PS[1][:, 0:8], in1=mifb[:, l, :], op=ALU.add),
                     reads=['PS1', 'prm'], writes=['m_if'])
                wb, wk = wload(W_in[:, C_MO:C_MO + 512], 512, l, ti)
                proj_tm(wb, wk, 0, 512, PS[2][:], 'PS2')
                p.op('act', lambda e: e.activation(out=m_o[:], in_=PS[2][:], func=AF.Sigmoid), reads=['PS2'], writes=['m_o'])
                p.op('act', lambda e: e.activation(out=sm[:, 8:12], in_=m_if[:, 4:8], func=AF.Exp, scale=-1.0),
                     reads=['m_if'], writes=['sm8'])
                p.op('act', lambda e: e.activation(out=sm[:, 8:12], in_=sm[:, 8:12], func=AF.Ln, bias=epsb[:, 1:2]),
                     reads=['sm8', 'epsb'], writes=['sm8'])
                p.op('pe', lambda e: e.matmul(PS[3][:, 0:4], lhsT=tri_incl[:], rhs=sm[:, 8:12], start=True, stop=True),
                     reads=['sm8', 'k_tri_incl'], writes=['PS3'])
                p.op('act', lambda e: e.copy(out=sm[:, 12:16], in_=PS[3][:, 0:4]), reads=['PS3'], writes=['sm12'])
                p.op('dve', lambda e: e.tensor_tensor(out=sm[:, 16:20], in0=PS[3][:, 0:4], in1=m_if[:, 0:4], op=ALU.add),
                     reads=['PS3', 'm_if'], writes=['sm16'])
                p.op('act', lambda e: e.activation(out=sm[:, 16:20], in_=sm[:, 16:20], func=AF.Exp), reads=['sm16'],
                     writes=['sm16'])
                p.op('act', lambda e: e.activation(out=sm[:, 20:24], in_=sm[:, 12:16], func=AF.Exp, scale=-1.0),
                     reads=['sm12'], writes=['sm20'])
                p.op('pe', lambda e: e.matmul(PS[3][0:64, 8:12], lhsT=sel_last[:, 0:64], rhs=sm[:, 12:16], start=True, stop=True),
                     reads=['sm12', 'k_sel_last'], writes=['PS3'])
                p.op('act', lambda e: e.activation(out=sm[0:64, 24:28], in_=PS[3][0:64, 8:12], func=AF.Exp, scale=-1.0),
                     reads=['PS3'], writes=['sm24'])
                for h in range(4):
                    p.op('dve', (lambda e, h=h: e.tensor_scalar(out=m_va[:, h, 0:128], in0=m_v[:, h * 128:(h + 1) * 128],
                                                                scalar1=sm[:, 16 + h:17 + h], scalar2=None, op0=ALU.mult)),
                         reads=['m_v', 'sm16'], writes=['m_va'])
                    p.op('dve', (lambda e, h=h: e.tensor_copy(out=m_va[:, h, 128:129], in_=sm[:, 16 + h:17 + h])),
                         reads=['sm16'], writes=['m_va'])
                for h in range(4):
                    p.op('pe', (lambda e, h=h: e.matmul(PS[4][:, 0:128], lhsT=m_kT[:, h, :], rhs=m_q[:, h, :], start=True,
                                                        stop=True)), reads=['m_kT', 'm_q'], writes=['PS4'])
                    p.op('dve', lambda e: e.tensor_tensor(out=m_AT[:], in0=PS[4][:, 0:128], in1=tri_incl[:], op=ALU.mult),
                         reads=['PS4', 'k_tri_incl'], writes=['m_AT'])
                    p.op('pe', (lambda e, h=h: e.matmul(PS[5][:, 0:129], lhsT=m_AT[:], rhs=m_va[:, h, 0:129], start=True,
                                                        stop=False)), reads=['m_AT', 'm_va'], writes=['PS5'], inc=False)
                    p.op('pe', (lambda e, h=h: e.matmul(PS[5][:, 0:129], lhsT=m_q[:, h, :], rhs=m_Cb[:, l, h, 0:129], start=False,
                                                        stop=True)), reads=['m_q', 'm_Cb'], writes=['PS5'])
                    p.op('dve', (lambda e, h=h: e.tensor_tensor(out=sm[:, 28:29], in0=PS[5][:, 128:129], in1=sm[:, 20 + h:21 + h],
                                                                op=ALU.mult)), reads=['PS5', 'sm20'], writes=['sm28'])
                    p.op('dve', lambda e: e.scalar_tensor_tensor(out=sm[:, 31:32], in0=sm[:, 28:29], scalar=-1.0, in1=sm[:, 28:29],
                                                                 op0=ALU.mult, op1=ALU.max), reads=['sm28'], writes=['sm31'])
                    p.op('dve', lambda e: e.tensor_scalar(out=sm[:, 28:29], in0=sm[:, 31:32], scalar1=1.0, scalar2=None,
                                                          op0=ALU.max), reads=['sm31'], writes=['sm28'])
                    p.op('dve', lambda e: e.reciprocal(out=sm[:, 29:30], in_=sm[:, 28:29]), reads=['sm28'], writes=['sm29'])
                    p.op('dve', (lambda e, h=h: e.tensor_tensor(out=sm[:, 30:31], in0=sm[:, 29:30], in1=sm[:, 20 + h:21 + h],
                                                                op=ALU.mult)), reads=['sm29', 'sm20'], writes=['sm30'])
                    p.op('act', lambda e: e.activation(out=tmpA[:, 0:128], in_=PS[5][:, 0:128], func=AF.Copy, scale=sm[:, 30:31]),
                         reads=['PS5', 'sm30'], writes=['tmpA'])
                    p.op('pe', (lambda e, h=h: e.matmul(PS[6][0:64, 0:129], lhsT=m_k[:, h * 64:(h + 1) * 64], rhs=m_va[:, h, 0:129],
                                                        start=True, stop=True)), reads=['m_k', 'm_va'], writes=['PS6'])
                    p.op('dve', (lambda e, h=h: e.tensor_tensor(out=m_C[:, l, h, 0:129], in0=m_C[:, l, h, 0:129],
                                                                in1=PS[6][0:64, 0:129], op=ALU.add)),
                         reads=['m_C', 'PS6'], writes=['m_C'])
                    p.op('dve', (lambda e, h=h: e.tensor_scalar(out=m_C[:, l, h, 0:129], in0=m_C[:, l, h, 0:129],
                                                                scalar1=sm[0:64, 24 + h:25 + h], scalar2=None, op0=ALU.mult)),
                         reads=['m_C', 'sm24'], writes=['m_C'])
                    p.op('act', (lambda e, h=h: e.copy(out=m_Cb[:, l, h, 0:129], in_=m_C[:, l, h, 0:129])), reads=['m_C'],
                         writes=['m_Cb'])
                    head_norm_gate(tmpA[:, 0:128], 'tmpA', [], mog[:, l, h * 128:(h + 1) * 128], m_o[:, h * 128:(h + 1) * 128],
                                   'm_o', obf[:, 1, h * 128:(h + 1) * 128], 32)

            if enable[2]:
                wb, wk = wload(W_in[:, C_AQ:C_AQ + 512], 512, l, ti)
                wb2, wk2 = wload(W_in[:, C_AK:C_AK + 256], 256, l, ti)
                for g in range(8):
                    proj_fm(wb, wk, g * 64, 64, PS[g // 4][0:64, (g % 4) * 128:(g % 4 + 1) * 128], f'PS{g // 4}')
                for kv in range(2):
                    proj_fm(wb2, wk2, kv * 64, 64, PS[2][0:64, kv * 128:(kv + 1) * 128], 'PS2')
                az2 = a_z[:].rearrange("p a b -> p (a b)")
                asq2 = a_sq[:].rearrange("p a b -> p (a b)")
                p.op('act', lambda e: e.copy(out=az2[:, 0:512], in_=PS[0][0:64, :]), reads=['PS0'], writes=['a_z'])
                p.op('act', lambda e: e.copy(out=az2[:, 512:1024], in_=PS[1][0:64, :]), reads=['PS1'], writes=['a_z'])
                p.op('act', lambda e: e.copy(out=az2[:, 1024:1280], in_=PS[2][0:64, 0:256]), reads=['PS2'], writes=['a_z'])
                p.op('dve', lambda e: e.tensor_tensor(out=asq2, in0=az2, in1=az2, op=ALU.mult), reads=['a_z'], writes=['a_sq'])
                for i3 in range(3):
                    w3 = 512 if i3 < 2 else 256
                    p.op('pe', (lambda e, i3=i3, w3=w3: e.matmul(PS[3][0:64, 0:w3], lhsT=ones_f[0:64, 0:64],
                                                                 rhs=asq2[:, i3 * 512:i3 * 512 + w3], start=True, stop=True)),
                         reads=['a_sq', 'k_ones'], writes=['PS3'])
                    p.op('act', (lambda e, i3=i3, w3=w3: e.activation(out=asq2[:, i3 * 512:i3 * 512 + w3], in_=PS[3][0:64, 0:w3],
                                                                      func=AF.Ln, scale=1.0 / 64, bias=epsb[0:64, 0:1])),
                         reads=['PS3', 'epsb'], writes=['a_sq'])
                p.op('act', lambda e: e.activation(out=asq2, in_=asq2, func=AF.Exp, scale=-0.5), reads=['a_sq'], writes=['a_sq'])
                p.op('dve', lambda e: e.tensor_tensor(out=a_q[:].rearrange("p a b -> p (a b)"), in0=az2[:, 0:1024],
                                                      in1=asq2[:, 0:1024], op=ALU.mult), reads=['a_z', 'a_sq'], writes=['a_q'])
                for kv in range(2):
                    p.op('dve', (lambda e, kv=kv: e.scalar_tensor_tensor(
                        out=a_kT[:, l, kv, par, :], in0=a_z[:, 8 + kv, :], scalar=gk8[:, l:l + 1], in1=a_sq[:, 8 + kv, :],
                        op0=ALU.mult, op1=ALU.mult)), reads=['a_z', 'a_sq', 'gk8'], writes=['a_kT'])
                proj_tm(wb2, wk2, 128, 128, PS[4][:, 0:128], 'PS4')
                for kv in range(2):
                    p.op('act', (lambda e, kv=kv: e.copy(out=a_v[:, l, par, kv, 0:64], in_=PS[4][:, kv * 64:(kv + 1) * 64])),
                         reads=['PS4'], writes=['a_v'])
                    p.op('dve', (lambda e, kv=kv: e.memset(a_v[:, l, par, kv, 64:65], 1.0)), writes=['a_v'])
                for g in range(8):
                    kv = g // 4
                    blks = [1] if ti == 0 else [0, 1]
                    for bi, blk in enumerate(blks):
                        slot = par if blk == 1 else 1 - par
                        p.op('pe', (lambda e, g=g, kv=kv, slot=slot: e.matmul(PS[5][:, 0:128], lhsT=a_kT[:, l, kv, slot, :],
                                                                              rhs=a_q[:, g, :], start=True, stop=True)),
                             reads=['a_kT', 'a_q'], writes=['PS5'])
                        p.op('act', lambda e: e.activation(out=a_P[:], in_=PS[5][:, 0:128], func=AF.Exp), reads=['PS5'],
                             writes=['a_P'])
                        p.op('dve', (lambda e, g=g, blk=blk: e.tensor_tensor(out=a_PT[:, blk, :], in0=a_P[:], in1=EB[:, blk, g, :],
                                                                             op=ALU.mult)),
                             reads=['a_P', 'EB'], writes=['a_PT'])
                    for bi, blk in enumerate(blks):
                        slot = par if blk == 1 else 1 - par
                        p.op('pe', (lambda e, kv=kv, slot=slot, blk=blk, bi=bi: e.matmul(
                            PS[6][:, 0:65], lhsT=a_PT[:, blk, :], rhs=a_v[:, l, slot, kv, 0:65], start=(bi == 0),
                            stop=(bi == len(blks) - 1))), reads=['a_PT', 'a_v'], writes=['PS6'], inc=(bi == len(blks) - 1))
                    p.op('dve', (lambda e, g=g: e.tensor_tensor(out=sm[:, 40:41], in0=PS[6][:, 64:65], in1=esink[:, l, g:g + 1],
                                                                op=ALU.add)), reads=['PS6', 'esink'], writes=['sm40'])
                    p.op('dve', lambda e: e.reciprocal(out=sm[:, 41:42], in_=sm[:, 40:41]), reads=['sm40'], writes=['sm41'])
                    p.op('dve', (lambda e, g=g: e.tensor_scalar(out=obf[:, 2, g * 64:(g + 1) * 64], in0=PS[6][:, 0:64],
                                                                scalar1=sm[:, 41:42], scalar2=None, op0=ALU.mult)),
                         reads=['PS6', 'sm41'], writes=['obf'])

            if enable[3]:
                for c3 in range(3):
                    wb, wk = wload(W_in[:, C_DQKV + c3 * 512:C_DQKV + (c3 + 1) * 512], 512, l, ti)
                    for c4 in range(4):
                        proj_fm(wb, wk, c4 * 128, 128, PS[c3][:, c4 * 128:(c4 + 1) * 128], f'PS{c3}')
                    p.op('act', (lambda e, c3=c3: e.copy(out=d_x[:, l, c3 * 4:(c3 + 1) * 4, 3:131],
                                                         in_=PS[c3][:].rearrange("p (a b) -> p a b", a=4))),
                         reads=[f'PS{c3}'], writes=['d_x'])
                for cc in range(12):
                    p.op('dve', (lambda e, cc=cc: e.tensor_scalar(out=d_y[:, cc, :], in0=d_x[:, l, cc, 0:128],
                                                                  scalar1=convw[:, l, 0, cc:cc + 1], scalar2=None, op0=ALU.mult)),
                         reads=['d_x', 'prm'], writes=['d_y'])
                    for j in range(1, 4):
                        p.op('dve', (lambda e, cc=cc, j=j: e.scalar_tensor_tensor(
                            out=d_y[:, cc, :], in0=d_x[:, l, cc, j:j + 128], scalar=convw[:, l, j, cc:cc + 1], in1=d_y[:, cc, :],
                            op0=ALU.mult, op1=ALU.add)), reads=['d_x', 'prm', 'd_y'], writes=['d_y'])
                p.op('pool', lambda e: e.tensor_copy(out=d_x[:, l, :, 0:3], in_=d_x[:, l, :, 128:131]), reads=['d_x', 'd_y'],
                     writes=['d_x'])
                dy2 = d_y[:].rearrange("p a b -> p (a b)")
                p.op('act', lambda e: e.activation(out=dy2, in_=dy2, func=AF.Silu), reads=['d_y'], writes=['d_y'])
                for cc in range(8):
                    p.op('dve', (lambda e, cc=cc: e.tensor_tensor(out=tmpA[:, 0:128], in0=d_y[:, cc, :], in1=d_y[:, cc, :],
                                                                  op=ALU.mult)), reads=['d_y'], writes=['tmpA'])
                    p.op('pe', lambda e: e.matmul(PS[3][:, 0:128], lhsT=ones_f[:], rhs=tmpA[:, 0:128], start=True, stop=True),
                         reads=['tmpA', 'k_ones'], writes=['PS3'])
                    p.op('act', lambda e: e.activation(out=tmpB[:, 0:128], in_=PS[3][:, 0:128], func=AF.Ln, bias=epsb[:, 0:1]),
                         reads=['PS3', 'epsb'], writes=['tmpB'])
                    p.op('act', lambda e: e.activation(out=tmpB[:, 0:128], in_=tmpB[:, 0:128], func=AF.Exp, scale=-0.5),
                         reads=['tmpB'], writes=['tmpB'])
                    if cc < 4:
                        p.op('dve', (lambda e, cc=cc: e.scalar_tensor_tensor(
                            out=d_qT[:, cc, :], in0=d_y[:, cc, :], scalar=float(128 ** -0.5), in1=tmpB[:, 0:128],
                            op0=ALU.mult, op1=ALU.mult)), reads=['d_y', 'tmpB'], writes=['d_qT'])
                    else:
                        p.op('dve', (lambda e, cc=cc: e.tensor_tensor(out=d_kT[:, cc - 4, :], in0=d_y[:, cc, :],
                                                                      in1=tmpB[:, 0:128], op=ALU.mult)),
                             reads=['d_y', 'tmpB'], writes=['d_kT'])
                p.op('dve', lambda e: e.tensor_copy(out=d_vT[:], in_=d_y[:, 8:12, :]), reads=['d_y'], writes=['d_vT'])
                for h in range(4):
                    p.op('pe', (lambda e, h=h: e.transpose(out=PT[:, h * 128:(h + 1) * 128], in_=d_vT[:, h, :],
                                                           identity=ident_b[:])), reads=['d_vT', 'k_ident'], writes=['PT'],
                         inc=False)
                    p.op('pe', (lambda e, h=h: e.transpose(out=PT[:, 512 + h * 128:512 + (h + 1) * 128], in_=d_kT[:, h, :],
                                                           identity=ident_b[:])), reads=['d_kT', 'k_ident'], writes=['PT'],
                         inc=(h == 3))
                p.op('act', lambda e: e.copy(out=d_v[:].rearrange("p a b -> p (a b)"), in_=PT[:, 0:512]), reads=['PT'],
                     writes=['d_v'])
                p.op('act', lambda e: e.copy(out=d_k[:].rearrange("p a b -> p (a b)"), in_=PT[:, 512:1024]), reads=['PT'],
                     writes=['d_k'])
                wb, wk = wload(W_in[:, C_DB:C_DB + 520], 520, l, ti)
                proj_tm(wb, wk, 0, 8, PS[0][:, 0:8], 'PS0')
                p.op('act', lambda e: e.copy(out=d_ba[:], in_=PS[0][:, 0:8]), reads=['PS0'], writes=['d_ba'])
                proj_fm(wb, wk, 4, 4, PS[1][0:4, 0:128], 'PS1')
                p.op('act', lambda e: e.copy(out=d_arow[:], in_=PS[1][0:4, 0:128]), reads=['PS1'], writes=['d_arow'])
                proj_tm(wb, wk, 8, 512, PS[2][:], 'PS2')
                p.op('act', lambda e: e.activation(out=d_z[:], in_=PS[2][:], func=AF.Silu), reads=['PS2'], writes=['d_z'])
                p.op('act', lambda e: e.activation(out=sm[:, 44:48], in_=d_ba[:, 0:4], func=AF.Sigmoid), reads=['d_ba'],
                     writes=['sm44'])
                p.op('dve', lambda e: e.tensor_tensor(out=sm[:, 48:52], in0=d_ba[:, 4:8], in1=dtb[:, l, :], op=ALU.add),
                     reads=['d_ba', 'prm'], writes=['sm48'])
                p.op('act', lambda e: e.activation(out=sm[:, 48:52], in_=sm[:, 48:52], func=AF.Exp), reads=['sm48'],
                     writes=['sm48'])
                p.op('act', lambda e: e.activation(out=sm[:, 48:52], in_=sm[:, 48:52], func=AF.Ln, bias=epsb[:, 1:2]),
                     reads=['sm48', 'epsb'], writes=['sm48'])
                p.op('dve', lambda e: e.tensor_tensor(out=sm[:, 48:52], in0=sm[:, 48:52], in1=nexpa[:, l, :], op=ALU.mult),
                     reads=['sm48', 'nexpa'], writes=['sm48'])
                p.op('pe', lambda e: e.matmul(PS[3][:, 0:4], lhsT=tri_incl[:], rhs=sm[:, 48:52], start=True, stop=True),
                     reads=['sm48', 'k_tri_incl'], writes=['PS3'])
                p.op('act', lambda e: e.copy(out=sm[:, 52:56], in_=PS[3][:, 0:4]), reads=['PS3'], writes=['sm52'])
                p.op('dve', lambda e: e.tensor_scalar(out=sm[:, 56:60], in0=sm[:, 52:56], scalar1=-1.0, scalar2=None,
                                                      op0=ALU.mult), reads=['sm52'], writes=['sm56'])
                p.op('pe', lambda e: e.matmul(PS[3][:, 8:12], lhsT=sel_last[:], rhs=sm[:, 52:56], start=True, stop=True),
                     reads=['sm52', 'k_sel_last'], writes=['PS3'])
                p.op('act', lambda e: e.activation(out=sm[:, 60:64], in_=PS[3][:, 8:12], func=AF.Exp), reads=['PS3'],
                     writes=['sm60'])
                p.op('dve', lambda e: e.tensor_tensor(out=tmpC[:, 500:504], in0=PS[3][:, 8:12], in1=sm[:, 52:56],
                                                      op=ALU.subtract), reads=['PS3', 'sm52'], writes=['tmpC5'])
                p.op('act', lambda e: e.activation(out=tmpC[:, 500:504], in_=tmpC[:, 500:504], func=AF.Exp), reads=['tmpC5'],
                     writes=['tmpC5'])
                p.op('dve', lambda e: e.tensor_tensor(out=tmpC[:, 504:508], in0=tmpC[:, 500:504], in1=sm[:, 44:48], op=ALU.mult),
                     reads=['tmpC5', 'sm44'], writes=['tmpC6'])
                p.op('act', lambda e: e.activation(out=tmpC[:, 508:512], in_=sm[:, 52:56], func=AF.Exp), reads=['sm52'],
                     writes=['tmpC7'])
                p.op('dve', lambda e: e.tensor_scalar(out=tmpC[:, 496:500], in0=tmpC[:, 508:512], scalar1=-1.0, scalar2=None,
                                                      op0=ALU.mult), reads=['tmpC7'], writes=['tmpC4'])
                p.op('dve', lambda e: e.tensor_scalar(out=tmpC[:, 492:496], in0=sm[:, 44:48], scalar1=-1.0, scalar2=None,
                                                      op0=ALU.mult), reads=['sm44'], writes=['tmpC3'])
                p.op('act', lambda e: e.activation(out=d_grow[:], in_=d_arow[:], func=AF.Exp, bias=dtbrow[:, l:l + 1]),
                     reads=['d_arow', 'dtbrow'], writes=['d_grow'])
                p.op('act', lambda e: e.activation(out=d_grow[:], in_=d_grow[:], func=AF.Ln, bias=epsb[0:4, 1:2]),
                     reads=['d_grow', 'epsb'], writes=['d_grow'])
                p.op('dve', lambda e: e.tensor_scalar(out=d_grow[:], in0=d_grow[:], scalar1=nexparow[:, l:l + 1], scalar2=None,
                                                      op0=ALU.mult), reads=['d_grow', 'nexparow'], writes=['d_grow'])
                p.op('dve', lambda e: e.tensor_tensor_scan(out=d_grow[:], data0=ones_f[0:4, :], data1=d_grow[:], initial=0.0,
                                                           op0=ALU.mult, op1=ALU.add), reads=['d_grow', 'k_ones'],
                     writes=['d_grow'])
                for h in range(4):
                    p.op('pe', (lambda e, h=h: e.matmul(PS[4][:, 0:128], lhsT=selh[:, h * 128:(h + 1) * 128], rhs=d_grow[:],
                                                        start=True, stop=True)), reads=['k_selh', 'd_grow'], writes=['PS4'])
                    p.op('dve', lambda e: e.tensor_tensor(out=tmpA[:, 0:128], in0=PS[4][:, 0:128], in1=negmask[:], op=ALU.add),
                         reads=['PS4', 'k_negmask'], writes=['tmpA'])
                    p.op('act', (lambda e, h=h: e.activation(out=d_E1[:], in_=tmpA[:, 0:128], func=AF.Exp,
                                                             bias=sm[:, 56 + h:57 + h])), reads=['tmpA', 'sm56'], writes=['d_E1'])
                    p.op('dve', lambda e: e.tensor_tensor(out=d_E1s[:], in0=d_E1[:], in1=tri_strict[:], op=ALU.mult),
                         reads=['d_E1', 'k_tri_strict'], writes=['d_E1s'])
                    p.op('pe', (lambda e, h=h: e.matmul(PS[4][:, 128:256], lhsT=d_kT[:, h, :], rhs=d_kT[:, h, :], start=True,
                                                        stop=True)), reads=['d_kT'], writes=['PS4'])
                    p.op('dve', (lambda e, h=h: e.scalar_tensor_tensor(out=d_PT[0][:], in0=PS[4][:, 128:256],
                                                                       scalar=tmpC[:, 492 + h:493 + h], in1=d_E1s[:],
                                                                       op0=ALU.mult, op1=ALU.mult)),
                         reads=['PS4', 'tmpC3', 'd_E1s'], writes=['d_PT0'])
                    p.op('pe', lambda e: e.transpose(out=PS[5][:, 0:128], in_=d_PT[0][:], identity=ident_f[:]),
                         reads=['d_PT0', 'k_ident'], writes=['PS5'])
                    p.op('act', lambda e: e.copy(out=d_P[0][:], in_=PS[5][:, 0:128]), reads=['PS5'], writes=['d_P0'])
                    p.op('dve', lambda e: e.tensor_tensor(out=d_TT[:], in0=d_PT[0][:], in1=ident_f[:], op=ALU.add),
                         reads=['d_PT0', 'k_ident'], writes=['d_TT'])
                    cur = 0
                    for lvl in range(6):
                        nxt = 1 - cur
                        p.op('pe', (lambda e, cur=cur: e.matmul(PS[5][:, 0:128], lhsT=d_PT[cur][:], rhs=d_P[cur][:], start=True,
                                                                stop=True)), reads=[f'd_PT{cur}', f'd_P{cur}'], writes=['PS5'])
                        if lvl < 5:
                            p.op('pe', (lambda e, cur=cur: e.matmul(PS[5][:, 128:256], lhsT=d_P[cur][:], rhs=d_PT[cur][:],
                                                                    start=True, stop=True)),
                                 reads=[f'd_PT{cur}', f'd_P{cur}'], writes=['PS5b'])
                        p.op('act', (lambda e, nxt=nxt: e.copy(out=d_P[nxt][:], in_=PS[5][:, 0:128])), reads=['PS5'],
                             writes=[f'd_P{nxt}'])
                        if lvl < 5:
                            p.op('act', (lambda e, nxt=nxt: e.copy(out=d_PT[nxt][:], in_=PS[5][:, 128:256])), reads=['PS5b'],
                                 writes=[f'd_PT{nxt}'])
                        p.op('pe', (lambda e, nxt=nxt: e.matmul(PS[6][:, 0:128], lhsT=d_P[nxt][:], rhs=d_TT[:], start=True,
                                                                stop=True)), reads=[f'd_P{nxt}', 'd_TT'], writes=['PS6'])
                        p.op('dve', lambda e: e.tensor_tensor(out=d_TT[:], in0=d_TT[:], in1=PS[6][:, 0:128], op=ALU.add),
                             reads=['d_TT', 'PS6'], writes=['d_TT'])
                        cur = nxt
                    p.op('pe', (lambda e, h=h: e.matmul(PS[4][:, 256:384], lhsT=d_kT[:, h, :], rhs=d_Sb[:, l, h, :], start=True,
                                                        stop=True)), reads=['d_kT', 'd_Sb'], writes=['PS4'])
                    p.op('dve', (lambda e, h=h: e.scalar_tensor_tensor(out=d_r[:], in0=PS[4][:, 256:384],
                                                                       scalar=tmpC[:, 496 + h:497 + h], in1=d_v[:, h, :],
                                                                       op0=ALU.mult, op1=ALU.add)),
                         reads=['PS4', 'tmpC4', 'd_v'], writes=['d_r'])
                    p.op('pe', lambda e: e.matmul(PS[5][:, 256:384], lhsT=d_TT[:], rhs=d_r[:], start=True, stop=True),
                         reads=['d_TT', 'd_r'], writes=['PS5c'])
                    p.op('dve', (lambda e, h=h: e.tensor_scalar(out=d_vn[:], in0=PS[5][:, 256:384], scalar1=sm[:, 44 + h:45 + h],
                                                                scalar2=None, op0=ALU.mult)), reads=['PS5c', 'sm44'],
                         writes=['d_vn'])
                    p.op('dve', (lambda e, h=h: e.tensor_scalar(out=d_vc[:], in0=PS[5][:, 256:384],
                                                                scalar1=tmpC[:, 504 + h:505 + h], scalar2=None, op0=ALU.mult)),
                         reads=['PS5c', 'tmpC6'], writes=['d_vc'])
                    p.op('pe', (lambda e, h=h: e.matmul(PS[4][:, 384:512], lhsT=d_kT[:, h, :], rhs=d_qT[:, h, :], start=True,
                                                        stop=True)), reads=['d_kT', 'd_qT'], writes=['PS4'])
                    p.op('dve', lambda e: e.tensor_tensor(out=d_aT[:], in0=PS[4][:, 384:512], in1=d_E1[:], op=ALU.mult),
                         reads=['PS4', 'd_E1'], writes=['d_aT'])
                    p.op('pe', (lambda e, h=h: e.matmul(PS[6][:, 128:256], lhsT=d_qT[:, h, :], rhs=d_Sb[:, l, h, :], start=True,
                                                        stop=True)), reads=['d_qT', 'd_Sb'], writes=['PS6b'])
                    p.op('act', (lambda e, h=h: e.activation(out=d_o1[:], in_=PS[6][:, 128:256], func=AF.Copy,
                                                             scale=tmpC[:, 508 + h:509 + h])), reads=['PS6b', 'tmpC7'],
                         writes=['d_o1'])
                    p.op('pe', lambda e: e.matmul(PS[6][:, 256:384], lhsT=d_aT[:], rhs=d_vn[:], start=True, stop=True),
                         reads=['d_aT', 'd_vn'], writes=['PS6c'])
                    p.op('dve', lambda e: e.tensor_tensor(out=tmpB[:, 0:128], in0=PS[6][:, 256:384], in1=d_o1[:], op=ALU.add),
                         reads=['PS6c', 'd_o1'], writes=['tmpB'])
                    p.op('pe', (lambda e, h=h: e.matmul(PS[6][:, 384:512], lhsT=d_k[:, h, :], rhs=d_vc[:], start=True, stop=True)),
                         reads=['d_k', 'd_vc'], writes=['PS6d'])
                    p.op('dve', (lambda e, h=h: e.scalar_tensor_tensor(out=d_S[:, l, h, :], in0=d_S[:, l, h, :],
                                                                       scalar=sm[:, 60 + h:61 + h], in1=PS[6][:, 384:512],
                                                                       op0=ALU.mult, op1=ALU.add)),
                         reads=['d_S', 'sm60', 'PS6d'], writes=['d_S'])
                    p.op('act', (lambda e, h=h: e.copy(out=d_Sb[:, l, h, :], in_=d_S[:, l, h, :])), reads=['d_S'],
                         writes=['d_Sb'])
                    head_norm_gate(tmpB[:, 0:128], 'tmpB', [], dog[:, l, :], d_z[:, h * 128:(h + 1) * 128], 'd_z',
                                   obf[:, 3, h * 128:(h + 1) * 128], 36)

            for gi in range(8):
                wb, wk = wload(W_in[:, C_GATE + gi * 512:C_GATE + (gi + 1) * 512], 512, l, ti)
                pst, pk = (PS[0], 'PS0') if gi % 2 == 0 else (PS[1], 'PS1')
                proj_tm(wb, wk, 0, 512, pst[:], pk)
                p.op('act', (lambda e, gi=gi, pst=pst: e.activation(out=gates[:, gi * 512:(gi + 1) * 512], in_=pst[:],
                                                                    func=AF.Sigmoid)), reads=[pk], writes=['gates'])
            for half in range(2):
                for c8 in range(8):
                    cc = half * 8 + c8
                    p.op('pe', (lambda e, cc=cc, c8=c8: e.transpose(
                        out=PT[:, c8 * 128:(c8 + 1) * 128],
                        in_=obf[:].rearrange("p a b -> p (a b)")[:, cc * 128:(cc + 1) * 128], identity=ident_b[:])),
                        reads=['obf', 'k_ident'], writes=['PT'], inc=(c8 == 7))
                p.op('act', (lambda e, half=half: e.copy(out=oT[:, half * 8:(half + 1) * 8, :].rearrange("p a b -> p (a b)"),
                                                         in_=PT[:])), reads=['PT'], writes=['oT'])
            for n in range(4):
                wv, wk = wload_rows(prm['w_branch'][l, n], l, ti)
                for half in range(2):
                    pst, pk = (PS[2], 'PS2') if half == 0 else (PS[3], 'PS3')
                    for wc in range(4):
                        p.op('pe', (lambda e, n=n, wc=wc, half=half, pst=pst: e.matmul(
                            pst[:], lhsT=oT[:, n * 4 + wc, :], rhs=wv[:, wc, half * 512:(half + 1) * 512], start=(wc == 0),
                            stop=(wc == 3))), reads=['oT', wk], writes=[pk], inc=(wc == 3))
                    if n == 0:
                        p.op('dve', (lambda e, n=n, half=half, pst=pst: e.tensor_tensor(
                            out=merged[:, half * 512:(half + 1) * 512], in0=pst[:],
                            in1=gates[:, n * 1024 + half * 512:n * 1024 + (half + 1) * 512], op=ALU.mult)),
                            reads=[pk, 'gates'], writes=['merged'])
                    else:
                        p.op('dve', (lambda e, n=n, half=half, pst=pst: e.tensor_tensor(
                            out=tmpA[:], in0=pst[:], in1=gates[:, n * 1024 + half * 512:n * 1024 + (half + 1) * 512],
                            op=ALU.mult)), reads=[pk, 'gates'], writes=['tmpA'])
                        p.op('dve', (lambda e, half=half: e.tensor_tensor(
                            out=merged[:, half * 512:(half + 1) * 512], in0=merged[:, half * 512:(half + 1) * 512], in1=tmpA[:],
                            op=ALU.add)), reads=['tmpA', 'merged'], writes=['merged'])
            p.op('act', lambda e: e.copy(out=mbf[:], in_=merged[:]), reads=['merged'], writes=['mbf'])
            for kc in range(8):
                p.op('pe', (lambda e, kc=kc: e.transpose(out=PT[:, kc * 128:(kc + 1) * 128], in_=mbf[:, kc * 128:(kc + 1) * 128],
                                                         identity=ident_b[:])), reads=['mbf', 'k_ident'], writes=['PT'],
                     inc=(kc == 7))
            p.op('act', lambda e: e.copy(out=mT[:].rearrange("p a b -> p (a b)"), in_=PT[:]), reads=['PT'], writes=['mT'])
            for half in range(2):
                wb, wk = wload(prm['w_out'][l][:, half * 512:(half + 1) * 512], 512, l, ti)
                pst, pk = (PS[0], 'PS0') if half == 0 else (PS[1], 'PS1')
                for kc in range(8):
                    p.op('pe', (lambda e, kc=kc, pst=pst, wb=wb: e.matmul(pst[:], lhsT=mT[:, kc, :], rhs=wb[:, kc, 0:512],
                                                                          start=(kc == 0), stop=(kc == 7))),
                         reads=['mT', wk], writes=[pk], inc=(kc == 7))
                p.op('dve', (lambda e, half=half, pst=pst: e.tensor_tensor(out=xt[:, half * 512:(half + 1) * 512],
                                                                           in0=xt[:, half * 512:(half + 1) * 512], in1=pst[:],
                                                                           op=ALU.add)), reads=[pk, 'xt'], writes=['xt'])
            rmsnorm_to_T(xt, gmlp, hT, 'hT', l)
            for fi in range(8):
                wb, wk = wload(prm['w_up'][l][:, fi * 512:(fi + 1) * 512], 512, l, ti)
                pst, pk = (PS[2], 'PS2') if fi % 2 == 0 else (PS[3], 'PS3')
                for f4 in range(4):
                    proj_fm(wb, wk, f4 * 128, 128, pst[:, f4 * 128:(f4 + 1) * 128], pk)
                p.op('act', (lambda e, pst=pst: e.activation(out=tmpB[:], in_=pst[:], func=AF.Relu)), reads=[pk], writes=['tmpB'])
                p.op('dve', (lambda e, fi=fi, pst=pst: e.tensor_tensor(
                    out=uT[:, fi * 4:(fi + 1) * 4, :].rearrange("p a b -> p (a b)"), in0=tmpB[:], in1=pst[:], op=ALU.mult)),
                    reads=['tmpB', pk], writes=['uT'])
            for fi in range(8):
                wv, wk = wload_rows(prm['w_down'][l][fi * 512:(fi + 1) * 512, :], l, ti)
                for half in range(2):
                    pk = 'PS0' if half == 0 else 'PS1'
                    pst = PS[0] if half == 0 else PS[1]
                    for f4 in range(4):
                        p.op('pe', (lambda e, fi=fi, f4=f4, half=half, pst=pst, wv=wv: e.matmul(
                            pst[:], lhsT=uT[:, fi * 4 + f4, :], rhs=wv[:, f4, half * 512:(half + 1) * 512],
                            start=(fi == 0 and f4 == 0), stop=(fi == 7 and f4 == 3))),
                            reads=['uT', wk], writes=[pk], inc=(f4 == 3))
            for half in range(2):
                pk = 'PS0' if half == 0 else 'PS1'
                pst = PS[0] if half == 0 else PS[1]
                p.op('dve', (lambda e, half=half, pst=pst: e.tensor_tensor(out=xt[:, half * 512:(half + 1) * 512],
                                                                           in0=xt[:, half * 512:(half + 1) * 512], in1=pst[:],
                                                                           op=ALU.add)), reads=[pk, 'xt'], writes=['xt'])
        p.dma('sp', y_out[ti * 128:(ti + 1) * 128, :], xt[:], reads=['xt'], writes=['yout'])
    p.final_wait('sp', ['yout'])
    p.emit()
    return nc


def host_params(inputs):
    f = lambda k: np.ascontiguousarray(np.asarray(inputs[k], dtype=np.float32))
    m = {}
    for k in ['norm_mix_g', 'w_in', 'hgrn_out_g', 'mlstm_out_g', 'attn_sinks', 'rel_bias_table', 'dn_a_log', 'dn_dt_bias',
              'dn_out_g', 'w_branch', 'w_out', 'norm_mlp_g', 'w_up', 'w_down']:
        m[k] = f(k)
    m['mlstm_if_bias'] = f('mlstm_if_bias').reshape(2, 8)
    m['hgrn_lb_table'] = np.ascontiguousarray(f('hgrn_lb_table').reshape(2, 4, 128).transpose(0, 2, 1))
    m['attn_q_norm_g'] = f('attn_q_norm_g').reshape(2, 64, 1)
    m['attn_k_norm_g'] = f('attn_k_norm_g').reshape(2, 64, 1)
    m['dn_conv_w'] = np.ascontiguousarray(f('dn_conv_w').reshape(2, 4, 12, 128).transpose(0, 1, 3, 2))
    m['dn_alog_row'] = np.ascontiguousarray(f('dn_a_log').T)
    m['dn_dtb_row'] = np.ascontiguousarray(f('dn_dt_bias').T)
    return m


def kernel(**inputs):
    x = np.ascontiguousarray(np.asarray(inputs['x'], dtype=np.float32))
    B, T, _ = x.shape
    nc = build(T, 2)
    consts = host_consts()
    hp = host_params(inputs)
    in_maps = []
    for b in range(B):
        m = {'x': x[b]}
        m.update(hp)
        for k, v in consts.items():
            m['c_' + k] = v
        in_maps.append(m)
    res = run_bass_kernel_spmd(nc, in_maps, core_ids=list(range(B)))
    return np.stack([np.asarray(r['y'], dtype=np.float32) for r in res.results], axis=0)
```

```python
import types
import numpy as np
from contextlib import ExitStack
import concourse.bass as bass
import concourse.mybir as mybir
from concourse.bass_utils import run_bass_kernel_spmd

F32 = mybir.dt.float32
BF16 = mybir.dt.bfloat16
AF = mybir.ActivationFunctionType
ALU = mybir.AluOpType
AX = mybir.AxisListType
ENGS = ['pe', 'act', 'dve', 'pool', 'sp']

D = 1024
NIN = 10512
EPS = 1e-6


def _freeze(fn):
    if fn is None or fn.__closure__ is None:
        return fn
    cells = []
    for c in fn.__closure__:
        try:
            cells.append(types.CellType(c.cell_contents))
        except ValueError:
            cells.append(c)
    return types.FunctionType(fn.__code__, fn.__globals__, fn.__name__, fn.__defaults__, tuple(cells))


class V3:
    def __init__(self, ap):
        self.ap = ap

    def __getitem__(self, k):
        return self.ap[k]


class Prog:
    def __init__(self, nc, nd=8):
        self.nc = nc
        self.ND = nd
        self.lists = {e: [] for e in ENGS}
        self.cnt = {e: 0 for e in ENGS}
        self.waited = {e: {} for e in ENGS}
        self.W = {}
        self.Rd = {}
        self.dma_n = {e: 0 for e in ENGS}
        self.semkeys = set()
        self.st = ExitStack()
        self.ntens = 0
        self.alias = {}

    def sb(self, shape, dt, name=None):
        self.ntens += 1
        return self.st.enter_context(self.nc.sbuf_tensor(name or f"t{self.ntens}", list(shape), dt))

    def ps(self, shape, dt=F32, name=None):
        self.ntens += 1
        return self.st.enter_context(self.nc.psum_tensor(name or f"p{self.ntens}", list(shape), dt))

    def _deps(self, eng, reads, writes):
        deps = {}

        def add(d):
            for sk, v in d.items():
                if deps.get(sk, 0) < v:
                    deps[sk] = v
        for r in reads:
            add(self.W.get(r, {}))
        for w in writes:
            add(self.W.get(w, {}))
            add(self.Rd.get(w, {}))
        out = []
        for sk, v in deps.items():
            if sk == ('c', 'pe') and eng == 'pe':
                continue
            if self.waited[eng].get(sk, 0) >= v:
                continue
            self.waited[eng][sk] = v
            out.append((sk, v))
        return out

    def _rec(self, tok, reads, writes):
        sk, v = tok
        for r in reads:
            d = self.Rd.setdefault(r, {})
            d[sk] = max(d.get(sk, 0), v)
        for w in writes:
            d = self.W.setdefault(w, {})
            d[sk] = max(d.get(sk, 0), v)

    def _x(self, keys):
        out = []
        for k in keys:
            if isinstance(k, (list, tuple)):
                out.extend(self._x(k))
            elif k in self.alias:
                out.extend(self.alias[k])
            else:
                out.append(k)
        return out

    def op(self, eng, fn, reads=(), writes=(), inc=True):
        reads = self._x(reads)
        writes = self._x(writes)
        waits = self._deps(eng, reads, writes)
        sk = ('c', eng)
        tok = (sk, self.cnt[eng] + 1)
        if inc:
            self.cnt[eng] += 1
        self.semkeys.add(sk)
        self.lists[eng].append((waits, _freeze(fn), sk if inc else None, 1))
        self._rec(tok, reads, writes)

    def dma(self, q, out, in_, reads=(), writes=(), **kw):
        reads = self._x(reads)
        writes = self._x(writes)
        j = self.dma_n[q]
        self.dma_n[q] += 1
        slot = j % self.ND
        val = 16 * (j // self.ND + 1)
        sk = ('d', q, slot)
        self.semkeys.add(sk)
        waits = self._deps(q, reads, writes)
        if j >= self.ND and self.waited[q].get(sk, 0) < val - 16:
            self.waited[q][sk] = val - 16
            waits.append((sk, val - 16))
        self.lists[q].append((waits, (lambda e, o=out, i=in_, k=kw: e.dma_start(out=o, in_=i, **k)), sk, 16))
        self._rec((sk, val), reads, writes)

    def final_wait(self, eng, keys):
        keys = self._x(keys)
        waits = self._deps(eng, keys, ())
        self.lists[eng].append((waits, None, None, 0))

    def emit(self):
        nc = self.nc
        st = self.st
        sems = {}
        for sk in sorted(self.semkeys, key=str):
            sems[sk] = st.enter_context(nc.semaphore("s_" + "_".join(map(str, sk))))
        block = st.enter_context(nc.Block())
        lists = self.lists

        def run(name, e):
            for waits, fn, sk, incv in lists[name]:
                for wsk, v in waits:
                    e.wait_ge(sems[wsk], v)
                if fn is None:
                    continue
                ins = fn(e)
                if sk is not None:
                    ins.then_inc(sems[sk], incv)

        @block.tensor
        def _(e):
            run('pe', e)

        @block.scalar
        def _(e):
            run('act', e)

        @block.vector
        def _(e):
            run('dve', e)

        @block.gpsimd
        def _(e):
            run('pool', e)

        @block.sync
        def _(e):
            run('sp', e)
        st.close()


def _t5_bucket_np(n):
    max_exact = 16
    nf = np.maximum(n, max_exact).astype(np.float32)
    large = max_exact + (np.log(nf / max_exact) / np.log(np.float32(128 / max_exact)) * 16).astype(np.int32)
    large = np.minimum(large, 31)
    return np.where(n < max_exact, n, large)


def host_consts():
    c = {}
    s = np.arange(128)[:, None]
    t = np.arange(128)[None, :]
    c['tri_incl'] = (s <= t).astype(np.float32)
    c['tri_strict'] = (s < t).astype(np.float32)
    c['hgmask'] = ((s <= t) & (s // 32 == t // 32)).astype(np.float32)
    c['negmask'] = np.where(s <= t, 0.0, -1e30).astype(np.float32)
    c['ident'] = np.eye(128, dtype=np.float32)
    sel = np.zeros((128, 128), np.float32)
    sel[127, :] = 1.0
    c['sel_last'] = sel
    rm = np.zeros((128, 4), np.float32)
    for j in range(4):
        rm[32 * j:32 * j + 32, j] = 1.0
    c['rowm'] = rm
    rs = np.ones((128, 128), np.float32)
    rs[:, 0::32] = 0.0
    c['resetm'] = rs
    selh = np.zeros((4, 4, 128), np.float32)
    for h in range(4):
        selh[h, h, :] = 1.0
    c['selh'] = selh.reshape(4, 512)
    c['ones'] = np.ones((128, 128), np.float32)
    bk = _t5_bucket_np(np.arange(128))
    oh = np.zeros((32, 128), np.float32)
    oh[bk, np.arange(128)] = 1.0
    c['bias_oh'] = oh
    ab = np.zeros((128, 384), np.float32)
    for dd in range(128):
        ab[dd, 255 - dd] = 1.0
    c['antiband'] = ab
    return c


CONST_SHAPES = {
    'tri_incl': (128, 128), 'tri_strict': (128, 128), 'hgmask': (128, 128), 'negmask': (128, 128),
    'ident': (128, 128), 'sel_last': (128, 128), 'rowm': (128, 4), 'resetm': (128, 128),
    'selh': (4, 512), 'ones': (128, 128), 'bias_oh': (32, 128), 'antiband': (128, 384),
}

PARAM_SHAPES = {
    'norm_mix_g': (2, 1024), 'w_in': (2, 1024, NIN), 'hgrn_lb_table': (2, 128, 4), 'hgrn_out_g': (2, 512),
    'mlstm_if_bias': (2, 8), 'mlstm_out_g': (2, 512), 'attn_q_norm_g': (2, 64, 1), 'attn_k_norm_g': (2, 64, 1),
    'attn_sinks': (2, 8), 'rel_bias_table': (32, 8), 'dn_conv_w': (2, 4, 128, 12), 'dn_a_log': (2, 4),
    'dn_dt_bias': (2, 4), 'dn_alog_row': (4, 2), 'dn_dtb_row': (4, 2), 'dn_out_g': (2, 128), 'w_branch': (2, 4, 512, 1024), 'w_out': (2, 1024, 1024),
    'norm_mlp_g': (2, 1024), 'w_up': (2, 1024, 4096), 'w_down': (2, 4096, 1024),
}

C_HQ, C_HF, C_HI, C_HG = 0, 512, 1024, 1536
C_MQ, C_MK, C_MV, C_MI, C_MF, C_MO = 2048, 2304, 2560, 3072, 3076, 3080
C_AQ, C_AK, C_AV = 3592, 4104, 4232
C_DQKV, C_DB, C_DA, C_DZ = 4360, 5896, 5900, 5904
C_GATE = 6416


def build(T, L=2, enable=(1, 1, 1, 1), dbg=None):
    nc = bass.Bass("TRN2", target_bir_lowering=False)
    NTILES = T // 128
    x_in = nc.dram_tensor("x", [T, D], F32, kind="ExternalInput").ap()
    y_out = nc.dram_tensor("y", [T, D], F32, kind="ExternalOutput").ap()
    prm = {k: nc.dram_tensor(k, list(s), F32, kind="ExternalInput").ap() for k, s in PARAM_SHAPES.items()}
    cst = {k: nc.dram_tensor("c_" + k, list(s), F32, kind="ExternalInput").ap() for k, s in CONST_SHAPES.items()}
    dbg_out = {}
    p = Prog(nc)
    sb, ps = p.sb, p.ps

    def load_const(name, dt=F32, q='sp'):
        shp = CONST_SHAPES[name]
        t_ = sb(shp, dt, "k_" + name + ("_b" if dt == BF16 else ""))
        p.dma(q, t_[:], cst[name], writes=['k_' + name])
        return t_
    tri_incl = load_const('tri_incl')
    tri_strict = load_const('tri_strict')
    hgmask = load_const('hgmask')
    negmask = load_const('negmask')
    ident_f = load_const('ident')
    ident_b = load_const('ident', BF16, 'pool')
    sel_last = load_const('sel_last')
    rowm = load_const('rowm')
    resetm = load_const('resetm')
    selh = load_const('selh')
    ones_f = load_const('ones')
    ones_b = load_const('ones', BF16, 'pool')
    KC = ['k_tri_incl', 'k_tri_strict', 'k_hgmask', 'k_negmask', 'k_ident', 'k_sel_last', 'k_rowm', 'k_resetm',
          'k_selh', 'k_ones']

    gmix = sb([128, L, 1024], BF16, "gmix")
    gmlp = sb([128, L, 1024], BF16, "gmlp")
    hog = sb([128, L, 512], BF16, "hog")
    mog = sb([128, L, 512], BF16, "mog")
    dog = sb([128, L, 128], F32, "dog")
    mifb = sb([128, L, 8], F32, "mifb")
    sinks = sb([128, L, 8], F32, "sinks")
    esink = sb([128, L, 8], F32, "esink")
    alog = sb([128, L, 4], F32, "alog")
    nexpa = sb([128, L, 4], F32, "nexpa")
    dtb = sb([128, L, 4], F32, "dtb")
    lbt = sb([128, 2, 4], F32, "lbt")
    lb = sb([128, 2, 4], F32, "lb")
    oml = sb([128, 2, 4], F32, "oml")
    convw = sb([128, L, 4, 12], F32, "convw")
    qkg = sb([64, L, 2], F32, "qkg")
    gk8 = sb([64, L], F32, "gk8")
    dtbrow = sb([4, 2], F32, "dtbrow")
    nexparow = sb([4, 2], F32, "nexparow")
    for l in range(L):
        p.dma('pool', gmix[:, l, :], prm['norm_mix_g'][l:l + 1, :].partition_broadcast(128), writes=['prm'])
        p.dma('pool', gmlp[:, l, :], prm['norm_mlp_g'][l:l + 1, :].partition_broadcast(128), writes=['prm'])
        p.dma('pool', hog[:, l, :], prm['hgrn_out_g'][l:l + 1, :].partition_broadcast(128), writes=['prm'])
        p.dma('pool', mog[:, l, :], prm['mlstm_out_g'][l:l + 1, :].partition_broadcast(128), writes=['prm'])
        p.dma('sp', dog[:, l, :], prm['dn_out_g'][l:l + 1, :].partition_broadcast(128), writes=['prm'])
        p.dma('sp', mifb[:, l, :], prm['mlstm_if_bias'][l:l + 1, :].partition_broadcast(128), writes=['prm'])
        p.dma('sp', sinks[:, l, :], prm['attn_sinks'][l:l + 1, :].partition_broadcast(128), writes=['prm'])
        p.dma('sp', alog[:, l, :], prm['dn_a_log'][l:l + 1, :].partition_broadcast(128), writes=['prm'])
        p.dma('sp', dtb[:, l, :], prm['dn_dt_bias'][l:l + 1, :].partition_broadcast(128), writes=['prm'])
        for j in range(4):
            p.dma('sp', convw[:, l, j, :], prm['dn_conv_w'][l, j], writes=['prm'])
        p.dma('sp', qkg[:, l, 0:1], prm['attn_q_norm_g'][l], writes=['prm'])
        p.dma('sp', qkg[:, l, 1:2], prm['attn_k_norm_g'][l], writes=['prm'])
    for l2 in range(2):
        p.dma('sp', lbt[:, l2, :], prm['hgrn_lb_table'][l2], writes=['prm'])
    p.dma('sp', dtbrow[:], prm['dn_dtb_row'], writes=['dtbrow'])
    p.dma('sp', nexparow[:], prm['dn_alog_row'], writes=['nexparow'])
    p.op('act', lambda e: e.activation(out=nexparow[:], in_=nexparow[:], func=AF.Exp), reads=['nexparow'], writes=['nexparow'])
    p.op('dve', lambda e: e.tensor_scalar(out=nexparow[:], in0=nexparow[:], scalar1=-1.0, scalar2=None, op0=ALU.mult),
         reads=['nexparow'], writes=['nexparow'])
    p.op('dve', lambda e: e.memset(lb[:], 0.0), writes=['lb'])
    p.op('dve', lambda e: e.tensor_sub(out=lb[:, 1, :], in0=lbt[:, 1, :], in1=lbt[:, 0, :]), reads=['prm', 'lb'],
         writes=['lb'])
    p.op('act', lambda e: e.activation(out=lb[:, 1, :], in_=lb[:, 1, :], func=AF.Sigmoid), reads=['lb'], writes=['lb'])
    p.op('dve', lambda e: e.tensor_scalar(out=oml[:], in0=lb[:], scalar1=-1.0, scalar2=1.0, op0=ALU.mult, op1=ALU.add),
         reads=['lb'], writes=['oml'])
    p.op('act', lambda e: e.activation(out=esink[:], in_=sinks[:], func=AF.Exp), reads=['prm'], writes=['esink'])
    p.op('act', lambda e: e.activation(out=nexpa[:], in_=alog[:], func=AF.Exp), reads=['prm'], writes=['nexpa'])
    p.op('dve', lambda e: e.tensor_scalar(out=nexpa[:], in0=nexpa[:], scalar1=-1.0, scalar2=None, op0=ALU.mult),
         reads=['nexpa'], writes=['nexpa'])
    p.op('dve', lambda e: e.tensor_tensor(out=gk8[:], in0=qkg[:, :, 0], in1=qkg[:, :, 1], op=ALU.mult), reads=['prm'],
         writes=['gk8'])
    p.op('dve', lambda e: e.tensor_scalar(out=gk8[:], in0=gk8[:], scalar1=0.125, scalar2=None, op0=ALU.mult),
         reads=['gk8'], writes=['gk8'])

    PS = [ps([128, 512], F32, f"PS{i}") for i in range(7)]
    PT = ps([128, 1024], BF16, "PSTR")

    def K(i, lo=0, hi=512):
        return [f'PS{i}.bank']
    for i in range(7):
        p.alias[f'PS{i}'] = K(i)
    p.alias['PS5b'] = K(5, 128, 256)
    p.alias['PS5c'] = K(5, 256, 384)
    p.alias['PS6b'] = K(6, 128, 256)
    p.alias['PS6c'] = K(6, 256, 384)
    p.alias['PS6d'] = K(6, 384, 512)
    for i in range(5):
        p.alias[f'scr{i}'] = [f'scr.{i}']
    for nm in ['tmpA', 'tmpB', 'junk']:
        p.alias[nm] = [f'{nm}.{i}' for i in range(4)]
    p.alias['h_q'] = ['scr.0']
    p.alias['h_f'] = ['scr.1']
    p.alias['h_k'] = ['scr.2']
    p.alias['h_cum'] = ['scr.3']
    p.alias['h_e'] = ['scr.4']
    p.alias['a_z'] = ['scr.0', 'scr.1', 'scr.2']
    p.alias['a_sq'] = ['scr.2', 'scr.3', 'scr.4']
    p.alias['d_y'] = ['scr.0', 'scr.1', 'scr.2']
    EB = sb([128, 2, 8, 128], F32, "EB")
    relt = sb([32, 8], F32, "relt")
    boh = sb([32, 128], F32, "boh")
    aband = sb([128, 384], F32, "aband")
    vecE = sb([128, 8], F32, "vecE")
    p.dma('sp', relt[:], prm['rel_bias_table'], writes=['relt'])
    p.dma('sp', boh[:], cst['bias_oh'], writes=['boh'])
    p.dma('sp', aband[:], cst['antiband'], writes=['aband'])
    p.op('pe', lambda e: e.matmul(PS[0][:, 0:8], lhsT=boh[:], rhs=relt[:], start=True, stop=True), reads=['boh', 'relt'],
         writes=K(0))
    p.op('act', lambda e: e.activation(out=vecE[:], in_=PS[0][:, 0:8], func=AF.Exp), reads=K(0), writes=['vecE'])
    for blk in range(2):
        for t0 in range(0, 128, 64):
            pst = PS[1]
            for tt in range(64):
                off = 255 - (t0 + tt + (128 if blk == 0 else 0))
                p.op('pe', (lambda e, off=off, tt=tt: e.matmul(pst[:, tt * 8:(tt + 1) * 8], lhsT=aband[:, off:off + 128],
                                                               rhs=vecE[:], start=True, stop=True)),
                     reads=['aband', 'vecE'], writes=K(1), inc=(tt == 63))
            p.op('dve', (lambda e, blk=blk, t0=t0: e.tensor_copy(
                out=EB[:, blk, :, t0:t0 + 64], in_=pst[:].rearrange("p (t g) -> p g t", g=8))),
                reads=K(1), writes=['EB'])

    xt = sb([128, 1024], F32, "xt")
    hbf = sb([128, 1024], BF16, "hbf")
    hT = sb([128, 8, 128], BF16, "hT")
    NWB = 3
    wbuf = [sb([128, 8, 520], BF16, f"wbuf{i}") for i in range(NWB)]
    wb_n = [0]
    sm = sb([128, 128], F32, "small")
    obf = sb([128, 4, 512], BF16, "obf")
    oT = sb([128, 16, 128], BF16, "oT")
    gates = sb([128, 4096], BF16, "gates")
    merged = sb([128, 1024], F32, "merged")
    mbf = sb([128, 1024], BF16, "mbf")
    mT = sb([128, 8, 128], BF16, "mT")
    uT = sb([128, 32, 128], BF16, "uT")
    tmpA = sb([128, 512], F32, "tmpA")
    tmpB = sb([128, 512], F32, "tmpB")
    tmpC = sb([128, 512], F32, "tmpC")
    junk = sb([128, 1024], BF16, "junk")

    scr = sb([128, 2560], F32, "scr")

    class V:
        def __init__(self, ap):
            self.ap = ap

        def __getitem__(self, k):
            return self.ap[k]
    h_q = V(scr[:, 0:512].rearrange("p (a b) -> p a b", a=4))
    h_f = V(scr[:, 512:1024].rearrange("p (a b) -> p a b", a=4))
    h_k = V(scr[:, 1024:1536].rearrange("p (a b) -> p a b", a=4))
    h_cum = V(scr[:, 1536:2048].rearrange("p (a b) -> p a b", a=4))
    h_e = V(scr[:, 2048:2560].rearrange("p (a b) -> p a b", a=4))
    h_qp = sb([128, 4, 128], BF16, "h_qp")
    h_kp = sb([128, 4, 128], BF16, "h_kp")
    h_kpp = sb([128, 4, 128], BF16, "h_kpp")
    h_edec = sb([128, 4, 4], F32, "h_edec")
    h_v = sb([128, 512], BF16, "h_v")
    h_g = sb([128, 512], F32, "h_g")
    h_AT = sb([128, 128], BF16, "h_AT")
    h_kj = sb([128, 4, 128], BF16, "h_kj")
    h_qj = sb([128, 4, 4, 128], BF16, "h_qj")
    h_S = [sb([128, L, 4, 128], F32, "h_S")]
    h_Sb = sb([128, L, 4, 128], BF16, "h_Sb")

    m_q = sb([64, 4, 128], BF16, "m_q")
    m_kT = sb([64, 4, 128], BF16, "m_kT")
    m_k = sb([128, 256], BF16, "m_k")
    m_v = sb([128, 512], F32, "m_v")
    m_if = sb([128, 8], F32, "m_if")
    m_o = sb([128, 512], F32, "m_o")
    m_va = sb([128, 4, 130], BF16, "m_va")
    m_AT = sb([128, 128], BF16, "m_AT")
    m_C = sb([64, L, 4, 130], F32, "m_C")
    m_Cb = sb([64, L, 4, 130], BF16, "m_Cb")

    a_q = sb([64, 8, 128], BF16, "a_q")
    a_z = V(scr[0:64, 0:1280].rearrange("p (a b) -> p a b", a=10))
    a_sq = V(scr[0:64, 1280:2560].rearrange("p (a b) -> p a b", a=10))
    a_kT = sb([64, L, 2, 2, 128], BF16, "a_kT")
    a_v = sb([128, L, 2, 2, 66], BF16, "a_v")
    a_P = sb([128, 128], F32, "a_P")
    a_PT = sb([128, 2, 128], BF16, "a_PT")

    d_x = sb([128, L, 12, 132], BF16, "d_x")
    d_y = V(scr[:, 0:1536].rearrange("p (a b) -> p a b", a=12))
    d_sq = sb([128, 128], BF16, "d_sq")
    d_qT = sb([128, 4, 128], BF16, "d_qT")
    d_kT = sb([128, 4, 128], BF16, "d_kT")
    d_vT = sb([128, 4, 128], BF16, "d_vT")
    d_v = sb([128, 4, 128], F32, "d_v")
    d_k = sb([128, 4, 128], BF16, "d_k")
    d_ba = sb([128, 8], F32, "d_ba")
    d_arow = sb([4, 128], F32, "d_arow")
    d_grow = sb([4, 128], F32, "d_grow")
    d_z = sb([128, 512], F32, "d_z")
    dS_E1 = [sb([128, 128], F32, f"d_E1_{i}") for i in range(2)]
    dS_E1s = [sb([128, 128], F32, f"d_E1s_{i}") for i in range(2)]
    dS_P = [[sb([128, 128], F32, f"d_P{j}_{i}") for j in range(2)] for i in range(2)]
    dS_PT = [[sb([128, 128], F32, f"d_PT{j}_{i}") for j in range(2)] for i in range(2)]
    dS_TT = [sb([128, 128], F32, f"d_TT_{i}") for i in range(2)]
    dS_r = [sb([128, 128], F32, f"d_r_{i}") for i in range(2)]
    dS_vn = [sb([128, 128], BF16, f"d_vn_{i}") for i in range(2)]
    dS_vc = [sb([128, 128], BF16, f"d_vc_{i}") for i in range(2)]
    dS_aT = [sb([128, 128], BF16, f"d_aT_{i}") for i in range(2)]
    dS_o1 = [sb([128, 128], F32, f"d_o1_{i}") for i in range(2)]
    d_S = sb([128, L, 4, 128], F32, "d_S")
    d_Sb = sb([128, L, 4, 128], BF16, "d_Sb")

    for (t_, k) in [(h_S[0], 'h_S'), (h_Sb, 'h_Sb'), (m_C, 'm_C'), (m_Cb, 'm_Cb'), (d_S, 'd_S'), (d_Sb, 'd_Sb'),
                    (d_x, 'd_x'), (h_qj, 'h_qj'), (a_kT, 'a_kT'), (a_v, 'a_v')]:
        p.op('pool', (lambda e, t_=t_: e.memset(t_[:], 0.0)), writes=[k])

    NPIECE = 43
    wscr = nc.dram_tensor("wscr", [L, NPIECE, 128, 4160], BF16, kind="Internal").ap()
    piece_ctr = {}

    def _piece(l_, ti_):
        k = (l_, ti_)
        i = piece_ctr.get(k, 0)
        piece_ctr[k] = i + 1
        assert i < NPIECE
        return i

    def wload(src_ap, ncol, l_, ti_):
        pi = _piece(l_, ti_)
        scr_ap = wscr[l_, pi, :, 0:8 * ncol]
        skey = ('wscr', l_, pi)
        if ti_ == 0:
            p.dma('pool', scr_ap.rearrange("p (kc n) -> p kc n", kc=8), src_ap.rearrange("(kc p) n -> p kc n", p=128),
                  writes=[skey])
        i = wb_n[0] % NWB
        wb_n[0] += 1
        key = f'wbuf{i}'
        dst = wbuf[i][:].rearrange("p a b -> p (a b)")[:, 0:8 * ncol]
        p.dma('sp', dst, scr_ap, reads=[skey], writes=[key])
        return V3(dst.rearrange("p (kc n) -> p kc n", kc=8)), key

    def wload_rows(src_ap, l_, ti_):
        pi = _piece(l_, ti_)
        scr_ap = wscr[l_, pi, :, 0:4096]
        skey = ('wscr', l_, pi)
        if ti_ == 0:
            p.dma('pool', scr_ap.rearrange("p (r n) -> p r n", r=4), src_ap.rearrange("(r p) n -> p r n", p=128),
                  writes=[skey])
        i = wb_n[0] % NWB
        wb_n[0] += 1
        key = f'wbuf{i}'
        dst = wbuf[i][:].rearrange("p a b -> p (a b)")[:, 0:4096]
        p.dma('sp', dst, scr_ap, reads=[skey], writes=[key])
        return V3(dst.rearrange("p (r n) -> p r n", r=4)), key

    def proj_fm(wb, wkey, c0, ncol, ps_ap, pskey):
        for kc in range(8):
            p.op('pe', (lambda e, kc=kc: e.matmul(ps_ap, lhsT=wb[:, kc, c0:c0 + ncol], rhs=hT[:, kc, :],
                                                  start=(kc == 0), stop=(kc == 7))),
                 reads=[wkey, 'hT'], writes=[pskey], inc=(kc == 7))

    def proj_tm(wb, wkey, c0, ncol, ps_ap, pskey):
        for kc in range(8):
            p.op('pe', (lambda e, kc=kc: e.matmul(ps_ap, lhsT=hT[:, kc, :], rhs=wb[:, kc, c0:c0 + ncol],
                                                  start=(kc == 0), stop=(kc == 7))),
                 reads=[wkey, 'hT'], writes=[pskey], inc=(kc == 7))

    def rmsnorm_to_T(src, gt, dstT, dstkey, l):
        p.op('act', lambda e: e.activation(out=junk[:], in_=src[:], func=AF.Square, accum_out=sm[:, 0:1]),
             reads=['xt'], writes=['junk.0', 'junk.1', 'junk.2', 'junk.3', 'sm0'])
        p.op('act', lambda e: e.activation(out=sm[:, 1:2], in_=sm[:, 0:1], func=AF.Ln, scale=1.0 / 1024, bias=epsb[:, 0:1]),
             reads=['sm0', 'epsb'], writes=['sm1'])
        p.op('act', lambda e: e.activation(out=sm[:, 2:3], in_=sm[:, 1:2], func=AF.Exp, scale=-0.5),
             reads=['sm1'], writes=['sm2'])
        p.op('dve', lambda e: e.scalar_tensor_tensor(out=hbf[:], in0=src[:], scalar=sm[:, 2:3], in1=gt[:, l, :],
                                                     op0=ALU.mult, op1=ALU.mult),
             reads=['xt', 'sm2', 'prm'], writes=['hbf'])
        for kc in range(8):
            p.op('pe', (lambda e, kc=kc: e.transpose(out=PT[:, kc * 128:(kc + 1) * 128], in_=hbf[:, kc * 128:(kc + 1) * 128],
                                                     identity=ident_b[:])),
                 reads=['hbf', 'k_ident'], writes=['PT'], inc=(kc == 7))
        p.op('act', lambda e: e.copy(out=dstT[:].rearrange("p a b -> p (a b)"), in_=PT[:]), reads=['PT'], writes=[dstkey])

    epsb = sb([128, 2], F32, "epsb")
    p.op('dve', lambda e: e.memset(epsb[:, 0:1], EPS), writes=['epsb'])
    p.op('dve', lambda e: e.memset(epsb[:, 1:2], 1.0), writes=['epsb'])

    def head_norm_gate(src_ap, srckey, srcreads, gtile_ap, gate_ap, gatekey, out_ap, smc, sl=0):
        jk = junk[:, sl * 128:(sl + 1) * 128]
        tc_ = tmpC[:, sl * 128:(sl + 1) * 128]
        p.op('act', lambda e: e.activation(out=jk, in_=src_ap, func=AF.Square, accum_out=sm[:, smc:smc + 1]),
             reads=[srckey] + srcreads, writes=[f'junk.{sl}', f'sm{smc}'])
        p.op('act', lambda e: e.activation(out=sm[:, smc + 1:smc + 2], in_=sm[:, smc:smc + 1], func=AF.Ln, scale=1.0 / 128,
                                           bias=epsb[:, 0:1]), reads=[f'sm{smc}', 'epsb'], writes=[f'sm{smc+1}'])
        p.op('act', lambda e: e.activation(out=sm[:, smc + 2:smc + 3], in_=sm[:, smc + 1:smc + 2], func=AF.Exp, scale=-0.5),
             reads=[f'sm{smc+1}'], writes=[f'sm{smc+2}'])
        p.op('dve', lambda e: e.scalar_tensor_tensor(out=tc_, in0=src_ap, scalar=sm[:, smc + 2:smc + 3],
                                                     in1=gtile_ap, op0=ALU.mult, op1=ALU.mult),
             reads=[srckey, f'sm{smc+2}', 'prm'] + srcreads, writes=[f'tmpC.{sl}'])
        p.op('dve', lambda e: e.tensor_tensor(out=out_ap, in0=tc_, in1=gate_ap, op=ALU.mult),
             reads=[f'tmpC.{sl}', gatekey], writes=['obf'])

    def interleave(gens):
        gens = list(gens)
        while gens:
            for g in list(gens):
                try:
                    next(g)
                except StopIteration:
                    gens.remove(g)

    for ti in range(NTILES):
        p.dma('sp', xt[:], x_in[ti * 128:(ti + 1) * 128, :], writes=['xt'])
        for l in range(L):
            par = ti % 2
            W_in = prm['w_in'][l]
            rmsnorm_to_T(xt, gmix, hT, 'hT', l)
            if not all(enable):
                p.op('pool', lambda e: e.memset(obf[:], 0.0), writes=['obf'])

            if enable[0]:
                wb, wk = wload(W_in[:, C_HQ:C_HQ + 512], 512, l, ti)
                for h in range(4):
                    proj_fm(wb, wk, h * 128, 128, PS[0][:, h * 128:(h + 1) * 128], 'PS0')
                p.op('act', lambda e: e.activation(out=h_q[:].rearrange("p a b -> p (a b)"), in_=PS[0][:], func=AF.Silu),
                     reads=['PS0'], writes=['h_q'])
                wb, wk = wload(W_in[:, C_HF:C_HF + 512], 512, l, ti)
                for h in range(4):
                    proj_fm(wb, wk, h * 128, 128, PS[1][:, h * 128:(h + 1) * 128], 'PS1')
                p.op('act', lambda e: e.activation(out=h_f[:].rearrange("p a b -> p (a b)"), in_=PS[1][:], func=AF.Sigmoid),
                     reads=['PS1'], writes=['h_f'])
                for h in range(4):
                    p.op('dve', (lambda e, h=h: e.tensor_scalar(out=h_f[:, h, :], in0=h_f[:, h, :], scalar1=oml[:, l, h:h + 1],
                                                                scalar2=lb[:, l, h:h + 1], op0=ALU.mult, op1=ALU.add)),
                         reads=['h_f', 'oml', 'lb'], writes=['h_f'])
                hf2 = h_f[:].rearrange("p a b -> p (a b)")
                hk2 = h_k[:].rearrange("p a b -> p (a b)")
                hc2 = h_cum[:].rearrange("p a b -> p (a b)")
                he2 = h_e[:].rearrange("p a b -> p (a b)")
                hq2 = h_q[:].rearrange("p a b -> p (a b)")
                p.op('dve', lambda e: e.tensor_scalar(out=hk2, in0=hf2, scalar1=-1.0, scalar2=1.0, op0=ALU.mult, op1=ALU.add),
                     reads=['h_f'], writes=['h_k'])
                p.op('act', lambda e: e.activation(out=hf2, in_=hf2, func=AF.Ln), reads=['h_f', 'h_k'], writes=['h_f'])
                for h in range(4):
                    p.op('dve', (lambda e, h=h: e.tensor_tensor_scan(out=h_cum[:, h, :], data0=resetm[:], data1=h_f[:, h, :],
                                                                     initial=0.0, op0=ALU.mult, op1=ALU.add)),
                         reads=['h_f', 'k_resetm'], writes=['h_cum'])
                p.op('act', lambda e: e.activation(out=he2, in_=hc2, func=AF.Exp), reads=['h_cum'], writes=['h_e'])
                p.op('dve', lambda e: e.tensor_tensor(out=h_qp[:].rearrange("p a b -> p (a b)"), in0=hq2, in1=he2, op=ALU.mult),
                     reads=['h_q', 'h_e'], writes=['h_qp'])
                p.op('act', lambda e: e.activation(out=he2, in_=hc2, func=AF.Exp, scale=-1.0), reads=['h_cum', 'h_qp'],
                     writes=['h_e'])
                p.op('dve', lambda e: e.tensor_tensor(out=h_kp[:].rearrange("p a b -> p (a b)"), in0=hk2, in1=he2, op=ALU.mult),
                     reads=['h_k', 'h_e'], writes=['h_kp'])
                cl = h_cum[:].rearrange("p a (j i) -> p a j i", i=32)[:, :, :, 31]
                p.op('act', lambda e: e.activation(out=h_edec[:], in_=cl, func=AF.Exp), reads=['h_cum'], writes=['h_edec'])
                p.op('dve', lambda e: e.tensor_tensor(
                    out=h_e[:].rearrange("p a (j i) -> p a j i", i=32),
                    in0=h_cum[:].rearrange("p a (j i) -> p a j i", i=32)[:, :, :, 31:32].broadcast_to([128, 4, 4, 32]),
                    in1=h_cum[:].rearrange("p a (j i) -> p a j i", i=32), op=ALU.subtract),
                    reads=['h_cum', 'h_kp'], writes=['h_e'])
                p.op('act', lambda e: e.activation(out=he2, in_=he2, func=AF.Exp), reads=['h_e'], writes=['h_e'])
                p.op('dve', lambda e: e.tensor_tensor(out=h_kpp[:].rearrange("p a b -> p (a b)"), in0=hk2, in1=he2, op=ALU.mult),
                     reads=['h_k', 'h_e'], writes=['h_kpp'])
                wb, wk = wload(W_in[:, C_HI:C_HI + 512], 512, l, ti)
                proj_tm(wb, wk, 0, 512, PS[0][:], 'PS0')
                p.op('act', lambda e: e.copy(out=h_v[:], in_=PS[0][:]), reads=['PS0'], writes=['h_v'])
                wb, wk = wload(W_in[:, C_HG:C_HG + 512], 512, l, ti)
                proj_tm(wb, wk, 0, 512, PS[1][:], 'PS1')
                p.op('act', lambda e: e.activation(out=h_g[:], in_=PS[1][:], func=AF.Silu), reads=['PS1'], writes=['h_g'])
                for h in range(4):
                    p.op('pe', (lambda e, h=h: e.matmul(PS[2][:, 0:128], lhsT=h_kp[:, h, :], rhs=h_qp[:, h, :], start=True,
                                                        stop=True)), reads=['h_kp', 'h_qp'], writes=['PS2'])
                    p.op('dve', lambda e: e.tensor_tensor(out=h_AT[:], in0=PS[2][:, 0:128], in1=hgmask[:], op=ALU.mult),
                         reads=['PS2', 'k_hgmask'], writes=['h_AT'])
                    p.op('pe', (lambda e, h=h: e.transpose(out=PT[:, 0:128], in_=h_kpp[:, h, :], identity=ident_b[:])),
                         reads=['h_kpp', 'k_ident'], writes=['PT'])
                    for j in range(4):
                        p.op('dve', (lambda e, j=j: e.tensor_scalar(out=h_kj[:, j, :], in0=PT[:, 0:128], scalar1=rowm[:, j:j + 1],
                                                                    scalar2=None, op0=ALU.mult)),
                             reads=['PT', 'k_rowm'], writes=['h_kj'])
                        p.op('act', (lambda e, h=h, j=j: e.copy(out=h_qj[:, h, j, 32 * j:32 * j + 32],
                                                                in_=h_qp[:, h, 32 * j:32 * j + 32])),
                             reads=['h_qp'], writes=['h_qj'])
                    p.op('pe', (lambda e, h=h: e.matmul(PS[3][:, 0:128], lhsT=h_AT[:], rhs=h_v[:, h * 128:(h + 1) * 128],
                                                        start=True, stop=False)), reads=['h_AT', 'h_v'], writes=['PS3'])
                    for j in range(4):
                        p.op('pe', (lambda e, h=h, j=j: e.matmul(PS[3][:, 0:128], lhsT=h_qj[:, h, j, :], rhs=h_Sb[:, l, h, :],
                                                                 start=False, stop=(j == 3))),
                             reads=['h_qj', 'h_Sb'], writes=['PS3'])
                        p.op('pe', (lambda e, h=h, j=j: e.matmul(PS[4][:, 0:128], lhsT=h_kj[:, j, :],
                                                                 rhs=h_v[:, h * 128:(h + 1) * 128], start=True, stop=True)),
                             reads=['h_kj', 'h_v'], writes=['PS4'])
                        p.op('dve', (lambda e, h=h, j=j: e.scalar_tensor_tensor(
                            out=h_S[0][:, l, h, :], in0=h_S[0][:, l, h, :], scalar=h_edec[:, h, j:j + 1], in1=PS[4][:, 0:128],
                            op0=ALU.mult, op1=ALU.add)), reads=['h_S', 'h_edec', 'PS4'], writes=['h_S'])
                        p.op('act', (lambda e, h=h: e.copy(out=h_Sb[:, l, h, :], in_=h_S[0][:, l, h, :])),
                             reads=['h_S'], writes=['h_Sb'])
                    head_norm_gate(PS[3][:, 0:128], 'PS3', [], hog[:, l, h * 128:(h + 1) * 128], h_g[:, h * 128:(h + 1) * 128],
                                   'h_g', obf[:, 0, h * 128:(h + 1) * 128], 4)

            if enable[1]:
                wb, wk = wload(W_in[:, C_MQ:C_MQ + 512], 512, l, ti)
                for h in range(4):
                    proj_fm(wb, wk, h * 64, 64, PS[0][0:64, h * 128:(h + 1) * 128], 'PS0')
                    proj_fm(wb, wk, 256 + h * 64, 64, PS[1][0:64, h * 128:(h + 1) * 128], 'PS1')
                p.op('act', lambda e: e.copy(out=m_q[:].rearrange("p a b -> p (a b)"), in_=PS[0][0:64, :]), reads=['PS0'],
                     writes=['m_q'])
                p.op('act', lambda e: e.mul(out=m_kT[:].rearrange("p a b -> p (a b)"), in_=PS[1][0:64, :], mul=0.125),
                     reads=['PS1'], writes=['m_kT'])
                proj_tm(wb, wk, 256, 256, PS[2][:, 0:256], 'PS2')
                p.op('act', lambda e: e.mul(out=m_k[:], in_=PS[2][:, 0:256], mul=0.125), reads=['PS2'], writes=['m_k'])
                wb, wk = wload(W_in[:, C_MV:C_MV + 520], 520, l, ti)
                proj_tm(wb, wk, 0, 512, PS[0][:], 'PS0')
                p.op('act', lambda e: e.copy(out=m_v[:], in_=PS[0][:]), reads=['PS0'], writes=['m_v'])
                proj_tm(wb, wk, 512, 8, PS[1][:, 0:8], 'PS1')
                p.op('dve', lambda e: e.tensor_tensor(out=m_if[:], in0=PS[1][:, 0:8], in1=mifb[:, l, :], op=ALU.add),
                     reads=['PS1', 'prm'], writes=['m_if'])
                wb, wk = wload(W_in[:, C_MO:C_MO + 512], 512, l, ti)
                proj_tm(wb, wk, 0, 512, PS[2][:], 'PS2')
                p.op('act', lambda e: e.activation(out=m_o[:], in_=PS[2][:], func=AF.Sigmoid), reads=['PS2'], writes=['m_o'])
                p.op('act', lambda e: e.activation(out=sm[:, 8:12], in_=m_if[:, 4:8], func=AF.Exp, scale=-1.0),
                     reads=['m_if'], writes=['sm8'])
                p.op('act', lambda e: e.activation(out=sm[:, 8:12], in_=sm[:, 8:12], func=AF.Ln, bias=epsb[:, 1:2]),
                     reads=['sm8', 'epsb'], writes=['sm8'])
                p.op('pe', lambda e: e.matmul(PS[3][:, 0:4], lhsT=tri_incl[:], rhs=sm[:, 8:12], start=True, stop=True),
                     reads=['sm8', 'k_tri_incl'], writes=['PS3'])
                p.op('act', lambda e: e.copy(out=sm[:, 12:16], in_=PS[3][:, 0:4]), reads=['PS3'], writes=['sm12'])
                p.op('dve', lambda e: e.tensor_tensor(out=sm[:, 16:20], in0=PS[3][:, 0:4], in1=m_if[:, 0:4], op=ALU.add),
                     reads=['PS3', 'm_if'], writes=['sm16'])
                p.op('act', lambda e: e.activation(out=sm[:, 16:20], in_=sm[:, 16:20], func=AF.Exp), reads=['sm16'],
                     writes=['sm16'])
                p.op('act', lambda e: e.activation(out=sm[:, 20:24], in_=sm[:, 12:16], func=AF.Exp, scale=-1.0),
                     reads=['sm12'], writes=['sm20'])
                p.op('pe', lambda e: e.matmul(PS[3][0:64, 8:12], lhsT=sel_last[:, 0:64], rhs=sm[:, 12:16], start=True, stop=True),
                     reads=['sm12', 'k_sel_last'], writes=['PS3'])
                p.op('act', lambda e: e.activation(out=sm[0:64, 24:28], in_=PS[3][0:64, 8:12], func=AF.Exp, scale=-1.0),
                     reads=['PS3'], writes=['sm24'])
                for h in range(4):
                    p.op('dve', (lambda e, h=h: e.tensor_scalar(out=m_va[:, h, 0:128], in0=m_v[:, h * 128:(h + 1) * 128],
                                                                scalar1=sm[:, 16 + h:17 + h], scalar2=None, op0=ALU.mult)),
                         reads=['m_v', 'sm16'], writes=['m_va'])
                    p.op('dve', (lambda e, h=h: e.tensor_copy(out=m_va[:, h, 128:129], in_=sm[:, 16 + h:17 + h])),
                         reads=['sm16'], writes=['m_va'])
                for h in range(4):
                    p.op('pe', (lambda e, h=h: e.matmul(PS[4][:, 0:128], lhsT=m_kT[:, h, :], rhs=m_q[:, h, :], start=True,
                                                        stop=True)), reads=['m_kT', 'm_q'], writes=['PS4'])
                    p.op('dve', lambda e: e.tensor_tensor(out=m_AT[:], in0=PS[4][:, 0:128], in1=tri_incl[:], op=ALU.mult),
                         reads=['PS4', 'k_tri_incl'], writes=['m_AT'])
                    p.op('pe', (lambda e, h=h: e.matmul(PS[5][:, 0:129], lhsT=m_AT[:], rhs=m_va[:, h, 0:129], start=True,
                                                        stop=False)), reads=['m_AT', 'm_va'], writes=['PS5'], inc=False)
                    p.op('pe', (lambda e, h=h: e.matmul(PS[5][:, 0:129], lhsT=m_q[:, h, :], rhs=m_Cb[:, l, h, 0:129], start=False,
                                                        stop=True)), reads=['m_q', 'm_Cb'], writes=['PS5'])
                    p.op('dve', (lambda e, h=h: e.tensor_tensor(out=sm[:, 28:29], in0=PS[5][:, 128:129], in1=sm[:, 20 + h:21 + h],
                                                                op=ALU.mult)), reads=['PS5', 'sm20'], writes=['sm28'])
                    p.op('dve', lambda e: e.scalar_tensor_tensor(out=sm[:, 31:32], in0=sm[:, 28:29], scalar=-1.0, in1=sm[:, 28:29],
                                                                 op0=ALU.mult, op1=ALU.max), reads=['sm28'], writes=['sm31'])
                    p.op('dve', lambda e: e.tensor_scalar(out=sm[:, 28:29], in0=sm[:, 31:32], scalar1=1.0, scalar2=None,
                                                          op0=ALU.max), reads=['sm31'], writes=['sm28'])
                    p.op('dve', lambda e: e.reciprocal(out=sm[:, 29:30], in_=sm[:, 28:29]), reads=['sm28'], writes=['sm29'])
                    p.op('dve', (lambda e, h=h: e.tensor_tensor(out=sm[:, 30:31], in0=sm[:, 29:30], in1=sm[:, 20 + h:21 + h],
                                                                op=ALU.mult)), reads=['sm29', 'sm20'], writes=['sm30'])
                    p.op('act', lambda e: e.activation(out=tmpA[:, 0:128], in_=PS[5][:, 0:128], func=AF.Copy, scale=sm[:, 30:31]),
                         reads=['PS5', 'sm30'], writes=['tmpA'])
                    p.op('pe', (lambda e, h=h: e.matmul(PS[6][0:64, 0:129], lhsT=m_k[:, h * 64:(h + 1) * 64], rhs=m_va[:, h, 0:129],
                                                        start=True, stop=True)), reads=['m_k', 'm_va'], writes=['PS6'])
                    p.op('dve', (lambda e, h=h: e.tensor_tensor(out=m_C[:, l, h, 0:129], in0=m_C[:, l, h, 0:129],
                                                                in1=PS[6][0:64, 0:129], op=ALU.add)),
                         reads=['m_C', 'PS6'], writes=['m_C'])
                    p.op('dve', (lambda e, h=h: e.tensor_scalar(out=m_C[:, l, h, 0:129], in0=m_C[:, l, h, 0:129],
                                                                scalar1=sm[0:64, 24 + h:25 + h], scalar2=None, op0=ALU.mult)),
                         reads=['m_C', 'sm24'], writes=['m_C'])
                    p.op('act', (lambda e, h=h: e.copy(out=m_Cb[:, l, h, 0:129], in_=m_C[:, l, h, 0:129])), reads=['m_C'],
                         writes=['m_Cb'])
                    head_norm_gate(tmpA[:, 0:128], 'tmpA', [], mog[:, l, h * 128:(h + 1) * 128], m_o[:, h * 128:(h + 1) * 128],
                                   'm_o', obf[:, 1, h * 128:(h + 1) * 128], 32)

            if enable[2]:
                wb, wk = wload(W_in[:, C_AQ:C_AQ + 512], 512, l, ti)
                wb2, wk2 = wload(W_in[:, C_AK:C_AK + 256], 256, l, ti)
                for g in range(8):
                    proj_fm(wb, wk, g * 64, 64, PS[g // 4][0:64, (g % 4) * 128:(g % 4 + 1) * 128], f'PS{g // 4}')
                for kv in range(2):
                    proj_fm(wb2, wk2, kv * 64, 64, PS[2][0:64, kv * 128:(kv + 1) * 128], 'PS2')
                az2 = a_z[:].rearrange("p a b -> p (a b)")
                asq2 = a_sq[:].rearrange("p a b -> p (a b)")
                p.op('act', lambda e: e.copy(out=az2[:, 0:512], in_=PS[0][0:64, :]), reads=['PS0'], writes=['a_z'])
                p.op('act', lambda e: e.copy(out=az2[:, 512:1024], in_=PS[1][0:64, :]), reads=['PS1'], writes=['a_z'])
                p.op('act', lambda e: e.copy(out=az2[:, 1024:1280], in_=PS[2][0:64, 0:256]), reads=['PS2'], writes=['a_z'])
                p.op('dve', lambda e: e.tensor_tensor(out=asq2, in0=az2, in1=az2, op=ALU.mult), reads=['a_z'], writes=['a_sq'])
                for i3 in range(3):
                    w3 = 512 if i3 < 2 else 256
                    p.op('pe', (lambda e, i3=i3, w3=w3: e.matmul(PS[3][0:64, 0:w3], lhsT=ones_f[0:64, 0:64],
                                                                 rhs=asq2[:, i3 * 512:i3 * 512 + w3], start=True, stop=True)),
                         reads=['a_sq', 'k_ones'], writes=['PS3'])
                    p.op('act', (lambda e, i3=i3, w3=w3: e.activation(out=asq2[:, i3 * 512:i3 * 512 + w3], in_=PS[3][0:64, 0:w3],
                                                                      func=AF.Ln, scale=1.0 / 64, bias=epsb[0:64, 0:1])),
                         reads=['PS3', 'epsb'], writes=['a_sq'])
                p.op('act', lambda e: e.activation(out=asq2, in_=asq2, func=AF.Exp, scale=-0.5), reads=['a_sq'], writes=['a_sq'])
                p.op('dve', lambda e: e.tensor_tensor(out=a_q[:].rearrange("p a b -> p (a b)"), in0=az2[:, 0:1024],
                                                      in1=asq2[:, 0:1024], op=ALU.mult), reads=['a_z', 'a_sq'], writes=['a_q'])
                for kv in range(2):
                    p.op('dve', (lambda e, kv=kv: e.scalar_tensor_tensor(
                        out=a_kT[:, l, kv, par, :], in0=a_z[:, 8 + kv, :], scalar=gk8[:, l:l + 1], in1=a_sq[:, 8 + kv, :],
                        op0=ALU.mult, op1=ALU.mult)), reads=['a_z', 'a_sq', 'gk8'], writes=['a_kT'])
                proj_tm(wb2, wk2, 128, 128, PS[4][:, 0:128], 'PS4')
                for kv in range(2):
                    p.op('act', (lambda e, kv=kv: e.copy(out=a_v[:, l, par, kv, 0:64], in_=PS[4][:, kv * 64:(kv + 1) * 64])),
                         reads=['PS4'], writes=['a_v'])
                    p.op('dve', (lambda e, kv=kv: e.memset(a_v[:, l, par, kv, 64:65], 1.0)), writes=['a_v'])
                for g in range(8):
                    kv = g // 4
                    blks = [1] if ti == 0 else [0, 1]
                    for bi, blk in enumerate(blks):
                        slot = par if blk == 1 else 1 - par
                        p.op('pe', (lambda e, g=g, kv=kv, slot=slot: e.matmul(PS[5][:, 0:128], lhsT=a_kT[:, l, kv, slot, :],
                                                                              rhs=a_q[:, g, :], start=True, stop=True)),
                             reads=['a_kT', 'a_q'], writes=['PS5'])
                        p.op('act', lambda e: e.activation(out=a_P[:], in_=PS[5][:, 0:128], func=AF.Exp), reads=['PS5'],
                             writes=['a_P'])
                        p.op('dve', (lambda e, g=g, blk=blk: e.tensor_tensor(out=a_PT[:, blk, :], in0=a_P[:], in1=EB[:, blk, g, :],
                                                                             op=ALU.mult)),
                             reads=['a_P', 'EB'], writes=['a_PT'])
                    for bi, blk in enumerate(blks):
                        slot = par if blk == 1 else 1 - par
                        p.op('pe', (lambda e, kv=kv, slot=slot, blk=blk, bi=bi: e.matmul(
                            PS[6][:, 0:65], lhsT=a_PT[:, blk, :], rhs=a_v[:, l, slot, kv, 0:65], start=(bi == 0),
                            stop=(bi == len(blks) - 1))), reads=['a_PT', 'a_v'], writes=['PS6'], inc=(bi == len(blks) - 1))
                    p.op('dve', (lambda e, g=g: e.tensor_tensor(out=sm[:, 40:41], in0=PS[6][:, 64:65], in1=esink[:, l, g:g + 1],
                                                                op=ALU.add)), reads=['PS6', 'esink'], writes=['sm40'])
                    p.op('dve', lambda e: e.reciprocal(out=sm[:, 41:42], in_=sm[:, 40:41]), reads=['sm40'], writes=['sm41'])
                    p.op('dve', (lambda e, g=g: e.tensor_scalar(out=obf[:, 2, g * 64:(g + 1) * 64], in0=PS[6][:, 0:64],
                                                                scalar1=sm[:, 41:42], scalar2=None, op0=ALU.mult)),
                         reads=['PS6', 'sm41'], writes=['obf'])

            if enable[3]:
                for c3 in range(3):
                    wb, wk = wload(W_in[:, C_DQKV + c3 * 512:C_DQKV + (c3 + 1) * 512], 512, l, ti)
                    for c4 in range(4):
                        proj_fm(wb, wk, c4 * 128, 128, PS[c3][:, c4 * 128:(c4 + 1) * 128], f'PS{c3}')
                    p.op('act', (lambda e, c3=c3: e.copy(out=d_x[:, l, c3 * 4:(c3 + 1) * 4, 3:131],
                                                         in_=PS[c3][:].rearrange("p (a b) -> p a b", a=4))),
                         reads=[f'PS{c3}'], writes=['d_x'])
                for cc in range(12):
                    p.op('dve', (lambda e, cc=cc: e.tensor_scalar(out=d_y[:, cc, :], in0=d_x[:, l, cc, 0:128],
                                                                  scalar1=convw[:, l, 0, cc:cc + 1], scalar2=None, op0=ALU.mult)),
                         reads=['d_x', 'prm'], writes=['d_y'])
                    for j in range(1, 4):
                        p.op('dve', (lambda e, cc=cc, j=j: e.scalar_tensor_tensor(
                            out=d_y[:, cc, :], in0=d_x[:, l, cc, j:j + 128], scalar=convw[:, l, j, cc:cc + 1], in1=d_y[:, cc, :],
                            op0=ALU.mult, op1=ALU.add)), reads=['d_x', 'prm', 'd_y'], writes=['d_y'])
                p.op('pool', lambda e: e.tensor_copy(out=d_x[:, l, :, 0:3], in_=d_x[:, l, :, 128:131]), reads=['d_x', 'd_y'],
                     writes=['d_x'])
                dy2 = d_y[:].rearrange("p a b -> p (a b)")
                p.op('act', lambda e: e.activation(out=dy2, in_=dy2, func=AF.Silu), reads=['d_y'], writes=['d_y'])
                for cc in range(8):
                    p.op('dve', (lambda e, cc=cc: e.tensor_tensor(out=tmpA[:, 0:128], in0=d_y[:, cc, :], in1=d_y[:, cc, :],
                                                                  op=ALU.mult)), reads=['d_y'], writes=['tmpA'])
                    p.op('pe', lambda e: e.matmul(PS[3][:, 0:128], lhsT=ones_f[:], rhs=tmpA[:, 0:128], start=True, stop=True),
                         reads=['tmpA', 'k_ones'], writes=['PS3'])
                    p.op('act', lambda e: e.activation(out=tmpB[:, 0:128], in_=PS[3][:, 0:128], func=AF.Ln, bias=epsb[:, 0:1]),
                         reads=['PS3', 'epsb'], writes=['tmpB'])
                    p.op('act', lambda e: e.activation(out=tmpB[:, 0:128], in_=tmpB[:, 0:128], func=AF.Exp, scale=-0.5),
                         reads=['tmpB'], writes=['tmpB'])
                    if cc < 4:
                        p.op('dve', (lambda e, cc=cc: e.scalar_tensor_tensor(
                            out=d_qT[:, cc, :], in0=d_y[:, cc, :], scalar=float(128 ** -0.5), in1=tmpB[:, 0:128],
                            op0=ALU.mult, op1=ALU.mult)), reads=['d_y', 'tmpB'], writes=['d_qT'])
                    else:
                        p.op('dve', (lambda e, cc=cc: e.tensor_tensor(out=d_kT[:, cc - 4, :], in0=d_y[:, cc, :],
                                                                      in1=tmpB[:, 0:128], op=ALU.mult)),
                             reads=['d_y', 'tmpB'], writes=['d_kT'])
                p.op('dve', lambda e: e.tensor_copy(out=d_vT[:], in_=d_y[:, 8:12, :]), reads=['d_y'], writes=['d_vT'])
                for h in range(4):
                    p.op('pe', (lambda e, h=h: e.transpose(out=PT[:, h * 128:(h + 1) * 128], in_=d_vT[:, h, :],
                                                           identity=ident_b[:])), reads=['d_vT', 'k_ident'], writes=['PT'],
                         inc=False)
                    p.op('pe', (lambda e, h=h: e.transpose(out=PT[:, 512 + h * 128:512 + (h + 1) * 128], in_=d_kT[:, h, :],
                                                           identity=ident_b[:])), reads=['d_kT', 'k_ident'], writes=['PT'],
                         inc=(h == 3))
                p.op('act', lambda e: e.copy(out=d_v[:].rearrange("p a b -> p (a b)"), in_=PT[:, 0:512]), reads=['PT'],
                     writes=['d_v'])
                p.op('act', lambda e: e.copy(out=d_k[:].rearrange("p a b -> p (a b)"), in_=PT[:, 512:1024]), reads=['PT'],
                     writes=['d_k'])
                wb, wk = wload(W_in[:, C_DB:C_DB + 520], 520, l, ti)
                proj_tm(wb, wk, 0, 8, PS[0][:, 0:8], 'PS0')
                p.op('act', lambda e: e.copy(out=d_ba[:], in_=PS[0][:, 0:8]), reads=['PS0'], writes=['d_ba'])
                proj_fm(wb, wk, 4, 4, PS[1][0:4, 0:128], 'PS1')
                p.op('act', lambda e: e.copy(out=d_arow[:], in_=PS[1][0:4, 0:128]), reads=['PS1'], writes=['d_arow'])
                proj_tm(wb, wk, 8, 512, PS[2][:], 'PS2')
                p.op('act', lambda e: e.activation(out=d_z[:], in_=PS[2][:], func=AF.Silu), reads=['PS2'], writes=['d_z'])
                p.op('act', lambda e: e.activation(out=sm[:, 44:48], in_=d_ba[:, 0:4], func=AF.Sigmoid), reads=['d_ba'],
                     writes=['sm44'])
                p.op('dve', lambda e: e.tensor_tensor(out=sm[:, 48:52], in0=d_ba[:, 4:8], in1=dtb[:, l, :], op=ALU.add),
                     reads=['d_ba', 'prm'], writes=['sm48'])
                p.op('act', lambda e: e.activation(out=sm[:, 48:52], in_=sm[:, 48:52], func=AF.Exp), reads=['sm48'],
                     writes=['sm48'])
                p.op('act', lambda e: e.activation(out=sm[:, 48:52], in_=sm[:, 48:52], func=AF.Ln, bias=epsb[:, 1:2]),
                     reads=['sm48', 'epsb'], writes=['sm48'])
                p.op('dve', lambda e: e.tensor_tensor(out=sm[:, 48:52], in0=sm[:, 48:52], in1=nexpa[:, l, :], op=ALU.mult),
                     reads=['sm48', 'nexpa'], writes=['sm48'])
                p.op('pe', lambda e: e.matmul(PS[3][:, 0:4], lhsT=tri_incl[:], rhs=sm[:, 48:52], start=True, stop=True),
                     reads=['sm48', 'k_tri_incl'], writes=['PS3'])
                p.op('act', lambda e: e.copy(out=sm[:, 52:56], in_=PS[3][:, 0:4]), reads=['PS3'], writes=['sm52'])
                p.op('dve', lambda e: e.tensor_scalar(out=sm[:, 56:60], in0=sm[:, 52:56], scalar1=-1.0, scalar2=None,
                                                      op0=ALU.mult), reads=['sm52'], writes=['sm56'])
                p.op('pe', lambda e: e.matmul(PS[3][:, 8:12], lhsT=sel_last[:], rhs=sm[:, 52:56], start=True, stop=True),
                     reads=['sm52', 'k_sel_last'], writes=['PS3'])
                p.op('act', lambda e: e.activation(out=sm[:, 60:64], in_=PS[3][:, 8:12], func=AF.Exp), reads=['PS3'],
                     writes=['sm60'])
                p.op('dve', lambda e: e.tensor_tensor(out=tmpC[:, 500:504], in0=PS[3][:, 8:12], in1=sm[:, 52:56],
                                                      op=ALU.subtract), reads=['PS3', 'sm52'], writes=['tmpC5'])
                p.op('act', lambda e: e.activation(out=tmpC[:, 500:504], in_=tmpC[:, 500:504], func=AF.Exp), reads=['tmpC5'],
                     writes=['tmpC5'])
                p.op('dve', lambda e: e.tensor_tensor(out=tmpC[:, 504:508], in0=tmpC[:, 500:504], in1=sm[:, 44:48], op=ALU.mult),
                     reads=['tmpC5', 'sm44'], writes=['tmpC6'])
                p.op('act', lambda e: e.activation(out=tmpC[:, 508:512], in_=sm[:, 52:56], func=AF.Exp), reads=['sm52'],
                     writes=['tmpC7'])
                p.op('dve', lambda e: e.tensor_scalar(out=tmpC[:, 496:500], in0=tmpC[:, 508:512], scalar1=-1.0, scalar2=None,
                                                      op0=ALU.mult), reads=['tmpC7'], writes=['tmpC4'])
                p.op('dve', lambda e: e.tensor_scalar(out=tmpC[:, 492:496], in0=sm[:, 44:48], scalar1=-1.0, scalar2=None,
                                                      op0=ALU.mult), reads=['sm44'], writes=['tmpC3'])
                p.op('act', lambda e: e.activation(out=d_grow[:], in_=d_arow[:], func=AF.Exp, bias=dtbrow[:, l:l + 1]),
                     reads=['d_arow', 'dtbrow'], writes=['d_grow'])
                p.op('act', lambda e: e.activation(out=d_grow[:], in_=d_grow[:], func=AF.Ln, bias=epsb[0:4, 1:2]),
                     reads=['d_grow', 'epsb'], writes=['d_grow'])
                p.op('dve', lambda e: e.tensor_scalar(out=d_grow[:], in0=d_grow[:], scalar1=nexparow[:, l:l + 1], scalar2=None,
                                                      op0=ALU.mult), reads=['d_grow', 'nexparow'], writes=['d_grow'])
                p.op('dve', lambda e: e.tensor_tensor_scan(out=d_grow[:], data0=ones_f[0:4, :], data1=d_grow[:], initial=0.0,
                                                           op0=ALU.mult, op1=ALU.add), reads=['d_grow', 'k_ones'],
                     writes=['d_grow'])
                def dn_head(h, sl):
                    bA, bB, bC = PS[1 + 3 * sl], PS[2 + 3 * sl], PS[3 + 3 * sl]
                    kA, kB, kC = f'PS{1 + 3 * sl}', f'PS{2 + 3 * sl}', f'PS{3 + 3 * sl}'
                    E1, E1s, TT, rr = dS_E1[sl], dS_E1s[sl], dS_TT[sl], dS_r[sl]
                    Pb, PTb = dS_P[sl], dS_PT[sl]
                    vn, vc, aT, o1 = dS_vn[sl], dS_vc[sl], dS_aT[sl], dS_o1[sl]
                    tA = tmpA[:, sl * 128:(sl + 1) * 128]
                    tB = tmpB[:, sl * 128:(sl + 1) * 128]
                    n = f'.{sl}'
                    p.op('pe', lambda e: e.matmul(bA[:, 0:128], lhsT=selh[:, h * 128:(h + 1) * 128], rhs=d_grow[:],
                                                  start=True, stop=True), reads=['k_selh', 'd_grow'], writes=[kA])
                    p.op('dve', lambda e: e.tensor_tensor(out=tA, in0=bA[:, 0:128], in1=negmask[:], op=ALU.add),
                         reads=[kA, 'k_negmask'], writes=['tmpA' + n])
                    p.op('act', lambda e: e.activation(out=E1[:], in_=tA, func=AF.Exp, bias=sm[:, 56 + h:57 + h]),
                         reads=['tmpA' + n, 'sm56'], writes=['d_E1' + n])
                    yield
                    p.op('pe', lambda e: e.matmul(bA[:, 128:256], lhsT=d_kT[:, h, :], rhs=d_kT[:, h, :], start=True, stop=True),
                         reads=['d_kT'], writes=[kA])
                    p.op('dve', lambda e: e.tensor_tensor(out=E1s[:], in0=E1[:], in1=tri_strict[:], op=ALU.mult),
                         reads=['d_E1' + n, 'k_tri_strict'], writes=['d_E1s' + n])
                    p.op('dve', lambda e: e.scalar_tensor_tensor(out=PTb[0][:], in0=bA[:, 128:256],
                                                                 scalar=tmpC[:, 492 + h:493 + h], in1=E1s[:],
                                                                 op0=ALU.mult, op1=ALU.mult),
                         reads=[kA, 'tmpC3', 'd_E1s' + n], writes=['d_PT0' + n])
                    yield
                    p.op('pe', lambda e: e.transpose(out=bB[:, 0:128], in_=PTb[0][:], identity=ident_f[:]),
                         reads=['d_PT0' + n, 'k_ident'], writes=[kB])
                    p.op('act', lambda e: e.copy(out=Pb[0][:], in_=bB[:, 0:128]), reads=[kB], writes=['d_P0' + n])
                    p.op('dve', lambda e: e.tensor_tensor(out=TT[:], in0=PTb[0][:], in1=ident_f[:], op=ALU.add),
                         reads=['d_PT0' + n, 'k_ident'], writes=['d_TT' + n])
                    yield
                    cur = 0
                    for lvl in range(6):
                        nxt = 1 - cur
                        p.op('pe', lambda e: e.matmul(bB[:, 0:128], lhsT=PTb[cur][:], rhs=Pb[cur][:], start=True, stop=True),
                             reads=[f'd_PT{cur}' + n, f'd_P{cur}' + n], writes=[kB])
                        if lvl < 5:
                            p.op('pe', lambda e: e.matmul(bB[:, 128:256], lhsT=Pb[cur][:], rhs=PTb[cur][:], start=True, stop=True),
                                 reads=[f'd_PT{cur}' + n, f'd_P{cur}' + n], writes=[kB])
                        p.op('act', lambda e: e.copy(out=Pb[nxt][:], in_=bB[:, 0:128]), reads=[kB], writes=[f'd_P{nxt}' + n])
                        if lvl < 5:
                            p.op('act', lambda e: e.copy(out=PTb[nxt][:], in_=bB[:, 128:256]), reads=[kB],
                                 writes=[f'd_PT{nxt}' + n])
                        yield
                        p.op('pe', lambda e: e.matmul(bC[:, 0:128], lhsT=Pb[nxt][:], rhs=TT[:], start=True, stop=True),
                             reads=[f'd_P{nxt}' + n, 'd_TT' + n], writes=[kC])
                        p.op('dve', lambda e: e.tensor_tensor(out=TT[:], in0=TT[:], in1=bC[:, 0:128], op=ALU.add),
                             reads=['d_TT' + n, kC], writes=['d_TT' + n])
                        yield
                        cur = nxt
                    p.op('pe', lambda e: e.matmul(bA[:, 256:384], lhsT=d_kT[:, h, :], rhs=d_Sb[:, l, h, :], start=True, stop=True),
                         reads=['d_kT', 'd_Sb'], writes=[kA])
                    p.op('dve', lambda e: e.scalar_tensor_tensor(out=rr[:], in0=bA[:, 256:384], scalar=tmpC[:, 496 + h:497 + h],
                                                                 in1=d_v[:, h, :], op0=ALU.mult, op1=ALU.add),
                         reads=[kA, 'tmpC4', 'd_v'], writes=['d_r' + n])
                    yield
                    p.op('pe', lambda e: e.matmul(bB[:, 256:384], lhsT=TT[:], rhs=rr[:], start=True, stop=True),
                         reads=['d_TT' + n, 'd_r' + n], writes=[kB])
                    p.op('dve', lambda e: e.tensor_scalar(out=vn[:], in0=bB[:, 256:384], scalar1=sm[:, 44 + h:45 + h],
                                                          scalar2=None, op0=ALU.mult), reads=[kB, 'sm44'], writes=['d_vn' + n])
                    p.op('dve', lambda e: e.tensor_scalar(out=vc[:], in0=bB[:, 256:384], scalar1=tmpC[:, 504 + h:505 + h],
                                                          scalar2=None, op0=ALU.mult), reads=[kB, 'tmpC6'], writes=['d_vc' + n])
                    p.op('pe', lambda e: e.matmul(bA[:, 384:512], lhsT=d_kT[:, h, :], rhs=d_qT[:, h, :], start=True, stop=True),
                         reads=['d_kT', 'd_qT'], writes=[kA])
                    p.op('dve', lambda e: e.tensor_tensor(out=aT[:], in0=bA[:, 384:512], in1=E1[:], op=ALU.mult),
                         reads=[kA, 'd_E1' + n], writes=['d_aT' + n])
                    yield
                    p.op('pe', lambda e: e.matmul(bC[:, 128:256], lhsT=d_qT[:, h, :], rhs=d_Sb[:, l, h, :], start=True, stop=True),
                         reads=['d_qT', 'd_Sb'], writes=[kC])
                    p.op('act', lambda e: e.activation(out=o1[:], in_=bC[:, 128:256], func=AF.Copy,
                                                       scale=tmpC[:, 508 + h:509 + h]), reads=[kC, 'tmpC7'], writes=['d_o1' + n])
                    yield
                    p.op('pe', lambda e: e.matmul(bC[:, 256:384], lhsT=aT[:], rhs=vn[:], start=True, stop=True),
                         reads=['d_aT' + n, 'd_vn' + n], writes=[kC])
                    p.op('dve', lambda e: e.tensor_tensor(out=tB, in0=bC[:, 256:384], in1=o1[:], op=ALU.add),
                         reads=[kC, 'd_o1' + n], writes=['tmpB' + n])
                    yield
                    p.op('pe', lambda e: e.matmul(bC[:, 384:512], lhsT=d_k[:, h, :], rhs=vc[:], start=True, stop=True),
                         reads=['d_k', 'd_vc' + n], writes=[kC])
                    p.op('dve', lambda e: e.scalar_tensor_tensor(out=d_S[:, l, h, :], in0=d_S[:, l, h, :],
                                                                 scalar=sm[:, 60 + h:61 + h], in1=bC[:, 384:512],
                                                                 op0=ALU.mult, op1=ALU.add),
                         reads=['d_S', 'sm60', kC], writes=['d_S'])
                    p.op('act', lambda e: e.copy(out=d_Sb[:, l, h, :], in_=d_S[:, l, h, :]), reads=['d_S'], writes=['d_Sb'])
                    yield
                    head_norm_gate(tB, 'tmpB' + n, [], dog[:, l, :], d_z[:, h * 128:(h + 1) * 128], 'd_z',
                                   obf[:, 3, h * 128:(h + 1) * 128], 64 + 4 * sl, sl)
                for pair in range(2):
                    interleave([dn_head(2 * pair, 0), dn_head(2 * pair + 1, 1)])

            for gi in range(8):
                wb, wk = wload(W_in[:, C_GATE + gi * 512:C_GATE + (gi + 1) * 512], 512, l, ti)
                pst, pk = (PS[0], 'PS0') if gi % 2 == 0 else (PS[1], 'PS1')
                proj_tm(wb, wk, 0, 512, pst[:], pk)
                p.op('act', (lambda e, gi=gi, pst=pst: e.activation(out=gates[:, gi * 512:(gi + 1) * 512], in_=pst[:],
                                                                    func=AF.Sigmoid)), reads=[pk], writes=['gates'])
            for half in range(2):
                for c8 in range(8):
                    cc = half * 8 + c8
                    p.op('pe', (lambda e, cc=cc, c8=c8: e.transpose(
                        out=PT[:, c8 * 128:(c8 + 1) * 128],
                        in_=obf[:].rearrange("p a b -> p (a b)")[:, cc * 128:(cc + 1) * 128], identity=ident_b[:])),
                        reads=['obf', 'k_ident'], writes=['PT'], inc=(c8 == 7))
                p.op('act', (lambda e, half=half: e.copy(out=oT[:, half * 8:(half + 1) * 8, :].rearrange("p a b -> p (a b)"),
                                                         in_=PT[:])), reads=['PT'], writes=['oT'])
            for n in range(4):
                wv, wk = wload_rows(prm['w_branch'][l, n], l, ti)
                for half in range(2):
                    pst, pk = (PS[2], 'PS2') if half == 0 else (PS[3], 'PS3')
                    for wc in range(4):
                        p.op('pe', (lambda e, n=n, wc=wc, half=half, pst=pst: e.matmul(
                            pst[:], lhsT=oT[:, n * 4 + wc, :], rhs=wv[:, wc, half * 512:(half + 1) * 512], start=(wc == 0),
                            stop=(wc == 3))), reads=['oT', wk], writes=[pk], inc=(wc == 3))
                    if n == 0:
                        p.op('dve', (lambda e, n=n, half=half, pst=pst: e.tensor_tensor(
                            out=merged[:, half * 512:(half + 1) * 512], in0=pst[:],
                            in1=gates[:, n * 1024 + half * 512:n * 1024 + (half + 1) * 512], op=ALU.mult)),
                            reads=[pk, 'gates'], writes=['merged'])
                    else:
                        p.op('dve', (lambda e, n=n, half=half, pst=pst: e.tensor_tensor(
                            out=tmpA[:], in0=pst[:], in1=gates[:, n * 1024 + half * 512:n * 1024 + (half + 1) * 512],
                            op=ALU.mult)), reads=[pk, 'gates'], writes=['tmpA'])
                        p.op('dve', (lambda e, half=half: e.tensor_tensor(
                            out=merged[:, half * 512:(half + 1) * 512], in0=merged[:, half * 512:(half + 1) * 512], in1=tmpA[:],
                            op=ALU.add)), reads=['tmpA', 'merged'], writes=['merged'])
            p.op('act', lambda e: e.copy(out=mbf[:], in_=merged[:]), reads=['merged'], writes=['mbf'])
            for kc in range(8):
                p.op('pe', (lambda e, kc=kc: e.transpose(out=PT[:, kc * 128:(kc + 1) * 128], in_=mbf[:, kc * 128:(kc + 1) * 128],
                                                         identity=ident_b[:])), reads=['mbf', 'k_ident'], writes=['PT'],
                     inc=(kc == 7))
            p.op('act', lambda e: e.copy(out=mT[:].rearrange("p a b -> p (a b)"), in_=PT[:]), reads=['PT'], writes=['mT'])
            for half in range(2):
                wb, wk = wload(prm['w_out'][l][:, half * 512:(half + 1) * 512], 512, l, ti)
                pst, pk = (PS[0], 'PS0') if half == 0 else (PS[1], 'PS1')
                for kc in range(8):
                    p.op('pe', (lambda e, kc=kc, pst=pst, wb=wb: e.matmul(pst[:], lhsT=mT[:, kc, :], rhs=wb[:, kc, 0:512],
                                                                          start=(kc == 0), stop=(kc == 7))),
                         reads=['mT', wk], writes=[pk], inc=(kc == 7))
                p.op('dve', (lambda e, half=half, pst=pst: e.tensor_tensor(out=xt[:, half * 512:(half + 1) * 512],
                                                                           in0=xt[:, half * 512:(half + 1) * 512], in1=pst[:],
                                                                           op=ALU.add)), reads=[pk, 'xt'], writes=['xt'])
            rmsnorm_to_T(xt, gmlp, hT, 'hT', l)
            for fi in range(8):
                wb, wk = wload(prm['w_up'][l][:, fi * 512:(fi + 1) * 512], 512, l, ti)
                pst, pk = (PS[2], 'PS2') if fi % 2 == 0 else (PS[3], 'PS3')
                for f4 in range(4):
                    proj_fm(wb, wk, f4 * 128, 128, pst[:, f4 * 128:(f4 + 1) * 128], pk)
                p.op('act', (lambda e, pst=pst: e.activation(out=tmpB[:], in_=pst[:], func=AF.Relu)), reads=[pk], writes=['tmpB'])
                p.op('dve', (lambda e, fi=fi, pst=pst: e.tensor_tensor(
                    out=uT[:, fi * 4:(fi + 1) * 4, :].rearrange("p a b -> p (a b)"), in0=tmpB[:], in1=pst[:], op=ALU.mult)),
                    reads=['tmpB', pk], writes=['uT'])
            for fi in range(8):
                wv, wk = wload_rows(prm['w_down'][l][fi * 512:(fi + 1) * 512, :], l, ti)
                for half in range(2):
                    pk = 'PS0' if half == 0 else 'PS1'
                    pst = PS[0] if half == 0 else PS[1]
                    for f4 in range(4):
                        p.op('pe', (lambda e, fi=fi, f4=f4, half=half, pst=pst, wv=wv: e.matmul(
                            pst[:], lhsT=uT[:, fi * 4 + f4, :], rhs=wv[:, f4, half * 512:(half + 1) * 512],
                            start=(fi == 0 and f4 == 0), stop=(fi == 7 and f4 == 3))),
                            reads=['uT', wk], writes=[pk], inc=(f4 == 3))
            for half in range(2):
                pk = 'PS0' if half == 0 else 'PS1'
                pst = PS[0] if half == 0 else PS[1]
                p.op('dve', (lambda e, half=half, pst=pst: e.tensor_tensor(out=xt[:, half * 512:(half + 1) * 512],
                                                                           in0=xt[:, half * 512:(half + 1) * 512], in1=pst[:],
                                                                           op=ALU.add)), reads=[pk, 'xt'], writes=['xt'])
        p.dma('sp', y_out[ti * 128:(ti + 1) * 128, :], xt[:], reads=['xt'], writes=['yout'])
    p.final_wait('sp', ['yout'])
    p.emit()
    return nc


def host_params(inputs):
    f = lambda k: np.ascontiguousarray(np.asarray(inputs[k], dtype=np.float32))
    m = {}
    for k in ['norm_mix_g', 'w_in', 'hgrn_out_g', 'mlstm_out_g', 'attn_sinks', 'rel_bias_table', 'dn_a_log', 'dn_dt_bias',
              'dn_out_g', 'w_branch', 'w_out', 'norm_mlp_g', 'w_up', 'w_down']:
        m[k] = f(k)
    m['mlstm_if_bias'] = f('mlstm_if_bias').reshape(2, 8)
    m['hgrn_lb_table'] = np.ascontiguousarray(f('hgrn_lb_table').reshape(2, 4, 128).transpose(0, 2, 1))
    m['attn_q_norm_g'] = f('attn_q_norm_g').reshape(2, 64, 1)
    m['attn_k_norm_g'] = f('attn_k_norm_g').reshape(2, 64, 1)
    m['dn_conv_w'] = np.ascontiguousarray(f('dn_conv_w').reshape(2, 4, 12, 128).transpose(0, 1, 3, 2))
    m['dn_alog_row'] = np.ascontiguousarray(f('dn_a_log').T)
    m['dn_dtb_row'] = np.ascontiguousarray(f('dn_dt_bias').T)
    return m


def kernel(**inputs):
    x = np.ascontiguousarray(np.asarray(inputs['x'], dtype=np.float32))
    B, T, _ = x.shape
    nc = build(T, 2)
    consts = host_consts()
    hp = host_params(inputs)
    in_maps = []
    for b in range(B):
        m = {'x': x[b]}
        m.update(hp)
        for k, v in consts.items():
            m['c_' + k] = v
        in_maps.append(m)
    res = run_bass_kernel_spmd(nc, in_maps, core_ids=list(range(B)))
    return np.stack([np.asarray(r['y'], dtype=np.float32) for r in res.results], axis=0)
```

```python
import types
import numpy as np
from contextlib import ExitStack
import concourse.bass as bass
import concourse.mybir as mybir
from concourse.bass_utils import run_bass_kernel_spmd

F32 = mybir.dt.float32
BF16 = mybir.dt.bfloat16
AF = mybir.ActivationFunctionType
ALU = mybir.AluOpType
AX = mybir.AxisListType
ENGS = ['pe', 'act', 'dve', 'pool', 'sp']

D = 1024
NIN = 10512
EPS = 1e-6


def _freeze(fn):
    if fn is None or fn.__closure__ is None:
        return fn
    cells = []
    for c in fn.__closure__:
        try:
            cells.append(types.CellType(c.cell_contents))
        except ValueError:
            cells.append(c)
    return types.FunctionType(fn.__code__, fn.__globals__, fn.__name__, fn.__defaults__, tuple(cells))


class V3:
    def __init__(self, ap):
        self.ap = ap

    def __getitem__(self, k):
        return self.ap[k]


class Prog:
    def __init__(self, nc, nd=8):
        self.nc = nc
        self.ND = nd
        self.lists = {e: [] for e in ENGS}
        self.cnt = {e: 0 for e in ENGS}
        self.waited = {e: {} for e in ENGS}
        self.W = {}
        self.Rd = {}
        self.dma_n = {e: 0 for e in ENGS}
        self.semkeys = set()
        self.st = ExitStack()
        self.ntens = 0
        self.alias = {}

    def sb(self, shape, dt, name=None):
        self.ntens += 1
        return self.st.enter_context(self.nc.sbuf_tensor(name or f"t{self.ntens}", list(shape), dt))

    def ps(self, shape, dt=F32, name=None):
        self.ntens += 1
        return self.st.enter_context(self.nc.psum_tensor(name or f"p{self.ntens}", list(shape), dt))

    def _deps(self, eng, reads, writes):
        deps = {}

        def add(d):
            for sk, v in d.items():
                if deps.get(sk, 0) < v:
                    deps[sk] = v
        for r in reads:
            add(self.W.get(r, {}))
        for w in writes:
            add(self.W.get(w, {}))
            add(self.Rd.get(w, {}))
        out = []
        for sk, v in deps.items():
            if sk == ('c', 'pe') and eng == 'pe':
                continue
            if self.waited[eng].get(sk, 0) >= v:
                continue
            self.waited[eng][sk] = v
            out.append((sk, v))
        return out

    def _rec(self, tok, reads, writes):
        sk, v = tok
        for r in reads:
            d = self.Rd.setdefault(r, {})
            d[sk] = max(d.get(sk, 0), v)
        for w in writes:
            d = self.W.setdefault(w, {})
            d[sk] = max(d.get(sk, 0), v)

    def _x(self, keys):
        out = []
        for k in keys:
            if isinstance(k, (list, tuple)):
                out.extend(self._x(k))
            elif k in self.alias:
                out.extend(self.alias[k])
            else:
                out.append(k)
        return out

    def op(self, eng, fn, reads=(), writes=(), inc=True):
        reads = self._x(reads)
        writes = self._x(writes)
        waits = self._deps(eng, reads, writes)
        sk = ('c', eng)
        tok = (sk, self.cnt[eng] + 1)
        if inc:
            self.cnt[eng] += 1
        self.semkeys.add(sk)
        self.lists[eng].append((waits, _freeze(fn), sk if inc else None, 1))
        self._rec(tok, reads, writes)

    def dma(self, q, out, in_, reads=(), writes=(), **kw):
        reads = self._x(reads)
        writes = self._x(writes)
        j = self.dma_n[q]
        self.dma_n[q] += 1
        slot = j % self.ND
        val = 16 * (j // self.ND + 1)
        sk = ('d', q, slot)
        self.semkeys.add(sk)
        waits = self._deps(q, reads, writes)
        if j >= self.ND and self.waited[q].get(sk, 0) < val - 16:
            self.waited[q][sk] = val - 16
            waits.append((sk, val - 16))
        self.lists[q].append((waits, (lambda e, o=out, i=in_, k=kw: e.dma_start(out=o, in_=i, **k)), sk, 16))
        self._rec((sk, val), reads, writes)

    def coll(self, kind, op, groups, ins_ap, outs_ap, reads=(), writes=()):
        reads = self._x(reads)
        writes = self._x(writes)
        waits = self._deps('pool', reads, writes)
        sk = ('cc',)
        self.cc_n = getattr(self, 'cc_n', 0) + 1
        self.semkeys.add(sk)
        self.lists['pool'].append((waits, (lambda e: e.collective_compute(kind, op, replica_groups=groups, ins=[ins_ap.opt()],
                                                                          outs=[outs_ap.opt()])), sk, 1))
        self._rec((sk, self.cc_n), reads, writes)

    def final_wait(self, eng, keys):
        keys = self._x(keys)
        waits = self._deps(eng, keys, ())
        self.lists[eng].append((waits, None, None, 0))

    def emit(self):
        nc = self.nc
        st = self.st
        sems = {}
        for sk in sorted(self.semkeys, key=str):
            sems[sk] = st.enter_context(nc.semaphore("s_" + "_".join(map(str, sk))))
        block = st.enter_context(nc.Block())
        lists = self.lists

        def run(name, e):
            for waits, fn, sk, incv in lists[name]:
                for wsk, v in waits:
                    e.wait_ge(sems[wsk], v)
                if fn is None:
                    continue
                ins = fn(e)
                if sk is not None:
                    ins.then_inc(sems[sk], incv)

        @block.tensor
        def _(e):
            run('pe', e)

        @block.scalar
        def _(e):
            run('act', e)

        @block.vector
        def _(e):
            run('dve', e)

        @block.gpsimd
        def _(e):
            run('pool', e)

        @block.sync
        def _(e):
            run('sp', e)
        st.close()


def _t5_bucket_np(n):
    max_exact = 16
    nf = np.maximum(n, max_exact).astype(np.float32)
    large = max_exact + (np.log(nf / max_exact) / np.log(np.float32(128 / max_exact)) * 16).astype(np.int32)
    large = np.minimum(large, 31)
    return np.where(n < max_exact, n, large)


def host_consts():
    c = {}
    s = np.arange(128)[:, None]
    t = np.arange(128)[None, :]
    c['tri_incl'] = (s <= t).astype(np.float32)
    c['tri_strict'] = (s < t).astype(np.float32)
    c['hgmask'] = ((s <= t) & (s // 32 == t // 32)).astype(np.float32)
    c['negmask'] = np.where(s <= t, 0.0, -1e30).astype(np.float32)
    c['ident'] = np.eye(128, dtype=np.float32)
    sel = np.zeros((128, 128), np.float32)
    sel[127, :] = 1.0
    c['sel_last'] = sel
    rm = np.zeros((128, 4), np.float32)
    for j in range(4):
        rm[32 * j:32 * j + 32, j] = 1.0
    c['rowm'] = rm
    rs = np.ones((128, 128), np.float32)
    rs[:, 0::32] = 0.0
    c['resetm'] = rs
    selh = np.zeros((4, 4, 128), np.float32)
    for h in range(4):
        selh[h, h, :] = 1.0
    c['selh'] = selh.reshape(4, 512)
    c['ones'] = np.ones((128, 128), np.float32)
    bk = _t5_bucket_np(np.arange(128))
    oh = np.zeros((32, 128), np.float32)
    oh[bk, np.arange(128)] = 1.0
    c['bias_oh'] = oh
    ab = np.zeros((128, 384), np.float32)
    for dd in range(128):
        ab[dd, 255 - dd] = 1.0
    c['antiband'] = ab
    return c


CONST_SHAPES = {
    'tri_incl': (128, 128), 'tri_strict': (128, 128), 'hgmask': (128, 128), 'negmask': (128, 128),
    'ident': (128, 128), 'sel_last': (128, 128), 'rowm': (128, 4), 'resetm': (128, 128),
    'selh': (4, 512), 'ones': (128, 128), 'bias_oh': (32, 128), 'antiband': (128, 384),
}

PARAM_SHAPES = {
    'norm_mix_g': (2, 1024), 'w_in': (2, 1024, NIN), 'hgrn_lb_table': (2, 128, 4), 'hgrn_out_g': (2, 512),
    'mlstm_if_bias': (2, 8), 'mlstm_out_g': (2, 512), 'attn_q_norm_g': (2, 64, 1), 'attn_k_norm_g': (2, 64, 1),
    'attn_sinks': (2, 8), 'rel_bias_table': (32, 8), 'dn_conv_w': (2, 4, 128, 12), 'dn_a_log': (2, 4),
    'dn_dt_bias': (2, 4), 'dn_alog_row': (4, 2), 'dn_dtb_row': (4, 2), 'dn_out_g': (2, 128), 'w_branch': (2, 4, 512, 1024), 'w_out': (2, 1024, 1024),
    'norm_mlp_g': (2, 1024), 'w_up': (2, 1024, 4096), 'w_down': (2, 4096, 1024),
}

C_HQ, C_HF, C_HI, C_HG = 0, 512, 1024, 1536
C_MQ, C_MK, C_MV, C_MI, C_MF, C_MO = 2048, 2304, 2560, 3072, 3076, 3080
C_AQ, C_AK, C_AV = 3592, 4104, 4232
C_DQKV, C_DB, C_DA, C_DZ = 4360, 5896, 5900, 5904
C_GATE = 6416


def build(T, L=2, enable=(1, 1, 1, 1), dbg=None, pipe=False, npairs=4):
    nc = bass.Bass("TRN2", target_bir_lowering=False)
    NTILES = T // 128
    x_in = nc.dram_tensor("x", [T, D], F32, kind="ExternalInput").ap()
    y_out = nc.dram_tensor("y", [T, D], F32, kind="ExternalOutput").ap()
    prm = {k: nc.dram_tensor(k, list(s), F32, kind="ExternalInput").ap() for k, s in PARAM_SHAPES.items()}
    cst = {k: nc.dram_tensor("c_" + k, list(s), F32, kind="ExternalInput").ap() for k, s in CONST_SHAPES.items()}
    NIT = NTILES + 1 if pipe else NTILES
    if pipe:
        assert L == 1
        role_in = nc.dram_tensor("role", [128, 2], F32, kind="ExternalInput").ap()
        pflag_in = nc.dram_tensor("pflag", [128, NIT], F32, kind="ExternalInput").ap()
        sendb = nc.dram_tensor("sendb", [2, 128, 1024], F32).ap()
        recvb = nc.dram_tensor("recvb", [2, 128, 1024], F32).ap()
    dbg_out = {}
    p = Prog(nc)
    sb, ps = p.sb, p.ps

    def load_const(name, dt=F32, q='sp'):
        shp = CONST_SHAPES[name]
        t_ = sb(shp, dt, "k_" + name + ("_b" if dt == BF16 else ""))
        p.dma(q, t_[:], cst[name], writes=['k_' + name])
        return t_
    tri_incl = load_const('tri_incl')
    tri_strict = load_const('tri_strict')
    hgmask = load_const('hgmask')
    negmask = load_const('negmask')
    ident_f = load_const('ident')
    ident_b = load_const('ident', BF16, 'pool')
    sel_last = load_const('sel_last')
    rowm = load_const('rowm')
    resetm = load_const('resetm')
    selh = load_const('selh')
    ones_f = load_const('ones')
    ones_b = load_const('ones', BF16, 'pool')
    KC = ['k_tri_incl', 'k_tri_strict', 'k_hgmask', 'k_negmask', 'k_ident', 'k_sel_last', 'k_rowm', 'k_resetm',
          'k_selh', 'k_ones']

    gmix = sb([128, L, 1024], BF16, "gmix")
    gmlp = sb([128, L, 1024], BF16, "gmlp")
    hog = sb([128, L, 512], BF16, "hog")
    mog = sb([128, L, 512], BF16, "mog")
    dog = sb([128, L, 128], F32, "dog")
    mifb = sb([128, L, 8], F32, "mifb")
    sinks = sb([128, L, 8], F32, "sinks")
    esink = sb([128, L, 8], F32, "esink")
    alog = sb([128, L, 4], F32, "alog")
    nexpa = sb([128, L, 4], F32, "nexpa")
    dtb = sb([128, L, 4], F32, "dtb")
    lbt = sb([128, 2, 4], F32, "lbt")
    lb = sb([128, 2, 4], F32, "lb")
    oml = sb([128, 2, 4], F32, "oml")
    convw = sb([128, L, 4, 12], F32, "convw")
    qkg = sb([64, L, 2], F32, "qkg")
    gk8 = sb([64, L], F32, "gk8")
    dtbrow = sb([4, 2], F32, "dtbrow")
    nexparow = sb([4, 2], F32, "nexparow")
    for l in range(L):
        p.dma('pool', gmix[:, l, :], prm['norm_mix_g'][l:l + 1, :].partition_broadcast(128), writes=['prm'])
        p.dma('pool', gmlp[:, l, :], prm['norm_mlp_g'][l:l + 1, :].partition_broadcast(128), writes=['prm'])
        p.dma('pool', hog[:, l, :], prm['hgrn_out_g'][l:l + 1, :].partition_broadcast(128), writes=['prm'])
        p.dma('pool', mog[:, l, :], prm['mlstm_out_g'][l:l + 1, :].partition_broadcast(128), writes=['prm'])
        p.dma('sp', dog[:, l, :], prm['dn_out_g'][l:l + 1, :].partition_broadcast(128), writes=['prm'])
        p.dma('sp', mifb[:, l, :], prm['mlstm_if_bias'][l:l + 1, :].partition_broadcast(128), writes=['prm'])
        p.dma('sp', sinks[:, l, :], prm['attn_sinks'][l:l + 1, :].partition_broadcast(128), writes=['prm'])
        p.dma('sp', alog[:, l, :], prm['dn_a_log'][l:l + 1, :].partition_broadcast(128), writes=['prm'])
        p.dma('sp', dtb[:, l, :], prm['dn_dt_bias'][l:l + 1, :].partition_broadcast(128), writes=['prm'])
        for j in range(4):
            p.dma('sp', convw[:, l, j, :], prm['dn_conv_w'][l, j], writes=['prm'])
        p.dma('sp', qkg[:, l, 0:1], prm['attn_q_norm_g'][l], writes=['prm'])
        p.dma('sp', qkg[:, l, 1:2], prm['attn_k_norm_g'][l], writes=['prm'])
    for l2 in range(2):
        p.dma('sp', lbt[:, l2, :], prm['hgrn_lb_table'][l2], writes=['prm'])
    p.dma('sp', dtbrow[:], prm['dn_dtb_row'], writes=['dtbrow'])
    p.dma('sp', nexparow[:], prm['dn_alog_row'], writes=['nexparow'])
    p.op('act', lambda e: e.activation(out=nexparow[:], in_=nexparow[:], func=AF.Exp), reads=['nexparow'], writes=['nexparow'])
    p.op('dve', lambda e: e.tensor_scalar(out=nexparow[:], in0=nexparow[:], scalar1=-1.0, scalar2=None, op0=ALU.mult),
         reads=['nexparow'], writes=['nexparow'])
    p.op('dve', lambda e: e.memset(lb[:], 0.0), writes=['lb'])
    p.op('dve', lambda e: e.tensor_sub(out=lb[:, 1, :], in0=lbt[:, 1, :], in1=lbt[:, 0, :]), reads=['prm', 'lb'],
         writes=['lb'])
    p.op('act', lambda e: e.activation(out=lb[:, 1, :], in_=lb[:, 1, :], func=AF.Sigmoid), reads=['lb'], writes=['lb'])
    if pipe:
        role = sb([128, 2], F32, "role_sb")
        pflag = sb([128, NIT], F32, "pflag_sb")
        p.dma('sp', role[:], role_in, writes=['role'])
        p.dma('sp', pflag[:], pflag_in, writes=['pflag'])
        p.op('dve', lambda e: e.tensor_scalar(out=lb[:, 0, :], in0=lb[:, 1, :], scalar1=role[:, 1:2], scalar2=None,
                                              op0=ALU.mult), reads=['lb', 'role'], writes=['lb'])
    p.op('dve', lambda e: e.tensor_scalar(out=oml[:], in0=lb[:], scalar1=-1.0, scalar2=1.0, op0=ALU.mult, op1=ALU.add),
         reads=['lb'], writes=['oml'])
    p.op('act', lambda e: e.activation(out=esink[:], in_=sinks[:], func=AF.Exp), reads=['prm'], writes=['esink'])
    p.op('act', lambda e: e.activation(out=nexpa[:], in_=alog[:], func=AF.Exp), reads=['prm'], writes=['nexpa'])
    p.op('dve', lambda e: e.tensor_scalar(out=nexpa[:], in0=nexpa[:], scalar1=-1.0, scalar2=None, op0=ALU.mult),
         reads=['nexpa'], writes=['nexpa'])
    p.op('dve', lambda e: e.tensor_tensor(out=gk8[:], in0=qkg[:, :, 0], in1=qkg[:, :, 1], op=ALU.mult), reads=['prm'],
         writes=['gk8'])
    p.op('dve', lambda e: e.tensor_scalar(out=gk8[:], in0=gk8[:], scalar1=0.125, scalar2=None, op0=ALU.mult),
         reads=['gk8'], writes=['gk8'])

    PS = [ps([128, 512], F32, f"PS{i}") for i in range(7)]
    PT = ps([128, 1024], BF16, "PSTR")

    def K(i, lo=0, hi=512):
        return [f'PS{i}.bank']
    for i in range(7):
        p.alias[f'PS{i}'] = K(i)
    p.alias['PS5b'] = K(5, 128, 256)
    p.alias['PS5c'] = K(5, 256, 384)
    p.alias['PS6b'] = K(6, 128, 256)
    p.alias['PS6c'] = K(6, 256, 384)
    p.alias['PS6d'] = K(6, 384, 512)
    for i in range(5):
        p.alias[f'scr{i}'] = [f'scr.{i}']
    for nm in ['tmpA', 'tmpB', 'junk']:
        p.alias[nm] = [f'{nm}.{i}' for i in range(4)]
    p.alias['h_q'] = ['scr.0']
    p.alias['h_f'] = ['scr.1']
    p.alias['h_k'] = ['scr.2']
    p.alias['h_cum'] = ['scr.3']
    p.alias['h_e'] = ['scr.4']
    p.alias['a_z'] = ['scr.0', 'scr.1', 'scr.2']
    p.alias['a_sq'] = ['scr.2', 'scr.3', 'scr.4']
    p.alias['d_y'] = ['scr.0', 'scr.1', 'scr.2']
    EB = sb([128, 2, 8, 128], F32, "EB")
    relt = sb([32, 8], F32, "relt")
    boh = sb([32, 128], F32, "boh")
    aband = sb([128, 384], F32, "aband")
    vecE = sb([128, 8], F32, "vecE")
    p.dma('sp', relt[:], prm['rel_bias_table'], writes=['relt'])
    p.dma('sp', boh[:], cst['bias_oh'], writes=['boh'])
    p.dma('sp', aband[:], cst['antiband'], writes=['aband'])
    p.op('pe', lambda e: e.matmul(PS[0][:, 0:8], lhsT=boh[:], rhs=relt[:], start=True, stop=True), reads=['boh', 'relt'],
         writes=K(0))
    p.op('act', lambda e: e.activation(out=vecE[:], in_=PS[0][:, 0:8], func=AF.Exp), reads=K(0), writes=['vecE'])
    for blk in range(2):
        for t0 in range(0, 128, 64):
            pst = PS[1]
            for tt in range(64):
                off = 255 - (t0 + tt + (128 if blk == 0 else 0))
                p.op('pe', (lambda e, off=off, tt=tt: e.matmul(pst[:, tt * 8:(tt + 1) * 8], lhsT=aband[:, off:off + 128],
                                                               rhs=vecE[:], start=True, stop=True)),
                     reads=['aband', 'vecE'], writes=K(1), inc=(tt == 63))
            p.op('dve', (lambda e, blk=blk, t0=t0: e.tensor_copy(
                out=EB[:, blk, :, t0:t0 + 64], in_=pst[:].rearrange("p (t g) -> p g t", g=8))),
                reads=K(1), writes=['EB'])

    xt = sb([128, 1024], F32, "xt")
    hbf = sb([128, 1024], BF16, "hbf")
    hT = sb([128, 8, 128], BF16, "hT")
    NWB = 3
    wbuf = [sb([128, 8, 520], BF16, f"wbuf{i}") for i in range(NWB)]
    wb_n = [0]
    sm = sb([128, 128], F32, "small")
    obf = sb([128, 4, 512], BF16, "obf")
    oT = sb([128, 16, 128], BF16, "oT")
    gates = sb([128, 4096], BF16, "gates")
    merged = sb([128, 1024], F32, "merged")
    mbf = sb([128, 1024], BF16, "mbf")
    mT = sb([128, 8, 128], BF16, "mT")
    uT = sb([128, 32, 128], BF16, "uT")
    tmpA = sb([128, 512], F32, "tmpA")
    tmpB = sb([128, 512], F32, "tmpB")
    tmpC = sb([128, 512], F32, "tmpC")
    junk = sb([128, 1024], BF16, "junk")

    scr = sb([128, 2560], F32, "scr")

    class V:
        def __init__(self, ap):
            self.ap = ap

        def __getitem__(self, k):
            return self.ap[k]
    h_q = V(scr[:, 0:512].rearrange("p (a b) -> p a b", a=4))
    h_f = V(scr[:, 512:1024].rearrange("p (a b) -> p a b", a=4))
    h_k = V(scr[:, 1024:1536].rearrange("p (a b) -> p a b", a=4))
    h_cum = V(scr[:, 1536:2048].rearrange("p (a b) -> p a b", a=4))
    h_e = V(scr[:, 2048:2560].rearrange("p (a b) -> p a b", a=4))
    h_qp = sb([128, 4, 128], BF16, "h_qp")
    h_kp = sb([128, 4, 128], BF16, "h_kp")
    h_kpp = sb([128, 4, 128], BF16, "h_kpp")
    h_edec = sb([128, 4, 4], F32, "h_edec")
    h_v = sb([128, 512], BF16, "h_v")
    h_g = sb([128, 512], F32, "h_g")
    h_AT = sb([128, 128], BF16, "h_AT")
    h_kj = sb([128, 4, 128], BF16, "h_kj")
    h_qj = sb([128, 4, 4, 128], BF16, "h_qj")
    h_S = [sb([128, L, 4, 128], F32, "h_S")]
    h_Sb = sb([128, L, 4, 128], BF16, "h_Sb")

    m_q = sb([64, 4, 128], BF16, "m_q")
    m_kT = sb([64, 4, 128], BF16, "m_kT")
    m_k = sb([128, 256], BF16, "m_k")
    m_v = sb([128, 512], F32, "m_v")
    m_if = sb([128, 8], F32, "m_if")
    m_o = sb([128, 512], F32, "m_o")
    m_va = sb([128, 4, 130], BF16, "m_va")
    m_AT = sb([128, 128], BF16, "m_AT")
    m_C = sb([64, L, 4, 130], F32, "m_C")
    m_Cb = sb([64, L, 4, 130], BF16, "m_Cb")

    a_q = sb([64, 8, 128], BF16, "a_q")
    a_z = V(scr[0:64, 0:1280].rearrange("p (a b) -> p a b", a=10))
    a_sq = V(scr[0:64, 1280:2560].rearrange("p (a b) -> p a b", a=10))
    a_kT = sb([64, L, 2, 2, 128], BF16, "a_kT")
    a_v = sb([128, L, 2, 2, 66], BF16, "a_v")
    a_P = sb([128, 128], F32, "a_P")
    a_PT = sb([128, 2, 128], BF16, "a_PT")

    d_x = sb([128, L, 12, 132], BF16, "d_x")
    d_y = V(scr[:, 0:1536].rearrange("p (a b) -> p a b", a=12))
    d_sq = sb([128, 128], BF16, "d_sq")
    d_qT = sb([128, 4, 128], BF16, "d_qT")
    d_kT = sb([128, 4, 128], BF16, "d_kT")
    d_vT = sb([128, 4, 128], BF16, "d_vT")
    d_v = sb([128, 4, 128], F32, "d_v")
    d_k = sb([128, 4, 128], BF16, "d_k")
    d_ba = sb([128, 8], F32, "d_ba")
    d_arow = sb([4, 128], F32, "d_arow")
    d_grow = sb([4, 128], F32, "d_grow")
    d_z = sb([128, 512], F32, "d_z")
    dS_E1 = [sb([128, 128], F32, f"d_E1_{i}") for i in range(2)]
    dS_E1s = [sb([128, 128], F32, f"d_E1s_{i}") for i in range(2)]
    dS_P = [[sb([128, 128], F32, f"d_P{j}_{i}") for j in range(2)] for i in range(2)]
    dS_PT = [[sb([128, 128], F32, f"d_PT{j}_{i}") for j in range(2)] for i in range(2)]
    dS_TT = [sb([128, 128], F32, f"d_TT_{i}") for i in range(2)]
    dS_r = [sb([128, 128], F32, f"d_r_{i}") for i in range(2)]
    dS_vn = [sb([128, 128], BF16, f"d_vn_{i}") for i in range(2)]
    dS_vc = [sb([128, 128], BF16, f"d_vc_{i}") for i in range(2)]
    dS_aT = [sb([128, 128], BF16, f"d_aT_{i}") for i in range(2)]
    dS_o1 = [sb([128, 128], F32, f"d_o1_{i}") for i in range(2)]
    d_S = sb([128, L, 4, 128], F32, "d_S")
    d_Sb = sb([128, L, 4, 128], BF16, "d_Sb")

    for (t_, k) in [(h_S[0], 'h_S'), (h_Sb, 'h_Sb'), (m_C, 'm_C'), (m_Cb, 'm_Cb'), (d_S, 'd_S'), (d_Sb, 'd_Sb'),
                    (d_x, 'd_x'), (h_qj, 'h_qj'), (a_kT, 'a_kT'), (a_v, 'a_v')]:
        p.op('pool', (lambda e, t_=t_: e.memset(t_[:], 0.0)), writes=[k])

    NPIECE = 43
    wscr = nc.dram_tensor("wscr", [L, NPIECE, 128, 4160], BF16, kind="Internal").ap()
    piece_ctr = {}

    def _piece(l_, ti_):
        k = (l_, ti_)
        i = piece_ctr.get(k, 0)
        piece_ctr[k] = i + 1
        assert i < NPIECE
        return i

    def wload(src_ap, ncol, l_, ti_):
        pi = _piece(l_, ti_)
        scr_ap = wscr[l_, pi, :, 0:8 * ncol]
        skey = ('wscr', l_, pi)
        if ti_ == 0:
            p.dma('pool', scr_ap.rearrange("p (kc n) -> p kc n", kc=8), src_ap.rearrange("(kc p) n -> p kc n", p=128),
                  writes=[skey])
        i = wb_n[0] % NWB
        wb_n[0] += 1
        key = f'wbuf{i}'
        dst = wbuf[i][:].rearrange("p a b -> p (a b)")[:, 0:8 * ncol]
        p.dma('sp', dst, scr_ap, reads=[skey], writes=[key])
        return V3(dst.rearrange("p (kc n) -> p kc n", kc=8)), key

    def wload_rows(src_ap, l_, ti_):
        pi = _piece(l_, ti_)
        scr_ap = wscr[l_, pi, :, 0:4096]
        skey = ('wscr', l_, pi)
        if ti_ == 0:
            p.dma('pool', scr_ap.rearrange("p (r n) -> p r n", r=4), src_ap.rearrange("(r p) n -> p r n", p=128),
                  writes=[skey])
        i = wb_n[0] % NWB
        wb_n[0] += 1
        key = f'wbuf{i}'
        dst = wbuf[i][:].rearrange("p a b -> p (a b)")[:, 0:4096]
        p.dma('sp', dst, scr_ap, reads=[skey], writes=[key])
        return V3(dst.rearrange("p (r n) -> p r n", r=4)), key

    def proj_fm(wb, wkey, c0, ncol, ps_ap, pskey):
        for kc in range(8):
            p.op('pe', (lambda e, kc=kc: e.matmul(ps_ap, lhsT=wb[:, kc, c0:c0 + ncol], rhs=hT[:, kc, :],
                                                  start=(kc == 0), stop=(kc == 7))),
                 reads=[wkey, 'hT'], writes=[pskey], inc=(kc == 7))

    def proj_tm(wb, wkey, c0, ncol, ps_ap, pskey):
        for kc in range(8):
            p.op('pe', (lambda e, kc=kc: e.matmul(ps_ap, lhsT=hT[:, kc, :], rhs=wb[:, kc, c0:c0 + ncol],
                                                  start=(kc == 0), stop=(kc == 7))),
                 reads=[wkey, 'hT'], writes=[pskey], inc=(kc == 7))

    def rmsnorm_to_T(src, gt, dstT, dstkey, l):
        p.op('act', lambda e: e.activation(out=junk[:], in_=src[:], func=AF.Square, accum_out=sm[:, 0:1]),
             reads=['xt'], writes=['junk.0', 'junk.1', 'junk.2', 'junk.3', 'sm0'])
        p.op('act', lambda e: e.activation(out=sm[:, 1:2], in_=sm[:, 0:1], func=AF.Ln, scale=1.0 / 1024, bias=epsb[:, 0:1]),
             reads=['sm0', 'epsb'], writes=['sm1'])
        p.op('act', lambda e: e.activation(out=sm[:, 2:3], in_=sm[:, 1:2], func=AF.Exp, scale=-0.5),
             reads=['sm1'], writes=['sm2'])
        p.op('dve', lambda e: e.scalar_tensor_tensor(out=hbf[:], in0=src[:], scalar=sm[:, 2:3], in1=gt[:, l, :],
                                                     op0=ALU.mult, op1=ALU.mult),
             reads=['xt', 'sm2', 'prm'], writes=['hbf'])
        for kc in range(8):
            p.op('pe', (lambda e, kc=kc: e.transpose(out=PT[:, kc * 128:(kc + 1) * 128], in_=hbf[:, kc * 128:(kc + 1) * 128],
                                                     identity=ident_b[:])),
                 reads=['hbf', 'k_ident'], writes=['PT'], inc=(kc == 7))
        p.op('act', lambda e: e.copy(out=dstT[:].rearrange("p a b -> p (a b)"), in_=PT[:]), reads=['PT'], writes=[dstkey])

    epsb = sb([128, 2], F32, "epsb")
    p.op('dve', lambda e: e.memset(epsb[:, 0:1], EPS), writes=['epsb'])
    p.op('dve', lambda e: e.memset(epsb[:, 1:2], 1.0), writes=['epsb'])

    def head_norm_gate(src_ap, srckey, srcreads, gtile_ap, gate_ap, gatekey, out_ap, smc, sl=0):
        jk = junk[:, sl * 128:(sl + 1) * 128]
        tc_ = tmpC[:, sl * 128:(sl + 1) * 128]
        p.op('act', lambda e: e.activation(out=jk, in_=src_ap, func=AF.Square, accum_out=sm[:, smc:smc + 1]),
             reads=[srckey] + srcreads, writes=[f'junk.{sl}', f'sm{smc}'])
        p.op('act', lambda e: e.activation(out=sm[:, smc + 1:smc + 2], in_=sm[:, smc:smc + 1], func=AF.Ln, scale=1.0 / 128,
                                           bias=epsb[:, 0:1]), reads=[f'sm{smc}', 'epsb'], writes=[f'sm{smc+1}'])
        p.op('act', lambda e: e.activation(out=sm[:, smc + 2:smc + 3], in_=sm[:, smc + 1:smc + 2], func=AF.Exp, scale=-0.5),
             reads=[f'sm{smc+1}'], writes=[f'sm{smc+2}'])
        p.op('dve', lambda e: e.scalar_tensor_tensor(out=tc_, in0=src_ap, scalar=sm[:, smc + 2:smc + 3],
                                                     in1=gtile_ap, op0=ALU.mult, op1=ALU.mult),
             reads=[srckey, f'sm{smc+2}', 'prm'] + srcreads, writes=[f'tmpC.{sl}'])
        p.op('dve', lambda e: e.tensor_tensor(out=out_ap, in0=tc_, in1=gate_ap, op=ALU.mult),
             reads=[f'tmpC.{sl}', gatekey], writes=['obf'])

    def interleave(gens):
        gens = list(gens)
        while gens:
            for g in list(gens):
                try:
                    next(g)
                except StopIteration:
                    gens.remove(g)

    if pipe:
        xh = sb([128, 1024], F32, "xh")
        xr = sb([128, 1024], F32, "xr")
        p.op('pool', lambda e: e.memset(xr[:], 0.0), writes=['xr'])
        for j in range(2):
            p.dma('sp', sendb[j], xr[:], reads=['xr'], writes=[('sendb', j)])
    for ti in range(NIT):
        if pipe:
            tix = min(ti, NTILES - 1)
            p.dma('sp', xh[:], x_in[tix * 128:(tix + 1) * 128, :], writes=['xh'])
            p.coll("AllReduce", ALU.add, [[2 * i_, 2 * i_ + 1] for i_ in range(npairs)], sendb[ti % 2], recvb[ti % 2],
                   reads=[('sendb', ti % 2)], writes=[('recvb', ti % 2)])
            p.dma('sp', xr[:], recvb[ti % 2], reads=[('recvb', ti % 2)], writes=['xr'])
            p.op('dve', lambda e: e.scalar_tensor_tensor(out=xt[:], in0=xr[:], scalar=role[:, 1:2], in1=xh[:], op0=ALU.mult,
                                                         op1=ALU.add), reads=['xr', 'xh', 'role'], writes=['xt'])
        else:
            p.dma('sp', xt[:], x_in[ti * 128:(ti + 1) * 128, :], writes=['xt'])
        for l in range(L):
            par = ti % 2
            W_in = prm['w_in'][l]
            rmsnorm_to_T(xt, gmix, hT, 'hT', l)
            if not all(enable):
                p.op('pool', lambda e: e.memset(obf[:], 0.0), writes=['obf'])

            if enable[0]:
                wb, wk = wload(W_in[:, C_HQ:C_HQ + 512], 512, l, ti)
                for h in range(4):
                    proj_fm(wb, wk, h * 128, 128, PS[0][:, h * 128:(h + 1) * 128], 'PS0')
                p.op('act', lambda e: e.activation(out=h_q[:].rearrange("p a b -> p (a b)"), in_=PS[0][:], func=AF.Silu),
                     reads=['PS0'], writes=['h_q'])
                wb, wk = wload(W_in[:, C_HF:C_HF + 512], 512, l, ti)
                for h in range(4):
                    proj_fm(wb, wk, h * 128, 128, PS[1][:, h * 128:(h + 1) * 128], 'PS1')
                p.op('act', lambda e: e.activation(out=h_f[:].rearrange("p a b -> p (a b)"), in_=PS[1][:], func=AF.Sigmoid),
                     reads=['PS1'], writes=['h_f'])
                for h in range(4):
                    p.op('dve', (lambda e, h=h: e.tensor_scalar(out=h_f[:, h, :], in0=h_f[:, h, :], scalar1=oml[:, l, h:h + 1],
                                                                scalar2=lb[:, l, h:h + 1], op0=ALU.mult, op1=ALU.add)),
                         reads=['h_f', 'oml', 'lb'], writes=['h_f'])
                hf2 = h_f[:].rearrange("p a b -> p (a b)")
                hk2 = h_k[:].rearrange("p a b -> p (a b)")
                hc2 = h_cum[:].rearrange("p a b -> p (a b)")
                he2 = h_e[:].rearrange("p a b -> p (a b)")
                hq2 = h_q[:].rearrange("p a b -> p (a b)")
                p.op('dve', lambda e: e.tensor_scalar(out=hk2, in0=hf2, scalar1=-1.0, scalar2=1.0, op0=ALU.mult, op1=ALU.add),
                     reads=['h_f'], writes=['h_k'])
                p.op('act', lambda e: e.activation(out=hf2, in_=hf2, func=AF.Ln), reads=['h_f', 'h_k'], writes=['h_f'])
                for h in range(4):
                    p.op('dve', (lambda e, h=h: e.tensor_tensor_scan(out=h_cum[:, h, :], data0=resetm[:], data1=h_f[:, h, :],
                                                                     initial=0.0, op0=ALU.mult, op1=ALU.add)),
                         reads=['h_f', 'k_resetm'], writes=['h_cum'])
                p.op('act', lambda e: e.activation(out=he2, in_=hc2, func=AF.Exp), reads=['h_cum'], writes=['h_e'])
                p.op('dve', lambda e: e.tensor_tensor(out=h_qp[:].rearrange("p a b -> p (a b)"), in0=hq2, in1=he2, op=ALU.mult),
                     reads=['h_q', 'h_e'], writes=['h_qp'])
                p.op('act', lambda e: e.activation(out=he2, in_=hc2, func=AF.Exp, scale=-1.0), reads=['h_cum', 'h_qp'],
                     writes=['h_e'])
                p.op('dve', lambda e: e.tensor_tensor(out=h_kp[:].rearrange("p a b -> p (a b)"), in0=hk2, in1=he2, op=ALU.mult),
                     reads=['h_k', 'h_e'], writes=['h_kp'])
                cl = h_cum[:].rearrange("p a (j i) -> p a j i", i=32)[:, :, :, 31]
                p.op('act', lambda e: e.activation(out=h_edec[:], in_=cl, func=AF.Exp), reads=['h_cum'], writes=['h_edec'])
                p.op('dve', lambda e: e.tensor_tensor(
                    out=h_e[:].rearrange("p a (j i) -> p a j i", i=32),
                    in0=h_cum[:].rearrange("p a (j i) -> p a j i", i=32)[:, :, :, 31:32].broadcast_to([128, 4, 4, 32]),
                    in1=h_cum[:].rearrange("p a (j i) -> p a j i", i=32), op=ALU.subtract),
                    reads=['h_cum', 'h_kp'], writes=['h_e'])
                p.op('act', lambda e: e.activation(out=he2, in_=he2, func=AF.Exp), reads=['h_e'], writes=['h_e'])
                p.op('dve', lambda e: e.tensor_tensor(out=h_kpp[:].rearrange("p a b -> p (a b)"), in0=hk2, in1=he2, op=ALU.mult),
                     reads=['h_k', 'h_e'], writes=['h_kpp'])
                wb, wk = wload(W_in[:, C_HI:C_HI + 512], 512, l, ti)
                proj_tm(wb, wk, 0, 512, PS[0][:], 'PS0')
                p.op('act', lambda e: e.copy(out=h_v[:], in_=PS[0][:]), reads=['PS0'], writes=['h_v'])
                wb, wk = wload(W_in[:, C_HG:C_HG + 512], 512, l, ti)
                proj_tm(wb, wk, 0, 512, PS[1][:], 'PS1')
                p.op('act', lambda e: e.activation(out=h_g[:], in_=PS[1][:], func=AF.Silu), reads=['PS1'], writes=['h_g'])
                for h in range(4):
                    p.op('pe', (lambda e, h=h: e.matmul(PS[2][:, 0:128], lhsT=h_kp[:, h, :], rhs=h_qp[:, h, :], start=True,
                                                        stop=True)), reads=['h_kp', 'h_qp'], writes=['PS2'])
                    p.op('dve', lambda e: e.tensor_tensor(out=h_AT[:], in0=PS[2][:, 0:128], in1=hgmask[:], op=ALU.mult),
                         reads=['PS2', 'k_hgmask'], writes=['h_AT'])
                    p.op('pe', (lambda e, h=h: e.transpose(out=PT[:, 0:128], in_=h_kpp[:, h, :], identity=ident_b[:])),
                         reads=['h_kpp', 'k_ident'], writes=['PT'])
                    for j in range(4):
                        p.op('dve', (lambda e, j=j: e.tensor_scalar(out=h_kj[:, j, :], in0=PT[:, 0:128], scalar1=rowm[:, j:j + 1],
                                                                    scalar2=None, op0=ALU.mult)),
                             reads=['PT', 'k_rowm'], writes=['h_kj'])
                        p.op('act', (lambda e, h=h, j=j: e.copy(out=h_qj[:, h, j, 32 * j:32 * j + 32],
                                                                in_=h_qp[:, h, 32 * j:32 * j + 32])),
                             reads=['h_qp'], writes=['h_qj'])
                    p.op('pe', (lambda e, h=h: e.matmul(PS[3][:, 0:128], lhsT=h_AT[:], rhs=h_v[:, h * 128:(h + 1) * 128],
                                                        start=True, stop=False)), reads=['h_AT', 'h_v'], writes=['PS3'])
                    for j in range(4):
                        p.op('pe', (lambda e, h=h, j=j: e.matmul(PS[3][:, 0:128], lhsT=h_qj[:, h, j, :], rhs=h_Sb[:, l, h, :],
                                                                 start=False, stop=(j == 3))),
                             reads=['h_qj', 'h_Sb'], writes=['PS3'])
                        p.op('pe', (lambda e, h=h, j=j: e.matmul(PS[4][:, 0:128], lhsT=h_kj[:, j, :],
                                                                 rhs=h_v[:, h * 128:(h + 1) * 128], start=True, stop=True)),
                             reads=['h_kj', 'h_v'], writes=['PS4'])
                        p.op('dve', (lambda e, h=h, j=j: e.scalar_tensor_tensor(
                            out=h_S[0][:, l, h, :], in0=h_S[0][:, l, h, :], scalar=h_edec[:, h, j:j + 1], in1=PS[4][:, 0:128],
                            op0=ALU.mult, op1=ALU.add)), reads=['h_S', 'h_edec', 'PS4'], writes=['h_S'])
                        p.op('act', (lambda e, h=h: e.copy(out=h_Sb[:, l, h, :], in_=h_S[0][:, l, h, :])),
                             reads=['h_S'], writes=['h_Sb'])
                    head_norm_gate(PS[3][:, 0:128], 'PS3', [], hog[:, l, h * 128:(h + 1) * 128], h_g[:, h * 128:(h + 1) * 128],
                                   'h_g', obf[:, 0, h * 128:(h + 1) * 128], 4)

            if enable[1]:
                wb, wk = wload(W_in[:, C_MQ:C_MQ + 512], 512, l, ti)
                for h in range(4):
                    proj_fm(wb, wk, h * 64, 64, PS[0][0:64, h * 128:(h + 1) * 128], 'PS0')
                    proj_fm(wb, wk, 256 + h * 64, 64, PS[1][0:64, h * 128:(h + 1) * 128], 'PS1')
                p.op('act', lambda e: e.copy(out=m_q[:].rearrange("p a b -> p (a b)"), in_=PS[0][0:64, :]), reads=['PS0'],
                     writes=['m_q'])
                p.op('act', lambda e: e.mul(out=m_kT[:].rearrange("p a b -> p (a b)"), in_=PS[1][0:64, :], mul=0.125),
                     reads=['PS1'], writes=['m_kT'])
                proj_tm(wb, wk, 256, 256, PS[2][:, 0:256], 'PS2')
                p.op('act', lambda e: e.mul(out=m_k[:], in_=PS[2][:, 0:256], mul=0.125), reads=['PS2'], writes=['m_k'])
                wb, wk = wload(W_in[:, C_MV:C_MV + 520], 520, l, ti)
                proj_tm(wb, wk, 0, 512, PS[0][:], 'PS0')
                p.op('act', lambda e: e.copy(out=m_v[:], in_=PS[0][:]), reads=['PS0'], writes=['m_v'])
                proj_tm(wb, wk, 512, 8, PS[1][:, 0:8], 'PS1')
                p.op('dve', lambda e: e.tensor_tensor(out=m_if[:], in0=PS[1][:, 0:8], in1=mifb[:, l, :], op=ALU.add),
                     reads=['PS1', 'prm'], writes=['m_if'])
                wb, wk = wload(W_in[:, C_MO:C_MO + 512], 512, l, ti)
                proj_tm(wb, wk, 0, 512, PS[2][:], 'PS2')
                p.op('act', lambda e: e.activation(out=m_o[:], in_=PS[2][:], func=AF.Sigmoid), reads=['PS2'], writes=['m_o'])
                p.op('act', lambda e: e.activation(out=sm[:, 8:12], in_=m_if[:, 4:8], func=AF.Exp, scale=-1.0),
                     reads=['m_if'], writes=['sm8'])
                p.op('act', lambda e: e.activation(out=sm[:, 8:12], in_=sm[:, 8:12], func=AF.Ln, bias=epsb[:, 1:2]),
                     reads=['sm8', 'epsb'], writes=['sm8'])
                p.op('pe', lambda e: e.matmul(PS[3][:, 0:4], lhsT=tri_incl[:], rhs=sm[:, 8:12], start=True, stop=True),
                     reads=['sm8', 'k_tri_incl'], writes=['PS3'])
                p.op('act', lambda e: e.copy(out=sm[:, 12:16], in_=PS[3][:, 0:4]), reads=['PS3'], writes=['sm12'])
                p.op('dve', lambda e: e.tensor_tensor(out=sm[:, 16:20], in0=PS[3][:, 0:4], in1=m_if[:, 0:4], op=ALU.add),
                     reads=['PS3', 'm_if'], writes=['sm16'])
                p.op('act', lambda e: e.activation(out=sm[:, 16:20], in_=sm[:, 16:20], func=AF.Exp), reads=['sm16'],
                     writes=['sm16'])
                p.op('act', lambda e: e.activation(out=sm[:, 20:24], in_=sm[:, 12:16], func=AF.Exp, scale=-1.0),
                     reads=['sm12'], writes=['sm20'])
                p.op('pe', lambda e: e.matmul(PS[3][0:64, 8:12], lhsT=sel_last[:, 0:64], rhs=sm[:, 12:16], start=True, stop=True),
                     reads=['sm12', 'k_sel_last'], writes=['PS3'])
                p.op('act', lambda e: e.activation(out=sm[0:64, 24:28], in_=PS[3][0:64, 8:12], func=AF.Exp, scale=-1.0),
                     reads=['PS3'], writes=['sm24'])
                for h in range(4):
                    p.op('dve', (lambda e, h=h: e.tensor_scalar(out=m_va[:, h, 0:128], in0=m_v[:, h * 128:(h + 1) * 128],
                                                                scalar1=sm[:, 16 + h:17 + h], scalar2=None, op0=ALU.mult)),
                         reads=['m_v', 'sm16'], writes=['m_va'])
                    p.op('dve', (lambda e, h=h: e.tensor_copy(out=m_va[:, h, 128:129], in_=sm[:, 16 + h:17 + h])),
                         reads=['sm16'], writes=['m_va'])
                for h in range(4):
                    p.op('pe', (lambda e, h=h: e.matmul(PS[4][:, 0:128], lhsT=m_kT[:, h, :], rhs=m_q[:, h, :], start=True,
                                                        stop=True)), reads=['m_kT', 'm_q'], writes=['PS4'])
                    p.op('dve', lambda e: e.tensor_tensor(out=m_AT[:], in0=PS[4][:, 0:128], in1=tri_incl[:], op=ALU.mult),
                         reads=['PS4', 'k_tri_incl'], writes=['m_AT'])
                    p.op('pe', (lambda e, h=h: e.matmul(PS[5][:, 0:129], lhsT=m_AT[:], rhs=m_va[:, h, 0:129], start=True,
                                                        stop=False)), reads=['m_AT', 'm_va'], writes=['PS5'], inc=False)
                    p.op('pe', (lambda e, h=h: e.matmul(PS[5][:, 0:129], lhsT=m_q[:, h, :], rhs=m_Cb[:, l, h, 0:129], start=False,
                                                        stop=True)), reads=['m_q', 'm_Cb'], writes=['PS5'])
                    p.op('dve', (lambda e, h=h: e.tensor_tensor(out=sm[:, 28:29], in0=PS[5][:, 128:129], in1=sm[:, 20 + h:21 + h],
                                                                op=ALU.mult)), reads=['PS5', 'sm20'], writes=['sm28'])
                    p.op('dve', lambda e: e.scalar_tensor_tensor(out=sm[:, 31:32], in0=sm[:, 28:29], scalar=-1.0, in1=sm[:, 28:29],
                                                                 op0=ALU.mult, op1=ALU.max), reads=['sm28'], writes=['sm31'])
                    p.op('dve', lambda e: e.tensor_scalar(out=sm[:, 28:29], in0=sm[:, 31:32], scalar1=1.0, scalar2=None,
                                                          op0=ALU.max), reads=['sm31'], writes=['sm28'])
                    p.op('dve', lambda e: e.reciprocal(out=sm[:, 29:30], in_=sm[:, 28:29]), reads=['sm28'], writes=['sm29'])
                    p.op('dve', (lambda e, h=h: e.tensor_tensor(out=sm[:, 30:31], in0=sm[:, 29:30], in1=sm[:, 20 + h:21 + h],
                                                                op=ALU.mult)), reads=['sm29', 'sm20'], writes=['sm30'])
                    p.op('act', lambda e: e.activation(out=tmpA[:, 0:128], in_=PS[5][:, 0:128], func=AF.Copy, scale=sm[:, 30:31]),
                         reads=['PS5', 'sm30'], writes=['tmpA'])
                    p.op('pe', (lambda e, h=h: e.matmul(PS[6][0:64, 0:129], lhsT=m_k[:, h * 64:(h + 1) * 64], rhs=m_va[:, h, 0:129],
                                                        start=True, stop=True)), reads=['m_k', 'm_va'], writes=['PS6'])
                    p.op('dve', (lambda e, h=h: e.tensor_tensor(out=m_C[:, l, h, 0:129], in0=m_C[:, l, h, 0:129],
                                                                in1=PS[6][0:64, 0:129], op=ALU.add)),
                         reads=['m_C', 'PS6'], writes=['m_C'])
                    p.op('dve', (lambda e, h=h: e.tensor_scalar(out=m_C[:, l, h, 0:129], in0=m_C[:, l, h, 0:129],
                                                                scalar1=sm[0:64, 24 + h:25 + h], scalar2=None, op0=ALU.mult)),
                         reads=['m_C', 'sm24'], writes=['m_C'])
                    p.op('act', (lambda e, h=h: e.copy(out=m_Cb[:, l, h, 0:129], in_=m_C[:, l, h, 0:129])), reads=['m_C'],
                         writes=['m_Cb'])
                    head_norm_gate(tmpA[:, 0:128], 'tmpA', [], mog[:, l, h * 128:(h + 1) * 128], m_o[:, h * 128:(h + 1) * 128],
                                   'm_o', obf[:, 1, h * 128:(h + 1) * 128], 32)

            if enable[2]:
                wb, wk = wload(W_in[:, C_AQ:C_AQ + 512], 512, l, ti)
                wb2, wk2 = wload(W_in[:, C_AK:C_AK + 256], 256, l, ti)
                for g in range(8):
                    proj_fm(wb, wk, g * 64, 64, PS[g // 4][0:64, (g % 4) * 128:(g % 4 + 1) * 128], f'PS{g // 4}')
                for kv in range(2):
                    proj_fm(wb2, wk2, kv * 64, 64, PS[2][0:64, kv * 128:(kv + 1) * 128], 'PS2')
                az2 = a_z[:].rearrange("p a b -> p (a b)")
                asq2 = a_sq[:].rearrange("p a b -> p (a b)")
                p.op('act', lambda e: e.copy(out=az2[:, 0:512], in_=PS[0][0:64, :]), reads=['PS0'], writes=['a_z'])
                p.op('act', lambda e: e.copy(out=az2[:, 512:1024], in_=PS[1][0:64, :]), reads=['PS1'], writes=['a_z'])
                p.op('act', lambda e: e.copy(out=az2[:, 1024:1280], in_=PS[2][0:64, 0:256]), reads=['PS2'], writes=['a_z'])
                p.op('dve', lambda e: e.tensor_tensor(out=asq2, in0=az2, in1=az2, op=ALU.mult), reads=['a_z'], writes=['a_sq'])
                for i3 in range(3):
                    w3 = 512 if i3 < 2 else 256
                    p.op('pe', (lambda e, i3=i3, w3=w3: e.matmul(PS[3][0:64, 0:w3], lhsT=ones_f[0:64, 0:64],
                                                                 rhs=asq2[:, i3 * 512:i3 * 512 + w3], start=True, stop=True)),
                         reads=['a_sq', 'k_ones'], writes=['PS3'])
                    p.op('act', (lambda e, i3=i3, w3=w3: e.activation(out=asq2[:, i3 * 512:i3 * 512 + w3], in_=PS[3][0:64, 0:w3],
                                                                      func=AF.Ln, scale=1.0 / 64, bias=epsb[0:64, 0:1])),
                         reads=['PS3', 'epsb'], writes=['a_sq'])
                p.op('act', lambda e: e.activation(out=asq2, in_=asq2, func=AF.Exp, scale=-0.5), reads=['a_sq'], writes=['a_sq'])
                p.op('dve', lambda e: e.tensor_tensor(out=a_q[:].rearrange("p a b -> p (a b)"), in0=az2[:, 0:1024],
                                                      in1=asq2[:, 0:1024], op=ALU.mult), reads=['a_z', 'a_sq'], writes=['a_q'])
                for kv in range(2):
                    p.op('dve', (lambda e, kv=kv: e.scalar_tensor_tensor(
                        out=a_kT[:, l, kv, par, :], in0=a_z[:, 8 + kv, :], scalar=gk8[:, l:l + 1], in1=a_sq[:, 8 + kv, :],
                        op0=ALU.mult, op1=ALU.mult)), reads=['a_z', 'a_sq', 'gk8'], writes=['a_kT'])
                proj_tm(wb2, wk2, 128, 128, PS[4][:, 0:128], 'PS4')
                for kv in range(2):
                    p.op('act', (lambda e, kv=kv: e.copy(out=a_v[:, l, par, kv, 0:64], in_=PS[4][:, kv * 64:(kv + 1) * 64])),
                         reads=['PS4'], writes=['a_v'])
                    p.op('dve', (lambda e, kv=kv: e.memset(a_v[:, l, par, kv, 64:65], 1.0)), writes=['a_v'])
                for g in range(8):
                    kv = g // 4
                    blks = [0, 1] if (pipe or ti > 0) else [1]
                    for bi, blk in enumerate(blks):
                        slot = par if blk == 1 else 1 - par
                        p.op('pe', (lambda e, g=g, kv=kv, slot=slot: e.matmul(PS[5][:, 0:128], lhsT=a_kT[:, l, kv, slot, :],
                                                                              rhs=a_q[:, g, :], start=True, stop=True)),
                             reads=['a_kT', 'a_q'], writes=['PS5'])
                        p.op('act', lambda e: e.activation(out=a_P[:], in_=PS[5][:, 0:128], func=AF.Exp), reads=['PS5'],
                             writes=['a_P'])
                        if pipe and blk == 0:
                            p.op('dve', (lambda e, g=g, blk=blk: e.scalar_tensor_tensor(
                                out=a_PT[:, blk, :], in0=a_P[:], scalar=pflag[:, ti:ti + 1], in1=EB[:, blk, g, :],
                                op0=ALU.mult, op1=ALU.mult)), reads=['a_P', 'EB', 'pflag'], writes=['a_PT'])
                        else:
                            p.op('dve', (lambda e, g=g, blk=blk: e.tensor_tensor(out=a_PT[:, blk, :], in0=a_P[:],
                                                                                 in1=EB[:, blk, g, :], op=ALU.mult)),
                                 reads=['a_P', 'EB'], writes=['a_PT'])
                    for bi, blk in enumerate(blks):
                        slot = par if blk == 1 else 1 - par
                        p.op('pe', (lambda e, kv=kv, slot=slot, blk=blk, bi=bi: e.matmul(
                            PS[6][:, 0:65], lhsT=a_PT[:, blk, :], rhs=a_v[:, l, slot, kv, 0:65], start=(bi == 0),
                            stop=(bi == len(blks) - 1))), reads=['a_PT', 'a_v'], writes=['PS6'], inc=(bi == len(blks) - 1))
                    p.op('dve', (lambda e, g=g: e.tensor_tensor(out=sm[:, 40:41], in0=PS[6][:, 64:65], in1=esink[:, l, g:g + 1],
                                                                op=ALU.add)), reads=['PS6', 'esink'], writes=['sm40'])
                    p.op('dve', lambda e: e.reciprocal(out=sm[:, 41:42], in_=sm[:, 40:41]), reads=['sm40'], writes=['sm41'])
                    p.op('dve', (lambda e, g=g: e.tensor_scalar(out=obf[:, 2, g * 64:(g + 1) * 64], in0=PS[6][:, 0:64],
                                                                scalar1=sm[:, 41:42], scalar2=None, op0=ALU.mult)),
                         reads=['PS6', 'sm41'], writes=['obf'])

            if enable[3]:
                for c3 in range(3):
                    wb, wk = wload(W_in[:, C_DQKV + c3 * 512:C_DQKV + (c3 + 1) * 512], 512, l, ti)
                    for c4 in range(4):
                        proj_fm(wb, wk, c4 * 128, 128, PS[c3][:, c4 * 128:(c4 + 1) * 128], f'PS{c3}')
                    p.op('act', (lambda e, c3=c3: e.copy(out=d_x[:, l, c3 * 4:(c3 + 1) * 4, 3:131],
                                                         in_=PS[c3][:].rearrange("p (a b) -> p a b", a=4))),
                         reads=[f'PS{c3}'], writes=['d_x'])
                for cc in range(12):
                    p.op('dve', (lambda e, cc=cc: e.tensor_scalar(out=d_y[:, cc, :], in0=d_x[:, l, cc, 0:128],
                                                                  scalar1=convw[:, l, 0, cc:cc + 1], scalar2=None, op0=ALU.mult)),
                         reads=['d_x', 'prm'], writes=['d_y'])
                    for j in range(1, 4):
                        p.op('dve', (lambda e, cc=cc, j=j: e.scalar_tensor_tensor(
                            out=d_y[:, cc, :], in0=d_x[:, l, cc, j:j + 128], scalar=convw[:, l, j, cc:cc + 1], in1=d_y[:, cc, :],
                            op0=ALU.mult, op1=ALU.add)), reads=['d_x', 'prm', 'd_y'], writes=['d_y'])
                p.op('pool', lambda e: e.tensor_copy(out=d_x[:, l, :, 0:3], in_=d_x[:, l, :, 128:131]), reads=['d_x', 'd_y'],
                     writes=['d_x'])
                dy2 = d_y[:].rearrange("p a b -> p (a b)")
                p.op('act', lambda e: e.activation(out=dy2, in_=dy2, func=AF.Silu), reads=['d_y'], writes=['d_y'])
                for cc in range(8):
                    p.op('dve', (lambda e, cc=cc: e.tensor_tensor(out=tmpA[:, 0:128], in0=d_y[:, cc, :], in1=d_y[:, cc, :],
                                                                  op=ALU.mult)), reads=['d_y'], writes=['tmpA'])
                    p.op('pe', lambda e: e.matmul(PS[3][:, 0:128], lhsT=ones_f[:], rhs=tmpA[:, 0:128], start=True, stop=True),
                         reads=['tmpA', 'k_ones'], writes=['PS3'])
                    p.op('act', lambda e: e.activation(out=tmpB[:, 0:128], in_=PS[3][:, 0:128], func=AF.Ln, bias=epsb[:, 0:1]),
                         reads=['PS3', 'epsb'], writes=['tmpB'])
                    p.op('act', lambda e: e.activation(out=tmpB[:, 0:128], in_=tmpB[:, 0:128], func=AF.Exp, scale=-0.5),
                         reads=['tmpB'], writes=['tmpB'])
                    if cc < 4:
                        p.op('dve', (lambda e, cc=cc: e.scalar_tensor_tensor(
                            out=d_qT[:, cc, :], in0=d_y[:, cc, :], scalar=float(128 ** -0.5), in1=tmpB[:, 0:128],
                            op0=ALU.mult, op1=ALU.mult)), reads=['d_y', 'tmpB'], writes=['d_qT'])
                    else:
                        p.op('dve', (lambda e, cc=cc: e.tensor_tensor(out=d_kT[:, cc - 4, :], in0=d_y[:, cc, :],
                                                                      in1=tmpB[:, 0:128], op=ALU.mult)),
                             reads=['d_y', 'tmpB'], writes=['d_kT'])
                p.op('dve', lambda e: e.tensor_copy(out=d_vT[:], in_=d_y[:, 8:12, :]), reads=['d_y'], writes=['d_vT'])
                for h in range(4):
                    p.op('pe', (lambda e, h=h: e.transpose(out=PT[:, h * 128:(h + 1) * 128], in_=d_vT[:, h, :],
                                                           identity=ident_b[:])), reads=['d_vT', 'k_ident'], writes=['PT'],
                         inc=False)
                    p.op('pe', (lambda e, h=h: e.transpose(out=PT[:, 512 + h * 128:512 + (h + 1) * 128], in_=d_kT[:, h, :],
                                                           identity=ident_b[:])), reads=['d_kT', 'k_ident'], writes=['PT'],
                         inc=(h == 3))
                p.op('act', lambda e: e.copy(out=d_v[:].rearrange("p a b -> p (a b)"), in_=PT[:, 0:512]), reads=['PT'],
                     writes=['d_v'])
                p.op('act', lambda e: e.copy(out=d_k[:].rearrange("p a b -> p (a b)"), in_=PT[:, 512:1024]), reads=['PT'],
                     writes=['d_k'])
                wb, wk = wload(W_in[:, C_DB:C_DB + 520], 520, l, ti)
                proj_tm(wb, wk, 0, 8, PS[0][:, 0:8], 'PS0')
                p.op('act', lambda e: e.copy(out=d_ba[:], in_=PS[0][:, 0:8]), reads=['PS0'], writes=['d_ba'])
                proj_fm(wb, wk, 4, 4, PS[1][0:4, 0:128], 'PS1')
                p.op('act', lambda e: e.copy(out=d_arow[:], in_=PS[1][0:4, 0:128]), reads=['PS1'], writes=['d_arow'])
                proj_tm(wb, wk, 8, 512, PS[2][:], 'PS2')
                p.op('act', lambda e: e.activation(out=d_z[:], in_=PS[2][:], func=AF.Silu), reads=['PS2'], writes=['d_z'])
                p.op('act', lambda e: e.activation(out=sm[:, 44:48], in_=d_ba[:, 0:4], func=AF.Sigmoid), reads=['d_ba'],
                     writes=['sm44'])
                p.op('dve', lambda e: e.tensor_tensor(out=sm[:, 48:52], in0=d_ba[:, 4:8], in1=dtb[:, l, :], op=ALU.add),
                     reads=['d_ba', 'prm'], writes=['sm48'])
                p.op('act', lambda e: e.activation(out=sm[:, 48:52], in_=sm[:, 48:52], func=AF.Exp), reads=['sm48'],
                     writes=['sm48'])
                p.op('act', lambda e: e.activation(out=sm[:, 48:52], in_=sm[:, 48:52], func=AF.Ln, bias=epsb[:, 1:2]),
                     reads=['sm48', 'epsb'], writes=['sm48'])
                p.op('dve', lambda e: e.tensor_tensor(out=sm[:, 48:52], in0=sm[:, 48:52], in1=nexpa[:, l, :], op=ALU.mult),
                     reads=['sm48', 'nexpa'], writes=['sm48'])
                p.op('pe', lambda e: e.matmul(PS[3][:, 0:4], lhsT=tri_incl[:], rhs=sm[:, 48:52], start=True, stop=True),
                     reads=['sm48', 'k_tri_incl'], writes=['PS3'])
                p.op('act', lambda e: e.copy(out=sm[:, 52:56], in_=PS[3][:, 0:4]), reads=['PS3'], writes=['sm52'])
                p.op('dve', lambda e: e.tensor_scalar(out=sm[:, 56:60], in0=sm[:, 52:56], scalar1=-1.0, scalar2=None,
                                                      op0=ALU.mult), reads=['sm52'], writes=['sm56'])
                p.op('pe', lambda e: e.matmul(PS[3][:, 8:12], lhsT=sel_last[:], rhs=sm[:, 52:56], start=True, stop=True),
                     reads=['sm52', 'k_sel_last'], writes=['PS3'])
                p.op('act', lambda e: e.activation(out=sm[:, 60:64], in_=PS[3][:, 8:12], func=AF.Exp), reads=['PS3'],
                     writes=['sm60'])
                p.op('dve', lambda e: e.tensor_tensor(out=tmpC[:, 500:504], in0=PS[3][:, 8:12], in1=sm[:, 52:56],
                                                      op=ALU.subtract), reads=['PS3', 'sm52'], writes=['tmpC5'])
                p.op('act', lambda e: e.activation(out=tmpC[:, 500:504], in_=tmpC[:, 500:504], func=AF.Exp), reads=['tmpC5'],
                     writes=['tmpC5'])
                p.op('dve', lambda e: e.tensor_tensor(out=tmpC[:, 504:508], in0=tmpC[:, 500:504], in1=sm[:, 44:48], op=ALU.mult),
                     reads=['tmpC5', 'sm44'], writes=['tmpC6'])
                p.op('act', lambda e: e.activation(out=tmpC[:, 508:512], in_=sm[:, 52:56], func=AF.Exp), reads=['sm52'],
                     writes=['tmpC7'])
                p.op('dve', lambda e: e.tensor_scalar(out=tmpC[:, 496:500], in0=tmpC[:, 508:512], scalar1=-1.0, scalar2=None,
                                                      op0=ALU.mult), reads=['tmpC7'], writes=['tmpC4'])
                p.op('dve', lambda e: e.tensor_scalar(out=tmpC[:, 492:496], in0=sm[:, 44:48], scalar1=-1.0, scalar2=None,
                                                      op0=ALU.mult), reads=['sm44'], writes=['tmpC3'])
                p.op('act', lambda e: e.activation(out=d_grow[:], in_=d_arow[:], func=AF.Exp, bias=dtbrow[:, l:l + 1]),
                     reads=['d_arow', 'dtbrow'], writes=['d_grow'])
                p.op('act', lambda e: e.activation(out=d_grow[:], in_=d_grow[:], func=AF.Ln, bias=epsb[0:4, 1:2]),
                     reads=['d_grow', 'epsb'], writes=['d_grow'])
                p.op('dve', lambda e: e.tensor_scalar(out=d_grow[:], in0=d_grow[:], scalar1=nexparow[:, l:l + 1], scalar2=None,
                                                      op0=ALU.mult), reads=['d_grow', 'nexparow'], writes=['d_grow'])
                p.op('dve', lambda e: e.tensor_tensor_scan(out=d_grow[:], data0=ones_f[0:4, :], data1=d_grow[:], initial=0.0,
                                                           op0=ALU.mult, op1=ALU.add), reads=['d_grow', 'k_ones'],
                     writes=['d_grow'])
                def dn_head(h, sl):
                    bA, bB, bC = PS[1 + 3 * sl], PS[2 + 3 * sl], PS[3 + 3 * sl]
                    kA, kB, kC = f'PS{1 + 3 * sl}', f'PS{2 + 3 * sl}', f'PS{3 + 3 * sl}'
                    E1, E1s, TT, rr = dS_E1[sl], dS_E1s[sl], dS_TT[sl], dS_r[sl]
                    Pb, PTb = dS_P[sl], dS_PT[sl]
                    vn, vc, aT, o1 = dS_vn[sl], dS_vc[sl], dS_aT[sl], dS_o1[sl]
                    tA = tmpA[:, sl * 128:(sl + 1) * 128]
                    tB = tmpB[:, sl * 128:(sl + 1) * 128]
                    n = f'.{sl}'
                    p.op('pe', lambda e: e.matmul(bA[:, 0:128], lhsT=selh[:, h * 128:(h + 1) * 128], rhs=d_grow[:],
                                                  start=True, stop=True), reads=['k_selh', 'd_grow'], writes=[kA])
                    p.op('dve', lambda e: e.tensor_tensor(out=tA, in0=bA[:, 0:128], in1=negmask[:], op=ALU.add),
                         reads=[kA, 'k_negmask'], writes=['tmpA' + n])
                    p.op('act', lambda e: e.activation(out=E1[:], in_=tA, func=AF.Exp, bias=sm[:, 56 + h:57 + h]),
                         reads=['tmpA' + n, 'sm56'], writes=['d_E1' + n])
                    yield
                    p.op('pe', lambda e: e.matmul(bA[:, 128:256], lhsT=d_kT[:, h, :], rhs=d_kT[:, h, :], start=True, stop=True),
                         reads=['d_kT'], writes=[kA])
                    p.op('dve', lambda e: e.tensor_tensor(out=E1s[:], in0=E1[:], in1=tri_strict[:], op=ALU.mult),
                         reads=['d_E1' + n, 'k_tri_strict'], writes=['d_E1s' + n])
                    p.op('dve', lambda e: e.scalar_tensor_tensor(out=PTb[0][:], in0=bA[:, 128:256],
                                                                 scalar=tmpC[:, 492 + h:493 + h], in1=E1s[:],
                                                                 op0=ALU.mult, op1=ALU.mult),
                         reads=[kA, 'tmpC3', 'd_E1s' + n], writes=['d_PT0' + n])
                    yield
                    p.op('pe', lambda e: e.transpose(out=bB[:, 0:128], in_=PTb[0][:], identity=ident_f[:]),
                         reads=['d_PT0' + n, 'k_ident'], writes=[kB])
                    p.op('act', lambda e: e.copy(out=Pb[0][:], in_=bB[:, 0:128]), reads=[kB], writes=['d_P0' + n])
                    p.op('dve', lambda e: e.tensor_tensor(out=TT[:], in0=PTb[0][:], in1=ident_f[:], op=ALU.add),
                         reads=['d_PT0' + n, 'k_ident'], writes=['d_TT' + n])
                    yield
                    cur = 0
                    for lvl in range(6):
                        nxt = 1 - cur
                        p.op('pe', lambda e: e.matmul(bB[:, 0:128], lhsT=PTb[cur][:], rhs=Pb[cur][:], start=True, stop=True),
                             reads=[f'd_PT{cur}' + n, f'd_P{cur}' + n], writes=[kB])
                        if lvl < 5:
                            p.op('pe', lambda e: e.matmul(bB[:, 128:256], lhsT=Pb[cur][:], rhs=PTb[cur][:], start=True, stop=True),
                                 reads=[f'd_PT{cur}' + n, f'd_P{cur}' + n], writes=[kB])
                        p.op('act', lambda e: e.copy(out=Pb[nxt][:], in_=bB[:, 0:128]), reads=[kB], writes=[f'd_P{nxt}' + n])
                        if lvl < 5:
                            p.op('act', lambda e: e.copy(out=PTb[nxt][:], in_=bB[:, 128:256]), reads=[kB],
                                 writes=[f'd_PT{nxt}' + n])
                        yield
                        p.op('pe', lambda e: e.matmul(bC[:, 0:128], lhsT=Pb[nxt][:], rhs=TT[:], start=True, stop=True),
                             reads=[f'd_P{nxt}' + n, 'd_TT' + n], writes=[kC])
                        p.op('dve', lambda e: e.tensor_tensor(out=TT[:], in0=TT[:], in1=bC[:, 0:128], op=ALU.add),
                             reads=['d_TT' + n, kC], writes=['d_TT' + n])
                        yield
                        cur = nxt
                    p.op('pe', lambda e: e.matmul(bA[:, 256:384], lhsT=d_kT[:, h, :], rhs=d_Sb[:, l, h, :], start=True, stop=True),
                         reads=['d_kT', 'd_Sb'], writes=[kA])
                    p.op('dve', lambda e: e.scalar_tensor_tensor(out=rr[:], in0=bA[:, 256:384], scalar=tmpC[:, 496 + h:497 + h],
                                                                 in1=d_v[:, h, :], op0=ALU.mult, op1=ALU.add),
                         reads=[kA, 'tmpC4', 'd_v'], writes=['d_r' + n])
                    yield
                    p.op('pe', lambda e: e.matmul(bB[:, 256:384], lhsT=TT[:], rhs=rr[:], start=True, stop=True),
                         reads=['d_TT' + n, 'd_r' + n], writes=[kB])
                    p.op('dve', lambda e: e.tensor_scalar(out=vn[:], in0=bB[:, 256:384], scalar1=sm[:, 44 + h:45 + h],
                                                          scalar2=None, op0=ALU.mult), reads=[kB, 'sm44'], writes=['d_vn' + n])
                    p.op('dve', lambda e: e.tensor_scalar(out=vc[:], in0=bB[:, 256:384], scalar1=tmpC[:, 504 + h:505 + h],
                                                          scalar2=None, op0=ALU.mult), reads=[kB, 'tmpC6'], writes=['d_vc' + n])
                    p.op('pe', lambda e: e.matmul(bA[:, 384:512], lhsT=d_kT[:, h, :], rhs=d_qT[:, h, :], start=True, stop=True),
                         reads=['d_kT', 'd_qT'], writes=[kA])
                    p.op('dve', lambda e: e.tensor_tensor(out=aT[:], in0=bA[:, 384:512], in1=E1[:], op=ALU.mult),
                         reads=[kA, 'd_E1' + n], writes=['d_aT' + n])
                    yield
                    p.op('pe', lambda e: e.matmul(bC[:, 128:256], lhsT=d_qT[:, h, :], rhs=d_Sb[:, l, h, :], start=True, stop=True),
                         reads=['d_qT', 'd_Sb'], writes=[kC])
                    p.op('act', lambda e: e.activation(out=o1[:], in_=bC[:, 128:256], func=AF.Copy,
                                                       scale=tmpC[:, 508 + h:509 + h]), reads=[kC, 'tmpC7'], writes=['d_o1' + n])
                    yield
                    p.op('pe', lambda e: e.matmul(bC[:, 256:384], lhsT=aT[:], rhs=vn[:], start=True, stop=True),
                         reads=['d_aT' + n, 'd_vn' + n], writes=[kC])
                    p.op('dve', lambda e: e.tensor_tensor(out=tB, in0=bC[:, 256:384], in1=o1[:], op=ALU.add),
                         reads=[kC, 'd_o1' + n], writes=['tmpB' + n])
                    yield
                    p.op('pe', lambda e: e.matmul(bC[:, 384:512], lhsT=d_k[:, h, :], rhs=vc[:], start=True, stop=True),
                         reads=['d_k', 'd_vc' + n], writes=[kC])
                    p.op('dve', lambda e: e.scalar_tensor_tensor(out=d_S[:, l, h, :], in0=d_S[:, l, h, :],
                                                                 scalar=sm[:, 60 + h:61 + h], in1=bC[:, 384:512],
                                                                 op0=ALU.mult, op1=ALU.add),
                         reads=['d_S', 'sm60', kC], writes=['d_S'])
                    p.op('act', lambda e: e.copy(out=d_Sb[:, l, h, :], in_=d_S[:, l, h, :]), reads=['d_S'], writes=['d_Sb'])
                    yield
                    head_norm_gate(tB, 'tmpB' + n, [], dog[:, l, :], d_z[:, h * 128:(h + 1) * 128], 'd_z',
                                   obf[:, 3, h * 128:(h + 1) * 128], 64 + 4 * sl, sl)
                for pair in range(2):
                    interleave([dn_head(2 * pair, 0), dn_head(2 * pair + 1, 1)])

            for gi in range(8):
                wb, wk = wload(W_in[:, C_GATE + gi * 512:C_GATE + (gi + 1) * 512], 512, l, ti)
                pst, pk = (PS[0], 'PS0') if gi % 2 == 0 else (PS[1], 'PS1')
                proj_tm(wb, wk, 0, 512, pst[:], pk)
                p.op('act', (lambda e, gi=gi, pst=pst: e.activation(out=gates[:, gi * 512:(gi + 1) * 512], in_=pst[:],
                                                                    func=AF.Sigmoid)), reads=[pk], writes=['gates'])
            for half in range(2):
                for c8 in range(8):
                    cc = half * 8 + c8
                    p.op('pe', (lambda e, cc=cc, c8=c8: e.transpose(
                        out=PT[:, c8 * 128:(c8 + 1) * 128],
                        in_=obf[:].rearrange("p a b -> p (a b)")[:, cc * 128:(cc + 1) * 128], identity=ident_b[:])),
                        reads=['obf', 'k_ident'], writes=['PT'], inc=(c8 == 7))
                p.op('act', (lambda e, half=half: e.copy(out=oT[:, half * 8:(half + 1) * 8, :].rearrange("p a b -> p (a b)"),
                                                         in_=PT[:])), reads=['PT'], writes=['oT'])
            for n in range(4):
                wv, wk = wload_rows(prm['w_branch'][l, n], l, ti)
                for half in range(2):
                    pst, pk = (PS[2], 'PS2') if half == 0 else (PS[3], 'PS3')
                    for wc in range(4):
                        p.op('pe', (lambda e, n=n, wc=wc, half=half, pst=pst: e.matmul(
                            pst[:], lhsT=oT[:, n * 4 + wc, :], rhs=wv[:, wc, half * 512:(half + 1) * 512], start=(wc == 0),
                            stop=(wc == 3))), reads=['oT', wk], writes=[pk], inc=(wc == 3))
                    if n == 0:
                        p.op('dve', (lambda e, n=n, half=half, pst=pst: e.tensor_tensor(
                            out=merged[:, half * 512:(half + 1) * 512], in0=pst[:],
                            in1=gates[:, n * 1024 + half * 512:n * 1024 + (half + 1) * 512], op=ALU.mult)),
                            reads=[pk, 'gates'], writes=['merged'])
                    else:
                        p.op('dve', (lambda e, n=n, half=half, pst=pst: e.tensor_tensor(
                            out=tmpA[:], in0=pst[:], in1=gates[:, n * 1024 + half * 512:n * 1024 + (half + 1) * 512],
                            op=ALU.mult)), reads=[pk, 'gates'], writes=['tmpA'])
                        p.op('dve', (lambda e, half=half: e.tensor_tensor(
                            out=merged[:, half * 512:(half + 1) * 512], in0=merged[:, half * 512:(half + 1) * 512], in1=tmpA[:],
                            op=ALU.add)), reads=['tmpA', 'merged'], writes=['merged'])
            p.op('act', lambda e: e.copy(out=mbf[:], in_=merged[:]), reads=['merged'], writes=['mbf'])
            for kc in range(8):
                p.op('pe', (lambda e, kc=kc: e.transpose(out=PT[:, kc * 128:(kc + 1) * 128], in_=mbf[:, kc * 128:(kc + 1) * 128],
                                                         identity=ident_b[:])), reads=['mbf', 'k_ident'], writes=['PT'],
                     inc=(kc == 7))
            p.op('act', lambda e: e.copy(out=mT[:].rearrange("p a b -> p (a b)"), in_=PT[:]), reads=['PT'], writes=['mT'])
            for half in range(2):
                wb, wk = wload(prm['w_out'][l][:, half * 512:(half + 1) * 512], 512, l, ti)
                pst, pk = (PS[0], 'PS0') if half == 0 else (PS[1], 'PS1')
                for kc in range(8):
                    p.op('pe', (lambda e, kc=kc, pst=pst, wb=wb: e.matmul(pst[:], lhsT=mT[:, kc, :], rhs=wb[:, kc, 0:512],
                                                                          start=(kc == 0), stop=(kc == 7))),
                         reads=['mT', wk], writes=[pk], inc=(kc == 7))
                p.op('dve', (lambda e, half=half, pst=pst: e.tensor_tensor(out=xt[:, half * 512:(half + 1) * 512],
                                                                           in0=xt[:, half * 512:(half + 1) * 512], in1=pst[:],
                                                                           op=ALU.add)), reads=[pk, 'xt'], writes=['xt'])
            rmsnorm_to_T(xt, gmlp, hT, 'hT', l)
            for fi in range(8):
                wb, wk = wload(prm['w_up'][l][:, fi * 512:(fi + 1) * 512], 512, l, ti)
                pst, pk = (PS[2], 'PS2') if fi % 2 == 0 else (PS[3], 'PS3')
                for f4 in range(4):
                    proj_fm(wb, wk, f4 * 128, 128, pst[:, f4 * 128:(f4 + 1) * 128], pk)
                p.op('act', (lambda e, pst=pst: e.activation(out=tmpB[:], in_=pst[:], func=AF.Relu)), reads=[pk], writes=['tmpB'])
                p.op('dve', (lambda e, fi=fi, pst=pst: e.tensor_tensor(
                    out=uT[:, fi * 4:(fi + 1) * 4, :].rearrange("p a b -> p (a b)"), in0=tmpB[:], in1=pst[:], op=ALU.mult)),
                    reads=['tmpB', pk], writes=['uT'])
            for fi in range(8):
                wv, wk = wload_rows(prm['w_down'][l][fi * 512:(fi + 1) * 512, :], l, ti)
                for half in range(2):
                    pk = 'PS0' if half == 0 else 'PS1'
                    pst = PS[0] if half == 0 else PS[1]
                    for f4 in range(4):
                        p.op('pe', (lambda e, fi=fi, f4=f4, half=half, pst=pst, wv=wv: e.matmul(
                            pst[:], lhsT=uT[:, fi * 4 + f4, :], rhs=wv[:, f4, half * 512:(half + 1) * 512],
                            start=(fi == 0 and f4 == 0), stop=(fi == 7 and f4 == 3))),
                            reads=['uT', wk], writes=[pk], inc=(f4 == 3))
            for half in range(2):
                pk = 'PS0' if half == 0 else 'PS1'
                pst = PS[0] if half == 0 else PS[1]
                p.op('dve', (lambda e, half=half, pst=pst: e.tensor_tensor(out=xt[:, half * 512:(half + 1) * 512],
                                                                           in0=xt[:, half * 512:(half + 1) * 512], in1=pst[:],
                                                                           op=ALU.add)), reads=[pk, 'xt'], writes=['xt'])
        if pipe:
            to = max(ti - 1, 0)
            p.dma('sp', y_out[to * 128:(to + 1) * 128, :], xt[:], reads=['xt'], writes=['yout'])
            p.op('act', lambda e: e.activation(out=xh[:], in_=xt[:], func=AF.Copy, scale=role[:, 0:1]),
                 reads=['xt', 'role'], writes=['xh'])
            p.dma('sp', sendb[(ti + 1) % 2], xh[:], reads=['xh'], writes=[('sendb', (ti + 1) % 2)])
        else:
            p.dma('sp', y_out[ti * 128:(ti + 1) * 128, :], xt[:], reads=['xt'], writes=['yout'])
    p.final_wait('sp', ['yout'])
    p.emit()
    return nc


def host_params(inputs):
    f = lambda k: np.ascontiguousarray(np.asarray(inputs[k], dtype=np.float32))
    m = {}
    for k in ['norm_mix_g', 'w_in', 'hgrn_out_g', 'mlstm_out_g', 'attn_sinks', 'rel_bias_table', 'dn_a_log', 'dn_dt_bias',
              'dn_out_g', 'w_branch', 'w_out', 'norm_mlp_g', 'w_up', 'w_down']:
        m[k] = f(k)
    m['mlstm_if_bias'] = f('mlstm_if_bias').reshape(2, 8)
    m['hgrn_lb_table'] = np.ascontiguousarray(f('hgrn_lb_table').reshape(2, 4, 128).transpose(0, 2, 1))
    m['attn_q_norm_g'] = f('attn_q_norm_g').reshape(2, 64, 1)
    m['attn_k_norm_g'] = f('attn_k_norm_g').reshape(2, 64, 1)
    m['dn_conv_w'] = np.ascontiguousarray(f('dn_conv_w').reshape(2, 4, 12, 128).transpose(0, 1, 3, 2))
    m['dn_alog_row'] = np.ascontiguousarray(f('dn_a_log').T)
    m['dn_dtb_row'] = np.ascontiguousarray(f('dn_dt_bias').T)
    return m


LAYERED = ['norm_mix_g', 'w_in', 'hgrn_out_g', 'mlstm_out_g', 'attn_sinks', 'dn_a_log', 'dn_dt_bias', 'dn_out_g',
           'w_branch', 'w_out', 'norm_mlp_g', 'w_up', 'w_down', 'mlstm_if_bias', 'attn_q_norm_g', 'attn_k_norm_g',
           'dn_conv_w']


def kernel(**inputs):
    x = np.ascontiguousarray(np.asarray(inputs['x'], dtype=np.float32))
    B, T, _ = x.shape
    NT = T // 128
    nc = build(T, 1, pipe=True, npairs=B)
    consts = host_consts()
    hp = host_params(inputs)
    hp_role = []
    for r in range(2):
        m = dict(hp)
        if r == 1:
            for k in LAYERED:
                m[k] = np.ascontiguousarray(hp[k][::-1])
            m['dn_alog_row'] = np.ascontiguousarray(hp['dn_alog_row'][:, ::-1])
            m['dn_dtb_row'] = np.ascontiguousarray(hp['dn_dtb_row'][:, ::-1])
        hp_role.append(m)
    zeros_x = np.zeros((T, D), np.float32)
    in_maps = []
    for b in range(B):
        for r in range(2):
            m = {'x': x[b] if r == 0 else zeros_x}
            m.update(hp_role[r])
            for k, v in consts.items():
                m['c_' + k] = v
            role = np.zeros((128, 2), np.float32)
            role[:, r] = 1.0
            pf = np.ones((128, NT + 1), np.float32)
            pf[:, 0:1 + r] = 0.0
            m['role'] = role
            m['pflag'] = pf
            in_maps.append(m)
    res = run_bass_kernel_spmd(nc, in_maps, core_ids=list(range(2 * B)))
    return np.stack([np.asarray(res.results[2 * b + 1]['y'], dtype=np.float32) for b in range(B)], axis=0)
```

```python
import types
import numpy as np
from contextlib import ExitStack
import concourse.bass as bass
import concourse.mybir as mybir
from concourse.bass_utils import run_bass_kernel_spmd

F32 = mybir.dt.float32
BF16 = mybir.dt.bfloat16
AF = mybir.ActivationFunctionType
ALU = mybir.AluOpType
AX = mybir.AxisListType
ENGS = ['pe', 'act', 'dve', 'pool', 'sp']

D = 1024
NIN = 10512
EPS = 1e-6


def _freeze(fn):
    if fn is None or fn.__closure__ is None:
        return fn
    cells = []
    for c in fn.__closure__:
        try:
            cells.append(types.CellType(c.cell_contents))
        except ValueError:
            cells.append(c)
    return types.FunctionType(fn.__code__, fn.__globals__, fn.__name__, fn.__defaults__, tuple(cells))


class V3:
    def __init__(self, ap):
        self.ap = ap

    def __getitem__(self, k):
        return self.ap[k]


class Prog:
    def __init__(self, nc, nd=8):
        self.nc = nc
        self.ND = nd
        self.lists = {e: [] for e in ENGS}
        self.cnt = {e: 0 for e in ENGS}
        self.waited = {e: {} for e in ENGS}
        self.W = {}
        self.Rd = {}
        self.dma_n = {e: 0 for e in ENGS}
        self.semkeys = set()
        self.st = ExitStack()
        self.ntens = 0
        self.alias = {}

    def sb(self, shape, dt, name=None):
        self.ntens += 1
        return self.st.enter_context(self.nc.sbuf_tensor(name or f"t{self.ntens}", list(shape), dt))

    def ps(self, shape, dt=F32, name=None):
        self.ntens += 1
        return self.st.enter_context(self.nc.psum_tensor(name or f"p{self.ntens}", list(shape), dt))

    def _deps(self, eng, reads, writes):
        deps = {}

        def add(d):
            for sk, v in d.items():
                if deps.get(sk, 0) < v:
                    deps[sk] = v
        for r in reads:
            add(self.W.get(r, {}))
        for w in writes:
            add(self.W.get(w, {}))
            add(self.Rd.get(w, {}))
        out = []
        for sk, v in deps.items():
            if sk == ('c', 'pe') and eng == 'pe':
                continue
            if self.waited[eng].get(sk, 0) >= v:
                continue
            self.waited[eng][sk] = v
            out.append((sk, v))
        return out

    def _rec(self, tok, reads, writes):
        sk, v = tok
        for r in reads:
            d = self.Rd.setdefault(r, {})
            d[sk] = max(d.get(sk, 0), v)
        for w in writes:
            d = self.W.setdefault(w, {})
            d[sk] = max(d.get(sk, 0), v)

    def _x(self, keys):
        out = []
        for k in keys:
            if isinstance(k, (list, tuple)):
                out.extend(self._x(k))
            elif k in self.alias:
                out.extend(self.alias[k])
            else:
                out.append(k)
        return out

    def op(self, eng, fn, reads=(), writes=(), inc=True):
        reads = self._x(reads)
        writes = self._x(writes)
        waits = self._deps(eng, reads, writes)
        sk = ('c', eng)
        tok = (sk, self.cnt[eng] + 1)
        if inc:
            self.cnt[eng] += 1
        self.semkeys.add(sk)
        self.lists[eng].append((waits, _freeze(fn), sk if inc else None, 1))
        self._rec(tok, reads, writes)

    def dma(self, q, out, in_, reads=(), writes=(), **kw):
        reads = self._x(reads)
        writes = self._x(writes)
        j = self.dma_n[q]
        self.dma_n[q] += 1
        slot = j % self.ND
        val = 16 * (j // self.ND + 1)
        sk = ('d', q, slot)
        self.semkeys.add(sk)
        waits = self._deps(q, reads, writes)
        if j >= self.ND and self.waited[q].get(sk, 0) < val - 16:
            self.waited[q][sk] = val - 16
            waits.append((sk, val - 16))
        self.lists[q].append((waits, (lambda e, o=out, i=in_, k=kw: e.dma_start(out=o, in_=i, **k)), sk, 16))
        self._rec((sk, val), reads, writes)

    def coll(self, kind, op, groups, ins_ap, outs_ap, reads=(), writes=()):
        reads = self._x(reads)
        writes = self._x(writes)
        waits = self._deps('pool', reads, writes)
        sk = ('cc',)
        self.cc_n = getattr(self, 'cc_n', 0) + 1
        self.semkeys.add(sk)
        self.lists['pool'].append((waits, (lambda e: e.collective_compute(kind, op, replica_groups=groups, ins=[ins_ap.opt()],
                                                                          outs=[outs_ap.opt()])), sk, 1))
        self._rec((sk, self.cc_n), reads, writes)

    def final_wait(self, eng, keys):
        keys = self._x(keys)
        waits = self._deps(eng, keys, ())
        self.lists[eng].append((waits, None, None, 0))

    def emit(self):
        nc = self.nc
        st = self.st
        sems = {}
        for sk in sorted(self.semkeys, key=str):
            sems[sk] = st.enter_context(nc.semaphore("s_" + "_".join(map(str, sk))))
        block = st.enter_context(nc.Block())
        lists = self.lists

        def run(name, e):
            for waits, fn, sk, incv in lists[name]:
                for wsk, v in waits:
                    e.wait_ge(sems[wsk], v)
                if fn is None:
                    continue
                ins = fn(e)
                if sk is not None:
                    ins.then_inc(sems[sk], incv)

        @block.tensor
        def _(e):
            run('pe', e)

        @block.scalar
        def _(e):
            run('act', e)

        @block.vector
        def _(e):
            run('dve', e)

        @block.gpsimd
        def _(e):
            run('pool', e)

        @block.sync
        def _(e):
            run('sp', e)
        st.close()


def _t5_bucket_np(n):
    max_exact = 16
    nf = np.maximum(n, max_exact).astype(np.float32)
    large = max_exact + (np.log(nf / max_exact) / np.log(np.float32(128 / max_exact)) * 16).astype(np.int32)
    large = np.minimum(large, 31)
    return np.where(n < max_exact, n, large)


def host_consts():
    c = {}
    s = np.arange(128)[:, None]
    t = np.arange(128)[None, :]
    c['tri_incl'] = (s <= t).astype(np.float32)
    c['tri_strict'] = (s < t).astype(np.float32)
    c['hgmask'] = ((s <= t) & (s // 32 == t // 32)).astype(np.float32)
    c['negmask'] = np.where(s <= t, 0.0, -1e30).astype(np.float32)
    c['ident'] = np.eye(128, dtype=np.float32)
    sel = np.zeros((128, 128), np.float32)
    sel[127, :] = 1.0
    c['sel_last'] = sel
    rm = np.zeros((128, 4), np.float32)
    for j in range(4):
        rm[32 * j:32 * j + 32, j] = 1.0
    c['rowm'] = rm
    rs = np.ones((128, 128), np.float32)
    rs[:, 0::32] = 0.0
    c['resetm'] = rs
    selh = np.zeros((4, 4, 128), np.float32)
    for h in range(4):
        selh[h, h, :] = 1.0
    c['selh'] = selh.reshape(4, 512)
    c['ones'] = np.ones((128, 128), np.float32)
    bk = _t5_bucket_np(np.arange(128))
    oh = np.zeros((32, 128), np.float32)
    oh[bk, np.arange(128)] = 1.0
    c['bias_oh'] = oh
    ab = np.zeros((128, 384), np.float32)
    for dd in range(128):
        ab[dd, 255 - dd] = 1.0
    c['antiband'] = ab
    return c


CONST_SHAPES = {
    'tri_incl': (128, 128), 'tri_strict': (128, 128), 'hgmask': (128, 128), 'negmask': (128, 128),
    'ident': (128, 128), 'sel_last': (128, 128), 'rowm': (128, 4), 'resetm': (128, 128),
    'selh': (4, 512), 'ones': (128, 128), 'bias_oh': (32, 128), 'antiband': (128, 384),
}

PARAM_SHAPES = {
    'norm_mix_g': (2, 1024), 'w_in': (2, 1024, NIN), 'hgrn_lb_table': (2, 128, 4), 'hgrn_out_g': (2, 512),
    'mlstm_if_bias': (2, 8), 'mlstm_out_g': (2, 512), 'attn_q_norm_g': (2, 64, 1), 'attn_k_norm_g': (2, 64, 1),
    'attn_sinks': (2, 8), 'rel_bias_table': (32, 8), 'dn_conv_w': (2, 4, 128, 12), 'dn_a_log': (2, 4),
    'dn_dt_bias': (2, 4), 'dn_alog_row': (4, 2), 'dn_dtb_row': (4, 2), 'dn_out_g': (2, 128), 'w_branch': (2, 4, 512, 1024), 'w_out': (2, 1024, 1024),
    'norm_mlp_g': (2, 1024), 'w_up': (2, 1024, 4096), 'w_down': (2, 4096, 1024),
}

C_HQ, C_HF, C_HI, C_HG = 0, 512, 1024, 1536
C_MQ, C_MK, C_MV, C_MI, C_MF, C_MO = 2048, 2304, 2560, 3072, 3076, 3080
C_AQ, C_AK, C_AV = 3592, 4104, 4232
C_DQKV, C_DB, C_DA, C_DZ = 4360, 5896, 5900, 5904
C_GATE = 6416


def build(T, L=2, enable=(1, 1, 1, 1), dbg=None, pipe=False, npairs=4):
    nc = bass.Bass("TRN2", target_bir_lowering=False)
    NTILES = T // 128
    x_in = nc.dram_tensor("x", [T, D], F32, kind="ExternalInput").ap()
    y_out = nc.dram_tensor("y", [T, D], F32, kind="ExternalOutput").ap()
    prm = {k: nc.dram_tensor(k, list(s), F32, kind="ExternalInput").ap() for k, s in PARAM_SHAPES.items()}
    cst = {k: nc.dram_tensor("c_" + k, list(s), F32, kind="ExternalInput").ap() for k, s in CONST_SHAPES.items()}
    NIT = NTILES + 1 if pipe else NTILES
    if pipe:
        assert L == 1
        role_in = nc.dram_tensor("role", [128, 2], F32, kind="ExternalInput").ap()
        pflag_in = nc.dram_tensor("pflag", [128, NIT], F32, kind="ExternalInput").ap()
        sendb = nc.dram_tensor("sendb", [2, 128, 1024], F32).ap()
        recvb = nc.dram_tensor("recvb", [2, 128, 1024], F32).ap()
    dbg_out = {}
    p = Prog(nc)
    sb, ps = p.sb, p.ps

    def load_const(name, dt=F32, q='sp'):
        shp = CONST_SHAPES[name]
        t_ = sb(shp, dt, "k_" + name + ("_b" if dt == BF16 else ""))
        p.dma(q, t_[:], cst[name], writes=['k_' + name])
        return t_
    tri_incl = load_const('tri_incl')
    tri_strict = load_const('tri_strict')
    hgmask = load_const('hgmask')
    negmask = load_const('negmask')
    ident_f = load_const('ident')
    ident_b = load_const('ident', BF16, 'pool')
    sel_last = load_const('sel_last')
    rowm = load_const('rowm')
    resetm = load_const('resetm')
    selh = load_const('selh')
    ones_f = load_const('ones')
    ones_b = load_const('ones', BF16, 'pool')
    KC = ['k_tri_incl', 'k_tri_strict', 'k_hgmask', 'k_negmask', 'k_ident', 'k_sel_last', 'k_rowm', 'k_resetm',
          'k_selh', 'k_ones']

    gmix = sb([128, L, 1024], BF16, "gmix")
    gmlp = sb([128, L, 1024], BF16, "gmlp")
    hog = sb([128, L, 512], BF16, "hog")
    mog = sb([128, L, 512], BF16, "mog")
    dog = sb([128, L, 128], F32, "dog")
    mifb = sb([128, L, 8], F32, "mifb")
    sinks = sb([128, L, 8], F32, "sinks")
    esink = sb([128, L, 8], F32, "esink")
    alog = sb([128, L, 4], F32, "alog")
    nexpa = sb([128, L, 4], F32, "nexpa")
    dtb = sb([128, L, 4], F32, "dtb")
    lbt = sb([128, 2, 4], F32, "lbt")
    lb = sb([128, 2, 4], F32, "lb")
    oml = sb([128, 2, 4], F32, "oml")
    convw = sb([128, L, 4, 12], F32, "convw")
    qkg = sb([64, L, 2], F32, "qkg")
    gk8 = sb([64, L], F32, "gk8")
    dtbrow = sb([4, 2], F32, "dtbrow")
    nexparow = sb([4, 2], F32, "nexparow")
    for l in range(L):
        p.dma('pool', gmix[:, l, :], prm['norm_mix_g'][l:l + 1, :].partition_broadcast(128), writes=['prm'])
        p.dma('pool', gmlp[:, l, :], prm['norm_mlp_g'][l:l + 1, :].partition_broadcast(128), writes=['prm'])
        p.dma('pool', hog[:, l, :], prm['hgrn_out_g'][l:l + 1, :].partition_broadcast(128), writes=['prm'])
        p.dma('pool', mog[:, l, :], prm['mlstm_out_g'][l:l + 1, :].partition_broadcast(128), writes=['prm'])
        p.dma('sp', dog[:, l, :], prm['dn_out_g'][l:l + 1, :].partition_broadcast(128), writes=['prm'])
        p.dma('sp', mifb[:, l, :], prm['mlstm_if_bias'][l:l + 1, :].partition_broadcast(128), writes=['prm'])
        p.dma('sp', sinks[:, l, :], prm['attn_sinks'][l:l + 1, :].partition_broadcast(128), writes=['prm'])
        p.dma('sp', alog[:, l, :], prm['dn_a_log'][l:l + 1, :].partition_broadcast(128), writes=['prm'])
        p.dma('sp', dtb[:, l, :], prm['dn_dt_bias'][l:l + 1, :].partition_broadcast(128), writes=['prm'])
        for j in range(4):
            p.dma('sp', convw[:, l, j, :], prm['dn_conv_w'][l, j], writes=['prm'])
        p.dma('sp', qkg[:, l, 0:1], prm['attn_q_norm_g'][l], writes=['prm'])
        p.dma('sp', qkg[:, l, 1:2], prm['attn_k_norm_g'][l], writes=['prm'])
    for l2 in range(2):
        p.dma('sp', lbt[:, l2, :], prm['hgrn_lb_table'][l2], writes=['prm'])
    p.dma('sp', dtbrow[:], prm['dn_dtb_row'], writes=['dtbrow'])
    p.dma('sp', nexparow[:], prm['dn_alog_row'], writes=['nexparow'])
    p.op('act', lambda e: e.activation(out=nexparow[:], in_=nexparow[:], func=AF.Exp), reads=['nexparow'], writes=['nexparow'])
    p.op('dve', lambda e: e.tensor_scalar(out=nexparow[:], in0=nexparow[:], scalar1=-1.0, scalar2=None, op0=ALU.mult),
         reads=['nexparow'], writes=['nexparow'])
    p.op('dve', lambda e: e.memset(lb[:], 0.0), writes=['lb'])
    p.op('dve', lambda e: e.tensor_sub(out=lb[:, 1, :], in0=lbt[:, 1, :], in1=lbt[:, 0, :]), reads=['prm', 'lb'],
         writes=['lb'])
    p.op('act', lambda e: e.activation(out=lb[:, 1, :], in_=lb[:, 1, :], func=AF.Sigmoid), reads=['lb'], writes=['lb'])
    if pipe:
        role = sb([128, 2], F32, "role_sb")
        pflag = sb([128, NIT], F32, "pflag_sb")
        p.dma('sp', role[:], role_in, writes=['role'])
        p.dma('sp', pflag[:], pflag_in, writes=['pflag'])
        p.op('dve', lambda e: e.tensor_scalar(out=lb[:, 0, :], in0=lb[:, 1, :], scalar1=role[:, 1:2], scalar2=None,
                                              op0=ALU.mult), reads=['lb', 'role'], writes=['lb'])
    p.op('dve', lambda e: e.tensor_scalar(out=oml[:], in0=lb[:], scalar1=-1.0, scalar2=1.0, op0=ALU.mult, op1=ALU.add),
         reads=['lb'], writes=['oml'])
    p.op('act', lambda e: e.activation(out=esink[:], in_=sinks[:], func=AF.Exp), reads=['prm'], writes=['esink'])
    p.op('act', lambda e: e.activation(out=nexpa[:], in_=alog[:], func=AF.Exp), reads=['prm'], writes=['nexpa'])
    p.op('dve', lambda e: e.tensor_scalar(out=nexpa[:], in0=nexpa[:], scalar1=-1.0, scalar2=None, op0=ALU.mult),
         reads=['nexpa'], writes=['nexpa'])
    p.op('dve', lambda e: e.tensor_tensor(out=gk8[:], in0=qkg[:, :, 0], in1=qkg[:, :, 1], op=ALU.mult), reads=['prm'],
         writes=['gk8'])
    p.op('dve', lambda e: e.tensor_scalar(out=gk8[:], in0=gk8[:], scalar1=0.125, scalar2=None, op0=ALU.mult),
         reads=['gk8'], writes=['gk8'])

    PS = [ps([128, 512], F32, f"PS{i}") for i in range(7)]
    PT = ps([128, 1024], BF16, "PSTR")

    def K(i, lo=0, hi=512):
        return [f'PS{i}.bank']
    for i in range(7):
        p.alias[f'PS{i}'] = K(i)
    p.alias['PS5b'] = K(5, 128, 256)
    p.alias['PS5c'] = K(5, 256, 384)
    p.alias['PS6b'] = K(6, 128, 256)
    p.alias['PS6c'] = K(6, 256, 384)
    p.alias['PS6d'] = K(6, 384, 512)
    for i in range(5):
        p.alias[f'scr{i}'] = [f'scr.{i}']
    for nm in ['tmpA', 'tmpB', 'junk']:
        p.alias[nm] = [f'{nm}.{i}' for i in range(4)]
    p.alias['h_S'] = [f'h_S.{i}' for i in range(4)]
    p.alias['h_Sb'] = [f'h_Sb.{i}' for i in range(4)]
    p.alias['h_qj'] = [f'h_qj.{i}' for i in range(4)]
    p.alias['m_C'] = [f'm_C.{i}' for i in range(4)]
    p.alias['m_Cb'] = [f'm_Cb.{i}' for i in range(4)]
    p.alias['h_q'] = ['scr.0']
    p.alias['h_f'] = ['scr.1']
    p.alias['h_k'] = ['scr.2']
    p.alias['h_cum'] = ['scr.3']
    p.alias['h_e'] = ['scr.4']
    p.alias['a_z'] = ['scr.0', 'scr.1', 'scr.2']
    p.alias['a_sq'] = ['scr.2', 'scr.3', 'scr.4']
    p.alias['d_y'] = ['scr.0', 'scr.1', 'scr.2']
    EB = sb([128, 2, 8, 128], F32, "EB")
    relt = sb([32, 8], F32, "relt")
    boh = sb([32, 128], F32, "boh")
    aband = sb([128, 384], F32, "aband")
    vecE = sb([128, 8], F32, "vecE")
    p.dma('sp', relt[:], prm['rel_bias_table'], writes=['relt'])
    p.dma('sp', boh[:], cst['bias_oh'], writes=['boh'])
    p.dma('sp', aband[:], cst['antiband'], writes=['aband'])
    p.op('pe', lambda e: e.matmul(PS[0][:, 0:8], lhsT=boh[:], rhs=relt[:], start=True, stop=True), reads=['boh', 'relt'],
         writes=K(0))
    p.op('act', lambda e: e.activation(out=vecE[:], in_=PS[0][:, 0:8], func=AF.Exp), reads=K(0), writes=['vecE'])
    for blk in range(2):
        for t0 in range(0, 128, 64):
            pst = PS[1]
            for tt in range(64):
                off = 255 - (t0 + tt + (128 if blk == 0 else 0))
                p.op('pe', (lambda e, off=off, tt=tt: e.matmul(pst[:, tt * 8:(tt + 1) * 8], lhsT=aband[:, off:off + 128],
                                                               rhs=vecE[:], start=True, stop=True)),
                     reads=['aband', 'vecE'], writes=K(1), inc=(tt == 63))
            p.op('dve', (lambda e, blk=blk, t0=t0: e.tensor_copy(
                out=EB[:, blk, :, t0:t0 + 64], in_=pst[:].rearrange("p (t g) -> p g t", g=8))),
                reads=K(1), writes=['EB'])

    xt = sb([128, 1024], F32, "xt")
    hbf = sb([128, 1024], BF16, "hbf")
    hT = sb([128, 8, 128], BF16, "hT")
    NWB = 6
    wbuf = [sb([128, 8, 520], BF16, f"wbuf{i}") for i in range(NWB)]
    wb_n = [0]
    sm = sb([128, 128], F32, "small")
    obf = sb([128, 4, 512], BF16, "obf")
    oT = sb([128, 16, 128], BF16, "oT")
    gates = sb([128, 4096], BF16, "gates")
    merged = sb([128, 1024], F32, "merged")
    mbf = sb([128, 1024], BF16, "mbf")
    mT = sb([128, 8, 128], BF16, "mT")
    uT = sb([128, 32, 128], BF16, "uT")
    tmpA = sb([128, 512], F32, "tmpA")
    tmpB = sb([128, 512], F32, "tmpB")
    tmpC = sb([128, 512], F32, "tmpC")
    junk = sb([128, 1024], BF16, "junk")

    scr = sb([128, 2560], F32, "scr")

    class V:
        def __init__(self, ap):
            self.ap = ap

        def __getitem__(self, k):
            return self.ap[k]
    h_q = V(scr[:, 0:512].rearrange("p (a b) -> p a b", a=4))
    h_f = V(scr[:, 512:1024].rearrange("p (a b) -> p a b", a=4))
    h_k = V(scr[:, 1024:1536].rearrange("p (a b) -> p a b", a=4))
    h_cum = V(scr[:, 1536:2048].rearrange("p (a b) -> p a b", a=4))
    h_e = V(scr[:, 2048:2560].rearrange("p (a b) -> p a b", a=4))
    h_qp = sb([128, 4, 128], BF16, "h_qp")
    h_kp = sb([128, 4, 128], BF16, "h_kp")
    h_kpp = sb([128, 4, 128], BF16, "h_kpp")
    h_edec = sb([128, 4, 4], F32, "h_edec")
    h_v = sb([128, 512], BF16, "h_v")
    h_g = sb([128, 512], F32, "h_g")
    hS_AT = [sb([128, 128], BF16, f"h_AT{i}") for i in range(2)]
    hS_kj = [sb([128, 4, 128], BF16, f"h_kj{i}") for i in range(2)]
    h_qj = sb([128, 4, 4, 128], BF16, "h_qj")
    h_S = [sb([128, L, 4, 128], F32, "h_S")]
    h_Sb = sb([128, L, 4, 128], BF16, "h_Sb")

    m_q = sb([64, 4, 128], BF16, "m_q")
    m_kT = sb([64, 4, 128], BF16, "m_kT")
    m_k = sb([128, 256], BF16, "m_k")
    m_v = sb([128, 512], F32, "m_v")
    m_if = sb([128, 8], F32, "m_if")
    m_o = sb([128, 512], F32, "m_o")
    m_va = sb([128, 4, 130], BF16, "m_va")
    mS_AT = [sb([128, 128], BF16, f"m_AT{i}") for i in range(2)]
    m_C = sb([64, L, 4, 130], F32, "m_C")
    m_Cb = sb([64, L, 4, 130], BF16, "m_Cb")

    a_q = sb([64, 8, 128], BF16, "a_q")
    a_z = V(scr[0:64, 0:1280].rearrange("p (a b) -> p a b", a=10))
    a_sq = V(scr[0:64, 1280:2560].rearrange("p (a b) -> p a b", a=10))
    a_kT = sb([64, L, 2, 2, 128], BF16, "a_kT")
    a_v = sb([128, L, 2, 2, 66], BF16, "a_v")
    aS_P = [sb([128, 256], F32, f"a_P{i}") for i in range(2)]
    aS_PT = [sb([128, 2, 128], BF16, f"a_PT{i}") for i in range(2)]

    d_x = sb([128, L, 12, 132], BF16, "d_x")
    d_y = V(scr[:, 0:1536].rearrange("p (a b) -> p a b", a=12))
    d_sq = sb([128, 128], BF16, "d_sq")
    d_qT = sb([128, 4, 128], BF16, "d_qT")
    d_kT = sb([128, 4, 128], BF16, "d_kT")
    d_vT = sb([128, 4, 128], BF16, "d_vT")
    d_v = sb([128, 4, 128], F32, "d_v")
    d_k = sb([128, 4, 128], BF16, "d_k")
    d_ba = sb([128, 8], F32, "d_ba")
    d_arow = sb([4, 128], F32, "d_arow")
    d_grow = sb([4, 128], F32, "d_grow")
    d_z = sb([128, 512], F32, "d_z")
    dS_E1 = [sb([128, 128], F32, f"d_E1_{i}") for i in range(2)]
    dS_E1s = [sb([128, 128], F32, f"d_E1s_{i}") for i in range(2)]
    dS_P = [[sb([128, 128], F32, f"d_P{j}_{i}") for j in range(2)] for i in range(2)]
    dS_PT = [[sb([128, 128], F32, f"d_PT{j}_{i}") for j in range(2)] for i in range(2)]
    dS_TT = [sb([128, 128], F32, f"d_TT_{i}") for i in range(2)]
    dS_r = [sb([128, 128], F32, f"d_r_{i}") for i in range(2)]
    dS_vn = [sb([128, 128], BF16, f"d_vn_{i}") for i in range(2)]
    dS_vc = [sb([128, 128], BF16, f"d_vc_{i}") for i in range(2)]
    dS_aT = [sb([128, 128], BF16, f"d_aT_{i}") for i in range(2)]
    dS_o1 = [sb([128, 128], F32, f"d_o1_{i}") for i in range(2)]
    d_S = sb([128, L, 4, 128], F32, "d_S")
    d_Sb = sb([128, L, 4, 128], BF16, "d_Sb")

    for (t_, k) in [(h_S[0], 'h_S'), (h_Sb, 'h_Sb'), (m_C, 'm_C'), (m_Cb, 'm_Cb'), (d_S, 'd_S'), (d_Sb, 'd_Sb'),
                    (d_x, 'd_x'), (h_qj, 'h_qj'), (a_kT, 'a_kT'), (a_v, 'a_v')]:
        p.op('pool', (lambda e, t_=t_: e.memset(t_[:], 0.0)), writes=[k])

    NPIECE = 43
    wscr = nc.dram_tensor("wscr", [L, NPIECE, 128, 4160], BF16, kind="Internal").ap()
    piece_ctr = {}

    def _piece(l_, ti_):
        k = (l_, ti_)
        i = piece_ctr.get(k, 0)
        piece_ctr[k] = i + 1
        assert i < NPIECE
        return i

    def wload(src_ap, ncol, l_, ti_):
        pi = _piece(l_, ti_)
        scr_ap = wscr[l_, pi, :, 0:8 * ncol]
        skey = ('wscr', l_, pi)
        if ti_ == 0:
            p.dma('pool', scr_ap.rearrange("p (kc n) -> p kc n", kc=8), src_ap.rearrange("(kc p) n -> p kc n", p=128),
                  writes=[skey])
        i = wb_n[0] % NWB
        wb_n[0] += 1
        key = f'wbuf{i}'
        dst = wbuf[i][:].rearrange("p a b -> p (a b)")[:, 0:8 * ncol]
        p.dma('sp', dst, scr_ap, reads=[skey], writes=[key])
        return V3(dst.rearrange("p (kc n) -> p kc n", kc=8)), key

    def wload_rows(src_ap, l_, ti_):
        pi = _piece(l_, ti_)
        scr_ap = wscr[l_, pi, :, 0:4096]
        skey = ('wscr', l_, pi)
        if ti_ == 0:
            p.dma('pool', scr_ap.rearrange("p (r n) -> p r n", r=4), src_ap.rearrange("(r p) n -> p r n", p=128),
                  writes=[skey])
        i = wb_n[0] % NWB
        wb_n[0] += 1
        key = f'wbuf{i}'
        dst = wbuf[i][:].rearrange("p a b -> p (a b)")[:, 0:4096]
        p.dma('sp', dst, scr_ap, reads=[skey], writes=[key])
        return V3(dst.rearrange("p (r n) -> p r n", r=4)), key

    def proj_fm(wb, wkey, c0, ncol, ps_ap, pskey):
        for kc in range(8):
            p.op('pe', (lambda e, kc=kc: e.matmul(ps_ap, lhsT=wb[:, kc, c0:c0 + ncol], rhs=hT[:, kc, :],
                                                  start=(kc == 0), stop=(kc == 7))),
                 reads=[wkey, 'hT'], writes=[pskey], inc=(kc == 7))

    def proj_tm(wb, wkey, c0, ncol, ps_ap, pskey):
        for kc in range(8):
            p.op('pe', (lambda e, kc=kc: e.matmul(ps_ap, lhsT=hT[:, kc, :], rhs=wb[:, kc, c0:c0 + ncol],
                                                  start=(kc == 0), stop=(kc == 7))),
                 reads=[wkey, 'hT'], writes=[pskey], inc=(kc == 7))

    def rmsnorm_to_T(src, gt, dstT, dstkey, l):
        p.op('act', lambda e: e.activation(out=junk[:], in_=src[:], func=AF.Square, accum_out=sm[:, 0:1]),
             reads=['xt'], writes=['junk.0', 'junk.1', 'junk.2', 'junk.3', 'sm0'])
        p.op('act', lambda e: e.activation(out=sm[:, 1:2], in_=sm[:, 0:1], func=AF.Ln, scale=1.0 / 1024, bias=epsb[:, 0:1]),
             reads=['sm0', 'epsb'], writes=['sm1'])
        p.op('act', lambda e: e.activation(out=sm[:, 2:3], in_=sm[:, 1:2], func=AF.Exp, scale=-0.5),
             reads=['sm1'], writes=['sm2'])
        p.op('dve', lambda e: e.scalar_tensor_tensor(out=hbf[:], in0=src[:], scalar=sm[:, 2:3], in1=gt[:, l, :],
                                                     op0=ALU.mult, op1=ALU.mult),
             reads=['xt', 'sm2', 'prm'], writes=['hbf'])
        for kc in range(8):
            p.op('pe', (lambda e, kc=kc: e.transpose(out=PT[:, kc * 128:(kc + 1) * 128], in_=hbf[:, kc * 128:(kc + 1) * 128],
                                                     identity=ident_b[:])),
                 reads=['hbf', 'k_ident'], writes=['PT'], inc=(kc == 7))
        p.op('act', lambda e: e.copy(out=dstT[:].rearrange("p a b -> p (a b)"), in_=PT[:]), reads=['PT'], writes=[dstkey])

    epsb = sb([128, 2], F32, "epsb")
    p.op('dve', lambda e: e.memset(epsb[:, 0:1], EPS), writes=['epsb'])
    p.op('dve', lambda e: e.memset(epsb[:, 1:2], 1.0), writes=['epsb'])

    def head_norm_gate(src_ap, srckey, srcreads, gtile_ap, gate_ap, gatekey, out_ap, smc, sl=0):
        jk = junk[:, sl * 128:(sl + 1) * 128]
        tc_ = tmpC[:, sl * 128:(sl + 1) * 128]
        p.op('act', lambda e: e.activation(out=jk, in_=src_ap, func=AF.Square, accum_out=sm[:, smc:smc + 1]),
             reads=[srckey] + srcreads, writes=[f'junk.{sl}', f'sm{smc}'])
        p.op('act', lambda e: e.activation(out=sm[:, smc + 1:smc + 2], in_=sm[:, smc:smc + 1], func=AF.Ln, scale=1.0 / 128,
                                           bias=epsb[:, 0:1]), reads=[f'sm{smc}', 'epsb'], writes=[f'sm{smc+1}'])
        p.op('act', lambda e: e.activation(out=sm[:, smc + 2:smc + 3], in_=sm[:, smc + 1:smc + 2], func=AF.Exp, scale=-0.5),
             reads=[f'sm{smc+1}'], writes=[f'sm{smc+2}'])
        p.op('dve', lambda e: e.scalar_tensor_tensor(out=tc_, in0=src_ap, scalar=sm[:, smc + 2:smc + 3],
                                                     in1=gtile_ap, op0=ALU.mult, op1=ALU.mult),
             reads=[srckey, f'sm{smc+2}', 'prm'] + srcreads, writes=[f'tmpC.{sl}'])
        p.op('dve', lambda e: e.tensor_tensor(out=out_ap, in0=tc_, in1=gate_ap, op=ALU.mult),
             reads=[f'tmpC.{sl}', gatekey], writes=['obf'])

    def interleave(gens):
        gens = list(gens)
        while gens:
            for g in list(gens):
                try:
                    next(g)
                except StopIteration:
                    gens.remove(g)

    if pipe:
        xh = sb([128, 1024], F32, "xh")
        xr = sb([128, 1024], F32, "xr")
        p.op('pool', lambda e: e.memset(xr[:], 0.0), writes=['xr'])
        for j in range(2):
            p.dma('sp', sendb[j], xr[:], reads=['xr'], writes=[('sendb', j)])
            p.dma('sp', recvb[j], xr[:], reads=['xr'], writes=[('recvb', j)])
    for ti in range(NIT):
        if pipe:
            tix = min(ti, NTILES - 1)
            p.dma('sp', xh[:], x_in[tix * 128:(tix + 1) * 128, :], writes=['xh'])
            p.coll("AllReduce", ALU.add, [[2 * i_, 2 * i_ + 1] for i_ in range(npairs)], sendb[ti % 2], recvb[ti % 2],
                   reads=[('sendb', ti % 2)], writes=[('recvb', ti % 2)])
            p.dma('sp', xr[:], recvb[ti % 2], reads=[('recvb', ti % 2)], writes=['xr'])
            p.op('dve', lambda e: e.scalar_tensor_tensor(out=xt[:], in0=xr[:], scalar=role[:, 1:2], in1=xh[:], op0=ALU.mult,
                                                         op1=ALU.add), reads=['xr', 'xh', 'role'], writes=['xt'])
        else:
            p.dma('sp', xt[:], x_in[ti * 128:(ti + 1) * 128, :], writes=['xt'])
        for l in range(L):
            par = ti % 2
            W_in = prm['w_in'][l]
            rmsnorm_to_T(xt, gmix, hT, 'hT', l)
            gates_done = [0]

            def gate_piece(gi):
                wb, wk = wload(W_in[:, C_GATE + gi * 512:C_GATE + (gi + 1) * 512], 512, l, ti)
                pst, pk = (PS[0], 'PS0') if gi % 2 == 0 else (PS[1], 'PS1')
                proj_tm(wb, wk, 0, 512, pst[:], pk)
                p.op('act', lambda e: e.activation(out=gates[:, gi * 512:(gi + 1) * 512], in_=pst[:], func=AF.Sigmoid),
                     reads=[pk], writes=['gates'])
                gates_done[0] = gi + 1
            if not all(enable):
                p.op('pool', lambda e: e.memset(obf[:], 0.0), writes=['obf'])

            if enable[0]:
                wb, wk = wload(W_in[:, C_HQ:C_HQ + 512], 512, l, ti)
                for h in range(4):
                    proj_fm(wb, wk, h * 128, 128, PS[0][:, h * 128:(h + 1) * 128], 'PS0')
                p.op('act', lambda e: e.activation(out=h_q[:].rearrange("p a b -> p (a b)"), in_=PS[0][:], func=AF.Silu),
                     reads=['PS0'], writes=['h_q'])
                wb, wk = wload(W_in[:, C_HF:C_HF + 512], 512, l, ti)
                for h in range(4):
                    proj_fm(wb, wk, h * 128, 128, PS[1][:, h * 128:(h + 1) * 128], 'PS1')
                p.op('act', lambda e: e.activation(out=h_f[:].rearrange("p a b -> p (a b)"), in_=PS[1][:], func=AF.Sigmoid),
                     reads=['PS1'], writes=['h_f'])
                for h in range(4):
                    p.op('dve', (lambda e, h=h: e.tensor_scalar(out=h_f[:, h, :], in0=h_f[:, h, :], scalar1=oml[:, l, h:h + 1],
                                                                scalar2=lb[:, l, h:h + 1], op0=ALU.mult, op1=ALU.add)),
                         reads=['h_f', 'oml', 'lb'], writes=['h_f'])
                hf2 = h_f[:].rearrange("p a b -> p (a b)")
                hk2 = h_k[:].rearrange("p a b -> p (a b)")
                hc2 = h_cum[:].rearrange("p a b -> p (a b)")
                he2 = h_e[:].rearrange("p a b -> p (a b)")
                hq2 = h_q[:].rearrange("p a b -> p (a b)")
                p.op('dve', lambda e: e.tensor_scalar(out=hk2, in0=hf2, scalar1=-1.0, scalar2=1.0, op0=ALU.mult, op1=ALU.add),
                     reads=['h_f'], writes=['h_k'])
                p.op('act', lambda e: e.activation(out=hf2, in_=hf2, func=AF.Ln), reads=['h_f', 'h_k'], writes=['h_f'])
                for h in range(4):
                    p.op('dve', (lambda e, h=h: e.tensor_tensor_scan(out=h_cum[:, h, :], data0=resetm[:], data1=h_f[:, h, :],
                                                                     initial=0.0, op0=ALU.mult, op1=ALU.add)),
                         reads=['h_f', 'k_resetm'], writes=['h_cum'])
                p.op('act', lambda e: e.activation(out=he2, in_=hc2, func=AF.Exp), reads=['h_cum'], writes=['h_e'])
                p.op('dve', lambda e: e.tensor_tensor(out=h_qp[:].rearrange("p a b -> p (a b)"), in0=hq2, in1=he2, op=ALU.mult),
                     reads=['h_q', 'h_e'], writes=['h_qp'])
                p.op('act', lambda e: e.activation(out=he2, in_=hc2, func=AF.Exp, scale=-1.0), reads=['h_cum', 'h_qp'],
                     writes=['h_e'])
                p.op('dve', lambda e: e.tensor_tensor(out=h_kp[:].rearrange("p a b -> p (a b)"), in0=hk2, in1=he2, op=ALU.mult),
                     reads=['h_k', 'h_e'], writes=['h_kp'])
                cl = h_cum[:].rearrange("p a (j i) -> p a j i", i=32)[:, :, :, 31]
                p.op('act', lambda e: e.activation(out=h_edec[:], in_=cl, func=AF.Exp), reads=['h_cum'], writes=['h_edec'])
                p.op('dve', lambda e: e.tensor_tensor(
                    out=h_e[:].rearrange("p a (j i) -> p a j i", i=32),
                    in0=h_cum[:].rearrange("p a (j i) -> p a j i", i=32)[:, :, :, 31:32].broadcast_to([128, 4, 4, 32]),
                    in1=h_cum[:].rearrange("p a (j i) -> p a j i", i=32), op=ALU.subtract),
                    reads=['h_cum', 'h_kp'], writes=['h_e'])
                p.op('act', lambda e: e.activation(out=he2, in_=he2, func=AF.Exp), reads=['h_e'], writes=['h_e'])
                p.op('dve', lambda e: e.tensor_tensor(out=h_kpp[:].rearrange("p a b -> p (a b)"), in0=hk2, in1=he2, op=ALU.mult),
                     reads=['h_k', 'h_e'], writes=['h_kpp'])
                wb, wk = wload(W_in[:, C_HI:C_HI + 512], 512, l, ti)
                proj_tm(wb, wk, 0, 512, PS[0][:], 'PS0')
                p.op('act', lambda e: e.copy(out=h_v[:], in_=PS[0][:]), reads=['PS0'], writes=['h_v'])
                wb, wk = wload(W_in[:, C_HG:C_HG + 512], 512, l, ti)
                proj_tm(wb, wk, 0, 512, PS[1][:], 'PS1')
                p.op('act', lambda e: e.activation(out=h_g[:], in_=PS[1][:], func=AF.Silu), reads=['PS1'], writes=['h_g'])
                def hg_head(h, sl):
                    bX, bO = PS[2 + 2 * sl], PS[3 + 2 * sl]
                    kX, kO = f'PS{2 + 2 * sl}', f'PS{3 + 2 * sl}'
                    AT, kj = hS_AT[sl], hS_kj[sl]
                    n = f'.{sl}'
                    p.op('pe', lambda e: e.matmul(bX[:, 0:128], lhsT=h_kp[:, h, :], rhs=h_qp[:, h, :], start=True, stop=True),
                         reads=['h_kp', 'h_qp'], writes=[kX])
                    p.op('dve', lambda e: e.tensor_tensor(out=AT[:], in0=bX[:, 0:128], in1=hgmask[:], op=ALU.mult),
                         reads=[kX, 'k_hgmask'], writes=['h_AT' + n])
                    p.op('pe', lambda e: e.transpose(out=PT[:, sl * 128:(sl + 1) * 128], in_=h_kpp[:, h, :], identity=ident_b[:]),
                         reads=['h_kpp', 'k_ident'], writes=['PT'])
                    for j in range(4):
                        p.op('dve', lambda e: e.tensor_scalar(out=kj[:, j, :], in0=PT[:, sl * 128:(sl + 1) * 128],
                                                              scalar1=rowm[:, j:j + 1], scalar2=None, op0=ALU.mult),
                             reads=['PT', 'k_rowm'], writes=['h_kj' + n])
                        p.op('act', lambda e: e.copy(out=h_qj[:, h, j, 32 * j:32 * j + 32], in_=h_qp[:, h, 32 * j:32 * j + 32]),
                             reads=['h_qp'], writes=[f'h_qj.{h}'])
                    yield
                    p.op('pe', lambda e: e.matmul(bO[:, 0:128], lhsT=AT[:], rhs=h_v[:, h * 128:(h + 1) * 128], start=True,
                                                  stop=False), reads=['h_AT' + n, 'h_v'], writes=[kO])
                    for j in range(4):
                        p.op('pe', lambda e: e.matmul(bO[:, 0:128], lhsT=h_qj[:, h, j, :], rhs=h_Sb[:, l, h, :], start=False,
                                                      stop=(j == 3)), reads=[f'h_qj.{h}', f'h_Sb.{h}'], writes=[kO])
                        p.op('pe', lambda e: e.matmul(bX[:, 128:256], lhsT=kj[:, j, :], rhs=h_v[:, h * 128:(h + 1) * 128],
                                                      start=True, stop=True), reads=['h_kj' + n, 'h_v'], writes=[kX])
                        p.op('dve', lambda e: e.scalar_tensor_tensor(
                            out=h_S[0][:, l, h, :], in0=h_S[0][:, l, h, :], scalar=h_edec[:, h, j:j + 1], in1=bX[:, 128:256],
                            op0=ALU.mult, op1=ALU.add), reads=[f'h_S.{h}', 'h_edec', kX], writes=[f'h_S.{h}'])
                        p.op('act', lambda e: e.copy(out=h_Sb[:, l, h, :], in_=h_S[0][:, l, h, :]),
                             reads=[f'h_S.{h}'], writes=[f'h_Sb.{h}'])
                        yield
                    head_norm_gate(bO[:, 0:128], kO, [], hog[:, l, h * 128:(h + 1) * 128], h_g[:, h * 128:(h + 1) * 128],
                                   'h_g', obf[:, 0, h * 128:(h + 1) * 128], 4 + 72 * sl, sl)
                for pair in range(2):
                    interleave([hg_head(2 * pair, 0), hg_head(2 * pair + 1, 1)])
            gate_piece(0)
            gate_piece(1)
            if enable[1]:
                wb, wk = wload(W_in[:, C_MQ:C_MQ + 512], 512, l, ti)
                for h in range(4):
                    proj_fm(wb, wk, h * 64, 64, PS[0][0:64, h * 128:(h + 1) * 128], 'PS0')
                    proj_fm(wb, wk, 256 + h * 64, 64, PS[1][0:64, h * 128:(h + 1) * 128], 'PS1')
                p.op('act', lambda e: e.copy(out=m_q[:].rearrange("p a b -> p (a b)"), in_=PS[0][0:64, :]), reads=['PS0'],
                     writes=['m_q'])
                p.op('act', lambda e: e.mul(out=m_kT[:].rearrange("p a b -> p (a b)"), in_=PS[1][0:64, :], mul=0.125),
                     reads=['PS1'], writes=['m_kT'])
                proj_tm(wb, wk, 256, 256, PS[2][:, 0:256], 'PS2')
                p.op('act', lambda e: e.mul(out=m_k[:], in_=PS[2][:, 0:256], mul=0.125), reads=['PS2'], writes=['m_k'])
                wb, wk = wload(W_in[:, C_MV:C_MV + 520], 520, l, ti)
                proj_tm(wb, wk, 0, 512, PS[0][:], 'PS0')
                p.op('act', lambda e: e.copy(out=m_v[:], in_=PS[0][:]), reads=['PS0'], writes=['m_v'])
                proj_tm(wb, wk, 512, 8, PS[1][:, 0:8], 'PS1')
                p.op('dve', lambda e: e.tensor_tensor(out=m_if[:], in0=PS[1][:, 0:8], in1=mifb[:, l, :], op=ALU.add),
                     reads=['PS1', 'prm'], writes=['m_if'])
                wb, wk = wload(W_in[:, C_MO:C_MO + 512], 512, l, ti)
                proj_tm(wb, wk, 0, 512, PS[2][:], 'PS2')
                p.op('act', lambda e: e.activation(out=m_o[:], in_=PS[2][:], func=AF.Sigmoid), reads=['PS2'], writes=['m_o'])
                p.op('act', lambda e: e.activation(out=sm[:, 8:12], in_=m_if[:, 4:8], func=AF.Exp, scale=-1.0),
                     reads=['m_if'], writes=['sm8'])
                p.op('act', lambda e: e.activation(out=sm[:, 8:12], in_=sm[:, 8:12], func=AF.Ln, bias=epsb[:, 1:2]),
                     reads=['sm8', 'epsb'], writes=['sm8'])
                p.op('pe', lambda e: e.matmul(PS[3][:, 0:4], lhsT=tri_incl[:], rhs=sm[:, 8:12], start=True, stop=True),
                     reads=['sm8', 'k_tri_incl'], writes=['PS3'])
                p.op('act', lambda e: e.copy(out=sm[:, 12:16], in_=PS[3][:, 0:4]), reads=['PS3'], writes=['sm12'])
                p.op('dve', lambda e: e.tensor_tensor(out=sm[:, 16:20], in0=PS[3][:, 0:4], in1=m_if[:, 0:4], op=ALU.add),
                     reads=['PS3', 'm_if'], writes=['sm16'])
                p.op('act', lambda e: e.activation(out=sm[:, 16:20], in_=sm[:, 16:20], func=AF.Exp), reads=['sm16'],
                     writes=['sm16'])
                p.op('act', lambda e: e.activation(out=sm[:, 20:24], in_=sm[:, 12:16], func=AF.Exp, scale=-1.0),
                     reads=['sm12'], writes=['sm20'])
                p.op('pe', lambda e: e.matmul(PS[3][0:64, 8:12], lhsT=sel_last[:, 0:64], rhs=sm[:, 12:16], start=True, stop=True),
                     reads=['sm12', 'k_sel_last'], writes=['PS3'])
                p.op('act', lambda e: e.activation(out=sm[0:64, 24:28], in_=PS[3][0:64, 8:12], func=AF.Exp, scale=-1.0),
                     reads=['PS3'], writes=['sm24'])
                for h in range(4):
                    p.op('dve', (lambda e, h=h: e.tensor_scalar(out=m_va[:, h, 0:128], in0=m_v[:, h * 128:(h + 1) * 128],
                                                                scalar1=sm[:, 16 + h:17 + h], scalar2=None, op0=ALU.mult)),
                         reads=['m_v', 'sm16'], writes=['m_va'])
                    p.op('dve', (lambda e, h=h: e.tensor_copy(out=m_va[:, h, 128:129], in_=sm[:, 16 + h:17 + h])),
                         reads=['sm16'], writes=['m_va'])
                def ml_head(h, sl):
                    bX, bO = PS[3 + 2 * sl], PS[4 + 2 * sl]
                    kX, kO = f'PS{3 + 2 * sl}', f'PS{4 + 2 * sl}'
                    AT = mS_AT[sl]
                    n = f'.{sl}'
                    c0 = 28 if sl == 0 else 96
                    tA = tmpA[:, sl * 128:(sl + 1) * 128]
                    p.op('pe', lambda e: e.matmul(bX[:, 0:128], lhsT=m_kT[:, h, :], rhs=m_q[:, h, :], start=True, stop=True),
                         reads=['m_kT', 'm_q'], writes=[kX])
                    p.op('dve', lambda e: e.tensor_tensor(out=AT[:], in0=bX[:, 0:128], in1=tri_incl[:], op=ALU.mult),
                         reads=[kX, 'k_tri_incl'], writes=['m_AT' + n])
                    yield
                    p.op('pe', lambda e: e.matmul(bO[:, 0:129], lhsT=AT[:], rhs=m_va[:, h, 0:129], start=True, stop=False),
                         reads=['m_AT' + n, 'm_va'], writes=[kO], inc=False)
                    p.op('pe', lambda e: e.matmul(bO[:, 0:129], lhsT=m_q[:, h, :], rhs=m_Cb[:, l, h, 0:129], start=False,
                                                  stop=True), reads=['m_q', f'm_Cb.{h}'], writes=[kO])
                    p.op('pe', lambda e: e.matmul(bX[0:64, 128:257], lhsT=m_k[:, h * 64:(h + 1) * 64], rhs=m_va[:, h, 0:129],
                                                  start=True, stop=True), reads=['m_k', 'm_va'], writes=[kX])
                    p.op('dve', lambda e: e.tensor_tensor(out=sm[:, c0:c0 + 1], in0=bO[:, 128:129], in1=sm[:, 20 + h:21 + h],
                                                          op=ALU.mult), reads=[kO, 'sm20'], writes=[f'sm{c0}'])
                    p.op('dve', lambda e: e.scalar_tensor_tensor(out=sm[:, c0 + 3:c0 + 4], in0=sm[:, c0:c0 + 1], scalar=-1.0,
                                                                 in1=sm[:, c0:c0 + 1], op0=ALU.mult, op1=ALU.max),
                         reads=[f'sm{c0}'], writes=[f'sm{c0 + 3}'])
                    p.op('dve', lambda e: e.tensor_scalar(out=sm[:, c0:c0 + 1], in0=sm[:, c0 + 3:c0 + 4], scalar1=1.0,
                                                          scalar2=None, op0=ALU.max), reads=[f'sm{c0 + 3}'], writes=[f'sm{c0}'])
                    p.op('dve', lambda e: e.reciprocal(out=sm[:, c0 + 1:c0 + 2], in_=sm[:, c0:c0 + 1]), reads=[f'sm{c0}'],
                         writes=[f'sm{c0 + 1}'])
                    p.op('dve', lambda e: e.tensor_tensor(out=sm[:, c0 + 2:c0 + 3], in0=sm[:, c0 + 1:c0 + 2],
                                                          in1=sm[:, 20 + h:21 + h], op=ALU.mult),
                         reads=[f'sm{c0 + 1}', 'sm20'], writes=[f'sm{c0 + 2}'])
                    yield
                    p.op('act', lambda e: e.activation(out=tA, in_=bO[:, 0:128], func=AF.Copy, scale=sm[:, c0 + 2:c0 + 3]),
                         reads=[kO, f'sm{c0 + 2}'], writes=['tmpA' + n])
                    p.op('dve', lambda e: e.tensor_tensor(out=m_C[:, l, h, 0:129], in0=m_C[:, l, h, 0:129],
                                                          in1=bX[0:64, 128:257], op=ALU.add),
                         reads=[f'm_C.{h}', kX], writes=[f'm_C.{h}'])
                    p.op('dve', lambda e: e.tensor_scalar(out=m_C[:, l, h, 0:129], in0=m_C[:, l, h, 0:129],
                                                          scalar1=sm[0:64, 24 + h:25 + h], scalar2=None, op0=ALU.mult),
                         reads=[f'm_C.{h}', 'sm24'], writes=[f'm_C.{h}'])
                    p.op('act', lambda e: e.copy(out=m_Cb[:, l, h, 0:129], in_=m_C[:, l, h, 0:129]), reads=[f'm_C.{h}'],
                         writes=[f'm_Cb.{h}'])
                    yield
                    head_norm_gate(tA, 'tmpA' + n, [], mog[:, l, h * 128:(h + 1) * 128], m_o[:, h * 128:(h + 1) * 128],
                                   'm_o', obf[:, 1, h * 128:(h + 1) * 128], 32 if sl == 0 else 100, sl)
                for pair in range(2):
                    interleave([ml_head(2 * pair, 0), ml_head(2 * pair + 1, 1)])

            gate_piece(2)
            gate_piece(3)
            if enable[2]:
                wb, wk = wload(W_in[:, C_AQ:C_AQ + 512], 512, l, ti)
                wb2, wk2 = wload(W_in[:, C_AK:C_AK + 256], 256, l, ti)
                for g in range(8):
                    proj_fm(wb, wk, g * 64, 64, PS[g // 4][0:64, (g % 4) * 128:(g % 4 + 1) * 128], f'PS{g // 4}')
                for kv in range(2):
                    proj_fm(wb2, wk2, kv * 64, 64, PS[2][0:64, kv * 128:(kv + 1) * 128], 'PS2')
                az2 = a_z[:].rearrange("p a b -> p (a b)")
                asq2 = a_sq[:].rearrange("p a b -> p (a b)")
                p.op('act', lambda e: e.copy(out=az2[:, 0:512], in_=PS[0][0:64, :]), reads=['PS0'], writes=['a_z'])
                p.op('act', lambda e: e.copy(out=az2[:, 512:1024], in_=PS[1][0:64, :]), reads=['PS1'], writes=['a_z'])
                p.op('act', lambda e: e.copy(out=az2[:, 1024:1280], in_=PS[2][0:64, 0:256]), reads=['PS2'], writes=['a_z'])
                p.op('dve', lambda e: e.tensor_tensor(out=asq2, in0=az2, in1=az2, op=ALU.mult), reads=['a_z'], writes=['a_sq'])
                for i3 in range(3):
                    w3 = 512 if i3 < 2 else 256
                    p.op('pe', (lambda e, i3=i3, w3=w3: e.matmul(PS[3][0:64, 0:w3], lhsT=ones_f[0:64, 0:64],
                                                                 rhs=asq2[:, i3 * 512:i3 * 512 + w3], start=True, stop=True)),
                         reads=['a_sq', 'k_ones'], writes=['PS3'])
                    p.op('act', (lambda e, i3=i3, w3=w3: e.activation(out=asq2[:, i3 * 512:i3 * 512 + w3], in_=PS[3][0:64, 0:w3],
                                                                      func=AF.Ln, scale=1.0 / 64, bias=epsb[0:64, 0:1])),
                         reads=['PS3', 'epsb'], writes=['a_sq'])
                p.op('act', lambda e: e.activation(out=asq2, in_=asq2, func=AF.Exp, scale=-0.5), reads=['a_sq'], writes=['a_sq'])
                p.op('dve', lambda e: e.tensor_tensor(out=a_q[:].rearrange("p a b -> p (a b)"), in0=az2[:, 0:1024],
                                                      in1=asq2[:, 0:1024], op=ALU.mult), reads=['a_z', 'a_sq'], writes=['a_q'])
                for kv in range(2):
                    p.op('dve', (lambda e, kv=kv: e.scalar_tensor_tensor(
                        out=a_kT[:, l, kv, par, :], in0=a_z[:, 8 + kv, :], scalar=gk8[:, l:l + 1], in1=a_sq[:, 8 + kv, :],
                        op0=ALU.mult, op1=ALU.mult)), reads=['a_z', 'a_sq', 'gk8'], writes=['a_kT'])
                proj_tm(wb2, wk2, 128, 128, PS[4][:, 0:128], 'PS4')
                for kv in range(2):
                    p.op('act', (lambda e, kv=kv: e.copy(out=a_v[:, l, par, kv, 0:64], in_=PS[4][:, kv * 64:(kv + 1) * 64])),
                         reads=['PS4'], writes=['a_v'])
                    p.op('dve', (lambda e, kv=kv: e.memset(a_v[:, l, par, kv, 64:65], 1.0)), writes=['a_v'])
                def swa_head(g, sl):
                    kv = g // 4
                    bL, bO = PS[3 + 2 * sl], PS[4 + 2 * sl]
                    kL, kO = f'PS{3 + 2 * sl}', f'PS{4 + 2 * sl}'
                    aP, aPT = aS_P[sl], aS_PT[sl]
                    n = f'.{sl}'
                    c0 = 40 + 2 * sl
                    blks = [0, 1] if (pipe or ti > 0) else [1]
                    nb = len(blks)
                    for bi, blk in enumerate(blks):
                        slot = par if blk == 1 else 1 - par
                        p.op('pe', lambda e: e.matmul(bL[:, blk * 128:(blk + 1) * 128], lhsT=a_kT[:, l, kv, slot, :],
                                                      rhs=a_q[:, g, :], start=True, stop=True),
                             reads=['a_kT', 'a_q'], writes=[kL])
                    yield
                    p.op('act', lambda e: e.activation(out=aP[:, 0:nb * 128], in_=bL[:, blks[0] * 128:(blks[0] + nb) * 128],
                                                       func=AF.Exp), reads=[kL], writes=['a_P' + n])
                    for bi, blk in enumerate(blks):
                        if pipe and blk == 0:
                            p.op('dve', lambda e: e.scalar_tensor_tensor(
                                out=aPT[:, blk, :], in0=aP[:, bi * 128:(bi + 1) * 128], scalar=pflag[:, ti:ti + 1],
                                in1=EB[:, blk, g, :], op0=ALU.mult, op1=ALU.mult),
                                reads=['a_P' + n, 'EB', 'pflag'], writes=['a_PT' + n])
                        else:
                            p.op('dve', lambda e: e.tensor_tensor(out=aPT[:, blk, :], in0=aP[:, bi * 128:(bi + 1) * 128],
                                                                  in1=EB[:, blk, g, :], op=ALU.mult),
                                 reads=['a_P' + n, 'EB'], writes=['a_PT' + n])
                    yield
                    for bi, blk in enumerate(blks):
                        slot = par if blk == 1 else 1 - par
                        p.op('pe', lambda e: e.matmul(bO[:, 0:65], lhsT=aPT[:, blk, :], rhs=a_v[:, l, slot, kv, 0:65],
                                                      start=(bi == 0), stop=(bi == nb - 1)),
                             reads=['a_PT' + n, 'a_v'], writes=[kO], inc=(bi == nb - 1))
                    p.op('dve', lambda e: e.tensor_tensor(out=sm[:, c0:c0 + 1], in0=bO[:, 64:65], in1=esink[:, l, g:g + 1],
                                                          op=ALU.add), reads=[kO, 'esink'], writes=[f'sm{c0}'])
                    p.op('dve', lambda e: e.reciprocal(out=sm[:, c0 + 1:c0 + 2], in_=sm[:, c0:c0 + 1]), reads=[f'sm{c0}'],
                         writes=[f'sm{c0 + 1}'])
                    yield
                    p.op('dve', lambda e: e.tensor_scalar(out=obf[:, 2, g * 64:(g + 1) * 64], in0=bO[:, 0:64],
                                                          scalar1=sm[:, c0 + 1:c0 + 2], scalar2=None, op0=ALU.mult),
                         reads=[kO, f'sm{c0 + 1}'], writes=['obf'])
                for pair in range(4):
                    interleave([swa_head(2 * pair, 0), swa_head(2 * pair + 1, 1)])

            gate_piece(4)
            gate_piece(5)
            if enable[3]:
                for c3 in range(3):
                    wb, wk = wload(W_in[:, C_DQKV + c3 * 512:C_DQKV + (c3 + 1) * 512], 512, l, ti)
                    for c4 in range(4):
                        proj_fm(wb, wk, c4 * 128, 128, PS[c3][:, c4 * 128:(c4 + 1) * 128], f'PS{c3}')
                    p.op('act', (lambda e, c3=c3: e.copy(out=d_x[:, l, c3 * 4:(c3 + 1) * 4, 3:131],
                                                         in_=PS[c3][:].rearrange("p (a b) -> p a b", a=4))),
                         reads=[f'PS{c3}'], writes=['d_x'])
                for hf in range(2):
                    cs = slice(hf * 6, hf * 6 + 6)
                    yv = d_y[:, cs, :]
                    tAv = xr[:, 0:768].rearrange("p (a b) -> p a b", a=6)
                    tBv = xh[:, 0:768].rearrange("p (a b) -> p a b", a=6)

                    def wbc(j, cs=cs):
                        return convw[:, l, j, cs].unsqueeze(2).broadcast_to([128, 6, 128])
                    p.op('dve', lambda e: e.tensor_tensor(out=yv, in0=d_x[:, l, cs, 0:128], in1=wbc(0), op=ALU.mult),
                         reads=['d_x', 'prm'], writes=['d_y'])
                    p.op('pool', lambda e: e.tensor_tensor(out=tAv, in0=d_x[:, l, cs, 1:129], in1=wbc(1), op=ALU.mult),
                         reads=['d_x', 'prm'], writes=['xr'])
                    p.op('pool', lambda e: e.tensor_tensor(out=tBv, in0=d_x[:, l, cs, 2:130], in1=wbc(2), op=ALU.mult),
                         reads=['d_x', 'prm'], writes=['xh'])
                    p.op('dve', lambda e: e.tensor_tensor(out=yv, in0=yv, in1=tAv, op=ALU.add), reads=['d_y', 'xr'],
                         writes=['d_y'])
                    p.op('pool', lambda e: e.tensor_tensor(out=tAv, in0=d_x[:, l, cs, 3:131], in1=wbc(3), op=ALU.mult),
                         reads=['d_x', 'prm'], writes=['xr'])
                    p.op('dve', lambda e: e.tensor_tensor(out=yv, in0=yv, in1=tBv, op=ALU.add), reads=['d_y', 'xh'],
                         writes=['d_y'])
                    p.op('dve', lambda e: e.tensor_tensor(out=yv, in0=yv, in1=tAv, op=ALU.add), reads=['d_y', 'xr'],
                         writes=['d_y'])
                p.op('pool', lambda e: e.tensor_copy(out=d_x[:, l, :, 0:3], in_=d_x[:, l, :, 128:131]), reads=['d_x', 'd_y'],
                     writes=['d_x'])
                dy2 = d_y[:].rearrange("p a b -> p (a b)")
                p.op('act', lambda e: e.activation(out=dy2, in_=dy2, func=AF.Silu), reads=['d_y'], writes=['d_y'])
                p.op('dve', lambda e: e.tensor_tensor(out=merged[:], in0=dy2[:, 0:1024], in1=dy2[:, 0:1024], op=ALU.mult),
                     reads=['d_y'], writes=['merged'])
                for i2 in range(2):
                    tn = tmpA if i2 == 0 else tmpB
                    tk = 'tmpA' if i2 == 0 else 'tmpB'
                    p.op('pe', lambda e: e.matmul(PS[i2][:], lhsT=ones_f[:], rhs=merged[:, i2 * 512:(i2 + 1) * 512], start=True,
                                                  stop=True), reads=['merged', 'k_ones'], writes=[f'PS{i2}'])
                    p.op('act', lambda e: e.activation(out=tn[:], in_=PS[i2][:], func=AF.Ln, bias=epsb[:, 0:1]),
                         reads=[f'PS{i2}', 'epsb'], writes=[tk])
                    p.op('act', lambda e: e.activation(out=tn[:], in_=tn[:], func=AF.Exp, scale=-0.5), reads=[tk], writes=[tk])
                p.op('dve', lambda e: e.scalar_tensor_tensor(out=d_qT[:].rearrange("p a b -> p (a b)"), in0=dy2[:, 0:512],
                                                             scalar=float(128 ** -0.5), in1=tmpA[:], op0=ALU.mult, op1=ALU.mult),
                     reads=['d_y', 'tmpA'], writes=['d_qT'])
                p.op('dve', lambda e: e.tensor_tensor(out=d_kT[:].rearrange("p a b -> p (a b)"), in0=dy2[:, 512:1024],
                                                      in1=tmpB[:], op=ALU.mult), reads=['d_y', 'tmpB'], writes=['d_kT'])
                p.op('dve', lambda e: e.tensor_copy(out=d_vT[:], in_=d_y[:, 8:12, :]), reads=['d_y'], writes=['d_vT'])
                for h in range(4):
                    p.op('pe', (lambda e, h=h: e.transpose(out=PT[:, h * 128:(h + 1) * 128], in_=d_vT[:, h, :],
                                                           identity=ident_b[:])), reads=['d_vT', 'k_ident'], writes=['PT'],
                         inc=False)
                    p.op('pe', (lambda e, h=h: e.transpose(out=PT[:, 512 + h * 128:512 + (h + 1) * 128], in_=d_kT[:, h, :],
                                                           identity=ident_b[:])), reads=['d_kT', 'k_ident'], writes=['PT'],
                         inc=(h == 3))
                p.op('act', lambda e: e.copy(out=d_v[:].rearrange("p a b -> p (a b)"), in_=PT[:, 0:512]), reads=['PT'],
                     writes=['d_v'])
                p.op('act', lambda e: e.copy(out=d_k[:].rearrange("p a b -> p (a b)"), in_=PT[:, 512:1024]), reads=['PT'],
                     writes=['d_k'])
                wb, wk = wload(W_in[:, C_DB:C_DB + 520], 520, l, ti)
                proj_tm(wb, wk, 0, 8, PS[0][:, 0:8], 'PS0')
                p.op('act', lambda e: e.copy(out=d_ba[:], in_=PS[0][:, 0:8]), reads=['PS0'], writes=['d_ba'])
                proj_fm(wb, wk, 4, 4, PS[1][0:4, 0:128], 'PS1')
                p.op('act', lambda e: e.copy(out=d_arow[:], in_=PS[1][0:4, 0:128]), reads=['PS1'], writes=['d_arow'])
                proj_tm(wb, wk, 8, 512, PS[2][:], 'PS2')
                p.op('act', lambda e: e.activation(out=d_z[:], in_=PS[2][:], func=AF.Silu), reads=['PS2'], writes=['d_z'])
                p.op('act', lambda e: e.activation(out=sm[:, 44:48], in_=d_ba[:, 0:4], func=AF.Sigmoid), reads=['d_ba'],
                     writes=['sm44'])
                p.op('dve', lambda e: e.tensor_tensor(out=sm[:, 48:52], in0=d_ba[:, 4:8], in1=dtb[:, l, :], op=ALU.add),
                     reads=['d_ba', 'prm'], writes=['sm48'])
                p.op('act', lambda e: e.activation(out=sm[:, 48:52], in_=sm[:, 48:52], func=AF.Exp), reads=['sm48'],
                     writes=['sm48'])
                p.op('act', lambda e: e.activation(out=sm[:, 48:52], in_=sm[:, 48:52], func=AF.Ln, bias=epsb[:, 1:2]),
                     reads=['sm48', 'epsb'], writes=['sm48'])
                p.op('dve', lambda e: e.tensor_tensor(out=sm[:, 48:52], in0=sm[:, 48:52], in1=nexpa[:, l, :], op=ALU.mult),
                     reads=['sm48', 'nexpa'], writes=['sm48'])
                p.op('pe', lambda e: e.matmul(PS[3][:, 0:4], lhsT=tri_incl[:], rhs=sm[:, 48:52], start=True, stop=True),
                     reads=['sm48', 'k_tri_incl'], writes=['PS3'])
                p.op('act', lambda e: e.copy(out=sm[:, 52:56], in_=PS[3][:, 0:4]), reads=['PS3'], writes=['sm52'])
                p.op('dve', lambda e: e.tensor_scalar(out=sm[:, 56:60], in0=sm[:, 52:56], scalar1=-1.0, scalar2=None,
                                                      op0=ALU.mult), reads=['sm52'], writes=['sm56'])
                p.op('pe', lambda e: e.matmul(PS[3][:, 8:12], lhsT=sel_last[:], rhs=sm[:, 52:56], start=True, stop=True),
                     reads=['sm52', 'k_sel_last'], writes=['PS3'])
                p.op('act', lambda e: e.activation(out=sm[:, 60:64], in_=PS[3][:, 8:12], func=AF.Exp), reads=['PS3'],
                     writes=['sm60'])
                p.op('dve', lambda e: e.tensor_tensor(out=tmpC[:, 500:504], in0=PS[3][:, 8:12], in1=sm[:, 52:56],
                                                      op=ALU.subtract), reads=['PS3', 'sm52'], writes=['tmpC5'])
                p.op('act', lambda e: e.activation(out=tmpC[:, 500:504], in_=tmpC[:, 500:504], func=AF.Exp), reads=['tmpC5'],
                     writes=['tmpC5'])
                p.op('dve', lambda e: e.tensor_tensor(out=tmpC[:, 504:508], in0=tmpC[:, 500:504], in1=sm[:, 44:48], op=ALU.mult),
                     reads=['tmpC5', 'sm44'], writes=['tmpC6'])
                p.op('act', lambda e: e.activation(out=tmpC[:, 508:512], in_=sm[:, 52:56], func=AF.Exp), reads=['sm52'],
                     writes=['tmpC7'])
                p.op('dve', lambda e: e.tensor_scalar(out=tmpC[:, 496:500], in0=tmpC[:, 508:512], scalar1=-1.0, scalar2=None,
                                                      op0=ALU.mult), reads=['tmpC7'], writes=['tmpC4'])
                p.op('dve', lambda e: e.tensor_scalar(out=tmpC[:, 492:496], in0=sm[:, 44:48], scalar1=-1.0, scalar2=None,
                                                      op0=ALU.mult), reads=['sm44'], writes=['tmpC3'])
                p.op('act', lambda e: e.activation(out=d_grow[:], in_=d_arow[:], func=AF.Exp, bias=dtbrow[:, l:l + 1]),
                     reads=['d_arow', 'dtbrow'], writes=['d_grow'])
                p.op('act', lambda e: e.activation(out=d_grow[:], in_=d_grow[:], func=AF.Ln, bias=epsb[0:4, 1:2]),
                     reads=['d_grow', 'epsb'], writes=['d_grow'])
                p.op('dve', lambda e: e.tensor_scalar(out=d_grow[:], in0=d_grow[:], scalar1=nexparow[:, l:l + 1], scalar2=None,
                                                      op0=ALU.mult), reads=['d_grow', 'nexparow'], writes=['d_grow'])
                p.op('dve', lambda e: e.tensor_tensor_scan(out=d_grow[:], data0=ones_f[0:4, :], data1=d_grow[:], initial=0.0,
                                                           op0=ALU.mult, op1=ALU.add), reads=['d_grow', 'k_ones'],
                     writes=['d_grow'])
                def dn_head(h, sl):
                    bA, bB, bC = PS[1 + 3 * sl], PS[2 + 3 * sl], PS[3 + 3 * sl]
                    kA, kB, kC = f'PS{1 + 3 * sl}', f'PS{2 + 3 * sl}', f'PS{3 + 3 * sl}'
                    E1, E1s, TT, rr = dS_E1[sl], dS_E1s[sl], dS_TT[sl], dS_r[sl]
                    Pb, PTb = dS_P[sl], dS_PT[sl]
                    vn, vc, aT, o1 = dS_vn[sl], dS_vc[sl], dS_aT[sl], dS_o1[sl]
                    tA = tmpA[:, sl * 128:(sl + 1) * 128]
                    tB = tmpB[:, sl * 128:(sl + 1) * 128]
                    n = f'.{sl}'
                    p.op('pe', lambda e: e.matmul(bA[:, 0:128], lhsT=selh[:, h * 128:(h + 1) * 128], rhs=d_grow[:],
                                                  start=True, stop=True), reads=['k_selh', 'd_grow'], writes=[kA])
                    p.op('dve', lambda e: e.tensor_tensor(out=tA, in0=bA[:, 0:128], in1=negmask[:], op=ALU.add),
                         reads=[kA, 'k_negmask'], writes=['tmpA' + n])
                    p.op('act', lambda e: e.activation(out=E1[:], in_=tA, func=AF.Exp, bias=sm[:, 56 + h:57 + h]),
                         reads=['tmpA' + n, 'sm56'], writes=['d_E1' + n])
                    yield
                    p.op('pe', lambda e: e.matmul(bA[:, 128:256], lhsT=d_kT[:, h, :], rhs=d_kT[:, h, :], start=True, stop=True),
                         reads=['d_kT'], writes=[kA])
                    p.op('dve', lambda e: e.tensor_tensor(out=E1s[:], in0=E1[:], in1=tri_strict[:], op=ALU.mult),
                         reads=['d_E1' + n, 'k_tri_strict'], writes=['d_E1s' + n])
                    p.op('dve', lambda e: e.scalar_tensor_tensor(out=PTb[0][:], in0=bA[:, 128:256],
                                                                 scalar=tmpC[:, 492 + h:493 + h], in1=E1s[:],
                                                                 op0=ALU.mult, op1=ALU.mult),
                         reads=[kA, 'tmpC3', 'd_E1s' + n], writes=['d_PT0' + n])
                    yield
                    p.op('pe', lambda e: e.transpose(out=bB[:, 0:128], in_=PTb[0][:], identity=ident_f[:]),
                         reads=['d_PT0' + n, 'k_ident'], writes=[kB])
                    p.op('act', lambda e: e.copy(out=Pb[0][:], in_=bB[:, 0:128]), reads=[kB], writes=['d_P0' + n])
                    p.op('dve', lambda e: e.tensor_tensor(out=TT[:], in0=PTb[0][:], in1=ident_f[:], op=ALU.add),
                         reads=['d_PT0' + n, 'k_ident'], writes=['d_TT' + n])
                    yield
                    cur = 0
                    for lvl in range(6):
                        nxt = 1 - cur
                        p.op('pe', lambda e: e.matmul(bB[:, 0:128], lhsT=PTb[cur][:], rhs=Pb[cur][:], start=True, stop=True),
                             reads=[f'd_PT{cur}' + n, f'd_P{cur}' + n], writes=[kB])
                        if lvl < 5:
                            p.op('pe', lambda e: e.matmul(bB[:, 128:256], lhsT=Pb[cur][:], rhs=PTb[cur][:], start=True, stop=True),
                                 reads=[f'd_PT{cur}' + n, f'd_P{cur}' + n], writes=[kB])
                        p.op('act', lambda e: e.copy(out=Pb[nxt][:], in_=bB[:, 0:128]), reads=[kB], writes=[f'd_P{nxt}' + n])
                        if lvl < 5:
                            p.op('act', lambda e: e.copy(out=PTb[nxt][:], in_=bB[:, 128:256]), reads=[kB],
                                 writes=[f'd_PT{nxt}' + n])
                        yield
                        p.op('pe', lambda e: e.matmul(bC[:, 0:128], lhsT=Pb[nxt][:], rhs=TT[:], start=True, stop=True),
                             reads=[f'd_P{nxt}' + n, 'd_TT' + n], writes=[kC])
                        p.op('dve', lambda e: e.tensor_tensor(out=TT[:], in0=TT[:], in1=bC[:, 0:128], op=ALU.add),
                             reads=['d_TT' + n, kC], writes=['d_TT' + n])
                        yield
                        cur = nxt
                    p.op('pe', lambda e: e.matmul(bA[:, 256:384], lhsT=d_kT[:, h, :], rhs=d_Sb[:, l, h, :], start=True, stop=True),
                         reads=['d_kT', 'd_Sb'], writes=[kA])
                    p.op('dve', lambda e: e.scalar_tensor_tensor(out=rr[:], in0=bA[:, 256:384], scalar=tmpC[:, 496 + h:497 + h],
                                                                 in1=d_v[:, h, :], op0=ALU.mult, op1=ALU.add),
                         reads=[kA, 'tmpC4', 'd_v'], writes=['d_r' + n])
                    yield
                    p.op('pe', lambda e: e.matmul(bB[:, 256:384], lhsT=TT[:], rhs=rr[:], start=True, stop=True),
                         reads=['d_TT' + n, 'd_r' + n], writes=[kB])
                    p.op('dve', lambda e: e.tensor_scalar(out=vn[:], in0=bB[:, 256:384], scalar1=sm[:, 44 + h:45 + h],
                                                          scalar2=None, op0=ALU.mult), reads=[kB, 'sm44'], writes=['d_vn' + n])
                    p.op('dve', lambda e: e.tensor_scalar(out=vc[:], in0=bB[:, 256:384], scalar1=tmpC[:, 504 + h:505 + h],
                                                          scalar2=None, op0=ALU.mult), reads=[kB, 'tmpC6'], writes=['d_vc' + n])
                    p.op('pe', lambda e: e.matmul(bA[:, 384:512], lhsT=d_kT[:, h, :], rhs=d_qT[:, h, :], start=True, stop=True),
                         reads=['d_kT', 'd_qT'], writes=[kA])
                    p.op('dve', lambda e: e.tensor_tensor(out=aT[:], in0=bA[:, 384:512], in1=E1[:], op=ALU.mult),
                         reads=[kA, 'd_E1' + n], writes=['d_aT' + n])
                    yield
                    p.op('pe', lambda e: e.matmul(bC[:, 128:256], lhsT=d_qT[:, h, :], rhs=d_Sb[:, l, h, :], start=True, stop=True),
                         reads=['d_qT', 'd_Sb'], writes=[kC])
                    p.op('act', lambda e: e.activation(out=o1[:], in_=bC[:, 128:256], func=AF.Copy,
                                                       scale=tmpC[:, 508 + h:509 + h]), reads=[kC, 'tmpC7'], writes=['d_o1' + n])
                    yield
                    p.op('pe', lambda e: e.matmul(bC[:, 256:384], lhsT=aT[:], rhs=vn[:], start=True, stop=True),
                         reads=['d_aT' + n, 'd_vn' + n], writes=[kC])
                    p.op('dve', lambda e: e.tensor_tensor(out=tB, in0=bC[:, 256:384], in1=o1[:], op=ALU.add),
                         reads=[kC, 'd_o1' + n], writes=['tmpB' + n])
                    yield
                    p.op('pe', lambda e: e.matmul(bC[:, 384:512], lhsT=d_k[:, h, :], rhs=vc[:], start=True, stop=True),
                         reads=['d_k', 'd_vc' + n], writes=[kC])
                    p.op('dve', lambda e: e.scalar_tensor_tensor(out=d_S[:, l, h, :], in0=d_S[:, l, h, :],
                                                                 scalar=sm[:, 60 + h:61 + h], in1=bC[:, 384:512],
                                                                 op0=ALU.mult, op1=ALU.add),
                         reads=['d_S', 'sm60', kC], writes=['d_S'])
                    p.op('act', lambda e: e.copy(out=d_Sb[:, l, h, :], in_=d_S[:, l, h, :]), reads=['d_S'], writes=['d_Sb'])
                    yield
                    head_norm_gate(tB, 'tmpB' + n, [], dog[:, l, :], d_z[:, h * 128:(h + 1) * 128], 'd_z',
                                   obf[:, 3, h * 128:(h + 1) * 128], 64 + 4 * sl, sl)
                for pair in range(2):
                    interleave([dn_head(2 * pair, 0), dn_head(2 * pair + 1, 1)])

            for gi in range(gates_done[0], 8):
                gate_piece(gi)
            for half in range(2):
                for c8 in range(8):
                    cc = half * 8 + c8
                    p.op('pe', (lambda e, cc=cc, c8=c8: e.transpose(
                        out=PT[:, c8 * 128:(c8 + 1) * 128],
                        in_=obf[:].rearrange("p a b -> p (a b)")[:, cc * 128:(cc + 1) * 128], identity=ident_b[:])),
                        reads=['obf', 'k_ident'], writes=['PT'], inc=(c8 == 7))
                p.op('act', (lambda e, half=half: e.copy(out=oT[:, half * 8:(half + 1) * 8, :].rearrange("p a b -> p (a b)"),
                                                         in_=PT[:])), reads=['PT'], writes=['oT'])
            for n in range(4):
                wv, wk = wload_rows(prm['w_branch'][l, n], l, ti)
                for half in range(2):
                    pst, pk = (PS[2], 'PS2') if half == 0 else (PS[3], 'PS3')
                    for wc in range(4):
                        p.op('pe', (lambda e, n=n, wc=wc, half=half, pst=pst: e.matmul(
                            pst[:], lhsT=oT[:, n * 4 + wc, :], rhs=wv[:, wc, half * 512:(half + 1) * 512], start=(wc == 0),
                            stop=(wc == 3))), reads=['oT', wk], writes=[pk], inc=(wc == 3))
                    if n == 0:
                        p.op('dve', (lambda e, n=n, half=half, pst=pst: e.tensor_tensor(
                            out=merged[:, half * 512:(half + 1) * 512], in0=pst[:],
                            in1=gates[:, n * 1024 + half * 512:n * 1024 + (half + 1) * 512], op=ALU.mult)),
                            reads=[pk, 'gates'], writes=['merged'])
                    else:
                        p.op('dve', (lambda e, n=n, half=half, pst=pst: e.tensor_tensor(
                            out=tmpA[:], in0=pst[:], in1=gates[:, n * 1024 + half * 512:n * 1024 + (half + 1) * 512],
                            op=ALU.mult)), reads=[pk, 'gates'], writes=['tmpA'])
                        p.op('dve', (lambda e, half=half: e.tensor_tensor(
                            out=merged[:, half * 512:(half + 1) * 512], in0=merged[:, half * 512:(half + 1) * 512], in1=tmpA[:],
                            op=ALU.add)), reads=['tmpA', 'merged'], writes=['merged'])
            p.op('act', lambda e: e.copy(out=mbf[:], in_=merged[:]), reads=['merged'], writes=['mbf'])
            for kc in range(8):
                p.op('pe', (lambda e, kc=kc: e.transpose(out=PT[:, kc * 128:(kc + 1) * 128], in_=mbf[:, kc * 128:(kc + 1) * 128],
                                                         identity=ident_b[:])), reads=['mbf', 'k_ident'], writes=['PT'],
                     inc=(kc == 7))
            p.op('act', lambda e: e.copy(out=mT[:].rearrange("p a b -> p (a b)"), in_=PT[:]), reads=['PT'], writes=['mT'])
            for half in range(2):
                wb, wk = wload(prm['w_out'][l][:, half * 512:(half + 1) * 512], 512, l, ti)
                pst, pk = (PS[0], 'PS0') if half == 0 else (PS[1], 'PS1')
                for kc in range(8):
                    p.op('pe', (lambda e, kc=kc, pst=pst, wb=wb: e.matmul(pst[:], lhsT=mT[:, kc, :], rhs=wb[:, kc, 0:512],
                                                                          start=(kc == 0), stop=(kc == 7))),
                         reads=['mT', wk], writes=[pk], inc=(kc == 7))
                p.op('dve', (lambda e, half=half, pst=pst: e.tensor_tensor(out=xt[:, half * 512:(half + 1) * 512],
                                                                           in0=xt[:, half * 512:(half + 1) * 512], in1=pst[:],
                                                                           op=ALU.add)), reads=[pk, 'xt'], writes=['xt'])
            rmsnorm_to_T(xt, gmlp, hT, 'hT', l)
            for fi in range(8):
                wb, wk = wload(prm['w_up'][l][:, fi * 512:(fi + 1) * 512], 512, l, ti)
                pst, pk = (PS[2], 'PS2') if fi % 2 == 0 else (PS[3], 'PS3')
                for f4 in range(4):
                    proj_fm(wb, wk, f4 * 128, 128, pst[:, f4 * 128:(f4 + 1) * 128], pk)
                p.op('act', (lambda e, pst=pst: e.activation(out=tmpB[:], in_=pst[:], func=AF.Relu)), reads=[pk], writes=['tmpB'])
                p.op('dve', (lambda e, fi=fi, pst=pst: e.tensor_tensor(
                    out=uT[:, fi * 4:(fi + 1) * 4, :].rearrange("p a b -> p (a b)"), in0=tmpB[:], in1=pst[:], op=ALU.mult)),
                    reads=['tmpB', pk], writes=['uT'])
            for fi in range(8):
                wv, wk = wload_rows(prm['w_down'][l][fi * 512:(fi + 1) * 512, :], l, ti)
                for half in range(2):
                    pk = 'PS0' if half == 0 else 'PS1'
                    pst = PS[0] if half == 0 else PS[1]
                    for f4 in range(4):
                        p.op('pe', (lambda e, fi=fi, f4=f4, half=half, pst=pst, wv=wv: e.matmul(
                            pst[:], lhsT=uT[:, fi * 4 + f4, :], rhs=wv[:, f4, half * 512:(half + 1) * 512],
                            start=(fi == 0 and f4 == 0), stop=(fi == 7 and f4 == 3))),
                            reads=['uT', wk], writes=[pk], inc=(f4 == 3))
            for half in range(2):
                pk = 'PS0' if half == 0 else 'PS1'
                pst = PS[0] if half == 0 else PS[1]
                p.op('dve', (lambda e, half=half, pst=pst: e.tensor_tensor(out=xt[:, half * 512:(half + 1) * 512],
                                                                           in0=xt[:, half * 512:(half + 1) * 512], in1=pst[:],
                                                                           op=ALU.add)), reads=[pk, 'xt'], writes=['xt'])
        if pipe:
            to = max(ti - 1, 0)
            p.dma('sp', y_out[to * 128:(to + 1) * 128, :], xt[:], reads=['xt'], writes=['yout'])
            p.op('act', lambda e: e.activation(out=xh[:], in_=xt[:], func=AF.Copy, scale=role[:, 0:1]),
                 reads=['xt', 'role'], writes=['xh'])
            p.dma('sp', sendb[(ti + 1) % 2], xh[:], reads=['xh'], writes=[('sendb', (ti + 1) % 2)])
        else:
            p.dma('sp', y_out[ti * 128:(ti + 1) * 128, :], xt[:], reads=['xt'], writes=['yout'])
    p.final_wait('sp', ['yout'])
    p.emit()
    return nc


def host_params(inputs):
    f = lambda k: np.ascontiguousarray(np.asarray(inputs[k], dtype=np.float32))
    m = {}
    for k in ['norm_mix_g', 'w_in', 'hgrn_out_g', 'mlstm_out_g', 'attn_sinks', 'rel_bias_table', 'dn_a_log', 'dn_dt_bias',
              'dn_out_g', 'w_branch', 'w_out', 'norm_mlp_g', 'w_up', 'w_down']:
        m[k] = f(k)
    m['mlstm_if_bias'] = f('mlstm_if_bias').reshape(2, 8)
    m['hgrn_lb_table'] = np.ascontiguousarray(f('hgrn_lb_table').reshape(2, 4, 128).transpose(0, 2, 1))
    m['attn_q_norm_g'] = f('attn_q_norm_g').reshape(2, 64, 1)
    m['attn_k_norm_g'] = f('attn_k_norm_g').reshape(2, 64, 1)
    m['dn_conv_w'] = np.ascontiguousarray(f('dn_conv_w').reshape(2, 4, 12, 128).transpose(0, 1, 3, 2))
    m['dn_alog_row'] = np.ascontiguousarray(f('dn_a_log').T)
    m['dn_dtb_row'] = np.ascontiguousarray(f('dn_dt_bias').T)
    return m


LAYERED = ['norm_mix_g', 'w_in', 'hgrn_out_g', 'mlstm_out_g', 'attn_sinks', 'dn_a_log', 'dn_dt_bias', 'dn_out_g',
           'w_branch', 'w_out', 'norm_mlp_g', 'w_up', 'w_down', 'mlstm_if_bias', 'attn_q_norm_g', 'attn_k_norm_g',
           'dn_conv_w']


def kernel(**inputs):
    x = np.ascontiguousarray(np.asarray(inputs['x'], dtype=np.float32))
    B, T, _ = x.shape
    NT = T // 128
    nc = build(T, 1, pipe=True, npairs=B)
    consts = host_consts()
    hp = host_params(inputs)
    hp_role = []
    for r in range(2):
        m = dict(hp)
        if r == 1:
            for k in LAYERED:
                m[k] = np.ascontiguousarray(hp[k][::-1])
            m['dn_alog_row'] = np.ascontiguousarray(hp['dn_alog_row'][:, ::-1])
            m['dn_dtb_row'] = np.ascontiguousarray(hp['dn_dtb_row'][:, ::-1])
        hp_role.append(m)
    zeros_x = np.zeros((T, D), np.float32)
    in_maps = []
    for b in range(B):
        for r in range(2):
            m = {'x': x[b] if r == 0 else zeros_x}
            m.update(hp_role[r])
            for k, v in consts.items():
                m['c_' + k] = v
            role = np.zeros((128, 2), np.float32)
            role[:, r] = 1.0
            pf = np.ones((128, NT + 1), np.float32)
            pf[:, 0:1 + r] = 0.0
            m['role'] = role
            m['pflag'] = pf
            in_maps.append(m)
    res = run_bass_kernel_spmd(nc, in_maps, core_ids=list(range(2 * B)))
    return np.stack([np.asarray(res.results[2 * b + 1]['y'], dtype=np.float32) for b in range(B)], axis=0)
```

```python
import types
import numpy as np
from contextlib import ExitStack
import concourse.bass as bass
import concourse.mybir as mybir
from concourse.bass_utils import run_bass_kernel_spmd

F32 = mybir.dt.float32
BF16 = mybir.dt.bfloat16
AF = mybir.ActivationFunctionType
ALU = mybir.AluOpType
AX = mybir.AxisListType
ENGS = ['pe', 'act', 'dve', 'pool', 'sp']

D = 1024
NIN = 10512
EPS = 1e-6


def _freeze(fn):
    if fn is None or fn.__closure__ is None:
        return fn
    cells = []
    for c in fn.__closure__:
        try:
            cells.append(types.CellType(c.cell_contents))
        except ValueError:
            cells.append(c)
    return types.FunctionType(fn.__code__, fn.__globals__, fn.__name__, fn.__defaults__, tuple(cells))


class V3:
    def __init__(self, ap):
        self.ap = ap

    def __getitem__(self, k):
        return self.ap[k]


class Prog:
    def __init__(self, nc, nd=8):
        self.nc = nc
        self.ND = nd
        self.lists = {e: [] for e in ENGS}
        self.cnt = {e: 0 for e in ENGS}
        self.waited = {e: {} for e in ENGS}
        self.W = {}
        self.Rd = {}
        self.dma_n = {e: 0 for e in ENGS}
        self.semkeys = set()
        self.st = ExitStack()
        self.ntens = 0
        self.alias = {}

    def sb(self, shape, dt, name=None):
        self.ntens += 1
        return self.st.enter_context(self.nc.sbuf_tensor(name or f"t{self.ntens}", list(shape), dt))

    def ps(self, shape, dt=F32, name=None):
        self.ntens += 1
        return self.st.enter_context(self.nc.psum_tensor(name or f"p{self.ntens}", list(shape), dt))

    def _deps(self, eng, reads, writes):
        deps = {}

        def add(d):
            for sk, v in d.items():
                if deps.get(sk, 0) < v:
                    deps[sk] = v
        for r in reads:
            add(self.W.get(r, {}))
        for w in writes:
            add(self.W.get(w, {}))
            add(self.Rd.get(w, {}))
        out = []
        for sk, v in deps.items():
            if sk == ('c', 'pe') and eng == 'pe':
                continue
            if self.waited[eng].get(sk, 0) >= v:
                continue
            self.waited[eng][sk] = v
            out.append((sk, v))
        return out

    def _rec(self, tok, reads, writes):
        sk, v = tok
        for r in reads:
            d = self.Rd.setdefault(r, {})
            d[sk] = max(d.get(sk, 0), v)
        for w in writes:
            d = self.W.setdefault(w, {})
            d[sk] = max(d.get(sk, 0), v)

    def _x(self, keys):
        out = []
        for k in keys:
            if isinstance(k, (list, tuple)):
                out.extend(self._x(k))
            elif k in self.alias:
                out.extend(self.alias[k])
            else:
                out.append(k)
        return out

    def op(self, eng, fn, reads=(), writes=(), inc=True):
        reads = self._x(reads)
        writes = self._x(writes)
        waits = self._deps(eng, reads, writes)
        sk = ('c', eng)
        tok = (sk, self.cnt[eng] + 1)
        if inc:
            self.cnt[eng] += 1
        self.semkeys.add(sk)
        self.lists[eng].append((waits, _freeze(fn), sk if inc else None, 1))
        self._rec(tok, reads, writes)

    def dma(self, q, out, in_, reads=(), writes=(), **kw):
        reads = self._x(reads)
        writes = self._x(writes)
        j = self.dma_n[q]
        self.dma_n[q] += 1
        slot = j % self.ND
        val = 16 * (j // self.ND + 1)
        sk = ('d', q, slot)
        self.semkeys.add(sk)
        waits = self._deps(q, reads, writes)
        if j >= self.ND and self.waited[q].get(sk, 0) < val - 16:
            self.waited[q][sk] = val - 16
            waits.append((sk, val - 16))
        self.lists[q].append((waits, (lambda e, o=out, i=in_, k=kw: e.dma_start(out=o, in_=i, **k)), sk, 16))
        self._rec((sk, val), reads, writes)

    def coll(self, kind, op, groups, ins_ap, outs_ap, reads=(), writes=()):
        reads = self._x(reads)
        writes = self._x(writes)
        waits = self._deps('pool', reads, writes)
        sk = ('cc',)
        self.cc_n = getattr(self, 'cc_n', 0) + 1
        self.semkeys.add(sk)
        self.lists['pool'].append((waits, (lambda e: e.collective_compute(kind, op, replica_groups=groups, ins=[ins_ap.opt()],
                                                                          outs=[outs_ap.opt()])), sk, 1))
        self._rec((sk, self.cc_n), reads, writes)

    def final_wait(self, eng, keys):
        keys = self._x(keys)
        waits = self._deps(eng, keys, ())
        self.lists[eng].append((waits, None, None, 0))

    def emit(self):
        nc = self.nc
        st = self.st
        sems = {}
        for sk in sorted(self.semkeys, key=str):
            sems[sk] = st.enter_context(nc.semaphore("s_" + "_".join(map(str, sk))))
        block = st.enter_context(nc.Block())
        lists = self.lists

        def run(name, e):
            for waits, fn, sk, incv in lists[name]:
                for wsk, v in waits:
                    e.wait_ge(sems[wsk], v)
                if fn is None:
                    continue
                ins = fn(e)
                if sk is not None:
                    ins.then_inc(sems[sk], incv)

        @block.tensor
        def _(e):
            run('pe', e)

        @block.scalar
        def _(e):
            run('act', e)

        @block.vector
        def _(e):
            run('dve', e)

        @block.gpsimd
        def _(e):
            run('pool', e)

        @block.sync
        def _(e):
            run('sp', e)
        st.close()


def _t5_bucket_np(n):
    max_exact = 16
    nf = np.maximum(n, max_exact).astype(np.float32)
    large = max_exact + (np.log(nf / max_exact) / np.log(np.float32(128 / max_exact)) * 16).astype(np.int32)
    large = np.minimum(large, 31)
    return np.where(n < max_exact, n, large)


def host_consts():
    c = {}
    s = np.arange(128)[:, None]
    t = np.arange(128)[None, :]
    c['tri_incl'] = (s <= t).astype(np.float32)
    c['tri_strict'] = (s < t).astype(np.float32)
    c['hgmask'] = ((s <= t) & (s // 32 == t // 32)).astype(np.float32)
    c['negmask'] = np.where(s <= t, 0.0, -1e30).astype(np.float32)
    c['ident'] = np.eye(128, dtype=np.float32)
    sel = np.zeros((128, 128), np.float32)
    sel[127, :] = 1.0
    c['sel_last'] = sel
    rm = np.zeros((128, 4), np.float32)
    for j in range(4):
        rm[32 * j:32 * j + 32, j] = 1.0
    c['rowm'] = rm
    rs = np.ones((128, 128), np.float32)
    rs[:, 0::32] = 0.0
    c['resetm'] = rs
    selh = np.zeros((4, 4, 128), np.float32)
    for h in range(4):
        selh[h, h, :] = 1.0
    c['selh'] = selh.reshape(4, 512)
    c['ones'] = np.ones((128, 128), np.float32)
    bk = _t5_bucket_np(np.arange(128))
    oh = np.zeros((32, 128), np.float32)
    oh[bk, np.arange(128)] = 1.0
    c['bias_oh'] = oh
    ab = np.zeros((128, 384), np.float32)
    for dd in range(128):
        ab[dd, 255 - dd] = 1.0
    c['antiband'] = ab
    return c


CONST_SHAPES = {
    'tri_incl': (128, 128), 'tri_strict': (128, 128), 'hgmask': (128, 128), 'negmask': (128, 128),
    'ident': (128, 128), 'sel_last': (128, 128), 'rowm': (128, 4), 'resetm': (128, 128),
    'selh': (4, 512), 'ones': (128, 128), 'bias_oh': (32, 128), 'antiband': (128, 384),
}

PARAM_SHAPES = {
    'norm_mix_g': (2, 1024), 'w_in': (2, 1024, NIN), 'hgrn_lb_table': (2, 128, 4), 'hgrn_out_g': (2, 512),
    'mlstm_if_bias': (2, 8), 'mlstm_out_g': (2, 512), 'attn_q_norm_g': (2, 64, 1), 'attn_k_norm_g': (2, 64, 1),
    'attn_sinks': (2, 8), 'rel_bias_table': (32, 8), 'dn_conv_w': (2, 4, 128, 12), 'dn_a_log': (2, 4),
    'dn_dt_bias': (2, 4), 'dn_alog_row': (4, 2), 'dn_dtb_row': (4, 2), 'dn_out_g': (2, 128), 'w_branch': (2, 4, 512, 1024), 'w_out': (2, 1024, 1024),
    'norm_mlp_g': (2, 1024), 'w_up': (2, 1024, 4096), 'w_down': (2, 4096, 1024),
}

C_HQ, C_HF, C_HI, C_HG = 0, 512, 1024, 1536
C_MQ, C_MK, C_MV, C_MI, C_MF, C_MO = 2048, 2304, 2560, 3072, 3076, 3080
C_AQ, C_AK, C_AV = 3592, 4104, 4232
C_DQKV, C_DB, C_DA, C_DZ = 4360, 5896, 5900, 5904
C_GATE = 6416


def build(T, L=2, enable=(1, 1, 1, 1), dbg=None, pipe=False, npairs=4):
    nc = bass.Bass("TRN2", target_bir_lowering=False)
    NTILES = T // 128
    x_in = nc.dram_tensor("x", [T, D], F32, kind="ExternalInput").ap()
    y_out = nc.dram_tensor("y", [T, D], F32, kind="ExternalOutput").ap()
    prm = {k: nc.dram_tensor(k, list(s), F32, kind="ExternalInput").ap() for k, s in PARAM_SHAPES.items()}
    cst = {k: nc.dram_tensor("c_" + k, list(s), F32, kind="ExternalInput").ap() for k, s in CONST_SHAPES.items()}
    NIT = NTILES + 1 if pipe else NTILES
    if pipe:
        assert L == 1
        role_in = nc.dram_tensor("role", [128, 2], F32, kind="ExternalInput").ap()
        pflag_in = nc.dram_tensor("pflag", [128, NIT], F32, kind="ExternalInput").ap()
        sendb = nc.dram_tensor("sendb", [2, 128, 1024], F32).ap()
        recvb = nc.dram_tensor("recvb", [2, 256, 1024], F32).ap()
    dbg_out = {}
    p = Prog(nc)
    sb, ps = p.sb, p.ps

    def load_const(name, dt=F32, q='sp'):
        shp = CONST_SHAPES[name]
        t_ = sb(shp, dt, "k_" + name + ("_b" if dt == BF16 else ""))
        p.dma(q, t_[:], cst[name], writes=['k_' + name])
        return t_
    tri_incl = load_const('tri_incl')
    tri_strict = load_const('tri_strict')
    hgmask = load_const('hgmask')
    negmask = load_const('negmask')
    ident_f = load_const('ident')
    ident_b = load_const('ident', BF16, 'pool')
    sel_last = load_const('sel_last')
    rowm = load_const('rowm')
    resetm = load_const('resetm')
    selh = load_const('selh')
    ones_f = load_const('ones')
    ones_b = load_const('ones', BF16, 'pool')
    KC = ['k_tri_incl', 'k_tri_strict', 'k_hgmask', 'k_negmask', 'k_ident', 'k_sel_last', 'k_rowm', 'k_resetm',
          'k_selh', 'k_ones']

    gmix = sb([128, L, 1024], BF16, "gmix")
    gmlp = sb([128, L, 1024], BF16, "gmlp")
    hog = sb([128, L, 512], BF16, "hog")
    mog = sb([128, L, 512], BF16, "mog")
    dog = sb([128, L, 128], F32, "dog")
    mifb = sb([128, L, 8], F32, "mifb")
    sinks = sb([128, L, 8], F32, "sinks")
    esink = sb([128, L, 8], F32, "esink")
    alog = sb([128, L, 4], F32, "alog")
    nexpa = sb([128, L, 4], F32, "nexpa")
    dtb = sb([128, L, 4], F32, "dtb")
    lbt = sb([128, 2, 4], F32, "lbt")
    lb = sb([128, 2, 4], F32, "lb")
    oml = sb([128, 2, 4], F32, "oml")
    convw = sb([128, L, 4, 12], F32, "convw")
    qkg = sb([64, L, 2], F32, "qkg")
    gk8 = sb([64, L], F32, "gk8")
    dtbrow = sb([4, 2], F32, "dtbrow")
    nexparow = sb([4, 2], F32, "nexparow")
    for l in range(L):
        p.dma('pool', gmix[:, l, :], prm['norm_mix_g'][l:l + 1, :].partition_broadcast(128), writes=['prm'])
        p.dma('pool', gmlp[:, l, :], prm['norm_mlp_g'][l:l + 1, :].partition_broadcast(128), writes=['prm'])
        p.dma('pool', hog[:, l, :], prm['hgrn_out_g'][l:l + 1, :].partition_broadcast(128), writes=['prm'])
        p.dma('pool', mog[:, l, :], prm['mlstm_out_g'][l:l + 1, :].partition_broadcast(128), writes=['prm'])
        p.dma('sp', dog[:, l, :], prm['dn_out_g'][l:l + 1, :].partition_broadcast(128), writes=['prm'])
        p.dma('sp', mifb[:, l, :], prm['mlstm_if_bias'][l:l + 1, :].partition_broadcast(128), writes=['prm'])
        p.dma('sp', sinks[:, l, :], prm['attn_sinks'][l:l + 1, :].partition_broadcast(128), writes=['prm'])
        p.dma('sp', alog[:, l, :], prm['dn_a_log'][l:l + 1, :].partition_broadcast(128), writes=['prm'])
        p.dma('sp', dtb[:, l, :], prm['dn_dt_bias'][l:l + 1, :].partition_broadcast(128), writes=['prm'])
        for j in range(4):
            p.dma('sp', convw[:, l, j, :], prm['dn_conv_w'][l, j], writes=['prm'])
        p.dma('sp', qkg[:, l, 0:1], prm['attn_q_norm_g'][l], writes=['prm'])
        p.dma('sp', qkg[:, l, 1:2], prm['attn_k_norm_g'][l], writes=['prm'])
    for l2 in range(2):
        p.dma('sp', lbt[:, l2, :], prm['hgrn_lb_table'][l2], writes=['prm'])
    p.dma('sp', dtbrow[:], prm['dn_dtb_row'], writes=['dtbrow'])
    p.dma('sp', nexparow[:], prm['dn_alog_row'], writes=['nexparow'])
    p.op('act', lambda e: e.activation(out=nexparow[:], in_=nexparow[:], func=AF.Exp), reads=['nexparow'], writes=['nexparow'])
    p.op('dve', lambda e: e.tensor_scalar(out=nexparow[:], in0=nexparow[:], scalar1=-1.0, scalar2=None, op0=ALU.mult),
         reads=['nexparow'], writes=['nexparow'])
    p.op('dve', lambda e: e.memset(lb[:], 0.0), writes=['lb'])
    p.op('dve', lambda e: e.tensor_sub(out=lb[:, 1, :], in0=lbt[:, 1, :], in1=lbt[:, 0, :]), reads=['prm', 'lb'],
         writes=['lb'])
    p.op('act', lambda e: e.activation(out=lb[:, 1, :], in_=lb[:, 1, :], func=AF.Sigmoid), reads=['lb'], writes=['lb'])
    if pipe:
        role = sb([128, 2], F32, "role_sb")
        pflag = sb([128, NIT], F32, "pflag_sb")
        p.dma('sp', role[:], role_in, writes=['role'])
        p.dma('sp', pflag[:], pflag_in, writes=['pflag'])
        p.op('dve', lambda e: e.tensor_scalar(out=lb[:, 0, :], in0=lb[:, 1, :], scalar1=role[:, 1:2], scalar2=None,
                                              op0=ALU.mult), reads=['lb', 'role'], writes=['lb'])
    p.op('dve', lambda e: e.tensor_scalar(out=oml[:], in0=lb[:], scalar1=-1.0, scalar2=1.0, op0=ALU.mult, op1=ALU.add),
         reads=['lb'], writes=['oml'])
    p.op('act', lambda e: e.activation(out=esink[:], in_=sinks[:], func=AF.Exp), reads=['prm'], writes=['esink'])
    p.op('act', lambda e: e.activation(out=nexpa[:], in_=alog[:], func=AF.Exp), reads=['prm'], writes=['nexpa'])
    p.op('dve', lambda e: e.tensor_scalar(out=nexpa[:], in0=nexpa[:], scalar1=-1.0, scalar2=None, op0=ALU.mult),
         reads=['nexpa'], writes=['nexpa'])
    p.op('dve', lambda e: e.tensor_tensor(out=gk8[:], in0=qkg[:, :, 0], in1=qkg[:, :, 1], op=ALU.mult), reads=['prm'],
         writes=['gk8'])
    p.op('dve', lambda e: e.tensor_scalar(out=gk8[:], in0=gk8[:], scalar1=0.125, scalar2=None, op0=ALU.mult),
         reads=['gk8'], writes=['gk8'])

    PS = [ps([128, 512], F32, f"PS{i}") for i in range(7)]
    PT = ps([128, 1024], BF16, "PSTR")

    def K(i, lo=0, hi=512):
        return [f'PS{i}.bank']
    for i in range(7):
        p.alias[f'PS{i}'] = K(i)
    p.alias['PS5b'] = K(5, 128, 256)
    p.alias['PS5c'] = K(5, 256, 384)
    p.alias['PS6b'] = K(6, 128, 256)
    p.alias['PS6c'] = K(6, 256, 384)
    p.alias['PS6d'] = K(6, 384, 512)
    for i in range(5):
        p.alias[f'scr{i}'] = [f'scr.{i}']
    for nm in ['tmpA', 'tmpB', 'junk']:
        p.alias[nm] = [f'{nm}.{i}' for i in range(4)]
    p.alias['h_S'] = [f'h_S.{i}' for i in range(4)]
    p.alias['h_Sb'] = [f'h_Sb.{i}' for i in range(4)]
    p.alias['h_qj'] = [f'h_qj.{i}' for i in range(4)]
    p.alias['m_C'] = [f'm_C.{i}' for i in range(4)]
    p.alias['m_Cb'] = [f'm_Cb.{i}' for i in range(4)]
    p.alias['h_q'] = ['scr.0']
    p.alias['h_f'] = ['scr.1']
    p.alias['h_k'] = ['scr.2']
    p.alias['h_cum'] = ['scr.3']
    p.alias['h_e'] = ['scr.4']
    p.alias['a_z'] = ['scr.0', 'scr.1', 'scr.2']
    p.alias['a_sq'] = ['scr.2', 'scr.3', 'scr.4']
    p.alias['d_y'] = ['scr.0', 'scr.1', 'scr.2']
    EB = sb([128, 2, 8, 128], F32, "EB")
    relt = sb([32, 8], F32, "relt")
    boh = sb([32, 128], F32, "boh")
    aband = sb([128, 384], F32, "aband")
    vecE = sb([128, 8], F32, "vecE")
    p.dma('sp', relt[:], prm['rel_bias_table'], writes=['relt'])
    p.dma('sp', boh[:], cst['bias_oh'], writes=['boh'])
    p.dma('sp', aband[:], cst['antiband'], writes=['aband'])
    p.op('pe', lambda e: e.matmul(PS[0][:, 0:8], lhsT=boh[:], rhs=relt[:], start=True, stop=True), reads=['boh', 'relt'],
         writes=K(0))
    p.op('act', lambda e: e.activation(out=vecE[:], in_=PS[0][:, 0:8], func=AF.Exp), reads=K(0), writes=['vecE'])
    for blk in range(2):
        for t0 in range(0, 128, 64):
            pst = PS[1]
            for tt in range(64):
                off = 255 - (t0 + tt + (128 if blk == 0 else 0))
                p.op('pe', (lambda e, off=off, tt=tt: e.matmul(pst[:, tt * 8:(tt + 1) * 8], lhsT=aband[:, off:off + 128],
                                                               rhs=vecE[:], start=True, stop=True)),
                     reads=['aband', 'vecE'], writes=K(1), inc=(tt == 63))
            p.op('dve', (lambda e, blk=blk, t0=t0: e.tensor_copy(
                out=EB[:, blk, :, t0:t0 + 64], in_=pst[:].rearrange("p (t g) -> p g t", g=8))),
                reads=K(1), writes=['EB'])

    xt = sb([128, 1024], F32, "xt")
    hbf = sb([128, 1024], BF16, "hbf")
    hT = sb([128, 8, 128], BF16, "hT")
    NWB = 6
    wbuf = [sb([128, 8, 520], BF16, f"wbuf{i}") for i in range(NWB)]
    wb_n = [0]
    sm = sb([128, 128], F32, "small")
    obf = sb([128, 4, 512], BF16, "obf")
    oT = sb([128, 16, 128], BF16, "oT")
    gates = sb([128, 4096], BF16, "gates")
    merged = sb([128, 1024], F32, "merged")
    mbf = sb([128, 1024], BF16, "mbf")
    mT = sb([128, 8, 128], BF16, "mT")
    uT = sb([128, 32, 128], BF16, "uT")
    tmpA = sb([128, 512], F32, "tmpA")
    tmpB = sb([128, 512], F32, "tmpB")
    tmpC = sb([128, 512], F32, "tmpC")
    junk = sb([128, 1024], BF16, "junk")

    scr = sb([128, 2560], F32, "scr")

    class V:
        def __init__(self, ap):
            self.ap = ap

        def __getitem__(self, k):
            return self.ap[k]
    h_q = V(scr[:, 0:512].rearrange("p (a b) -> p a b", a=4))
    h_f = V(scr[:, 512:1024].rearrange("p (a b) -> p a b", a=4))
    h_k = V(scr[:, 1024:1536].rearrange("p (a b) -> p a b", a=4))
    h_cum = V(scr[:, 1536:2048].rearrange("p (a b) -> p a b", a=4))
    h_e = V(scr[:, 2048:2560].rearrange("p (a b) -> p a b", a=4))
    h_qp = sb([128, 4, 128], BF16, "h_qp")
    h_kp = sb([128, 4, 128], BF16, "h_kp")
    h_kpp = sb([128, 4, 128], BF16, "h_kpp")
    h_edec = sb([128, 4, 4], F32, "h_edec")
    h_v = sb([128, 512], BF16, "h_v")
    h_g = sb([128, 512], F32, "h_g")
    hS_AT = [sb([128, 128], BF16, f"h_AT{i}") for i in range(2)]
    hS_kj = [sb([128, 4, 128], BF16, f"h_kj{i}") for i in range(2)]
    h_qj = sb([128, 4, 4, 128], BF16, "h_qj")
    h_S = [sb([128, L, 4, 128], F32, "h_S")]
    h_Sb = sb([128, L, 4, 128], BF16, "h_Sb")

    m_q = sb([64, 4, 128], BF16, "m_q")
    m_kT = sb([64, 4, 128], BF16, "m_kT")
    m_k = sb([128, 256], BF16, "m_k")
    m_v = sb([128, 512], F32, "m_v")
    m_if = sb([128, 8], F32, "m_if")
    m_o = sb([128, 512], F32, "m_o")
    m_va = sb([128, 4, 130], BF16, "m_va")
    mS_AT = [sb([128, 128], BF16, f"m_AT{i}") for i in range(2)]
    m_C = sb([64, L, 4, 130], F32, "m_C")
    m_Cb = sb([64, L, 4, 130], BF16, "m_Cb")

    a_q = sb([64, 8, 128], BF16, "a_q")
    a_z = V(scr[0:64, 0:1280].rearrange("p (a b) -> p a b", a=10))
    a_sq = V(scr[0:64, 1280:2560].rearrange("p (a b) -> p a b", a=10))
    a_kT = sb([64, L, 2, 2, 128], BF16, "a_kT")
    a_v = sb([128, L, 2, 2, 66], BF16, "a_v")
    aS_P = [sb([128, 256], F32, f"a_P{i}") for i in range(2)]
    aS_PT = [sb([128, 2, 128], BF16, f"a_PT{i}") for i in range(2)]

    d_x = sb([128, L, 12, 132], BF16, "d_x")
    d_y = V(scr[:, 0:1536].rearrange("p (a b) -> p a b", a=12))
    d_sq = sb([128, 128], BF16, "d_sq")
    d_qT = sb([128, 4, 128], BF16, "d_qT")
    d_kT = sb([128, 4, 128], BF16, "d_kT")
    d_vT = sb([128, 4, 128], BF16, "d_vT")
    d_v = sb([128, 4, 128], F32, "d_v")
    d_k = sb([128, 4, 128], BF16, "d_k")
    d_ba = sb([128, 8], F32, "d_ba")
    d_arow = sb([4, 128], F32, "d_arow")
    d_grow = sb([4, 128], F32, "d_grow")
    d_z = sb([128, 512], F32, "d_z")
    dS_E1 = [sb([128, 128], F32, f"d_E1_{i}") for i in range(2)]
    dS_E1s = [sb([128, 128], F32, f"d_E1s_{i}") for i in range(2)]
    dS_P = [[sb([128, 128], F32, f"d_P{j}_{i}") for j in range(2)] for i in range(2)]
    dS_PT = [[sb([128, 128], F32, f"d_PT{j}_{i}") for j in range(2)] for i in range(2)]
    dS_TT = [sb([128, 128], F32, f"d_TT_{i}") for i in range(2)]
    dS_r = [sb([128, 128], F32, f"d_r_{i}") for i in range(2)]
    dS_vn = [sb([128, 128], BF16, f"d_vn_{i}") for i in range(2)]
    dS_vc = [sb([128, 128], BF16, f"d_vc_{i}") for i in range(2)]
    dS_aT = [sb([128, 128], BF16, f"d_aT_{i}") for i in range(2)]
    dS_o1 = [sb([128, 128], F32, f"d_o1_{i}") for i in range(2)]
    d_S = sb([128, L, 4, 128], F32, "d_S")
    d_Sb = sb([128, L, 4, 128], BF16, "d_Sb")

    for (t_, k) in [(h_S[0], 'h_S'), (h_Sb, 'h_Sb'), (m_C, 'm_C'), (m_Cb, 'm_Cb'), (d_S, 'd_S'), (d_Sb, 'd_Sb'),
                    (d_x, 'd_x'), (h_qj, 'h_qj'), (a_kT, 'a_kT'), (a_v, 'a_v')]:
        p.op('pool', (lambda e, t_=t_: e.memset(t_[:], 0.0)), writes=[k])

    NPIECE = 43
    wscr = nc.dram_tensor("wscr", [L, NPIECE, 128, 4160], BF16, kind="Internal").ap()
    piece_ctr = {}

    def _piece(l_, ti_):
        k = (l_, ti_)
        i = piece_ctr.get(k, 0)
        piece_ctr[k] = i + 1
        assert i < NPIECE
        return i

    def wload(src_ap, ncol, l_, ti_):
        pi = _piece(l_, ti_)
        scr_ap = wscr[l_, pi, :, 0:8 * ncol]
        skey = ('wscr', l_, pi)
        if ti_ == 0:
            p.dma('pool', scr_ap.rearrange("p (kc n) -> p kc n", kc=8), src_ap.rearrange("(kc p) n -> p kc n", p=128),
                  writes=[skey])
        i = wb_n[0] % NWB
        wb_n[0] += 1
        key = f'wbuf{i}'
        dst = wbuf[i][:].rearrange("p a b -> p (a b)")[:, 0:8 * ncol]
        p.dma('sp', dst, scr_ap, reads=[skey], writes=[key])
        return V3(dst.rearrange("p (kc n) -> p kc n", kc=8)), key

    def wload_rows(src_ap, l_, ti_):
        pi = _piece(l_, ti_)
        scr_ap = wscr[l_, pi, :, 0:4096]
        skey = ('wscr', l_, pi)
        if ti_ == 0:
            p.dma('pool', scr_ap.rearrange("p (r n) -> p r n", r=4), src_ap.rearrange("(r p) n -> p r n", p=128),
                  writes=[skey])
        i = wb_n[0] % NWB
        wb_n[0] += 1
        key = f'wbuf{i}'
        dst = wbuf[i][:].rearrange("p a b -> p (a b)")[:, 0:4096]
        p.dma('sp', dst, scr_ap, reads=[skey], writes=[key])
        return V3(dst.rearrange("p (r n) -> p r n", r=4)), key

    def proj_fm(wb, wkey, c0, ncol, ps_ap, pskey):
        for kc in range(8):
            p.op('pe', (lambda e, kc=kc: e.matmul(ps_ap, lhsT=wb[:, kc, c0:c0 + ncol], rhs=hT[:, kc, :],
                                                  start=(kc == 0), stop=(kc == 7))),
                 reads=[wkey, 'hT'], writes=[pskey], inc=(kc == 7))

    def proj_tm(wb, wkey, c0, ncol, ps_ap, pskey):
        for kc in range(8):
            p.op('pe', (lambda e, kc=kc: e.matmul(ps_ap, lhsT=hT[:, kc, :], rhs=wb[:, kc, c0:c0 + ncol],
                                                  start=(kc == 0), stop=(kc == 7))),
                 reads=[wkey, 'hT'], writes=[pskey], inc=(kc == 7))

    def rmsnorm_to_T(src, gt, dstT, dstkey, l):
        p.op('act', lambda e: e.activation(out=junk[:], in_=src[:], func=AF.Square, accum_out=sm[:, 0:1]),
             reads=['xt'], writes=['junk.0', 'junk.1', 'junk.2', 'junk.3', 'sm0'])
        p.op('act', lambda e: e.activation(out=sm[:, 1:2], in_=sm[:, 0:1], func=AF.Ln, scale=1.0 / 1024, bias=epsb[:, 0:1]),
             reads=['sm0', 'epsb'], writes=['sm1'])
        p.op('act', lambda e: e.activation(out=sm[:, 2:3], in_=sm[:, 1:2], func=AF.Exp, scale=-0.5),
             reads=['sm1'], writes=['sm2'])
        p.op('dve', lambda e: e.scalar_tensor_tensor(out=hbf[:], in0=src[:], scalar=sm[:, 2:3], in1=gt[:, l, :],
                                                     op0=ALU.mult, op1=ALU.mult),
             reads=['xt', 'sm2', 'prm'], writes=['hbf'])
        for kc in range(8):
            p.op('pe', (lambda e, kc=kc: e.transpose(out=PT[:, kc * 128:(kc + 1) * 128], in_=hbf[:, kc * 128:(kc + 1) * 128],
                                                     identity=ident_b[:])),
                 reads=['hbf', 'k_ident'], writes=['PT'], inc=(kc == 7))
        p.op('act', lambda e: e.copy(out=dstT[:].rearrange("p a b -> p (a b)"), in_=PT[:]), reads=['PT'], writes=[dstkey])

    epsb = sb([128, 2], F32, "epsb")
    p.op('dve', lambda e: e.memset(epsb[:, 0:1], EPS), writes=['epsb'])
    p.op('dve', lambda e: e.memset(epsb[:, 1:2], 1.0), writes=['epsb'])

    def head_norm_gate(src_ap, srckey, srcreads, gtile_ap, gate_ap, gatekey, out_ap, smc, sl=0):
        jk = junk[:, sl * 128:(sl + 1) * 128]
        tc_ = tmpC[:, sl * 128:(sl + 1) * 128]
        p.op('act', lambda e: e.activation(out=jk, in_=src_ap, func=AF.Square, accum_out=sm[:, smc:smc + 1]),
             reads=[srckey] + srcreads, writes=[f'junk.{sl}', f'sm{smc}'])
        p.op('act', lambda e: e.activation(out=sm[:, smc + 1:smc + 2], in_=sm[:, smc:smc + 1], func=AF.Ln, scale=1.0 / 128,
                                           bias=epsb[:, 0:1]), reads=[f'sm{smc}', 'epsb'], writes=[f'sm{smc+1}'])
        p.op('act', lambda e: e.activation(out=sm[:, smc + 2:smc + 3], in_=sm[:, smc + 1:smc + 2], func=AF.Exp, scale=-0.5),
             reads=[f'sm{smc+1}'], writes=[f'sm{smc+2}'])
        p.op('dve', lambda e: e.scalar_tensor_tensor(out=tc_, in0=src_ap, scalar=sm[:, smc + 2:smc + 3],
                                                     in1=gtile_ap, op0=ALU.mult, op1=ALU.mult),
             reads=[srckey, f'sm{smc+2}', 'prm'] + srcreads, writes=[f'tmpC.{sl}'])
        p.op('dve', lambda e: e.tensor_tensor(out=out_ap, in0=tc_, in1=gate_ap, op=ALU.mult),
             reads=[f'tmpC.{sl}', gatekey], writes=['obf'])

    def interleave(gens):
        gens = list(gens)
        while gens:
            for g in list(gens):
                try:
                    next(g)
                except StopIteration:
                    gens.remove(g)

    if pipe:
        xh = sb([128, 1024], F32, "xh")
        xr = sb([128, 1024], F32, "xr")
        p.op('pool', lambda e: e.memset(xr[:], 0.0), writes=['xr'])
        for j in range(2):
            p.dma('sp', sendb[j], xr[:], reads=['xr'], writes=[('sendb', j)])
    for ti in range(NIT):
        if pipe:
            tix = min(ti, NTILES - 1)
            p.dma('sp', xh[:], x_in[tix * 128:(tix + 1) * 128, :], writes=['xh'])
            p.coll("AllGather", ALU.bypass, [[2 * i_, 2 * i_ + 1] for i_ in range(npairs)], sendb[ti % 2], recvb[ti % 2],
                   reads=[('sendb', ti % 2)], writes=[('recvb', ti % 2)])
            p.dma('sp', xr[:], recvb[ti % 2][0:128, :], reads=[('recvb', ti % 2)], writes=['xr'])
            p.op('dve', lambda e: e.scalar_tensor_tensor(out=xt[:], in0=xr[:], scalar=role[:, 1:2], in1=xh[:], op0=ALU.mult,
                                                         op1=ALU.add), reads=['xr', 'xh', 'role'], writes=['xt'])
        else:
            p.dma('sp', xt[:], x_in[ti * 128:(ti + 1) * 128, :], writes=['xt'])
        for l in range(L):
            par = ti % 2
            W_in = prm['w_in'][l]
            rmsnorm_to_T(xt, gmix, hT, 'hT', l)
            gates_done = [0]

            def gate_piece(gi):
                wb, wk = wload(W_in[:, C_GATE + gi * 512:C_GATE + (gi + 1) * 512], 512, l, ti)
                pst, pk = (PS[0], 'PS0') if gi % 2 == 0 else (PS[1], 'PS1')
                proj_tm(wb, wk, 0, 512, pst[:], pk)
                p.op('act', lambda e: e.activation(out=gates[:, gi * 512:(gi + 1) * 512], in_=pst[:], func=AF.Sigmoid),
                     reads=[pk], writes=['gates'])
                gates_done[0] = gi + 1
            if not all(enable):
                p.op('pool', lambda e: e.memset(obf[:], 0.0), writes=['obf'])

            if enable[0]:
                wb, wk = wload(W_in[:, C_HQ:C_HQ + 512], 512, l, ti)
                for h in range(4):
                    proj_fm(wb, wk, h * 128, 128, PS[0][:, h * 128:(h + 1) * 128], 'PS0')
                p.op('act', lambda e: e.activation(out=h_q[:].rearrange("p a b -> p (a b)"), in_=PS[0][:], func=AF.Silu),
                     reads=['PS0'], writes=['h_q'])
                wb, wk = wload(W_in[:, C_HF:C_HF + 512], 512, l, ti)
                for h in range(4):
                    proj_fm(wb, wk, h * 128, 128, PS[1][:, h * 128:(h + 1) * 128], 'PS1')
                p.op('act', lambda e: e.activation(out=h_f[:].rearrange("p a b -> p (a b)"), in_=PS[1][:], func=AF.Sigmoid),
                     reads=['PS1'], writes=['h_f'])
                for h in range(4):
                    p.op('dve', (lambda e, h=h: e.tensor_scalar(out=h_f[:, h, :], in0=h_f[:, h, :], scalar1=oml[:, l, h:h + 1],
                                                                scalar2=lb[:, l, h:h + 1], op0=ALU.mult, op1=ALU.add)),
                         reads=['h_f', 'oml', 'lb'], writes=['h_f'])
                hf2 = h_f[:].rearrange("p a b -> p (a b)")
                hk2 = h_k[:].rearrange("p a b -> p (a b)")
                hc2 = h_cum[:].rearrange("p a b -> p (a b)")
                he2 = h_e[:].rearrange("p a b -> p (a b)")
                hq2 = h_q[:].rearrange("p a b -> p (a b)")
                p.op('dve', lambda e: e.tensor_scalar(out=hk2, in0=hf2, scalar1=-1.0, scalar2=1.0, op0=ALU.mult, op1=ALU.add),
                     reads=['h_f'], writes=['h_k'])
                p.op('act', lambda e: e.activation(out=hf2, in_=hf2, func=AF.Ln), reads=['h_f', 'h_k'], writes=['h_f'])
                for h in range(4):
                    p.op('dve', (lambda e, h=h: e.tensor_tensor_scan(out=h_cum[:, h, :], data0=resetm[:], data1=h_f[:, h, :],
                                                                     initial=0.0, op0=ALU.mult, op1=ALU.add)),
                         reads=['h_f', 'k_resetm'], writes=['h_cum'])
                p.op('act', lambda e: e.activation(out=he2, in_=hc2, func=AF.Exp), reads=['h_cum'], writes=['h_e'])
                p.op('dve', lambda e: e.tensor_tensor(out=h_qp[:].rearrange("p a b -> p (a b)"), in0=hq2, in1=he2, op=ALU.mult),
                     reads=['h_q', 'h_e'], writes=['h_qp'])
                p.op('act', lambda e: e.activation(out=he2, in_=hc2, func=AF.Exp, scale=-1.0), reads=['h_cum', 'h_qp'],
                     writes=['h_e'])
                p.op('dve', lambda e: e.tensor_tensor(out=h_kp[:].rearrange("p a b -> p (a b)"), in0=hk2, in1=he2, op=ALU.mult),
                     reads=['h_k', 'h_e'], writes=['h_kp'])
                cl = h_cum[:].rearrange("p a (j i) -> p a j i", i=32)[:, :, :, 31]
                p.op('act', lambda e: e.activation(out=h_edec[:], in_=cl, func=AF.Exp), reads=['h_cum'], writes=['h_edec'])
                p.op('dve', lambda e: e.tensor_tensor(
                    out=h_e[:].rearrange("p a (j i) -> p a j i", i=32),
                    in0=h_cum[:].rearrange("p a (j i) -> p a j i", i=32)[:, :, :, 31:32].broadcast_to([128, 4, 4, 32]),
                    in1=h_cum[:].rearrange("p a (j i) -> p a j i", i=32), op=ALU.subtract),
                    reads=['h_cum', 'h_kp'], writes=['h_e'])
                p.op('act', lambda e: e.activation(out=he2, in_=he2, func=AF.Exp), reads=['h_e'], writes=['h_e'])
                p.op('dve', lambda e: e.tensor_tensor(out=h_kpp[:].rearrange("p a b -> p (a b)"), in0=hk2, in1=he2, op=ALU.mult),
                     reads=['h_k', 'h_e'], writes=['h_kpp'])
                wb, wk = wload(W_in[:, C_HI:C_HI + 512], 512, l, ti)
                proj_tm(wb, wk, 0, 512, PS[0][:], 'PS0')
                p.op('act', lambda e: e.copy(out=h_v[:], in_=PS[0][:]), reads=['PS0'], writes=['h_v'])
                wb, wk = wload(W_in[:, C_HG:C_HG + 512], 512, l, ti)
                proj_tm(wb, wk, 0, 512, PS[1][:], 'PS1')
                p.op('act', lambda e: e.activation(out=h_g[:], in_=PS[1][:], func=AF.Silu), reads=['PS1'], writes=['h_g'])
                def hg_head(h, sl):
                    bX, bO = PS[2 + 2 * sl], PS[3 + 2 * sl]
                    kX, kO = f'PS{2 + 2 * sl}', f'PS{3 + 2 * sl}'
                    AT, kj = hS_AT[sl], hS_kj[sl]
                    n = f'.{sl}'
                    p.op('pe', lambda e: e.matmul(bX[:, 0:128], lhsT=h_kp[:, h, :], rhs=h_qp[:, h, :], start=True, stop=True),
                         reads=['h_kp', 'h_qp'], writes=[kX])
                    p.op('dve', lambda e: e.tensor_tensor(out=AT[:], in0=bX[:, 0:128], in1=hgmask[:], op=ALU.mult),
                         reads=[kX, 'k_hgmask'], writes=['h_AT' + n])
                    p.op('pe', lambda e: e.transpose(out=PT[:, sl * 128:(sl + 1) * 128], in_=h_kpp[:, h, :], identity=ident_b[:]),
                         reads=['h_kpp', 'k_ident'], writes=['PT'])
                    for j in range(4):
                        p.op('dve', lambda e: e.tensor_scalar(out=kj[:, j, :], in0=PT[:, sl * 128:(sl + 1) * 128],
                                                              scalar1=rowm[:, j:j + 1], scalar2=None, op0=ALU.mult),
                             reads=['PT', 'k_rowm'], writes=['h_kj' + n])
                        p.op('act', lambda e: e.copy(out=h_qj[:, h, j, 32 * j:32 * j + 32], in_=h_qp[:, h, 32 * j:32 * j + 32]),
                             reads=['h_qp'], writes=[f'h_qj.{h}'])
                    yield
                    p.op('pe', lambda e: e.matmul(bO[:, 0:128], lhsT=AT[:], rhs=h_v[:, h * 128:(h + 1) * 128], start=True,
                                                  stop=False), reads=['h_AT' + n, 'h_v'], writes=[kO])
                    for j in range(4):
                        p.op('pe', lambda e: e.matmul(bO[:, 0:128], lhsT=h_qj[:, h, j, :], rhs=h_Sb[:, l, h, :], start=False,
                                                      stop=(j == 3)), reads=[f'h_qj.{h}', f'h_Sb.{h}'], writes=[kO])
                        p.op('pe', lambda e: e.matmul(bX[:, 128:256], lhsT=kj[:, j, :], rhs=h_v[:, h * 128:(h + 1) * 128],
                                                      start=True, stop=True), reads=['h_kj' + n, 'h_v'], writes=[kX])
                        p.op('dve', lambda e: e.scalar_tensor_tensor(
                            out=h_S[0][:, l, h, :], in0=h_S[0][:, l, h, :], scalar=h_edec[:, h, j:j + 1], in1=bX[:, 128:256],
                            op0=ALU.mult, op1=ALU.add), reads=[f'h_S.{h}', 'h_edec', kX], writes=[f'h_S.{h}'])
                        p.op('act', lambda e: e.copy(out=h_Sb[:, l, h, :], in_=h_S[0][:, l, h, :]),
                             reads=[f'h_S.{h}'], writes=[f'h_Sb.{h}'])
                        yield
                    head_norm_gate(bO[:, 0:128], kO, [], hog[:, l, h * 128:(h + 1) * 128], h_g[:, h * 128:(h + 1) * 128],
                                   'h_g', obf[:, 0, h * 128:(h + 1) * 128], 4 + 72 * sl, sl)
                for pair in range(2):
                    interleave([hg_head(2 * pair, 0), hg_head(2 * pair + 1, 1)])
            for gi_ in range(8):
                gate_piece(gi_)
            if enable[1]:
                wb, wk = wload(W_in[:, C_MQ:C_MQ + 512], 512, l, ti)
                for h in range(4):
                    proj_fm(wb, wk, h * 64, 64, PS[0][0:64, h * 128:(h + 1) * 128], 'PS0')
                    proj_fm(wb, wk, 256 + h * 64, 64, PS[1][0:64, h * 128:(h + 1) * 128], 'PS1')
                p.op('act', lambda e: e.copy(out=m_q[:].rearrange("p a b -> p (a b)"), in_=PS[0][0:64, :]), reads=['PS0'],
                     writes=['m_q'])
                p.op('act', lambda e: e.mul(out=m_kT[:].rearrange("p a b -> p (a b)"), in_=PS[1][0:64, :], mul=0.125),
                     reads=['PS1'], writes=['m_kT'])
                proj_tm(wb, wk, 256, 256, PS[2][:, 0:256], 'PS2')
                p.op('act', lambda e: e.mul(out=m_k[:], in_=PS[2][:, 0:256], mul=0.125), reads=['PS2'], writes=['m_k'])
                wb, wk = wload(W_in[:, C_MV:C_MV + 520], 520, l, ti)
                proj_tm(wb, wk, 0, 512, PS[0][:], 'PS0')
                p.op('act', lambda e: e.copy(out=m_v[:], in_=PS[0][:]), reads=['PS0'], writes=['m_v'])
                proj_tm(wb, wk, 512, 8, PS[1][:, 0:8], 'PS1')
                p.op('dve', lambda e: e.tensor_tensor(out=m_if[:], in0=PS[1][:, 0:8], in1=mifb[:, l, :], op=ALU.add),
                     reads=['PS1', 'prm'], writes=['m_if'])
                wb, wk = wload(W_in[:, C_MO:C_MO + 512], 512, l, ti)
                proj_tm(wb, wk, 0, 512, PS[2][:], 'PS2')
                p.op('act', lambda e: e.activation(out=m_o[:], in_=PS[2][:], func=AF.Sigmoid), reads=['PS2'], writes=['m_o'])
                p.op('act', lambda e: e.activation(out=sm[:, 8:12], in_=m_if[:, 4:8], func=AF.Exp, scale=-1.0),
                     reads=['m_if'], writes=['sm8'])
                p.op('act', lambda e: e.activation(out=sm[:, 8:12], in_=sm[:, 8:12], func=AF.Ln, bias=epsb[:, 1:2]),
                     reads=['sm8', 'epsb'], writes=['sm8'])
                p.op('pe', lambda e: e.matmul(PS[3][:, 0:4], lhsT=tri_incl[:], rhs=sm[:, 8:12], start=True, stop=True),
                     reads=['sm8', 'k_tri_incl'], writes=['PS3'])
                p.op('act', lambda e: e.copy(out=sm[:, 12:16], in_=PS[3][:, 0:4]), reads=['PS3'], writes=['sm12'])
                p.op('dve', lambda e: e.tensor_tensor(out=sm[:, 16:20], in0=PS[3][:, 0:4], in1=m_if[:, 0:4], op=ALU.add),
                     reads=['PS3', 'm_if'], writes=['sm16'])
                p.op('act', lambda e: e.activation(out=sm[:, 16:20], in_=sm[:, 16:20], func=AF.Exp), reads=['sm16'],
                     writes=['sm16'])
                p.op('act', lambda e: e.activation(out=sm[:, 20:24], in_=sm[:, 12:16], func=AF.Exp, scale=-1.0),
                     reads=['sm12'], writes=['sm20'])
                p.op('pe', lambda e: e.matmul(PS[3][0:64, 8:12], lhsT=sel_last[:, 0:64], rhs=sm[:, 12:16], start=True, stop=True),
                     reads=['sm12', 'k_sel_last'], writes=['PS3'])
                p.op('act', lambda e: e.activation(out=sm[0:64, 24:28], in_=PS[3][0:64, 8:12], func=AF.Exp, scale=-1.0),
                     reads=['PS3'], writes=['sm24'])
                for h in range(4):
                    p.op('dve', (lambda e, h=h: e.tensor_scalar(out=m_va[:, h, 0:128], in0=m_v[:, h * 128:(h + 1) * 128],
                                                                scalar1=sm[:, 16 + h:17 + h], scalar2=None, op0=ALU.mult)),
                         reads=['m_v', 'sm16'], writes=['m_va'])
                    p.op('dve', (lambda e, h=h: e.tensor_copy(out=m_va[:, h, 128:129], in_=sm[:, 16 + h:17 + h])),
                         reads=['sm16'], writes=['m_va'])
                def ml_head(h, sl):
                    bX, bO = PS[3 + 2 * sl], PS[4 + 2 * sl]
                    kX, kO = f'PS{3 + 2 * sl}', f'PS{4 + 2 * sl}'
                    AT = mS_AT[sl]
                    n = f'.{sl}'
                    c0 = 28 if sl == 0 else 96
                    tA = tmpA[:, sl * 128:(sl + 1) * 128]
                    p.op('pe', lambda e: e.matmul(bX[:, 0:128], lhsT=m_kT[:, h, :], rhs=m_q[:, h, :], start=True, stop=True),
                         reads=['m_kT', 'm_q'], writes=[kX])
                    p.op('dve', lambda e: e.tensor_tensor(out=AT[:], in0=bX[:, 0:128], in1=tri_incl[:], op=ALU.mult),
                         reads=[kX, 'k_tri_incl'], writes=['m_AT' + n])
                    yield
                    p.op('pe', lambda e: e.matmul(bO[:, 0:129], lhsT=AT[:], rhs=m_va[:, h, 0:129], start=True, stop=False),
                         reads=['m_AT' + n, 'm_va'], writes=[kO], inc=False)
                    p.op('pe', lambda e: e.matmul(bO[:, 0:129], lhsT=m_q[:, h, :], rhs=m_Cb[:, l, h, 0:129], start=False,
                                                  stop=True), reads=['m_q', f'm_Cb.{h}'], writes=[kO])
                    p.op('pe', lambda e: e.matmul(bX[0:64, 128:257], lhsT=m_k[:, h * 64:(h + 1) * 64], rhs=m_va[:, h, 0:129],
                                                  start=True, stop=True), reads=['m_k', 'm_va'], writes=[kX])
                    p.op('dve', lambda e: e.tensor_tensor(out=sm[:, c0:c0 + 1], in0=bO[:, 128:129], in1=sm[:, 20 + h:21 + h],
                                                          op=ALU.mult), reads=[kO, 'sm20'], writes=[f'sm{c0}'])
                    p.op('dve', lambda e: e.scalar_tensor_tensor(out=sm[:, c0 + 3:c0 + 4], in0=sm[:, c0:c0 + 1], scalar=-1.0,
                                                                 in1=sm[:, c0:c0 + 1], op0=ALU.mult, op1=ALU.max),
                         reads=[f'sm{c0}'], writes=[f'sm{c0 + 3}'])
                    p.op('dve', lambda e: e.tensor_scalar(out=sm[:, c0:c0 + 1], in0=sm[:, c0 + 3:c0 + 4], scalar1=1.0,
                                                          scalar2=None, op0=ALU.max), reads=[f'sm{c0 + 3}'], writes=[f'sm{c0}'])
                    p.op('dve', lambda e: e.reciprocal(out=sm[:, c0 + 1:c0 + 2], in_=sm[:, c0:c0 + 1]), reads=[f'sm{c0}'],
                         writes=[f'sm{c0 + 1}'])
                    p.op('dve', lambda e: e.tensor_tensor(out=sm[:, c0 + 2:c0 + 3], in0=sm[:, c0 + 1:c0 + 2],
                                                          in1=sm[:, 20 + h:21 + h], op=ALU.mult),
                         reads=[f'sm{c0 + 1}', 'sm20'], writes=[f'sm{c0 + 2}'])
                    yield
                    p.op('act', lambda e: e.activation(out=tA, in_=bO[:, 0:128], func=AF.Copy, scale=sm[:, c0 + 2:c0 + 3]),
                         reads=[kO, f'sm{c0 + 2}'], writes=['tmpA' + n])
                    p.op('dve', lambda e: e.tensor_tensor(out=m_C[:, l, h, 0:129], in0=m_C[:, l, h, 0:129],
                                                          in1=bX[0:64, 128:257], op=ALU.add),
                         reads=[f'm_C.{h}', kX], writes=[f'm_C.{h}'])
                    p.op('dve', lambda e: e.tensor_scalar(out=m_C[:, l, h, 0:129], in0=m_C[:, l, h, 0:129],
                                                          scalar1=sm[0:64, 24 + h:25 + h], scalar2=None, op0=ALU.mult),
                         reads=[f'm_C.{h}', 'sm24'], writes=[f'm_C.{h}'])
                    p.op('act', lambda e: e.copy(out=m_Cb[:, l, h, 0:129], in_=m_C[:, l, h, 0:129]), reads=[f'm_C.{h}'],
                         writes=[f'm_Cb.{h}'])
                    yield
                    head_norm_gate(tA, 'tmpA' + n, [], mog[:, l, h * 128:(h + 1) * 128], m_o[:, h * 128:(h + 1) * 128],
                                   'm_o', obf[:, 1, h * 128:(h + 1) * 128], 32 if sl == 0 else 100, sl)
                for pair in range(2):
                    interleave([ml_head(2 * pair, 0), ml_head(2 * pair + 1, 1)])

            if enable[2]:
                wb, wk = wload(W_in[:, C_AQ:C_AQ + 512], 512, l, ti)
                wb2, wk2 = wload(W_in[:, C_AK:C_AK + 256], 256, l, ti)
                for g in range(8):
                    proj_fm(wb, wk, g * 64, 64, PS[g // 4][0:64, (g % 4) * 128:(g % 4 + 1) * 128], f'PS{g // 4}')
                for kv in range(2):
                    proj_fm(wb2, wk2, kv * 64, 64, PS[2][0:64, kv * 128:(kv + 1) * 128], 'PS2')
                az2 = a_z[:].rearrange("p a b -> p (a b)")
                asq2 = a_sq[:].rearrange("p a b -> p (a b)")
                p.op('act', lambda e: e.copy(out=az2[:, 0:512], in_=PS[0][0:64, :]), reads=['PS0'], writes=['a_z'])
                p.op('act', lambda e: e.copy(out=az2[:, 512:1024], in_=PS[1][0:64, :]), reads=['PS1'], writes=['a_z'])
                p.op('act', lambda e: e.copy(out=az2[:, 1024:1280], in_=PS[2][0:64, 0:256]), reads=['PS2'], writes=['a_z'])
                p.op('dve', lambda e: e.tensor_tensor(out=asq2, in0=az2, in1=az2, op=ALU.mult), reads=['a_z'], writes=['a_sq'])
                for i3 in range(3):
                    w3 = 512 if i3 < 2 else 256
                    p.op('pe', (lambda e, i3=i3, w3=w3: e.matmul(PS[3][0:64, 0:w3], lhsT=ones_f[0:64, 0:64],
                                                                 rhs=asq2[:, i3 * 512:i3 * 512 + w3], start=True, stop=True)),
                         reads=['a_sq', 'k_ones'], writes=['PS3'])
                    p.op('act', (lambda e, i3=i3, w3=w3: e.activation(out=asq2[:, i3 * 512:i3 * 512 + w3], in_=PS[3][0:64, 0:w3],
                                                                      func=AF.Ln, scale=1.0 / 64, bias=epsb[0:64, 0:1])),
                         reads=['PS3', 'epsb'], writes=['a_sq'])
                p.op('act', lambda e: e.activation(out=asq2, in_=asq2, func=AF.Exp, scale=-0.5), reads=['a_sq'], writes=['a_sq'])
                p.op('dve', lambda e: e.tensor_tensor(out=a_q[:].rearrange("p a b -> p (a b)"), in0=az2[:, 0:1024],
                                                      in1=asq2[:, 0:1024], op=ALU.mult), reads=['a_z', 'a_sq'], writes=['a_q'])
                for kv in range(2):
                    p.op('dve', (lambda e, kv=kv: e.scalar_tensor_tensor(
                        out=a_kT[:, l, kv, par, :], in0=a_z[:, 8 + kv, :], scalar=gk8[:, l:l + 1], in1=a_sq[:, 8 + kv, :],
                        op0=ALU.mult, op1=ALU.mult)), reads=['a_z', 'a_sq', 'gk8'], writes=['a_kT'])
                proj_tm(wb2, wk2, 128, 128, PS[4][:, 0:128], 'PS4')
                for kv in range(2):
                    p.op('act', (lambda e, kv=kv: e.copy(out=a_v[:, l, par, kv, 0:64], in_=PS[4][:, kv * 64:(kv + 1) * 64])),
                         reads=['PS4'], writes=['a_v'])
                    p.op('dve', (lambda e, kv=kv: e.memset(a_v[:, l, par, kv, 64:65], 1.0)), writes=['a_v'])
                def swa_head(g, sl):
                    kv = g // 4
                    bL, bO = PS[3 + 2 * sl], PS[4 + 2 * sl]
                    kL, kO = f'PS{3 + 2 * sl}', f'PS{4 + 2 * sl}'
                    aP, aPT = aS_P[sl], aS_PT[sl]
                    n = f'.{sl}'
                    c0 = 40 + 2 * sl
                    blks = [0, 1] if (pipe or ti > 0) else [1]
                    nb = len(blks)
                    for bi, blk in enumerate(blks):
                        slot = par if blk == 1 else 1 - par
                        p.op('pe', lambda e: e.matmul(bL[:, blk * 128:(blk + 1) * 128], lhsT=a_kT[:, l, kv, slot, :],
                                                      rhs=a_q[:, g, :], start=True, stop=True),
                             reads=['a_kT', 'a_q'], writes=[kL])
                    yield
                    p.op('act', lambda e: e.activation(out=aP[:, 0:nb * 128], in_=bL[:, blks[0] * 128:(blks[0] + nb) * 128],
                                                       func=AF.Exp), reads=[kL], writes=['a_P' + n])
                    for bi, blk in enumerate(blks):
                        if pipe and blk == 0:
                            p.op('dve', lambda e: e.scalar_tensor_tensor(
                                out=aPT[:, blk, :], in0=aP[:, bi * 128:(bi + 1) * 128], scalar=pflag[:, ti:ti + 1],
                                in1=EB[:, blk, g, :], op0=ALU.mult, op1=ALU.mult),
                                reads=['a_P' + n, 'EB', 'pflag'], writes=['a_PT' + n])
                        else:
                            p.op('dve', lambda e: e.tensor_tensor(out=aPT[:, blk, :], in0=aP[:, bi * 128:(bi + 1) * 128],
                                                                  in1=EB[:, blk, g, :], op=ALU.mult),
                                 reads=['a_P' + n, 'EB'], writes=['a_PT' + n])
                    yield
                    for bi, blk in enumerate(blks):
                        slot = par if blk == 1 else 1 - par
                        p.op('pe', lambda e: e.matmul(bO[:, 0:65], lhsT=aPT[:, blk, :], rhs=a_v[:, l, slot, kv, 0:65],
                                                      start=(bi == 0), stop=(bi == nb - 1)),
                             reads=['a_PT' + n, 'a_v'], writes=[kO], inc=(bi == nb - 1))
                    p.op('dve', lambda e: e.tensor_tensor(out=sm[:, c0:c0 + 1], in0=bO[:, 64:65], in1=esink[:, l, g:g + 1],
                                                          op=ALU.add), reads=[kO, 'esink'], writes=[f'sm{c0}'])
                    p.op('dve', lambda e: e.reciprocal(out=sm[:, c0 + 1:c0 + 2], in_=sm[:, c0:c0 + 1]), reads=[f'sm{c0}'],
                         writes=[f'sm{c0 + 1}'])
                    yield
                    p.op('dve', lambda e: e.tensor_scalar(out=obf[:, 2, g * 64:(g + 1) * 64], in0=bO[:, 0:64],
                                                          scalar1=sm[:, c0 + 1:c0 + 2], scalar2=None, op0=ALU.mult),
                         reads=[kO, f'sm{c0 + 1}'], writes=['obf'])
                for pair in range(4):
                    interleave([swa_head(2 * pair, 0), swa_head(2 * pair + 1, 1)])

            if enable[3]:
                for c3 in range(3):
                    wb, wk = wload(W_in[:, C_DQKV + c3 * 512:C_DQKV + (c3 + 1) * 512], 512, l, ti)
                    for c4 in range(4):
                        proj_fm(wb, wk, c4 * 128, 128, PS[c3][:, c4 * 128:(c4 + 1) * 128], f'PS{c3}')
                    p.op('act', (lambda e, c3=c3: e.copy(out=d_x[:, l, c3 * 4:(c3 + 1) * 4, 3:131],
                                                         in_=PS[c3][:].rearrange("p (a b) -> p a b", a=4))),
                         reads=[f'PS{c3}'], writes=['d_x'])
                for hf in range(2):
                    cs = slice(hf * 6, hf * 6 + 6)
                    yv = d_y[:, cs, :]
                    tAv = xr[:, 0:768].rearrange("p (a b) -> p a b", a=6)
                    tBv = xh[:, 0:768].rearrange("p (a b) -> p a b", a=6)

                    def wbc(j, cs=cs):
                        return convw[:, l, j, cs].unsqueeze(2).broadcast_to([128, 6, 128])
                    p.op('dve', lambda e: e.tensor_tensor(out=yv, in0=d_x[:, l, cs, 0:128], in1=wbc(0), op=ALU.mult),
                         reads=['d_x', 'prm'], writes=['d_y'])
                    p.op('pool', lambda e: e.tensor_tensor(out=tAv, in0=d_x[:, l, cs, 1:129], in1=wbc(1), op=ALU.mult),
                         reads=['d_x', 'prm'], writes=['xr'])
                    p.op('pool', lambda e: e.tensor_tensor(out=tBv, in0=d_x[:, l, cs, 2:130], in1=wbc(2), op=ALU.mult),
                         reads=['d_x', 'prm'], writes=['xh'])
                    p.op('dve', lambda e: e.tensor_tensor(out=yv, in0=yv, in1=tAv, op=ALU.add), reads=['d_y', 'xr'],
                         writes=['d_y'])
                    p.op('pool', lambda e: e.tensor_tensor(out=tAv, in0=d_x[:, l, cs, 3:131], in1=wbc(3), op=ALU.mult),
                         reads=['d_x', 'prm'], writes=['xr'])
                    p.op('dve', lambda e: e.tensor_tensor(out=yv, in0=yv, in1=tBv, op=ALU.add), reads=['d_y', 'xh'],
                         writes=['d_y'])
                    p.op('dve', lambda e: e.tensor_tensor(out=yv, in0=yv, in1=tAv, op=ALU.add), reads=['d_y', 'xr'],
                         writes=['d_y'])
                p.op('pool', lambda e: e.tensor_copy(out=d_x[:, l, :, 0:3], in_=d_x[:, l, :, 128:131]), reads=['d_x', 'd_y'],
                     writes=['d_x'])
                dy2 = d_y[:].rearrange("p a b -> p (a b)")
                p.op('act', lambda e: e.activation(out=dy2, in_=dy2, func=AF.Silu), reads=['d_y'], writes=['d_y'])
                p.op('dve', lambda e: e.tensor_tensor(out=merged[:], in0=dy2[:, 0:1024], in1=dy2[:, 0:1024], op=ALU.mult),
                     reads=['d_y'], writes=['merged'])
                for i2 in range(2):
                    tn = tmpA if i2 == 0 else tmpB
                    tk = 'tmpA' if i2 == 0 else 'tmpB'
                    p.op('pe', lambda e: e.matmul(PS[i2][:], lhsT=ones_f[:], rhs=merged[:, i2 * 512:(i2 + 1) * 512], start=True,
                                                  stop=True), reads=['merged', 'k_ones'], writes=[f'PS{i2}'])
                    p.op('act', lambda e: e.activation(out=tn[:], in_=PS[i2][:], func=AF.Ln, bias=epsb[:, 0:1]),
                         reads=[f'PS{i2}', 'epsb'], writes=[tk])
                    p.op('act', lambda e: e.activation(out=tn[:], in_=tn[:], func=AF.Exp, scale=-0.5), reads=[tk], writes=[tk])
                p.op('dve', lambda e: e.scalar_tensor_tensor(out=d_qT[:].rearrange("p a b -> p (a b)"), in0=dy2[:, 0:512],
                                                             scalar=float(128 ** -0.5), in1=tmpA[:], op0=ALU.mult, op1=ALU.mult),
                     reads=['d_y', 'tmpA'], writes=['d_qT'])
                p.op('dve', lambda e: e.tensor_tensor(out=d_kT[:].rearrange("p a b -> p (a b)"), in0=dy2[:, 512:1024],
                                                      in1=tmpB[:], op=ALU.mult), reads=['d_y', 'tmpB'], writes=['d_kT'])
                p.op('dve', lambda e: e.tensor_copy(out=d_vT[:], in_=d_y[:, 8:12, :]), reads=['d_y'], writes=['d_vT'])
                for h in range(4):
                    p.op('pe', (lambda e, h=h: e.transpose(out=PT[:, h * 128:(h + 1) * 128], in_=d_vT[:, h, :],
                                                           identity=ident_b[:])), reads=['d_vT', 'k_ident'], writes=['PT'],
                         inc=False)
                    p.op('pe', (lambda e, h=h: e.transpose(out=PT[:, 512 + h * 128:512 + (h + 1) * 128], in_=d_kT[:, h, :],
                                                           identity=ident_b[:])), reads=['d_kT', 'k_ident'], writes=['PT'],
                         inc=(h == 3))
                p.op('act', lambda e: e.copy(out=d_v[:].rearrange("p a b -> p (a b)"), in_=PT[:, 0:512]), reads=['PT'],
                     writes=['d_v'])
                p.op('act', lambda e: e.copy(out=d_k[:].rearrange("p a b -> p (a b)"), in_=PT[:, 512:1024]), reads=['PT'],
                     writes=['d_k'])
                wb, wk = wload(W_in[:, C_DB:C_DB + 520], 520, l, ti)
                proj_tm(wb, wk, 0, 8, PS[0][:, 0:8], 'PS0')
                p.op('act', lambda e: e.copy(out=d_ba[:], in_=PS[0][:, 0:8]), reads=['PS0'], writes=['d_ba'])
                proj_fm(wb, wk, 4, 4, PS[1][0:4, 0:128], 'PS1')
                p.op('act', lambda e: e.copy(out=d_arow[:], in_=PS[1][0:4, 0:128]), reads=['PS1'], writes=['d_arow'])
                proj_tm(wb, wk, 8, 512, PS[2][:], 'PS2')
                p.op('act', lambda e: e.activation(out=d_z[:], in_=PS[2][:], func=AF.Silu), reads=['PS2'], writes=['d_z'])
                p.op('act', lambda e: e.activation(out=sm[:, 44:48], in_=d_ba[:, 0:4], func=AF.Sigmoid), reads=['d_ba'],
                     writes=['sm44'])
                p.op('dve', lambda e: e.tensor_tensor(out=sm[:, 48:52], in0=d_ba[:, 4:8], in1=dtb[:, l, :], op=ALU.add),
                     reads=['d_ba', 'prm'], writes=['sm48'])
                p.op('act', lambda e: e.activation(out=sm[:, 48:52], in_=sm[:, 48:52], func=AF.Exp), reads=['sm48'],
                     writes=['sm48'])
                p.op('act', lambda e: e.activation(out=sm[:, 48:52], in_=sm[:, 48:52], func=AF.Ln, bias=epsb[:, 1:2]),
                     reads=['sm48', 'epsb'], writes=['sm48'])
                p.op('dve', lambda e: e.tensor_tensor(out=sm[:, 48:52], in0=sm[:, 48:52], in1=nexpa[:, l, :], op=ALU.mult),
                     reads=['sm48', 'nexpa'], writes=['sm48'])
                p.op('pe', lambda e: e.matmul(PS[3][:, 0:4], lhsT=tri_incl[:], rhs=sm[:, 48:52], start=True, stop=True),
                     reads=['sm48', 'k_tri_incl'], writes=['PS3'])
                p.op('act', lambda e: e.copy(out=sm[:, 52:56], in_=PS[3][:, 0:4]), reads=['PS3'], writes=['sm52'])
                p.op('dve', lambda e: e.tensor_scalar(out=sm[:, 56:60], in0=sm[:, 52:56], scalar1=-1.0, scalar2=None,
                                                      op0=ALU.mult), reads=['sm52'], writes=['sm56'])
                p.op('pe', lambda e: e.matmul(PS[3][:, 8:12], lhsT=sel_last[:], rhs=sm[:, 52:56], start=True, stop=True),
                     reads=['sm52', 'k_sel_last'], writes=['PS3'])
                p.op('act', lambda e: e.activation(out=sm[:, 60:64], in_=PS[3][:, 8:12], func=AF.Exp), reads=['PS3'],
                     writes=['sm60'])
                p.op('dve', lambda e: e.tensor_tensor(out=tmpC[:, 500:504], in0=PS[3][:, 8:12], in1=sm[:, 52:56],
                                                      op=ALU.subtract), reads=['PS3', 'sm52'], writes=['tmpC5'])
                p.op('act', lambda e: e.activation(out=tmpC[:, 500:504], in_=tmpC[:, 500:504], func=AF.Exp), reads=['tmpC5'],
                     writes=['tmpC5'])
                p.op('dve', lambda e: e.tensor_tensor(out=tmpC[:, 504:508], in0=tmpC[:, 500:504], in1=sm[:, 44:48], op=ALU.mult),
                     reads=['tmpC5', 'sm44'], writes=['tmpC6'])
                p.op('act', lambda e: e.activation(out=tmpC[:, 508:512], in_=sm[:, 52:56], func=AF.Exp), reads=['sm52'],
                     writes=['tmpC7'])
                p.op('dve', lambda e: e.tensor_scalar(out=tmpC[:, 496:500], in0=tmpC[:, 508:512], scalar1=-1.0, scalar2=None,
                                                      op0=ALU.mult), reads=['tmpC7'], writes=['tmpC4'])
                p.op('dve', lambda e: e.tensor_scalar(out=tmpC[:, 492:496], in0=sm[:, 44:48], scalar1=-1.0, scalar2=None,
                                                      op0=ALU.mult), reads=['sm44'], writes=['tmpC3'])
                p.op('act', lambda e: e.activation(out=d_grow[:], in_=d_arow[:], func=AF.Exp, bias=dtbrow[:, l:l + 1]),
                     reads=['d_arow', 'dtbrow'], writes=['d_grow'])
                p.op('act', lambda e: e.activation(out=d_grow[:], in_=d_grow[:], func=AF.Ln, bias=epsb[0:4, 1:2]),
                     reads=['d_grow', 'epsb'], writes=['d_grow'])
                p.op('dve', lambda e: e.tensor_scalar(out=d_grow[:], in0=d_grow[:], scalar1=nexparow[:, l:l + 1], scalar2=None,
                                                      op0=ALU.mult), reads=['d_grow', 'nexparow'], writes=['d_grow'])
                p.op('dve', lambda e: e.tensor_tensor_scan(out=d_grow[:], data0=ones_f[0:4, :], data1=d_grow[:], initial=0.0,
                                                           op0=ALU.mult, op1=ALU.add), reads=['d_grow', 'k_ones'],
                     writes=['d_grow'])
                def dn_head(h, sl):
                    bA, bB, bC = PS[1 + 3 * sl], PS[2 + 3 * sl], PS[3 + 3 * sl]
                    kA, kB, kC = f'PS{1 + 3 * sl}', f'PS{2 + 3 * sl}', f'PS{3 + 3 * sl}'
                    E1, E1s, TT, rr = dS_E1[sl], dS_E1s[sl], dS_TT[sl], dS_r[sl]
                    Pb, PTb = dS_P[sl], dS_PT[sl]
                    vn, vc, aT, o1 = dS_vn[sl], dS_vc[sl], dS_aT[sl], dS_o1[sl]
                    tA = tmpA[:, sl * 128:(sl + 1) * 128]
                    tB = tmpB[:, sl * 128:(sl + 1) * 128]
                    n = f'.{sl}'
                    p.op('pe', lambda e: e.matmul(bA[:, 0:128], lhsT=selh[:, h * 128:(h + 1) * 128], rhs=d_grow[:],
                                                  start=True, stop=True), reads=['k_selh', 'd_grow'], writes=[kA])
                    p.op('dve', lambda e: e.tensor_tensor(out=tA, in0=bA[:, 0:128], in1=negmask[:], op=ALU.add),
                         reads=[kA, 'k_negmask'], writes=['tmpA' + n])
                    p.op('act', lambda e: e.activation(out=E1[:], in_=tA, func=AF.Exp, bias=sm[:, 56 + h:57 + h]),
                         reads=['tmpA' + n, 'sm56'], writes=['d_E1' + n])
                    yield
                    p.op('pe', lambda e: e.matmul(bA[:, 128:256], lhsT=d_kT[:, h, :], rhs=d_kT[:, h, :], start=True, stop=True),
                         reads=['d_kT'], writes=[kA])
                    p.op('dve', lambda e: e.tensor_tensor(out=E1s[:], in0=E1[:], in1=tri_strict[:], op=ALU.mult),
                         reads=['d_E1' + n, 'k_tri_strict'], writes=['d_E1s' + n])
                    p.op('dve', lambda e: e.scalar_tensor_tensor(out=PTb[0][:], in0=bA[:, 128:256],
                                                                 scalar=tmpC[:, 492 + h:493 + h], in1=E1s[:],
                                                                 op0=ALU.mult, op1=ALU.mult),
                         reads=[kA, 'tmpC3', 'd_E1s' + n], writes=['d_PT0' + n])
                    yield
                    p.op('pe', lambda e: e.transpose(out=bB[:, 0:128], in_=PTb[0][:], identity=ident_f[:]),
                         reads=['d_PT0' + n, 'k_ident'], writes=[kB])
                    p.op('act', lambda e: e.copy(out=Pb[0][:], in_=bB[:, 0:128]), reads=[kB], writes=['d_P0' + n])
                    p.op('dve', lambda e: e.tensor_tensor(out=TT[:], in0=PTb[0][:], in1=ident_f[:], op=ALU.add),
                         reads=['d_PT0' + n, 'k_ident'], writes=['d_TT' + n])
                    yield
                    cur = 0
                    for lvl in range(6):
                        nxt = 1 - cur
                        p.op('pe', lambda e: e.matmul(bB[:, 0:128], lhsT=PTb[cur][:], rhs=Pb[cur][:], start=True, stop=True),
                             reads=[f'd_PT{cur}' + n, f'd_P{cur}' + n], writes=[kB])
                        if lvl < 5:
                            p.op('pe', lambda e: e.matmul(bB[:, 128:256], lhsT=Pb[cur][:], rhs=PTb[cur][:], start=True, stop=True),
                                 reads=[f'd_PT{cur}' + n, f'd_P{cur}' + n], writes=[kB])
                        p.op('act', lambda e: e.copy(out=Pb[nxt][:], in_=bB[:, 0:128]), reads=[kB], writes=[f'd_P{nxt}' + n])
                        if lvl < 5:
                            p.op('act', lambda e: e.copy(out=PTb[nxt][:], in_=bB[:, 128:256]), reads=[kB],
                                 writes=[f'd_PT{nxt}' + n])
                        yield
                        p.op('pe', lambda e: e.matmul(bC[:, 0:128], lhsT=Pb[nxt][:], rhs=TT[:], start=True, stop=True),
                             reads=[f'd_P{nxt}' + n, 'd_TT' + n], writes=[kC])
                        p.op('dve', lambda e: e.tensor_tensor(out=TT[:], in0=TT[:], in1=bC[:, 0:128], op=ALU.add),
                             reads=['d_TT' + n, kC], writes=['d_TT' + n])
                        yield
                        cur = nxt
                    p.op('pe', lambda e: e.matmul(bA[:, 256:384], lhsT=d_kT[:, h, :], rhs=d_Sb[:, l, h, :], start=True, stop=True),
                         reads=['d_kT', 'd_Sb'], writes=[kA])
                    p.op('dve', lambda e: e.scalar_tensor_tensor(out=rr[:], in0=bA[:, 256:384], scalar=tmpC[:, 496 + h:497 + h],
                                                                 in1=d_v[:, h, :], op0=ALU.mult, op1=ALU.add),
                         reads=[kA, 'tmpC4', 'd_v'], writes=['d_r' + n])
                    yield
                    p.op('pe', lambda e: e.matmul(bB[:, 256:384], lhsT=TT[:], rhs=rr[:], start=True, stop=True),
                         reads=['d_TT' + n, 'd_r' + n], writes=[kB])
                    p.op('dve', lambda e: e.tensor_scalar(out=vn[:], in0=bB[:, 256:384], scalar1=sm[:, 44 + h:45 + h],
                                                          scalar2=None, op0=ALU.mult), reads=[kB, 'sm44'], writes=['d_vn' + n])
                    p.op('dve', lambda e: e.tensor_scalar(out=vc[:], in0=bB[:, 256:384], scalar1=tmpC[:, 504 + h:505 + h],
                                                          scalar2=None, op0=ALU.mult), reads=[kB, 'tmpC6'], writes=['d_vc' + n])
                    p.op('pe', lambda e: e.matmul(bA[:, 384:512], lhsT=d_kT[:, h, :], rhs=d_qT[:, h, :], start=True, stop=True),
                         reads=['d_kT', 'd_qT'], writes=[kA])
                    p.op('dve', lambda e: e.tensor_tensor(out=aT[:], in0=bA[:, 384:512], in1=E1[:], op=ALU.mult),
                         reads=[kA, 'd_E1' + n], writes=['d_aT' + n])
                    yield
                    p.op('pe', lambda e: e.matmul(bC[:, 128:256], lhsT=d_qT[:, h, :], rhs=d_Sb[:, l, h, :], start=True, stop=True),
                         reads=['d_qT', 'd_Sb'], writes=[kC])
                    p.op('act', lambda e: e.activation(out=o1[:], in_=bC[:, 128:256], func=AF.Copy,
                                                       scale=tmpC[:, 508 + h:509 + h]), reads=[kC, 'tmpC7'], writes=['d_o1' + n])
                    yield
                    p.op('pe', lambda e: e.matmul(bC[:, 256:384], lhsT=aT[:], rhs=vn[:], start=True, stop=True),
                         reads=['d_aT' + n, 'd_vn' + n], writes=[kC])
                    p.op('dve', lambda e: e.tensor_tensor(out=tB, in0=bC[:, 256:384], in1=o1[:], op=ALU.add),
                         reads=[kC, 'd_o1' + n], writes=['tmpB' + n])
                    yield
                    p.op('pe', lambda e: e.matmul(bC[:, 384:512], lhsT=d_k[:, h, :], rhs=vc[:], start=True, stop=True),
                         reads=['d_k', 'd_vc' + n], writes=[kC])
                    p.op('dve', lambda e: e.scalar_tensor_tensor(out=d_S[:, l, h, :], in0=d_S[:, l, h, :],
                                                                 scalar=sm[:, 60 + h:61 + h], in1=bC[:, 384:512],
                                                                 op0=ALU.mult, op1=ALU.add),
                         reads=['d_S', 'sm60', kC], writes=['d_S'])
                    p.op('act', lambda e: e.copy(out=d_Sb[:, l, h, :], in_=d_S[:, l, h, :]), reads=['d_S'], writes=['d_Sb'])
                    yield
                    head_norm_gate(tB, 'tmpB' + n, [], dog[:, l, :], d_z[:, h * 128:(h + 1) * 128], 'd_z',
                                   obf[:, 3, h * 128:(h + 1) * 128], 64 + 4 * sl, sl)
                for pair in range(2):
                    interleave([dn_head(2 * pair, 0), dn_head(2 * pair + 1, 1)])

            for gi in range(gates_done[0], 8):
                gate_piece(gi)
            for half in range(2):
                for c8 in range(8):
                    cc = half * 8 + c8
                    p.op('pe', (lambda e, cc=cc, c8=c8: e.transpose(
                        out=PT[:, c8 * 128:(c8 + 1) * 128],
                        in_=obf[:].rearrange("p a b -> p (a b)")[:, cc * 128:(cc + 1) * 128], identity=ident_b[:])),
                        reads=['obf', 'k_ident'], writes=['PT'], inc=(c8 == 7))
                p.op('act', (lambda e, half=half: e.copy(out=oT[:, half * 8:(half + 1) * 8, :].rearrange("p a b -> p (a b)"),
                                                         in_=PT[:])), reads=['PT'], writes=['oT'])
            for n in range(4):
                wv, wk = wload_rows(prm['w_branch'][l, n], l, ti)
                for half in range(2):
                    pst, pk = (PS[2], 'PS2') if half == 0 else (PS[3], 'PS3')
                    for wc in range(4):
                        p.op('pe', (lambda e, n=n, wc=wc, half=half, pst=pst: e.matmul(
                            pst[:], lhsT=oT[:, n * 4 + wc, :], rhs=wv[:, wc, half * 512:(half + 1) * 512], start=(wc == 0),
                            stop=(wc == 3))), reads=['oT', wk], writes=[pk], inc=(wc == 3))
                    if n == 0:
                        p.op('dve', (lambda e, n=n, half=half, pst=pst: e.tensor_tensor(
                            out=merged[:, half * 512:(half + 1) * 512], in0=pst[:],
                            in1=gates[:, n * 1024 + half * 512:n * 1024 + (half + 1) * 512], op=ALU.mult)),
                            reads=[pk, 'gates'], writes=['merged'])
                    else:
                        p.op('dve', (lambda e, n=n, half=half, pst=pst: e.tensor_tensor(
                            out=tmpA[:], in0=pst[:], in1=gates[:, n * 1024 + half * 512:n * 1024 + (half + 1) * 512],
                            op=ALU.mult)), reads=[pk, 'gates'], writes=['tmpA'])
                        p.op('dve', (lambda e, half=half: e.tensor_tensor(
                            out=merged[:, half * 512:(half + 1) * 512], in0=merged[:, half * 512:(half + 1) * 512], in1=tmpA[:],
                            op=ALU.add)), reads=['tmpA', 'merged'], writes=['merged'])
            p.op('act', lambda e: e.copy(out=mbf[:], in_=merged[:]), reads=['merged'], writes=['mbf'])
            for kc in range(8):
                p.op('pe', (lambda e, kc=kc: e.transpose(out=PT[:, kc * 128:(kc + 1) * 128], in_=mbf[:, kc * 128:(kc + 1) * 128],
                                                         identity=ident_b[:])), reads=['mbf', 'k_ident'], writes=['PT'],
                     inc=(kc == 7))
            p.op('act', lambda e: e.copy(out=mT[:].rearrange("p a b -> p (a b)"), in_=PT[:]), reads=['PT'], writes=['mT'])
            for half in range(2):
                wb, wk = wload(prm['w_out'][l][:, half * 512:(half + 1) * 512], 512, l, ti)
                pst, pk = (PS[0], 'PS0') if half == 0 else (PS[1], 'PS1')
                for kc in range(8):
                    p.op('pe', (lambda e, kc=kc, pst=pst, wb=wb: e.matmul(pst[:], lhsT=mT[:, kc, :], rhs=wb[:, kc, 0:512],
                                                                          start=(kc == 0), stop=(kc == 7))),
                         reads=['mT', wk], writes=[pk], inc=(kc == 7))
                p.op('dve', (lambda e, half=half, pst=pst: e.tensor_tensor(out=xt[:, half * 512:(half + 1) * 512],
                                                                           in0=xt[:, half * 512:(half + 1) * 512], in1=pst[:],
                                                                           op=ALU.add)), reads=[pk, 'xt'], writes=['xt'])
            rmsnorm_to_T(xt, gmlp, hT, 'hT', l)
            for fi in range(8):
                wb, wk = wload(prm['w_up'][l][:, fi * 512:(fi + 1) * 512], 512, l, ti)
                pst, pk = (PS[2], 'PS2') if fi % 2 == 0 else (PS[3], 'PS3')
                for f4 in range(4):
                    proj_fm(wb, wk, f4 * 128, 128, pst[:, f4 * 128:(f4 + 1) * 128], pk)
                p.op('act', (lambda e, pst=pst: e.activation(out=tmpB[:], in_=pst[:], func=AF.Relu)), reads=[pk], writes=['tmpB'])
                p.op('dve', (lambda e, fi=fi, pst=pst: e.tensor_tensor(
                    out=uT[:, fi * 4:(fi + 1) * 4, :].rearrange("p a b -> p (a b)"), in0=tmpB[:], in1=pst[:], op=ALU.mult)),
                    reads=['tmpB', pk], writes=['uT'])
            for fi in range(8):
                wv, wk = wload_rows(prm['w_down'][l][fi * 512:(fi + 1) * 512, :], l, ti)
                for half in range(2):
                    pk = 'PS0' if half == 0 else 'PS1'
                    pst = PS[0] if half == 0 else PS[1]
                    for f4 in range(4):
                        p.op('pe', (lambda e, fi=fi, f4=f4, half=half, pst=pst, wv=wv: e.matmul(
                            pst[:], lhsT=uT[:, fi * 4 + f4, :], rhs=wv[:, f4, half * 512:(half + 1) * 512],
                            start=(fi == 0 and f4 == 0), stop=(fi == 7 and f4 == 3))),
                            reads=['uT', wk], writes=[pk], inc=(f4 == 3))
            for half in range(2):
                pk = 'PS0' if half == 0 else 'PS1'
                pst = PS[0] if half == 0 else PS[1]
                p.op('dve', (lambda e, half=half, pst=pst: e.tensor_tensor(out=xt[:, half * 512:(half + 1) * 512],
                                                                           in0=xt[:, half * 512:(half + 1) * 512], in1=pst[:],
                                                                           op=ALU.add)), reads=[pk, 'xt'], writes=['xt'])
        if pipe:
            to = max(ti - 1, 0)
            p.dma('sp', y_out[to * 128:(to + 1) * 128, :], xt[:], reads=['xt'], writes=['yout'])
            p.dma('sp', sendb[(ti + 1) % 2], xt[:], reads=['xt'], writes=[('sendb', (ti + 1) % 2)])
        else:
            p.dma('sp', y_out[ti * 128:(ti + 1) * 128, :], xt[:], reads=['xt'], writes=['yout'])
    p.final_wait('sp', ['yout'])
    p.emit()
    return nc


def host_params(inputs):
    f = lambda k: np.ascontiguousarray(np.asarray(inputs[k], dtype=np.float32))
    m = {}
    for k in ['norm_mix_g', 'w_in', 'hgrn_out_g', 'mlstm_out_g', 'attn_sinks', 'rel_bias_table', 'dn_a_log', 'dn_dt_bias',
              'dn_out_g', 'w_branch', 'w_out', 'norm_mlp_g', 'w_up', 'w_down']:
        m[k] = f(k)
    m['mlstm_if_bias'] = f('mlstm_if_bias').reshape(2, 8)
    m['hgrn_lb_table'] = np.ascontiguousarray(f('hgrn_lb_table').reshape(2, 4, 128).transpose(0, 2, 1))
    m['attn_q_norm_g'] = f('attn_q_norm_g').reshape(2, 64, 1)
    m['attn_k_norm_g'] = f('attn_k_norm_g').reshape(2, 64, 1)
    m['dn_conv_w'] = np.ascontiguousarray(f('dn_conv_w').reshape(2, 4, 12, 128).transpose(0, 1, 3, 2))
    m['dn_alog_row'] = np.ascontiguousarray(f('dn_a_log').T)
    m['dn_dtb_row'] = np.ascontiguousarray(f('dn_dt_bias').T)
    return m


LAYERED = ['norm_mix_g', 'w_in', 'hgrn_out_g', 'mlstm_out_g', 'attn_sinks', 'dn_a_log', 'dn_dt_bias', 'dn_out_g',
           'w_branch', 'w_out', 'norm_mlp_g', 'w_up', 'w_down', 'mlstm_if_bias', 'attn_q_norm_g', 'attn_k_norm_g',
           'dn_conv_w']


def kernel(**inputs):
    x = np.ascontiguousarray(np.asarray(inputs['x'], dtype=np.float32))
    B, T, _ = x.shape
    NT = T // 128
    nc = build(T, 1, pipe=True, npairs=B)
    consts = host_consts()
    hp = host_params(inputs)
    hp_role = []
    for r in range(2):
        m = dict(hp)
        if r == 1:
            for k in LAYERED:
                m[k] = np.ascontiguousarray(hp[k][::-1])
            m['dn_alog_row'] = np.ascontiguousarray(hp['dn_alog_row'][:, ::-1])
            m['dn_dtb_row'] = np.ascontiguousarray(hp['dn_dtb_row'][:, ::-1])
        hp_role.append(m)
    zeros_x = np.zeros((T, D), np.float32)
    in_maps = []
    for b in range(B):
        for r in range(2):
            m = {'x': x[b] if r == 0 else zeros_x}
            m.update(hp_role[r])
            for k, v in consts.items():
                m['c_' + k] = v
            role = np.zeros((128, 2), np.float32)
            role[:, r] = 1.0
            pf = np.ones((128, NT + 1), np.float32)
            pf[:, 0:1 + r] = 0.0
            m['role'] = role
            m['pflag'] = pf
            in_maps.append(m)
    res = run_bass_kernel_spmd(nc, in_maps, core_ids=list(range(2 * B)))
    return np.stack([np.asarray(res.results[2 * b + 1]['y'], dtype=np.float32) for b in range(B)], axis=0)
```

```python
import types
import numpy as np
from contextlib import ExitStack
import concourse.bass as bass
import concourse.mybir as mybir
from concourse.bass_utils import run_bass_kernel_spmd

F32 = mybir.dt.float32
BF16 = mybir.dt.bfloat16
AF = mybir.ActivationFunctionType
ALU = mybir.AluOpType
AX = mybir.AxisListType
ENGS = ['pe', 'act', 'dve', 'pool', 'sp']

D = 1024
NIN = 10512
EPS = 1e-6


def _freeze(fn):
    if fn is None or fn.__closure__ is None:
        return fn
    cells = []
    for c in fn.__closure__:
        try:
            cells.append(types.CellType(c.cell_contents))
        except ValueError:
            cells.append(c)
    return types.FunctionType(fn.__code__, fn.__globals__, fn.__name__, fn.__defaults__, tuple(cells))


class V3:
    def __init__(self, ap):
        self.ap = ap

    def __getitem__(self, k):
        return self.ap[k]


class Prog:
    def __init__(self, nc, nd=8):
        self.nc = nc
        self.ND = nd
        self.lists = {e: [] for e in ENGS}
        self.cnt = {e: 0 for e in ENGS}
        self.waited = {e: {} for e in ENGS}
        self.W = {}
        self.Rd = {}
        self.dma_n = {e: 0 for e in ENGS}
        self.semkeys = set()
        self.st = ExitStack()
        self.ntens = 0
        self.alias = {}

    def sb(self, shape, dt, name=None):
        self.ntens += 1
        return self.st.enter_context(self.nc.sbuf_tensor(name or f"t{self.ntens}", list(shape), dt))

    def ps(self, shape, dt=F32, name=None):
        self.ntens += 1
        return self.st.enter_context(self.nc.psum_tensor(name or f"p{self.ntens}", list(shape), dt))

    def _deps(self, eng, reads, writes):
        deps = {}

        def add(d):
            for sk, v in d.items():
                if deps.get(sk, 0) < v:
                    deps[sk] = v
        for r in reads:
            add(self.W.get(r, {}))
        for w in writes:
            add(self.W.get(w, {}))
            add(self.Rd.get(w, {}))
        out = []
        for sk, v in deps.items():
            if sk == ('c', 'pe') and eng == 'pe':
                continue
            if self.waited[eng].get(sk, 0) >= v:
                continue
            self.waited[eng][sk] = v
            out.append((sk, v))
        return out

    def _rec(self, tok, reads, writes):
        sk, v = tok
        for r in reads:
            d = self.Rd.setdefault(r, {})
            d[sk] = max(d.get(sk, 0), v)
        for w in writes:
            d = self.W.setdefault(w, {})
            d[sk] = max(d.get(sk, 0), v)

    def _x(self, keys):
        out = []
        for k in keys:
            if isinstance(k, (list, tuple)):
                out.extend(self._x(k))
            elif k in self.alias:
                out.extend(self.alias[k])
            else:
                out.append(k)
        return out

    def op(self, eng, fn, reads=(), writes=(), inc=True):
        reads = self._x(reads)
        writes = self._x(writes)
        waits = self._deps(eng, reads, writes)
        sk = ('c', eng)
        tok = (sk, self.cnt[eng] + 1)
        if inc:
            self.cnt[eng] += 1
        self.semkeys.add(sk)
        self.lists[eng].append((waits, _freeze(fn), sk if inc else None, 1))
        self._rec(tok, reads, writes)

    def dma(self, q, out, in_, reads=(), writes=(), **kw):
        reads = self._x(reads)
        writes = self._x(writes)
        j = self.dma_n[q]
        self.dma_n[q] += 1
        slot = j % self.ND
        val = 16 * (j // self.ND + 1)
        sk = ('d', q, slot)
        self.semkeys.add(sk)
        waits = self._deps(q, reads, writes)
        if j >= self.ND and self.waited[q].get(sk, 0) < val - 16:
            self.waited[q][sk] = val - 16
            waits.append((sk, val - 16))
        self.lists[q].append((waits, (lambda e, o=out, i=in_, k=kw: e.dma_start(out=o, in_=i, **k)), sk, 16))
        self._rec((sk, val), reads, writes)

    def coll(self, kind, op, groups, ins_ap, outs_ap, reads=(), writes=()):
        reads = self._x(reads)
        writes = self._x(writes)
        waits = self._deps('pool', reads, writes)
        sk = ('cc',)
        self.cc_n = getattr(self, 'cc_n', 0) + 1
        self.semkeys.add(sk)
        self.lists['pool'].append((waits, (lambda e: e.collective_compute(kind, op, replica_groups=groups, ins=[ins_ap.opt()],
                                                                          outs=[outs_ap.opt()])), sk, 1))
        self._rec((sk, self.cc_n), reads, writes)

    def final_wait(self, eng, keys):
        keys = self._x(keys)
        waits = self._deps(eng, keys, ())
        self.lists[eng].append((waits, None, None, 0))

    def emit(self):
        nc = self.nc
        st = self.st
        sems = {}
        for sk in sorted(self.semkeys, key=str):
            sems[sk] = st.enter_context(nc.semaphore("s_" + "_".join(map(str, sk))))
        block = st.enter_context(nc.Block())
        lists = self.lists

        def run(name, e):
            for waits, fn, sk, incv in lists[name]:
                for wsk, v in waits:
                    e.wait_ge(sems[wsk], v)
                if fn is None:
                    continue
                ins = fn(e)
                if sk is not None:
                    ins.then_inc(sems[sk], incv)

        @block.tensor
        def _(e):
            run('pe', e)

        @block.scalar
        def _(e):
            run('act', e)

        @block.vector
        def _(e):
            run('dve', e)

        @block.gpsimd
        def _(e):
            run('pool', e)

        @block.sync
        def _(e):
            run('sp', e)
        st.close()


def _t5_bucket_np(n):
    max_exact = 16
    nf = np.maximum(n, max_exact).astype(np.float32)
    large = max_exact + (np.log(nf / max_exact) / np.log(np.float32(128 / max_exact)) * 16).astype(np.int32)
    large = np.minimum(large, 31)
    return np.where(n < max_exact, n, large)


def host_consts():
    c = {}
    s = np.arange(128)[:, None]
    t = np.arange(128)[None, :]
    c['tri_incl'] = (s <= t).astype(np.float32)
    c['tri_strict'] = (s < t).astype(np.float32)
    c['hgmask'] = ((s <= t) & (s // 32 == t // 32)).astype(np.float32)
    c['negmask'] = np.where(s <= t, 0.0, -1e30).astype(np.float32)
    c['ident'] = np.eye(128, dtype=np.float32)
    sel = np.zeros((128, 128), np.float32)
    sel[127, :] = 1.0
    c['sel_last'] = sel
    rm = np.zeros((128, 4), np.float32)
    for j in range(4):
        rm[32 * j:32 * j + 32, j] = 1.0
    c['rowm'] = rm
    rs = np.ones((128, 128), np.float32)
    rs[:, 0::32] = 0.0
    c['resetm'] = rs
    selh = np.zeros((4, 4, 128), np.float32)
    for h in range(4):
        selh[h, h, :] = 1.0
    c['selh'] = selh.reshape(4, 512)
    c['ones'] = np.ones((128, 128), np.float32)
    bk = _t5_bucket_np(np.arange(128))
    oh = np.zeros((32, 128), np.float32)
    oh[bk, np.arange(128)] = 1.0
    c['bias_oh'] = oh
    ab = np.zeros((128, 384), np.float32)
    for dd in range(128):
        ab[dd, 255 - dd] = 1.0
    c['antiband'] = ab
    return c


CONST_SHAPES = {
    'tri_incl': (128, 128), 'tri_strict': (128, 128), 'hgmask': (128, 128), 'negmask': (128, 128),
    'ident': (128, 128), 'sel_last': (128, 128), 'rowm': (128, 4), 'resetm': (128, 128),
    'selh': (4, 512), 'ones': (128, 128), 'bias_oh': (32, 128), 'antiband': (128, 384),
}

PARAM_SHAPES = {
    'norm_mix_g': (2, 1024), 'w_in': (2, 1024, NIN), 'hgrn_lb_table': (2, 128, 4), 'hgrn_out_g': (2, 512),
    'mlstm_if_bias': (2, 8), 'mlstm_out_g': (2, 512), 'attn_q_norm_g': (2, 64, 1), 'attn_k_norm_g': (2, 64, 1),
    'attn_sinks': (2, 8), 'rel_bias_table': (32, 8), 'dn_conv_w': (2, 4, 128, 12), 'dn_a_log': (2, 4),
    'dn_dt_bias': (2, 4), 'dn_alog_row': (4, 2), 'dn_dtb_row': (4, 2), 'dn_out_g': (2, 128), 'w_branch': (2, 4, 512, 1024), 'w_out': (2, 1024, 1024),
    'norm_mlp_g': (2, 1024), 'w_up': (2, 1024, 4096), 'w_down': (2, 4096, 1024),
}

C_HQ, C_HF, C_HI, C_HG = 0, 512, 1024, 1536
C_MQ, C_MK, C_MV, C_MI, C_MF, C_MO = 2048, 2304, 2560, 3072, 3076, 3080
C_AQ, C_AK, C_AV = 3592, 4104, 4232
C_DQKV, C_DB, C_DA, C_DZ = 4360, 5896, 5900, 5904
C_GATE = 6416


def build(T, L=2, enable=(1, 1, 1, 1), dbg=None, pipe=False, npairs=4):
    nc = bass.Bass("TRN2", target_bir_lowering=False)
    NTILES = T // 128
    x_in = nc.dram_tensor("x", [T, D], F32, kind="ExternalInput").ap()
    y_out = nc.dram_tensor("y", [T, D], F32, kind="ExternalOutput").ap()
    prm = {k: nc.dram_tensor(k, list(s), F32, kind="ExternalInput").ap() for k, s in PARAM_SHAPES.items()}
    cst = {k: nc.dram_tensor("c_" + k, list(s), F32, kind="ExternalInput").ap() for k, s in CONST_SHAPES.items()}
    NIT = NTILES + 1 if pipe else NTILES
    if pipe:
        assert L == 1
        role_in = nc.dram_tensor("role", [128, 2], F32, kind="ExternalInput").ap()
        pflag_in = nc.dram_tensor("pflag", [128, NIT], F32, kind="ExternalInput").ap()
        sendb = nc.dram_tensor("sendb", [2, 128, 1024], F32).ap()
        recvb = nc.dram_tensor("recvb", [2, 128, 1024], F32).ap()
    dbg_out = {}
    p = Prog(nc)
    sb, ps = p.sb, p.ps

    def load_const(name, dt=F32, q='sp'):
        shp = CONST_SHAPES[name]
        t_ = sb(shp, dt, "k_" + name + ("_b" if dt == BF16 else ""))
        p.dma(q, t_[:], cst[name], writes=['k_' + name])
        return t_
    tri_incl = load_const('tri_incl')
    tri_strict = load_const('tri_strict')
    hgmask = load_const('hgmask')
    negmask = load_const('negmask')
    ident_f = load_const('ident')
    ident_b = load_const('ident', BF16, 'pool')
    sel_last = load_const('sel_last')
    rowm = load_const('rowm')
    resetm = load_const('resetm')
    selh = load_const('selh')
    ones_f = load_const('ones')
    ones_b = load_const('ones', BF16, 'pool')
    KC = ['k_tri_incl', 'k_tri_strict', 'k_hgmask', 'k_negmask', 'k_ident', 'k_sel_last', 'k_rowm', 'k_resetm',
          'k_selh', 'k_ones']

    gmix = sb([128, L, 1024], BF16, "gmix")
    gmlp = sb([128, L, 1024], BF16, "gmlp")
    hog = sb([128, L, 512], BF16, "hog")
    mog = sb([128, L, 512], BF16, "mog")
    dog = sb([128, L, 128], F32, "dog")
    mifb = sb([128, L, 8], F32, "mifb")
    sinks = sb([128, L, 8], F32, "sinks")
    esink = sb([128, L, 8], F32, "esink")
    alog = sb([128, L, 4], F32, "alog")
    nexpa = sb([128, L, 4], F32, "nexpa")
    dtb = sb([128, L, 4], F32, "dtb")
    lbt = sb([128, 2, 4], F32, "lbt")
    lb = sb([128, 2, 4], F32, "lb")
    oml = sb([128, 2, 4], F32, "oml")
    convw = sb([128, L, 4, 12], F32, "convw")
    qkg = sb([64, L, 2], F32, "qkg")
    gk8 = sb([64, L], F32, "gk8")
    dtbrow = sb([4, 2], F32, "dtbrow")
    nexparow = sb([4, 2], F32, "nexparow")
    for l in range(L):
        p.dma('pool', gmix[:, l, :], prm['norm_mix_g'][l:l + 1, :].partition_broadcast(128), writes=['prm'])
        p.dma('pool', gmlp[:, l, :], prm['norm_mlp_g'][l:l + 1, :].partition_broadcast(128), writes=['prm'])
        p.dma('pool', hog[:, l, :], prm['hgrn_out_g'][l:l + 1, :].partition_broadcast(128), writes=['prm'])
        p.dma('pool', mog[:, l, :], prm['mlstm_out_g'][l:l + 1, :].partition_broadcast(128), writes=['prm'])
        p.dma('sp', dog[:, l, :], prm['dn_out_g'][l:l + 1, :].partition_broadcast(128), writes=['prm'])
        p.dma('sp', mifb[:, l, :], prm['mlstm_if_bias'][l:l + 1, :].partition_broadcast(128), writes=['prm'])
        p.dma('sp', sinks[:, l, :], prm['attn_sinks'][l:l + 1, :].partition_broadcast(128), writes=['prm'])
        p.dma('sp', alog[:, l, :], prm['dn_a_log'][l:l + 1, :].partition_broadcast(128), writes=['prm'])
        p.dma('sp', dtb[:, l, :], prm['dn_dt_bias'][l:l + 1, :].partition_broadcast(128), writes=['prm'])
        for j in range(4):
            p.dma('sp', convw[:, l, j, :], prm['dn_conv_w'][l, j], writes=['prm'])
        p.dma('sp', qkg[:, l, 0:1], prm['attn_q_norm_g'][l], writes=['prm'])
        p.dma('sp', qkg[:, l, 1:2], prm['attn_k_norm_g'][l], writes=['prm'])
    for l2 in range(2):
        p.dma('sp', lbt[:, l2, :], prm['hgrn_lb_table'][l2], writes=['prm'])
    p.dma('sp', dtbrow[:], prm['dn_dtb_row'], writes=['dtbrow'])
    p.dma('sp', nexparow[:], prm['dn_alog_row'], writes=['nexparow'])
    p.op('act', lambda e: e.activation(out=nexparow[:], in_=nexparow[:], func=AF.Exp), reads=['nexparow'], writes=['nexparow'])
    p.op('dve', lambda e: e.tensor_scalar(out=nexparow[:], in0=nexparow[:], scalar1=-1.0, scalar2=None, op0=ALU.mult),
         reads=['nexparow'], writes=['nexparow'])
    p.op('dve', lambda e: e.memset(lb[:], 0.0), writes=['lb'])
    p.op('dve', lambda e: e.tensor_sub(out=lb[:, 1, :], in0=lbt[:, 1, :], in1=lbt[:, 0, :]), reads=['prm', 'lb'],
         writes=['lb'])
    p.op('act', lambda e: e.activation(out=lb[:, 1, :], in_=lb[:, 1, :], func=AF.Sigmoid), reads=['lb'], writes=['lb'])
    if pipe:
        role = sb([128, 2], F32, "role_sb")
        pflag = sb([128, NIT], F32, "pflag_sb")
        p.dma('sp', role[:], role_in, writes=['role'])
        p.dma('sp', pflag[:], pflag_in, writes=['pflag'])
        p.op('dve', lambda e: e.tensor_scalar(out=lb[:, 0, :], in0=lb[:, 1, :], scalar1=role[:, 1:2], scalar2=None,
                                              op0=ALU.mult), reads=['lb', 'role'], writes=['lb'])
    p.op('dve', lambda e: e.tensor_scalar(out=oml[:], in0=lb[:], scalar1=-1.0, scalar2=1.0, op0=ALU.mult, op1=ALU.add),
         reads=['lb'], writes=['oml'])
    p.op('act', lambda e: e.activation(out=esink[:], in_=sinks[:], func=AF.Exp), reads=['prm'], writes=['esink'])
    p.op('act', lambda e: e.activation(out=nexpa[:], in_=alog[:], func=AF.Exp), reads=['prm'], writes=['nexpa'])
    p.op('dve', lambda e: e.tensor_scalar(out=nexpa[:], in0=nexpa[:], scalar1=-1.0, scalar2=None, op0=ALU.mult),
         reads=['nexpa'], writes=['nexpa'])
    p.op('dve', lambda e: e.tensor_tensor(out=gk8[:], in0=qkg[:, :, 0], in1=qkg[:, :, 1], op=ALU.mult), reads=['prm'],
         writes=['gk8'])
    p.op('dve', lambda e: e.tensor_scalar(out=gk8[:], in0=gk8[:], scalar1=0.125, scalar2=None, op0=ALU.mult),
         reads=['gk8'], writes=['gk8'])

    PS = [ps([128, 512], F32, f"PS{i}") for i in range(7)]
    PT = ps([128, 1024], BF16, "PSTR")

    def K(i, lo=0, hi=512):
        return [f'PS{i}.bank']
    for i in range(7):
        p.alias[f'PS{i}'] = K(i)
    p.alias['PS5b'] = K(5, 128, 256)
    p.alias['PS5c'] = K(5, 256, 384)
    p.alias['PS6b'] = K(6, 128, 256)
    p.alias['PS6c'] = K(6, 256, 384)
    p.alias['PS6d'] = K(6, 384, 512)
    for i in range(5):
        p.alias[f'scr{i}'] = [f'scr.{i}']
    for nm in ['tmpA', 'tmpB', 'junk']:
        p.alias[nm] = [f'{nm}.{i}' for i in range(4)]
    p.alias['h_S'] = [f'h_S.{i}' for i in range(4)]
    p.alias['h_Sb'] = [f'h_Sb.{i}' for i in range(4)]
    p.alias['h_qj'] = [f'h_qj.{i}' for i in range(4)]
    p.alias['m_C'] = [f'm_C.{i}' for i in range(4)]
    p.alias['m_Cb'] = [f'm_Cb.{i}' for i in range(4)]
    p.alias['h_q'] = ['scr.0']
    p.alias['h_f'] = ['scr.1']
    p.alias['h_k'] = ['scr.2']
    p.alias['h_cum'] = ['scr.3']
    p.alias['h_e'] = ['scr.4']
    p.alias['a_z'] = ['scr.0', 'scr.1', 'scr.2']
    p.alias['a_sq'] = ['scr.2', 'scr.3', 'scr.4']
    p.alias['d_y'] = ['scr.0', 'scr.1', 'scr.2']
    EB = sb([128, 2, 8, 128], F32, "EB")
    relt = sb([32, 8], F32, "relt")
    boh = sb([32, 128], F32, "boh")
    aband = sb([128, 384], F32, "aband")
    vecE = sb([128, 8], F32, "vecE")
    p.dma('sp', relt[:], prm['rel_bias_table'], writes=['relt'])
    p.dma('sp', boh[:], cst['bias_oh'], writes=['boh'])
    p.dma('sp', aband[:], cst['antiband'], writes=['aband'])
    p.op('pe', lambda e: e.matmul(PS[0][:, 0:8], lhsT=boh[:], rhs=relt[:], start=True, stop=True), reads=['boh', 'relt'],
         writes=K(0))
    p.op('act', lambda e: e.activation(out=vecE[:], in_=PS[0][:, 0:8], func=AF.Exp), reads=K(0), writes=['vecE'])
    for blk in range(2):
        for t0 in range(0, 128, 64):
            pst = PS[1]
            for tt in range(64):
                off = 255 - (t0 + tt + (128 if blk == 0 else 0))
                p.op('pe', (lambda e, off=off, tt=tt: e.matmul(pst[:, tt * 8:(tt + 1) * 8], lhsT=aband[:, off:off + 128],
                                                               rhs=vecE[:], start=True, stop=True)),
                     reads=['aband', 'vecE'], writes=K(1), inc=(tt == 63))
            p.op('dve', (lambda e, blk=blk, t0=t0: e.tensor_copy(
                out=EB[:, blk, :, t0:t0 + 64], in_=pst[:].rearrange("p (t g) -> p g t", g=8))),
                reads=K(1), writes=['EB'])

    xt = sb([128, 1024], F32, "xt")
    hbf = sb([128, 1024], BF16, "hbf")
    hT = sb([128, 8, 128], BF16, "hT")
    NWB = 7
    wbuf = [sb([128, 8, 520], BF16, f"wbuf{i}") for i in range(NWB)]
    wb_n = [0]
    sm = sb([128, 128], F32, "small")
    obf = sb([128, 4, 512], BF16, "obf")
    oT = sb([128, 16, 128], BF16, "oT")
    gates = sb([128, 4096], BF16, "gates")
    merged = sb([128, 1024], F32, "merged")
    mbf = sb([128, 1024], BF16, "mbf")
    mT = sb([128, 8, 128], BF16, "mT")
    uT = sb([128, 32, 128], BF16, "uT")
    tmpA = sb([128, 512], F32, "tmpA")
    tmpB = sb([128, 512], F32, "tmpB")
    tmpC = sb([128, 512], F32, "tmpC")
    junk = sb([128, 1024], BF16, "junk")

    scr = sb([128, 2560], F32, "scr")

    class V:
        def __init__(self, ap):
            self.ap = ap

        def __getitem__(self, k):
            return self.ap[k]
    h_q = V(scr[:, 0:512].rearrange("p (a b) -> p a b", a=4))
    h_f = V(scr[:, 512:1024].rearrange("p (a b) -> p a b", a=4))
    h_k = V(scr[:, 1024:1536].rearrange("p (a b) -> p a b", a=4))
    h_cum = V(scr[:, 1536:2048].rearrange("p (a b) -> p a b", a=4))
    h_e = V(scr[:, 2048:2560].rearrange("p (a b) -> p a b", a=4))
    h_qp = sb([128, 4, 128], BF16, "h_qp")
    h_kp = sb([128, 4, 128], BF16, "h_kp")
    h_kpp = sb([128, 4, 128], BF16, "h_kpp")
    h_edec = sb([128, 4, 4], F32, "h_edec")
    h_v = sb([128, 512], BF16, "h_v")
    h_g = sb([128, 512], F32, "h_g")
    hS_AT = [sb([128, 128], BF16, f"h_AT{i}") for i in range(2)]
    hS_kj = [sb([128, 4, 128], BF16, f"h_kj{i}") for i in range(2)]
    h_qj = sb([128, 4, 4, 128], BF16, "h_qj")
    h_S = [sb([128, L, 4, 128], F32, "h_S")]
    h_Sb = sb([128, L, 4, 128], BF16, "h_Sb")

    m_q = sb([64, 4, 128], BF16, "m_q")
    m_kT = sb([64, 4, 128], BF16, "m_kT")
    m_k = sb([128, 256], BF16, "m_k")
    m_v = sb([128, 512], F32, "m_v")
    m_if = sb([128, 8], F32, "m_if")
    m_o = sb([128, 512], F32, "m_o")
    m_va = sb([128, 4, 130], BF16, "m_va")
    mS_AT = [sb([128, 128], BF16, f"m_AT{i}") for i in range(2)]
    m_C = sb([64, L, 4, 130], F32, "m_C")
    m_Cb = sb([64, L, 4, 130], BF16, "m_Cb")

    a_q = sb([64, 8, 128], BF16, "a_q")
    a_z = V(scr[0:64, 0:1280].rearrange("p (a b) -> p a b", a=10))
    a_sq = V(scr[0:64, 1280:2560].rearrange("p (a b) -> p a b", a=10))
    a_kT = sb([64, L, 2, 2, 128], BF16, "a_kT")
    a_v = sb([128, L, 2, 2, 66], BF16, "a_v")
    aS_P = [sb([128, 256], F32, f"a_P{i}") for i in range(2)]
    aS_PT = [sb([128, 2, 128], BF16, f"a_PT{i}") for i in range(2)]

    d_x = sb([128, L, 12, 132], BF16, "d_x")
    d_y = V(scr[:, 0:1536].rearrange("p (a b) -> p a b", a=12))
    d_sq = sb([128, 128], BF16, "d_sq")
    d_qT = sb([128, 4, 128], BF16, "d_qT")
    d_kT = sb([128, 4, 128], BF16, "d_kT")
    d_vT = sb([128, 4, 128], BF16, "d_vT")
    d_v = sb([128, 4, 128], F32, "d_v")
    d_k = sb([128, 4, 128], BF16, "d_k")
    d_ba = sb([128, 8], F32, "d_ba")
    d_arow = sb([4, 128], F32, "d_arow")
    d_grow = sb([4, 128], F32, "d_grow")
    d_z = sb([128, 512], F32, "d_z")
    dS_E1 = [sb([128, 128], F32, f"d_E1_{i}") for i in range(2)]
    dS_E1s = [sb([128, 128], F32, f"d_E1s_{i}") for i in range(2)]
    dS_P = [[sb([128, 128], F32, f"d_P{j}_{i}") for j in range(2)] for i in range(2)]
    dS_PT = [[sb([128, 128], F32, f"d_PT{j}_{i}") for j in range(2)] for i in range(2)]
    dS_TT = [sb([128, 128], F32, f"d_TT_{i}") for i in range(2)]
    dS_r = [sb([128, 128], F32, f"d_r_{i}") for i in range(2)]
    dS_vn = [sb([128, 128], BF16, f"d_vn_{i}") for i in range(2)]
    dS_vc = [sb([128, 128], BF16, f"d_vc_{i}") for i in range(2)]
    dS_aT = [sb([128, 128], BF16, f"d_aT_{i}") for i in range(2)]
    dS_o1 = [sb([128, 128], F32, f"d_o1_{i}") for i in range(2)]
    d_S = sb([128, L, 4, 128], F32, "d_S")
    d_Sb = sb([128, L, 4, 128], BF16, "d_Sb")

    for (t_, k) in [(h_S[0], 'h_S'), (h_Sb, 'h_Sb'), (m_C, 'm_C'), (m_Cb, 'm_Cb'), (d_S, 'd_S'), (d_Sb, 'd_Sb'),
                    (d_x, 'd_x'), (h_qj, 'h_qj'), (a_kT, 'a_kT'), (a_v, 'a_v')]:
        p.op('pool', (lambda e, t_=t_: e.memset(t_[:], 0.0)), writes=[k])

    NPIECE = 43
    wscr = nc.dram_tensor("wscr", [L, NPIECE, 128, 4160], BF16, kind="Internal").ap()
    piece_ctr = {}

    def _piece(l_, ti_):
        k = (l_, ti_)
        i = piece_ctr.get(k, 0)
        piece_ctr[k] = i + 1
        assert i < NPIECE
        return i

    def wload(src_ap, ncol, l_, ti_):
        pi = _piece(l_, ti_)
        scr_ap = wscr[l_, pi, :, 0:8 * ncol]
        skey = ('wscr', l_, pi)
        if ti_ == 0:
            p.dma('pool', scr_ap.rearrange("p (kc n) -> p kc n", kc=8), src_ap.rearrange("(kc p) n -> p kc n", p=128),
                  writes=[skey])
        i = wb_n[0] % NWB
        wb_n[0] += 1
        key = f'wbuf{i}'
        dst = wbuf[i][:].rearrange("p a b -> p (a b)")[:, 0:8 * ncol]
        p.dma('sp', dst, scr_ap, reads=[skey], writes=[key])
        return V3(dst.rearrange("p (kc n) -> p kc n", kc=8)), key

    def wload_rows(src_ap, l_, ti_):
        pi = _piece(l_, ti_)
        scr_ap = wscr[l_, pi, :, 0:4096]
        skey = ('wscr', l_, pi)
        if ti_ == 0:
            p.dma('pool', scr_ap.rearrange("p (r n) -> p r n", r=4), src_ap.rearrange("(r p) n -> p r n", p=128),
                  writes=[skey])
        i = wb_n[0] % NWB
        wb_n[0] += 1
        key = f'wbuf{i}'
        dst = wbuf[i][:].rearrange("p a b -> p (a b)")[:, 0:4096]
        p.dma('sp', dst, scr_ap, reads=[skey], writes=[key])
        return V3(dst.rearrange("p (r n) -> p r n", r=4)), key

    def proj_fm(wb, wkey, c0, ncol, ps_ap, pskey):
        for kc in range(8):
            p.op('pe', (lambda e, kc=kc: e.matmul(ps_ap, lhsT=wb[:, kc, c0:c0 + ncol], rhs=hT[:, kc, :],
                                                  start=(kc == 0), stop=(kc == 7))),
                 reads=[wkey, 'hT'], writes=[pskey], inc=(kc == 7))

    def proj_tm(wb, wkey, c0, ncol, ps_ap, pskey):
        for kc in range(8):
            p.op('pe', (lambda e, kc=kc: e.matmul(ps_ap, lhsT=hT[:, kc, :], rhs=wb[:, kc, c0:c0 + ncol],
                                                  start=(kc == 0), stop=(kc == 7))),
                 reads=[wkey, 'hT'], writes=[pskey], inc=(kc == 7))

    def rmsnorm_to_T(src, gt, dstT, dstkey, l):
        p.op('act', lambda e: e.activation(out=junk[:], in_=src[:], func=AF.Square, accum_out=sm[:, 0:1]),
             reads=['xt'], writes=['junk.0', 'junk.1', 'junk.2', 'junk.3', 'sm0'])
        p.op('act', lambda e: e.activation(out=sm[:, 1:2], in_=sm[:, 0:1], func=AF.Ln, scale=1.0 / 1024, bias=epsb[:, 0:1]),
             reads=['sm0', 'epsb'], writes=['sm1'])
        p.op('act', lambda e: e.activation(out=sm[:, 2:3], in_=sm[:, 1:2], func=AF.Exp, scale=-0.5),
             reads=['sm1'], writes=['sm2'])
        p.op('dve', lambda e: e.scalar_tensor_tensor(out=hbf[:], in0=src[:], scalar=sm[:, 2:3], in1=gt[:, l, :],
                                                     op0=ALU.mult, op1=ALU.mult),
             reads=['xt', 'sm2', 'prm'], writes=['hbf'])
        for kc in range(8):
            p.op('pe', (lambda e, kc=kc: e.transpose(out=PT[:, kc * 128:(kc + 1) * 128], in_=hbf[:, kc * 128:(kc + 1) * 128],
                                                     identity=ident_b[:])),
                 reads=['hbf', 'k_ident'], writes=['PT'], inc=(kc == 7))
        p.op('act', lambda e: e.copy(out=dstT[:].rearrange("p a b -> p (a b)"), in_=PT[:]), reads=['PT'], writes=[dstkey])

    epsb = sb([128, 2], F32, "epsb")
    p.op('dve', lambda e: e.memset(epsb[:, 0:1], EPS), writes=['epsb'])
    p.op('dve', lambda e: e.memset(epsb[:, 1:2], 1.0), writes=['epsb'])

    def head_norm_gate(src_ap, srckey, srcreads, gtile_ap, gate_ap, gatekey, out_ap, smc, sl=0):
        jk = junk[:, sl * 128:(sl + 1) * 128]
        tc_ = tmpC[:, sl * 128:(sl + 1) * 128]
        p.op('act', lambda e: e.activation(out=jk, in_=src_ap, func=AF.Square, accum_out=sm[:, smc:smc + 1]),
             reads=[srckey] + srcreads, writes=[f'junk.{sl}', f'sm{smc}'])
        p.op('act', lambda e: e.activation(out=sm[:, smc + 1:smc + 2], in_=sm[:, smc:smc + 1], func=AF.Ln, scale=1.0 / 128,
                                           bias=epsb[:, 0:1]), reads=[f'sm{smc}', 'epsb'], writes=[f'sm{smc+1}'])
        p.op('act', lambda e: e.activation(out=sm[:, smc + 2:smc + 3], in_=sm[:, smc + 1:smc + 2], func=AF.Exp, scale=-0.5),
             reads=[f'sm{smc+1}'], writes=[f'sm{smc+2}'])
        p.op('dve', lambda e: e.scalar_tensor_tensor(out=tc_, in0=src_ap, scalar=sm[:, smc + 2:smc + 3],
                                                     in1=gtile_ap, op0=ALU.mult, op1=ALU.mult),
             reads=[srckey, f'sm{smc+2}', 'prm'] + srcreads, writes=[f'tmpC.{sl}'])
        p.op('dve', lambda e: e.tensor_tensor(out=out_ap, in0=tc_, in1=gate_ap, op=ALU.mult),
             reads=[f'tmpC.{sl}', gatekey], writes=['obf'])

    def interleave(gens):
        gens = list(gens)
        while gens:
            for g in list(gens):
                try:
                    next(g)
                except StopIteration:
                    gens.remove(g)

    if pipe:
        xh = sb([128, 1024], F32, "xh")
        xr = sb([128, 1024], F32, "xr")
        p.op('pool', lambda e: e.memset(xr[:], 0.0), writes=['xr'])
        for j in range(2):
            p.dma('sp', sendb[j], xr[:], reads=['xr'], writes=[('sendb', j)])
            p.dma('sp', recvb[j], xr[:], reads=['xr'], writes=[('recvb', j)])
    for ti in range(NIT):
        if pipe:
            tix = min(ti, NTILES - 1)
            p.dma('sp', xh[:], x_in[tix * 128:(tix + 1) * 128, :], writes=['xh'])
            p.coll("AllReduce", ALU.add, [[2 * i_, 2 * i_ + 1] for i_ in range(npairs)], sendb[ti % 2], recvb[ti % 2],
                   reads=[('sendb', ti % 2)], writes=[('recvb', ti % 2)])
            p.dma('sp', xr[:], recvb[ti % 2], reads=[('recvb', ti % 2)], writes=['xr'])
            p.op('dve', lambda e: e.scalar_tensor_tensor(out=xt[:], in0=xr[:], scalar=role[:, 1:2], in1=xh[:], op0=ALU.mult,
                                                         op1=ALU.add), reads=['xr', 'xh', 'role'], writes=['xt'])
        else:
            p.dma('sp', xt[:], x_in[ti * 128:(ti + 1) * 128, :], writes=['xt'])
        for l in range(L):
            par = ti % 2
            W_in = prm['w_in'][l]
            rmsnorm_to_T(xt, gmix, hT, 'hT', l)
            gates_done = [0]

            def gate_piece(gi):
                wb, wk = wload(W_in[:, C_GATE + gi * 512:C_GATE + (gi + 1) * 512], 512, l, ti)
                pst, pk = (PS[0], 'PS0') if gi % 2 == 0 else (PS[1], 'PS1')
                proj_tm(wb, wk, 0, 512, pst[:], pk)
                p.op('act', lambda e: e.activation(out=gates[:, gi * 512:(gi + 1) * 512], in_=pst[:], func=AF.Sigmoid),
                     reads=[pk], writes=['gates'])
                gates_done[0] = gi + 1
            if not all(enable):
                p.op('pool', lambda e: e.memset(obf[:], 0.0), writes=['obf'])

            if enable[0]:
                wb, wk = wload(W_in[:, C_HQ:C_HQ + 512], 512, l, ti)
                for h in range(4):
                    proj_fm(wb, wk, h * 128, 128, PS[0][:, h * 128:(h + 1) * 128], 'PS0')
                p.op('act', lambda e: e.activation(out=h_q[:].rearrange("p a b -> p (a b)"), in_=PS[0][:], func=AF.Silu),
                     reads=['PS0'], writes=['h_q'])
                wb, wk = wload(W_in[:, C_HF:C_HF + 512], 512, l, ti)
                for h in range(4):
                    proj_fm(wb, wk, h * 128, 128, PS[1][:, h * 128:(h + 1) * 128], 'PS1')
                p.op('act', lambda e: e.activation(out=h_f[:].rearrange("p a b -> p (a b)"), in_=PS[1][:], func=AF.Sigmoid),
                     reads=['PS1'], writes=['h_f'])
                for h in range(4):
                    p.op('dve', (lambda e, h=h: e.tensor_scalar(out=h_f[:, h, :], in0=h_f[:, h, :], scalar1=oml[:, l, h:h + 1],
                                                                scalar2=lb[:, l, h:h + 1], op0=ALU.mult, op1=ALU.add)),
                         reads=['h_f', 'oml', 'lb'], writes=['h_f'])
                hf2 = h_f[:].rearrange("p a b -> p (a b)")
                hk2 = h_k[:].rearrange("p a b -> p (a b)")
                hc2 = h_cum[:].rearrange("p a b -> p (a b)")
                he2 = h_e[:].rearrange("p a b -> p (a b)")
                hq2 = h_q[:].rearrange("p a b -> p (a b)")
                p.op('dve', lambda e: e.tensor_scalar(out=hk2, in0=hf2, scalar1=-1.0, scalar2=1.0, op0=ALU.mult, op1=ALU.add),
                     reads=['h_f'], writes=['h_k'])
                p.op('act', lambda e: e.activation(out=hf2, in_=hf2, func=AF.Ln), reads=['h_f', 'h_k'], writes=['h_f'])
                for h in range(4):
                    p.op('dve', (lambda e, h=h: e.tensor_tensor_scan(out=h_cum[:, h, :], data0=resetm[:], data1=h_f[:, h, :],
                                                                     initial=0.0, op0=ALU.mult, op1=ALU.add)),
                         reads=['h_f', 'k_resetm'], writes=['h_cum'])
                p.op('act', lambda e: e.activation(out=he2, in_=hc2, func=AF.Exp), reads=['h_cum'], writes=['h_e'])
                p.op('dve', lambda e: e.tensor_tensor(out=h_qp[:].rearrange("p a b -> p (a b)"), in0=hq2, in1=he2, op=ALU.mult),
                     reads=['h_q', 'h_e'], writes=['h_qp'])
                p.op('act', lambda e: e.activation(out=he2, in_=hc2, func=AF.Exp, scale=-1.0), reads=['h_cum', 'h_qp'],
                     writes=['h_e'])
                p.op('dve', lambda e: e.tensor_tensor(out=h_kp[:].rearrange("p a b -> p (a b)"), in0=hk2, in1=he2, op=ALU.mult),
                     reads=['h_k', 'h_e'], writes=['h_kp'])
                cl = h_cum[:].rearrange("p a (j i) -> p a j i", i=32)[:, :, :, 31]
                p.op('act', lambda e: e.activation(out=h_edec[:], in_=cl, func=AF.Exp), reads=['h_cum'], writes=['h_edec'])
                p.op('dve', lambda e: e.tensor_tensor(
                    out=h_e[:].rearrange("p a (j i) -> p a j i", i=32),
                    in0=h_cum[:].rearrange("p a (j i) -> p a j i", i=32)[:, :, :, 31:32].broadcast_to([128, 4, 4, 32]),
                    in1=h_cum[:].rearrange("p a (j i) -> p a j i", i=32), op=ALU.subtract),
                    reads=['h_cum', 'h_kp'], writes=['h_e'])
                p.op('act', lambda e: e.activation(out=he2, in_=he2, func=AF.Exp), reads=['h_e'], writes=['h_e'])
                p.op('dve', lambda e: e.tensor_tensor(out=h_kpp[:].rearrange("p a b -> p (a b)"), in0=hk2, in1=he2, op=ALU.mult),
                     reads=['h_k', 'h_e'], writes=['h_kpp'])
                wb, wk = wload(W_in[:, C_HI:C_HI + 512], 512, l, ti)
                proj_tm(wb, wk, 0, 512, PS[0][:], 'PS0')
                p.op('act', lambda e: e.copy(out=h_v[:], in_=PS[0][:]), reads=['PS0'], writes=['h_v'])
                wb, wk = wload(W_in[:, C_HG:C_HG + 512], 512, l, ti)
                proj_tm(wb, wk, 0, 512, PS[1][:], 'PS1')
                p.op('act', lambda e: e.activation(out=h_g[:], in_=PS[1][:], func=AF.Silu), reads=['PS1'], writes=['h_g'])
                def hg_head(h, sl):
                    bX, bO = PS[2 + 2 * sl], PS[3 + 2 * sl]
                    kX, kO = f'PS{2 + 2 * sl}', f'PS{3 + 2 * sl}'
                    AT, kj = hS_AT[sl], hS_kj[sl]
                    n = f'.{sl}'
                    p.op('pe', lambda e: e.matmul(bX[:, 0:128], lhsT=h_kp[:, h, :], rhs=h_qp[:, h, :], start=True, stop=True),
                         reads=['h_kp', 'h_qp'], writes=[kX])
                    p.op('dve', lambda e: e.tensor_tensor(out=AT[:], in0=bX[:, 0:128], in1=hgmask[:], op=ALU.mult),
                         reads=[kX, 'k_hgmask'], writes=['h_AT' + n])
                    p.op('pe', lambda e: e.transpose(out=PT[:, sl * 128:(sl + 1) * 128], in_=h_kpp[:, h, :], identity=ident_b[:]),
                         reads=['h_kpp', 'k_ident'], writes=['PT'])
                    for j in range(4):
                        p.op('dve', lambda e: e.tensor_scalar(out=kj[:, j, :], in0=PT[:, sl * 128:(sl + 1) * 128],
                                                              scalar1=rowm[:, j:j + 1], scalar2=None, op0=ALU.mult),
                             reads=['PT', 'k_rowm'], writes=['h_kj' + n])
                        p.op('act', lambda e: e.copy(out=h_qj[:, h, j, 32 * j:32 * j + 32], in_=h_qp[:, h, 32 * j:32 * j + 32]),
                             reads=['h_qp'], writes=[f'h_qj.{h}'])
                    yield
                    p.op('pe', lambda e: e.matmul(bO[:, 0:128], lhsT=AT[:], rhs=h_v[:, h * 128:(h + 1) * 128], start=True,
                                                  stop=False), reads=['h_AT' + n, 'h_v'], writes=[kO])
                    for j in range(4):
                        p.op('pe', lambda e: e.matmul(bO[:, 0:128], lhsT=h_qj[:, h, j, :], rhs=h_Sb[:, l, h, :], start=False,
                                                      stop=(j == 3)), reads=[f'h_qj.{h}', f'h_Sb.{h}'], writes=[kO])
                        p.op('pe', lambda e: e.matmul(bX[:, 128:256], lhsT=kj[:, j, :], rhs=h_v[:, h * 128:(h + 1) * 128],
                                                      start=True, stop=True), reads=['h_kj' + n, 'h_v'], writes=[kX])
                        p.op('dve', lambda e: e.scalar_tensor_tensor(
                            out=h_S[0][:, l, h, :], in0=h_S[0][:, l, h, :], scalar=h_edec[:, h, j:j + 1], in1=bX[:, 128:256],
                            op0=ALU.mult, op1=ALU.add), reads=[f'h_S.{h}', 'h_edec', kX], writes=[f'h_S.{h}'])
                        p.op('act', lambda e: e.copy(out=h_Sb[:, l, h, :], in_=h_S[0][:, l, h, :]),
                             reads=[f'h_S.{h}'], writes=[f'h_Sb.{h}'])
                        yield
                    head_norm_gate(bO[:, 0:128], kO, [], hog[:, l, h * 128:(h + 1) * 128], h_g[:, h * 128:(h + 1) * 128],
                                   'h_g', obf[:, 0, h * 128:(h + 1) * 128], 4 + 72 * sl, sl)
                for pair in range(2):
                    interleave([hg_head(2 * pair, 0), hg_head(2 * pair + 1, 1)])
            gate_piece(0)
            gate_piece(1)
            if enable[1]:
                wb, wk = wload(W_in[:, C_MQ:C_MQ + 512], 512, l, ti)
                for h in range(4):
                    proj_fm(wb, wk, h * 64, 64, PS[0][0:64, h * 128:(h + 1) * 128], 'PS0')
                    proj_fm(wb, wk, 256 + h * 64, 64, PS[1][0:64, h * 128:(h + 1) * 128], 'PS1')
                p.op('act', lambda e: e.copy(out=m_q[:].rearrange("p a b -> p (a b)"), in_=PS[0][0:64, :]), reads=['PS0'],
                     writes=['m_q'])
                p.op('act', lambda e: e.mul(out=m_kT[:].rearrange("p a b -> p (a b)"), in_=PS[1][0:64, :], mul=0.125),
                     reads=['PS1'], writes=['m_kT'])
                proj_tm(wb, wk, 256, 256, PS[2][:, 0:256], 'PS2')
                p.op('act', lambda e: e.mul(out=m_k[:], in_=PS[2][:, 0:256], mul=0.125), reads=['PS2'], writes=['m_k'])
                wb, wk = wload(W_in[:, C_MV:C_MV + 520], 520, l, ti)
                proj_tm(wb, wk, 0, 512, PS[0][:], 'PS0')
                p.op('act', lambda e: e.copy(out=m_v[:], in_=PS[0][:]), reads=['PS0'], writes=['m_v'])
                proj_tm(wb, wk, 512, 8, PS[1][:, 0:8], 'PS1')
                p.op('dve', lambda e: e.tensor_tensor(out=m_if[:], in0=PS[1][:, 0:8], in1=mifb[:, l, :], op=ALU.add),
                     reads=['PS1', 'prm'], writes=['m_if'])
                wb, wk = wload(W_in[:, C_MO:C_MO + 512], 512, l, ti)
                proj_tm(wb, wk, 0, 512, PS[2][:], 'PS2')
                p.op('act', lambda e: e.activation(out=m_o[:], in_=PS[2][:], func=AF.Sigmoid), reads=['PS2'], writes=['m_o'])
                p.op('act', lambda e: e.activation(out=sm[:, 8:12], in_=m_if[:, 4:8], func=AF.Exp, scale=-1.0),
                     reads=['m_if'], writes=['sm8'])
                p.op('act', lambda e: e.activation(out=sm[:, 8:12], in_=sm[:, 8:12], func=AF.Ln, bias=epsb[:, 1:2]),
                     reads=['sm8', 'epsb'], writes=['sm8'])
                p.op('pe', lambda e: e.matmul(PS[3][:, 0:4], lhsT=tri_incl[:], rhs=sm[:, 8:12], start=True, stop=True),
                     reads=['sm8', 'k_tri_incl'], writes=['PS3'])
                p.op('act', lambda e: e.copy(out=sm[:, 12:16], in_=PS[3][:, 0:4]), reads=['PS3'], writes=['sm12'])
                p.op('dve', lambda e: e.tensor_tensor(out=sm[:, 16:20], in0=PS[3][:, 0:4], in1=m_if[:, 0:4], op=ALU.add),
                     reads=['PS3', 'm_if'], writes=['sm16'])
                p.op('act', lambda e: e.activation(out=sm[:, 16:20], in_=sm[:, 16:20], func=AF.Exp), reads=['sm16'],
                     writes=['sm16'])
                p.op('act', lambda e: e.activation(out=sm[:, 20:24], in_=sm[:, 12:16], func=AF.Exp, scale=-1.0),
                     reads=['sm12'], writes=['sm20'])
                p.op('pe', lambda e: e.matmul(PS[3][0:64, 8:12], lhsT=sel_last[:, 0:64], rhs=sm[:, 12:16], start=True, stop=True),
                     reads=['sm12', 'k_sel_last'], writes=['PS3'])
                p.op('act', lambda e: e.activation(out=sm[0:64, 24:28], in_=PS[3][0:64, 8:12], func=AF.Exp, scale=-1.0),
                     reads=['PS3'], writes=['sm24'])
                for h in range(4):
                    p.op('dve', (lambda e, h=h: e.tensor_scalar(out=m_va[:, h, 0:128], in0=m_v[:, h * 128:(h + 1) * 128],
                                                                scalar1=sm[:, 16 + h:17 + h], scalar2=None, op0=ALU.mult)),
                         reads=['m_v', 'sm16'], writes=['m_va'])
                    p.op('dve', (lambda e, h=h: e.tensor_copy(out=m_va[:, h, 128:129], in_=sm[:, 16 + h:17 + h])),
                         reads=['sm16'], writes=['m_va'])
                def ml_head(h, sl):
                    bX, bO = PS[3 + 2 * sl], PS[4 + 2 * sl]
                    kX, kO = f'PS{3 + 2 * sl}', f'PS{4 + 2 * sl}'
                    AT = mS_AT[sl]
                    n = f'.{sl}'
                    c0 = 28 if sl == 0 else 96
                    tA = tmpA[:, sl * 128:(sl + 1) * 128]
                    p.op('pe', lambda e: e.matmul(bX[:, 0:128], lhsT=m_kT[:, h, :], rhs=m_q[:, h, :], start=True, stop=True),
                         reads=['m_kT', 'm_q'], writes=[kX])
                    p.op('dve', lambda e: e.tensor_tensor(out=AT[:], in0=bX[:, 0:128], in1=tri_incl[:], op=ALU.mult),
                         reads=[kX, 'k_tri_incl'], writes=['m_AT' + n])
                    yield
                    p.op('pe', lambda e: e.matmul(bO[:, 0:129], lhsT=AT[:], rhs=m_va[:, h, 0:129], start=True, stop=False),
                         reads=['m_AT' + n, 'm_va'], writes=[kO], inc=False)
                    p.op('pe', lambda e: e.matmul(bO[:, 0:129], lhsT=m_q[:, h, :], rhs=m_Cb[:, l, h, 0:129], start=False,
                                                  stop=True), reads=['m_q', f'm_Cb.{h}'], writes=[kO])
                    p.op('pe', lambda e: e.matmul(bX[0:64, 128:257], lhsT=m_k[:, h * 64:(h + 1) * 64], rhs=m_va[:, h, 0:129],
                                                  start=True, stop=True), reads=['m_k', 'm_va'], writes=[kX])
                    p.op('dve', lambda e: e.tensor_tensor(out=sm[:, c0:c0 + 1], in0=bO[:, 128:129], in1=sm[:, 20 + h:21 + h],
                                                          op=ALU.mult), reads=[kO, 'sm20'], writes=[f'sm{c0}'])
                    p.op('dve', lambda e: e.scalar_tensor_tensor(out=sm[:, c0 + 3:c0 + 4], in0=sm[:, c0:c0 + 1], scalar=-1.0,
                                                                 in1=sm[:, c0:c0 + 1], op0=ALU.mult, op1=ALU.max),
                         reads=[f'sm{c0}'], writes=[f'sm{c0 + 3}'])
                    p.op('dve', lambda e: e.tensor_scalar(out=sm[:, c0:c0 + 1], in0=sm[:, c0 + 3:c0 + 4], scalar1=1.0,
                                                          scalar2=None, op0=ALU.max), reads=[f'sm{c0 + 3}'], writes=[f'sm{c0}'])
                    p.op('dve', lambda e: e.reciprocal(out=sm[:, c0 + 1:c0 + 2], in_=sm[:, c0:c0 + 1]), reads=[f'sm{c0}'],
                         writes=[f'sm{c0 + 1}'])
                    p.op('dve', lambda e: e.tensor_tensor(out=sm[:, c0 + 2:c0 + 3], in0=sm[:, c0 + 1:c0 + 2],
                                                          in1=sm[:, 20 + h:21 + h], op=ALU.mult),
                         reads=[f'sm{c0 + 1}', 'sm20'], writes=[f'sm{c0 + 2}'])
                    yield
                    p.op('act', lambda e: e.activation(out=tA, in_=bO[:, 0:128], func=AF.Copy, scale=sm[:, c0 + 2:c0 + 3]),
                         reads=[kO, f'sm{c0 + 2}'], writes=['tmpA' + n])
                    p.op('dve', lambda e: e.tensor_tensor(out=m_C[:, l, h, 0:129], in0=m_C[:, l, h, 0:129],
                                                          in1=bX[0:64, 128:257], op=ALU.add),
                         reads=[f'm_C.{h}', kX], writes=[f'm_C.{h}'])
                    p.op('dve', lambda e: e.tensor_scalar(out=m_C[:, l, h, 0:129], in0=m_C[:, l, h, 0:129],
                                                          scalar1=sm[0:64, 24 + h:25 + h], scalar2=None, op0=ALU.mult),
                         reads=[f'm_C.{h}', 'sm24'], writes=[f'm_C.{h}'])
                    p.op('act', lambda e: e.copy(out=m_Cb[:, l, h, 0:129], in_=m_C[:, l, h, 0:129]), reads=[f'm_C.{h}'],
                         writes=[f'm_Cb.{h}'])
                    yield
                    head_norm_gate(tA, 'tmpA' + n, [], mog[:, l, h * 128:(h + 1) * 128], m_o[:, h * 128:(h + 1) * 128],
                                   'm_o', obf[:, 1, h * 128:(h + 1) * 128], 32 if sl == 0 else 100, sl)
                for pair in range(2):
                    interleave([ml_head(2 * pair, 0), ml_head(2 * pair + 1, 1)])

            gate_piece(2)
            gate_piece(3)
            if enable[2]:
                wb, wk = wload(W_in[:, C_AQ:C_AQ + 512], 512, l, ti)
                wb2, wk2 = wload(W_in[:, C_AK:C_AK + 256], 256, l, ti)
                for g in range(8):
                    proj_fm(wb, wk, g * 64, 64, PS[g // 4][0:64, (g % 4) * 128:(g % 4 + 1) * 128], f'PS{g // 4}')
                for kv in range(2):
                    proj_fm(wb2, wk2, kv * 64, 64, PS[2][0:64, kv * 128:(kv + 1) * 128], 'PS2')
                az2 = a_z[:].rearrange("p a b -> p (a b)")
                asq2 = a_sq[:].rearrange("p a b -> p (a b)")
                p.op('act', lambda e: e.copy(out=az2[:, 0:512], in_=PS[0][0:64, :]), reads=['PS0'], writes=['a_z'])
                p.op('act', lambda e: e.copy(out=az2[:, 512:1024], in_=PS[1][0:64, :]), reads=['PS1'], writes=['a_z'])
                p.op('act', lambda e: e.copy(out=az2[:, 1024:1280], in_=PS[2][0:64, 0:256]), reads=['PS2'], writes=['a_z'])
                p.op('dve', lambda e: e.tensor_tensor(out=asq2, in0=az2, in1=az2, op=ALU.mult), reads=['a_z'], writes=['a_sq'])
                for i3 in range(3):
                    w3 = 512 if i3 < 2 else 256
                    p.op('pe', (lambda e, i3=i3, w3=w3: e.matmul(PS[3][0:64, 0:w3], lhsT=ones_f[0:64, 0:64],
                                                                 rhs=asq2[:, i3 * 512:i3 * 512 + w3], start=True, stop=True)),
                         reads=['a_sq', 'k_ones'], writes=['PS3'])
                    p.op('act', (lambda e, i3=i3, w3=w3: e.activation(out=asq2[:, i3 * 512:i3 * 512 + w3], in_=PS[3][0:64, 0:w3],
                                                                      func=AF.Ln, scale=1.0 / 64, bias=epsb[0:64, 0:1])),
                         reads=['PS3', 'epsb'], writes=['a_sq'])
                p.op('act', lambda e: e.activation(out=asq2, in_=asq2, func=AF.Exp, scale=-0.5), reads=['a_sq'], writes=['a_sq'])
                p.op('dve', lambda e: e.tensor_tensor(out=a_q[:].rearrange("p a b -> p (a b)"), in0=az2[:, 0:1024],
                                                      in1=asq2[:, 0:1024], op=ALU.mult), reads=['a_z', 'a_sq'], writes=['a_q'])
                for kv in range(2):
                    p.op('dve', (lambda e, kv=kv: e.scalar_tensor_tensor(
                        out=a_kT[:, l, kv, par, :], in0=a_z[:, 8 + kv, :], scalar=gk8[:, l:l + 1], in1=a_sq[:, 8 + kv, :],
                        op0=ALU.mult, op1=ALU.mult)), reads=['a_z', 'a_sq', 'gk8'], writes=['a_kT'])
                proj_tm(wb2, wk2, 128, 128, PS[4][:, 0:128], 'PS4')
                for kv in range(2):
                    p.op('act', (lambda e, kv=kv: e.copy(out=a_v[:, l, par, kv, 0:64], in_=PS[4][:, kv * 64:(kv + 1) * 64])),
                         reads=['PS4'], writes=['a_v'])
                    p.op('dve', (lambda e, kv=kv: e.memset(a_v[:, l, par, kv, 64:65], 1.0)), writes=['a_v'])
                def swa_head(g, sl):
                    kv = g // 4
                    bL, bO = PS[3 + 2 * sl], PS[4 + 2 * sl]
                    kL, kO = f'PS{3 + 2 * sl}', f'PS{4 + 2 * sl}'
                    aP, aPT = aS_P[sl], aS_PT[sl]
                    n = f'.{sl}'
                    c0 = 40 + 2 * sl
                    blks = [0, 1] if (pipe or ti > 0) else [1]
                    nb = len(blks)
                    for bi, blk in enumerate(blks):
                        slot = par if blk == 1 else 1 - par
                        p.op('pe', lambda e: e.matmul(bL[:, blk * 128:(blk + 1) * 128], lhsT=a_kT[:, l, kv, slot, :],
                                                      rhs=a_q[:, g, :], start=True, stop=True),
                             reads=['a_kT', 'a_q'], writes=[kL])
                    yield
                    p.op('act', lambda e: e.activation(out=aP[:, 0:nb * 128], in_=bL[:, blks[0] * 128:(blks[0] + nb) * 128],
                                                       func=AF.Exp), reads=[kL], writes=['a_P' + n])
                    for bi, blk in enumerate(blks):
                        if pipe and blk == 0:
                            p.op('dve', lambda e: e.scalar_tensor_tensor(
                                out=aPT[:, blk, :], in0=aP[:, bi * 128:(bi + 1) * 128], scalar=pflag[:, ti:ti + 1],
                                in1=EB[:, blk, g, :], op0=ALU.mult, op1=ALU.mult),
                                reads=['a_P' + n, 'EB', 'pflag'], writes=['a_PT' + n])
                        else:
                            p.op('dve', lambda e: e.tensor_tensor(out=aPT[:, blk, :], in0=aP[:, bi * 128:(bi + 1) * 128],
                                                                  in1=EB[:, blk, g, :], op=ALU.mult),
                                 reads=['a_P' + n, 'EB'], writes=['a_PT' + n])
                    yield
                    for bi, blk in enumerate(blks):
                        slot = par if blk == 1 else 1 - par
                        p.op('pe', lambda e: e.matmul(bO[:, 0:65], lhsT=aPT[:, blk, :], rhs=a_v[:, l, slot, kv, 0:65],
                                                      start=(bi == 0), stop=(bi == nb - 1)),
                             reads=['a_PT' + n, 'a_v'], writes=[kO], inc=(bi == nb - 1))
                    p.op('dve', lambda e: e.tensor_tensor(out=sm[:, c0:c0 + 1], in0=bO[:, 64:65], in1=esink[:, l, g:g + 1],
                                                          op=ALU.add), reads=[kO, 'esink'], writes=[f'sm{c0}'])
                    p.op('dve', lambda e: e.reciprocal(out=sm[:, c0 + 1:c0 + 2], in_=sm[:, c0:c0 + 1]), reads=[f'sm{c0}'],
                         writes=[f'sm{c0 + 1}'])
                    yield
                    p.op('dve', lambda e: e.tensor_scalar(out=obf[:, 2, g * 64:(g + 1) * 64], in0=bO[:, 0:64],
                                                          scalar1=sm[:, c0 + 1:c0 + 2], scalar2=None, op0=ALU.mult),
                         reads=[kO, f'sm{c0 + 1}'], writes=['obf'])
                for pair in range(4):
                    interleave([swa_head(2 * pair, 0), swa_head(2 * pair + 1, 1)])

            gate_piece(4)
            gate_piece(5)
            if enable[3]:
                for c3 in range(3):
                    wb, wk = wload(W_in[:, C_DQKV + c3 * 512:C_DQKV + (c3 + 1) * 512], 512, l, ti)
                    for c4 in range(4):
                        proj_fm(wb, wk, c4 * 128, 128, PS[c3][:, c4 * 128:(c4 + 1) * 128], f'PS{c3}')
                    p.op('act', (lambda e, c3=c3: e.copy(out=d_x[:, l, c3 * 4:(c3 + 1) * 4, 3:131],
                                                         in_=PS[c3][:].rearrange("p (a b) -> p a b", a=4))),
                         reads=[f'PS{c3}'], writes=['d_x'])
                for hf in range(2):
                    cs = slice(hf * 6, hf * 6 + 6)
                    yv = d_y[:, cs, :]
                    tAv = xr[:, 0:768].rearrange("p (a b) -> p a b", a=6)
                    tBv = xh[:, 0:768].rearrange("p (a b) -> p a b", a=6)

                    def wbc(j, cs=cs):
                        return convw[:, l, j, cs].unsqueeze(2).broadcast_to([128, 6, 128])
                    p.op('dve', lambda e: e.tensor_tensor(out=yv, in0=d_x[:, l, cs, 0:128], in1=wbc(0), op=ALU.mult),
                         reads=['d_x', 'prm'], writes=['d_y'])
                    p.op('pool', lambda e: e.tensor_tensor(out=tAv, in0=d_x[:, l, cs, 1:129], in1=wbc(1), op=ALU.mult),
                         reads=['d_x', 'prm'], writes=['xr'])
                    p.op('pool', lambda e: e.tensor_tensor(out=tBv, in0=d_x[:, l, cs, 2:130], in1=wbc(2), op=ALU.mult),
                         reads=['d_x', 'prm'], writes=['xh'])
                    p.op('dve', lambda e: e.tensor_tensor(out=yv, in0=yv, in1=tAv, op=ALU.add), reads=['d_y', 'xr'],
                         writes=['d_y'])
                    p.op('pool', lambda e: e.tensor_tensor(out=tAv, in0=d_x[:, l, cs, 3:131], in1=wbc(3), op=ALU.mult),
                         reads=['d_x', 'prm'], writes=['xr'])
                    p.op('dve', lambda e: e.tensor_tensor(out=yv, in0=yv, in1=tBv, op=ALU.add), reads=['d_y', 'xh'],
                         writes=['d_y'])
                    p.op('dve', lambda e: e.tensor_tensor(out=yv, in0=yv, in1=tAv, op=ALU.add), reads=['d_y', 'xr'],
                         writes=['d_y'])
                p.op('pool', lambda e: e.tensor_copy(out=d_x[:, l, :, 0:3], in_=d_x[:, l, :, 128:131]), reads=['d_x', 'd_y'],
                     writes=['d_x'])
                dy2 = d_y[:].rearrange("p a b -> p (a b)")
                p.op('act', lambda e: e.activation(out=dy2, in_=dy2, func=AF.Silu), reads=['d_y'], writes=['d_y'])
                p.op('dve', lambda e: e.tensor_tensor(out=merged[:], in0=dy2[:, 0:1024], in1=dy2[:, 0:1024], op=ALU.mult),
                     reads=['d_y'], writes=['merged'])
                for i2 in range(2):
                    tn = tmpA if i2 == 0 else tmpB
                    tk = 'tmpA' if i2 == 0 else 'tmpB'
                    p.op('pe', lambda e: e.matmul(PS[i2][:], lhsT=ones_f[:], rhs=merged[:, i2 * 512:(i2 + 1) * 512], start=True,
                                                  stop=True), reads=['merged', 'k_ones'], writes=[f'PS{i2}'])
                    p.op('act', lambda e: e.activation(out=tn[:], in_=PS[i2][:], func=AF.Ln, bias=epsb[:, 0:1]),
                         reads=[f'PS{i2}', 'epsb'], writes=[tk])
                    p.op('act', lambda e: e.activation(out=tn[:], in_=tn[:], func=AF.Exp, scale=-0.5), reads=[tk], writes=[tk])
                p.op('dve', lambda e: e.scalar_tensor_tensor(out=d_qT[:].rearrange("p a b -> p (a b)"), in0=dy2[:, 0:512],
                                                             scalar=float(128 ** -0.5), in1=tmpA[:], op0=ALU.mult, op1=ALU.mult),
                     reads=['d_y', 'tmpA'], writes=['d_qT'])
                p.op('dve', lambda e: e.tensor_tensor(out=d_kT[:].rearrange("p a b -> p (a b)"), in0=dy2[:, 512:1024],
                                                      in1=tmpB[:], op=ALU.mult), reads=['d_y', 'tmpB'], writes=['d_kT'])
                p.op('dve', lambda e: e.tensor_copy(out=d_vT[:], in_=d_y[:, 8:12, :]), reads=['d_y'], writes=['d_vT'])
                for h in range(4):
                    p.op('pe', (lambda e, h=h: e.transpose(out=PT[:, h * 128:(h + 1) * 128], in_=d_vT[:, h, :],
                                                           identity=ident_b[:])), reads=['d_vT', 'k_ident'], writes=['PT'],
                         inc=False)
                    p.op('pe', (lambda e, h=h: e.transpose(out=PT[:, 512 + h * 128:512 + (h + 1) * 128], in_=d_kT[:, h, :],
                                                           identity=ident_b[:])), reads=['d_kT', 'k_ident'], writes=['PT'],
                         inc=(h == 3))
                p.op('act', lambda e: e.copy(out=d_v[:].rearrange("p a b -> p (a b)"), in_=PT[:, 0:512]), reads=['PT'],
                     writes=['d_v'])
                p.op('act', lambda e: e.copy(out=d_k[:].rearrange("p a b -> p (a b)"), in_=PT[:, 512:1024]), reads=['PT'],
                     writes=['d_k'])
                wb, wk = wload(W_in[:, C_DB:C_DB + 520], 520, l, ti)
                proj_tm(wb, wk, 0, 8, PS[0][:, 0:8], 'PS0')
                p.op('act', lambda e: e.copy(out=d_ba[:], in_=PS[0][:, 0:8]), reads=['PS0'], writes=['d_ba'])
                proj_fm(wb, wk, 4, 4, PS[1][0:4, 0:128], 'PS1')
                p.op('act', lambda e: e.copy(out=d_arow[:], in_=PS[1][0:4, 0:128]), reads=['PS1'], writes=['d_arow'])
                proj_tm(wb, wk, 8, 512, PS[2][:], 'PS2')
                p.op('act', lambda e: e.activation(out=d_z[:], in_=PS[2][:], func=AF.Silu), reads=['PS2'], writes=['d_z'])
                p.op('act', lambda e: e.activation(out=sm[:, 44:48], in_=d_ba[:, 0:4], func=AF.Sigmoid), reads=['d_ba'],
                     writes=['sm44'])
                p.op('dve', lambda e: e.tensor_tensor(out=sm[:, 48:52], in0=d_ba[:, 4:8], in1=dtb[:, l, :], op=ALU.add),
                     reads=['d_ba', 'prm'], writes=['sm48'])
                p.op('act', lambda e: e.activation(out=sm[:, 48:52], in_=sm[:, 48:52], func=AF.Exp), reads=['sm48'],
                     writes=['sm48'])
                p.op('act', lambda e: e.activation(out=sm[:, 48:52], in_=sm[:, 48:52], func=AF.Ln, bias=epsb[:, 1:2]),
                     reads=['sm48', 'epsb'], writes=['sm48'])
                p.op('dve', lambda e: e.tensor_tensor(out=sm[:, 48:52], in0=sm[:, 48:52], in1=nexpa[:, l, :], op=ALU.mult),
                     reads=['sm48', 'nexpa'], writes=['sm48'])
                p.op('pe', lambda e: e.matmul(PS[3][:, 0:4], lhsT=tri_incl[:], rhs=sm[:, 48:52], start=True, stop=True),
                     reads=['sm48', 'k_tri_incl'], writes=['PS3'])
                p.op('act', lambda e: e.copy(out=sm[:, 52:56], in_=PS[3][:, 0:4]), reads=['PS3'], writes=['sm52'])
                p.op('dve', lambda e: e.tensor_scalar(out=sm[:, 56:60], in0=sm[:, 52:56], scalar1=-1.0, scalar2=None,
                                                      op0=ALU.mult), reads=['sm52'], writes=['sm56'])
                p.op('pe', lambda e: e.matmul(PS[3][:, 8:12], lhsT=sel_last[:], rhs=sm[:, 52:56], start=True, stop=True),
                     reads=['sm52', 'k_sel_last'], writes=['PS3'])
                p.op('act', lambda e: e.activation(out=sm[:, 60:64], in_=PS[3][:, 8:12], func=AF.Exp), reads=['PS3'],
                     writes=['sm60'])
                p.op('dve', lambda e: e.tensor_tensor(out=tmpC[:, 500:504], in0=PS[3][:, 8:12], in1=sm[:, 52:56],
                                                      op=ALU.subtract), reads=['PS3', 'sm52'], writes=['tmpC5'])
                p.op('act', lambda e: e.activation(out=tmpC[:, 500:504], in_=tmpC[:, 500:504], func=AF.Exp), reads=['tmpC5'],
                     writes=['tmpC5'])
                p.op('dve', lambda e: e.tensor_tensor(out=tmpC[:, 504:508], in0=tmpC[:, 500:504], in1=sm[:, 44:48], op=ALU.mult),
                     reads=['tmpC5', 'sm44'], writes=['tmpC6'])
                p.op('act', lambda e: e.activation(out=tmpC[:, 508:512], in_=sm[:, 52:56], func=AF.Exp), reads=['sm52'],
                     writes=['tmpC7'])
                p.op('dve', lambda e: e.tensor_scalar(out=tmpC[:, 496:500], in0=tmpC[:, 508:512], scalar1=-1.0, scalar2=None,
                                                      op0=ALU.mult), reads=['tmpC7'], writes=['tmpC4'])
                p.op('dve', lambda e: e.tensor_scalar(out=tmpC[:, 492:496], in0=sm[:, 44:48], scalar1=-1.0, scalar2=None,
                                                      op0=ALU.mult), reads=['sm44'], writes=['tmpC3'])
                p.op('act', lambda e: e.activation(out=d_grow[:], in_=d_arow[:], func=AF.Exp, bias=dtbrow[:, l:l + 1]),
                     reads=['d_arow', 'dtbrow'], writes=['d_grow'])
                p.op('act', lambda e: e.activation(out=d_grow[:], in_=d_grow[:], func=AF.Ln, bias=epsb[0:4, 1:2]),
                     reads=['d_grow', 'epsb'], writes=['d_grow'])
                p.op('dve', lambda e: e.tensor_scalar(out=d_grow[:], in0=d_grow[:], scalar1=nexparow[:, l:l + 1], scalar2=None,
                                                      op0=ALU.mult), reads=['d_grow', 'nexparow'], writes=['d_grow'])
                p.op('dve', lambda e: e.tensor_tensor_scan(out=d_grow[:], data0=ones_f[0:4, :], data1=d_grow[:], initial=0.0,
                                                           op0=ALU.mult, op1=ALU.add), reads=['d_grow', 'k_ones'],
                     writes=['d_grow'])
                def dn_head(h, sl):
                    bA, bB, bC = PS[1 + 3 * sl], PS[2 + 3 * sl], PS[3 + 3 * sl]
                    kA, kB, kC = f'PS{1 + 3 * sl}', f'PS{2 + 3 * sl}', f'PS{3 + 3 * sl}'
                    E1, E1s, TT, rr = dS_E1[sl], dS_E1s[sl], dS_TT[sl], dS_r[sl]
                    Pb, PTb = dS_P[sl], dS_PT[sl]
                    vn, vc, aT, o1 = dS_vn[sl], dS_vc[sl], dS_aT[sl], dS_o1[sl]
                    tA = tmpA[:, sl * 128:(sl + 1) * 128]
                    tB = tmpB[:, sl * 128:(sl + 1) * 128]
                    n = f'.{sl}'
                    p.op('pe', lambda e: e.matmul(bA[:, 0:128], lhsT=selh[:, h * 128:(h + 1) * 128], rhs=d_grow[:],
                                                  start=True, stop=True), reads=['k_selh', 'd_grow'], writes=[kA])
                    p.op('dve', lambda e: e.tensor_tensor(out=tA, in0=bA[:, 0:128], in1=negmask[:], op=ALU.add),
                         reads=[kA, 'k_negmask'], writes=['tmpA' + n])
                    p.op('act', lambda e: e.activation(out=E1[:], in_=tA, func=AF.Exp, bias=sm[:, 56 + h:57 + h]),
                         reads=['tmpA' + n, 'sm56'], writes=['d_E1' + n])
                    yield
                    p.op('pe', lambda e: e.matmul(bA[:, 128:256], lhsT=d_kT[:, h, :], rhs=d_kT[:, h, :], start=True, stop=True),
                         reads=['d_kT'], writes=[kA])
                    p.op('dve', lambda e: e.tensor_tensor(out=E1s[:], in0=E1[:], in1=tri_strict[:], op=ALU.mult),
                         reads=['d_E1' + n, 'k_tri_strict'], writes=['d_E1s' + n])
                    p.op('dve', lambda e: e.scalar_tensor_tensor(out=PTb[0][:], in0=bA[:, 128:256],
                                                                 scalar=tmpC[:, 492 + h:493 + h], in1=E1s[:],
                                                                 op0=ALU.mult, op1=ALU.mult),
                         reads=[kA, 'tmpC3', 'd_E1s' + n], writes=['d_PT0' + n])
                    yield
                    p.op('pe', lambda e: e.transpose(out=bB[:, 0:128], in_=PTb[0][:], identity=ident_f[:]),
                         reads=['d_PT0' + n, 'k_ident'], writes=[kB])
                    p.op('act', lambda e: e.copy(out=Pb[0][:], in_=bB[:, 0:128]), reads=[kB], writes=['d_P0' + n])
                    p.op('dve', lambda e: e.tensor_tensor(out=TT[:], in0=PTb[0][:], in1=ident_f[:], op=ALU.add),
                         reads=['d_PT0' + n, 'k_ident'], writes=['d_TT' + n])
                    yield
                    cur = 0
                    for lvl in range(6):
                        nxt = 1 - cur
                        p.op('pe', lambda e: e.matmul(bB[:, 0:128], lhsT=PTb[cur][:], rhs=Pb[cur][:], start=True, stop=True),
                             reads=[f'd_PT{cur}' + n, f'd_P{cur}' + n], writes=[kB])
                        if lvl < 5:
                            p.op('pe', lambda e: e.matmul(bB[:, 128:256], lhsT=Pb[cur][:], rhs=PTb[cur][:], start=True, stop=True),
                                 reads=[f'd_PT{cur}' + n, f'd_P{cur}' + n], writes=[kB])
                        p.op('act', lambda e: e.copy(out=Pb[nxt][:], in_=bB[:, 0:128]), reads=[kB], writes=[f'd_P{nxt}' + n])
                        if lvl < 5:
                            p.op('act', lambda e: e.copy(out=PTb[nxt][:], in_=bB[:, 128:256]), reads=[kB],
                                 writes=[f'd_PT{nxt}' + n])
                        yield
                        p.op('pe', lambda e: e.matmul(bC[:, 0:128], lhsT=Pb[nxt][:], rhs=TT[:], start=True, stop=True),
                             reads=[f'd_P{nxt}' + n, 'd_TT' + n], writes=[kC])
                        p.op('dve', lambda e: e.tensor_tensor(out=TT[:], in0=TT[:], in1=bC[:, 0:128], op=ALU.add),
                             reads=['d_TT' + n, kC], writes=['d_TT' + n])
                        yield
                        cur = nxt
                    p.op('pe', lambda e: e.matmul(bA[:, 256:384], lhsT=d_kT[:, h, :], rhs=d_Sb[:, l, h, :], start=True, stop=True),
                         reads=['d_kT', 'd_Sb'], writes=[kA])
                    p.op('dve', lambda e: e.scalar_tensor_tensor(out=rr[:], in0=bA[:, 256:384], scalar=tmpC[:, 496 + h:497 + h],
                                                                 in1=d_v[:, h, :], op0=ALU.mult, op1=ALU.add),
                         reads=[kA, 'tmpC4', 'd_v'], writes=['d_r' + n])
                    yield
                    p.op('pe', lambda e: e.matmul(bB[:, 256:384], lhsT=TT[:], rhs=rr[:], start=True, stop=True),
                         reads=['d_TT' + n, 'd_r' + n], writes=[kB])
                    p.op('dve', lambda e: e.tensor_scalar(out=vn[:], in0=bB[:, 256:384], scalar1=sm[:, 44 + h:45 + h],
                                                          scalar2=None, op0=ALU.mult), reads=[kB, 'sm44'], writes=['d_vn' + n])
                    p.op('dve', lambda e: e.tensor_scalar(out=vc[:], in0=bB[:, 256:384], scalar1=tmpC[:, 504 + h:505 + h],
                                                          scalar2=None, op0=ALU.mult), reads=[kB, 'tmpC6'], writes=['d_vc' + n])
                    p.op('pe', lambda e: e.matmul(bA[:, 384:512], lhsT=d_kT[:, h, :], rhs=d_qT[:, h, :], start=True, stop=True),
                         reads=['d_kT', 'd_qT'], writes=[kA])
                    p.op('dve', lambda e: e.tensor_tensor(out=aT[:], in0=bA[:, 384:512], in1=E1[:], op=ALU.mult),
                         reads=[kA, 'd_E1' + n], writes=['d_aT' + n])
                    yield
                    p.op('pe', lambda e: e.matmul(bC[:, 128:256], lhsT=d_qT[:, h, :], rhs=d_Sb[:, l, h, :], start=True, stop=True),
                         reads=['d_qT', 'd_Sb'], writes=[kC])
                    p.op('act', lambda e: e.activation(out=o1[:], in_=bC[:, 128:256], func=AF.Copy,
                                                       scale=tmpC[:, 508 + h:509 + h]), reads=[kC, 'tmpC7'], writes=['d_o1' + n])
                    yield
                    p.op('pe', lambda e: e.matmul(bC[:, 256:384], lhsT=aT[:], rhs=vn[:], start=True, stop=True),
                         reads=['d_aT' + n, 'd_vn' + n], writes=[kC])
                    p.op('dve', lambda e: e.tensor_tensor(out=tB, in0=bC[:, 256:384], in1=o1[:], op=ALU.add),
                         reads=[kC, 'd_o1' + n], writes=['tmpB' + n])
                    yield
                    p.op('pe', lambda e: e.matmul(bC[:, 384:512], lhsT=d_k[:, h, :], rhs=vc[:], start=True, stop=True),
                         reads=['d_k', 'd_vc' + n], writes=[kC])
                    p.op('dve', lambda e: e.scalar_tensor_tensor(out=d_S[:, l, h, :], in0=d_S[:, l, h, :],
                                                                 scalar=sm[:, 60 + h:61 + h], in1=bC[:, 384:512],
                                                                 op0=ALU.mult, op1=ALU.add),
                         reads=['d_S', 'sm60', kC], writes=['d_S'])
                    p.op('act', lambda e: e.copy(out=d_Sb[:, l, h, :], in_=d_S[:, l, h, :]), reads=['d_S'], writes=['d_Sb'])
                    yield
                    head_norm_gate(tB, 'tmpB' + n, [], dog[:, l, :], d_z[:, h * 128:(h + 1) * 128], 'd_z',
                                   obf[:, 3, h * 128:(h + 1) * 128], 64 + 4 * sl, sl)
                for pair in range(2):
                    interleave([dn_head(2 * pair, 0), dn_head(2 * pair + 1, 1)])

            for gi in range(gates_done[0], 8):
                gate_piece(gi)
            for half in range(2):
                for c8 in range(8):
                    cc = half * 8 + c8
                    p.op('pe', (lambda e, cc=cc, c8=c8: e.transpose(
                        out=PT[:, c8 * 128:(c8 + 1) * 128],
                        in_=obf[:].rearrange("p a b -> p (a b)")[:, cc * 128:(cc + 1) * 128], identity=ident_b[:])),
                        reads=['obf', 'k_ident'], writes=['PT'], inc=(c8 == 7))
                p.op('act', (lambda e, half=half: e.copy(out=oT[:, half * 8:(half + 1) * 8, :].rearrange("p a b -> p (a b)"),
                                                         in_=PT[:])), reads=['PT'], writes=['oT'])
            for n in range(4):
                wv, wk = wload_rows(prm['w_branch'][l, n], l, ti)
                for half in range(2):
                    pst, pk = (PS[2], 'PS2') if half == 0 else (PS[3], 'PS3')
                    for wc in range(4):
                        p.op('pe', (lambda e, n=n, wc=wc, half=half, pst=pst: e.matmul(
                            pst[:], lhsT=oT[:, n * 4 + wc, :], rhs=wv[:, wc, half * 512:(half + 1) * 512], start=(wc == 0),
                            stop=(wc == 3))), reads=['oT', wk], writes=[pk], inc=(wc == 3))
                    if n == 0:
                        p.op('dve', (lambda e, n=n, half=half, pst=pst: e.tensor_tensor(
                            out=merged[:, half * 512:(half + 1) * 512], in0=pst[:],
                            in1=gates[:, n * 1024 + half * 512:n * 1024 + (half + 1) * 512], op=ALU.mult)),
                            reads=[pk, 'gates'], writes=['merged'])
                    else:
                        p.op('dve', (lambda e, n=n, half=half, pst=pst: e.tensor_tensor(
                            out=tmpA[:], in0=pst[:], in1=gates[:, n * 1024 + half * 512:n * 1024 + (half + 1) * 512],
                            op=ALU.mult)), reads=[pk, 'gates'], writes=['tmpA'])
                        p.op('dve', (lambda e, half=half: e.tensor_tensor(
                            out=merged[:, half * 512:(half + 1) * 512], in0=merged[:, half * 512:(half + 1) * 512], in1=tmpA[:],
                            op=ALU.add)), reads=['tmpA', 'merged'], writes=['merged'])
            p.op('act', lambda e: e.copy(out=mbf[:], in_=merged[:]), reads=['merged'], writes=['mbf'])
            for kc in range(8):
                p.op('pe', (lambda e, kc=kc: e.transpose(out=PT[:, kc * 128:(kc + 1) * 128], in_=mbf[:, kc * 128:(kc + 1) * 128],
                                                         identity=ident_b[:])), reads=['mbf', 'k_ident'], writes=['PT'],
                     inc=(kc == 7))
            p.op('act', lambda e: e.copy(out=mT[:].rearrange("p a b -> p (a b)"), in_=PT[:]), reads=['PT'], writes=['mT'])
            for half in range(2):
                wb, wk = wload(prm['w_out'][l][:, half * 512:(half + 1) * 512], 512, l, ti)
                pst, pk = (PS[0], 'PS0') if half == 0 else (PS[1], 'PS1')
                for kc in range(8):
                    p.op('pe', (lambda e, kc=kc, pst=pst, wb=wb: e.matmul(pst[:], lhsT=mT[:, kc, :], rhs=wb[:, kc, 0:512],
                                                                          start=(kc == 0), stop=(kc == 7))),
                         reads=['mT', wk], writes=[pk], inc=(kc == 7))
                p.op('dve', (lambda e, half=half, pst=pst: e.tensor_tensor(out=xt[:, half * 512:(half + 1) * 512],
                                                                           in0=xt[:, half * 512:(half + 1) * 512], in1=pst[:],
                                                                           op=ALU.add)), reads=[pk, 'xt'], writes=['xt'])
            rmsnorm_to_T(xt, gmlp, hT, 'hT', l)
            for fi in range(8):
                wb, wk = wload(prm['w_up'][l][:, fi * 512:(fi + 1) * 512], 512, l, ti)
                pst, pk = (PS[2], 'PS2') if fi % 2 == 0 else (PS[3], 'PS3')
                for f4 in range(4):
                    proj_fm(wb, wk, f4 * 128, 128, pst[:, f4 * 128:(f4 + 1) * 128], pk)
                p.op('act', (lambda e, pst=pst: e.activation(out=tmpB[:], in_=pst[:], func=AF.Relu)), reads=[pk], writes=['tmpB'])
                p.op('dve', (lambda e, fi=fi, pst=pst: e.tensor_tensor(
                    out=uT[:, fi * 4:(fi + 1) * 4, :].rearrange("p a b -> p (a b)"), in0=tmpB[:], in1=pst[:], op=ALU.mult)),
                    reads=['tmpB', pk], writes=['uT'])
            for fi in range(8):
                wv, wk = wload_rows(prm['w_down'][l][fi * 512:(fi + 1) * 512, :], l, ti)
                for half in range(2):
                    pk = 'PS0' if half == 0 else 'PS1'
                    pst = PS[0] if half == 0 else PS[1]
                    for f4 in range(4):
                        p.op('pe', (lambda e, fi=fi, f4=f4, half=half, pst=pst, wv=wv: e.matmul(
                            pst[:], lhsT=uT[:, fi * 4 + f4, :], rhs=wv[:, f4, half * 512:(half + 1) * 512],
                            start=(fi == 0 and f4 == 0), stop=(fi == 7 and f4 == 3))),
                            reads=['uT', wk], writes=[pk], inc=(f4 == 3))
            for half in range(2):
                pk = 'PS0' if half == 0 else 'PS1'
                pst = PS[0] if half == 0 else PS[1]
                p.op('dve', (lambda e, half=half, pst=pst: e.tensor_tensor(out=xt[:, half * 512:(half + 1) * 512],
                                                                           in0=xt[:, half * 512:(half + 1) * 512], in1=pst[:],
                                                                           op=ALU.add)), reads=[pk, 'xt'], writes=['xt'])
        if pipe:
            to = max(ti - 1, 0)
            p.dma('sp', y_out[to * 128:(to + 1) * 128, :], xt[:], reads=['xt'], writes=['yout'])
            p.op('act', lambda e: e.activation(out=xh[:], in_=xt[:], func=AF.Copy, scale=role[:, 0:1]),
                 reads=['xt', 'role'], writes=['xh'])
            p.dma('sp', sendb[(ti + 1) % 2], xh[:], reads=['xh'], writes=[('sendb', (ti + 1) % 2)])
        else:
            p.dma('sp', y_out[ti * 128:(ti + 1) * 128, :], xt[:], reads=['xt'], writes=['yout'])
    p.final_wait('sp', ['yout'])
    p.emit()
    return nc


def host_params(inputs):
    f = lambda k: np.ascontiguousarray(np.asarray(inputs[k], dtype=np.float32))
    m = {}
    for k in ['norm_mix_g', 'w_in', 'hgrn_out_g', 'mlstm_out_g', 'attn_sinks', 'rel_bias_table', 'dn_a_log', 'dn_dt_bias',
              'dn_out_g', 'w_branch', 'w_out', 'norm_mlp_g', 'w_up', 'w_down']:
        m[k] = f(k)
    m['mlstm_if_bias'] = f('mlstm_if_bias').reshape(2, 8)
    m['hgrn_lb_table'] = np.ascontiguousarray(f('hgrn_lb_table').reshape(2, 4, 128).transpose(0, 2, 1))
    m['attn_q_norm_g'] = f('attn_q_norm_g').reshape(2, 64, 1)
    m['attn_k_norm_g'] = f('attn_k_norm_g').reshape(2, 64, 1)
    m['dn_conv_w'] = np.ascontiguousarray(f('dn_conv_w').reshape(2, 4, 12, 128).transpose(0, 1, 3, 2))
    m['dn_alog_row'] = np.ascontiguousarray(f('dn_a_log').T)
    m['dn_dtb_row'] = np.ascontiguousarray(f('dn_dt_bias').T)
    return m


LAYERED = ['norm_mix_g', 'w_in', 'hgrn_out_g', 'mlstm_out_g', 'attn_sinks', 'dn_a_log', 'dn_dt_bias', 'dn_out_g',
           'w_branch', 'w_out', 'norm_mlp_g', 'w_up', 'w_down', 'mlstm_if_bias', 'attn_q_norm_g', 'attn_k_norm_g',
           'dn_conv_w']


def kernel(**inputs):
    x = np.ascontiguousarray(np.asarray(inputs['x'], dtype=np.float32))
    B, T, _ = x.shape
    NT = T // 128
    nc = build(T, 1, pipe=True, npairs=B)
    consts = host_consts()
    hp = host_params(inputs)
    hp_role = []
    for r in range(2):
        m = dict(hp)
        if r == 1:
            for k in LAYERED:
                m[k] = np.ascontiguousarray(hp[k][::-1])
            m['dn_alog_row'] = np.ascontiguousarray(hp['dn_alog_row'][:, ::-1])
            m['dn_dtb_row'] = np.ascontiguousarray(hp['dn_dtb_row'][:, ::-1])
        hp_role.append(m)
    zeros_x = np.zeros((T, D), np.float32)
    in_maps = []
    for b in range(B):
        for r in range(2):
            m = {'x': x[b] if r == 0 else zeros_x}
            m.update(hp_role[r])
            for k, v in consts.items():
                m['c_' + k] = v
            role = np.zeros((128, 2), np.float32)
            role[:, r] = 1.0
            pf = np.ones((128, NT + 1), np.float32)
            pf[:, 0:1 + r] = 0.0
            m['role'] = role
            m['pflag'] = pf
            in_maps.append(m)
    res = run_bass_kernel_spmd(nc, in_maps, core_ids=list(range(2 * B)))
    return np.stack([np.asarray(res.results[2 * b + 1]['y'], dtype=np.float32) for b in range(B)], axis=0)
```
